# Optimizing a Trainium2 kernel written in Bass

```python
import jax
import jax.numpy as jnp
from jax import lax
import numpy as np


D_MODEL = 1024
BATCH = 4
SEQ = 4096
DEPTH = 2

EPS = 1e-6
NEG_INF = -1e30
Q_BLOCK = 128

MLA_HEADS = 8
MLA_NOPE = 64
MLA_ROPE = 32
MLA_V = 64
MLA_Q_RANK = 384
MLA_KV_RANK = 256
ROPE_THETA = 10000.0

NSA_HEADS = 8
NSA_KV_HEADS = 2
NSA_REP = NSA_HEADS // NSA_KV_HEADS
NSA_DH = 64
L_CMP = 32
D_CMP = 16
CMP_HID = 128
L_SEL = 64
N_SEL = 16
W_WIN = 512

D_MIX = MLA_HEADS * MLA_V + NSA_HEADS * NSA_DH
NSA_KV = NSA_KV_HEADS * NSA_DH
D_IN = MLA_Q_RANK + MLA_KV_RANK + MLA_ROPE + NSA_HEADS * NSA_DH + 6 * NSA_KV + 3 * NSA_HEADS

D_FF = 2816
CONV_W = 3

kernel_name = 'mla_nsa_hybrid_convffn'


def rmsnorm(x, g):
    x32 = x.astype(jnp.float32)
    y = x32 * lax.rsqrt(jnp.mean(x32 * x32, axis=-1, keepdims=True) + EPS)
    return (y * g.astype(jnp.float32)).astype(x.dtype)


def alibi_slopes(n):
    return jnp.exp2(-8.0 * jnp.arange(1, n + 1, dtype=jnp.float32) / n)


def rope_tables(S, dim):
    inv = 1.0 / (ROPE_THETA ** (jnp.arange(0, dim, 2, dtype=jnp.float32) / dim))
    ang = jnp.arange(S, dtype=jnp.float32)[:, None] * inv[None, :]
    return jnp.cos(ang), jnp.sin(ang)


def apply_rope(x, cos, sin):
    x1, x2 = jnp.split(x, 2, axis=-1)
    c = cos.astype(x.dtype)
    s = sin.astype(x.dtype)
    return jnp.concatenate([x1 * c - x2 * s, x2 * c + x1 * s], axis=-1)


def mla_attention(c_q, c_kv, k_rope, q_norm, kv_norm, w_uq, w_ukv):
    B, S, _ = c_q.shape
    H = MLA_HEADS
    q = (rmsnorm(c_q, q_norm) @ w_uq).reshape(B, S, H, MLA_NOPE + MLA_ROPE)
    kv = (rmsnorm(c_kv, kv_norm) @ w_ukv).reshape(B, S, H, MLA_NOPE + MLA_V)
    q_nope, q_rope = q[..., :MLA_NOPE], q[..., MLA_NOPE:]
    k_nope, v = kv[..., :MLA_NOPE], kv[..., MLA_NOPE:]
    cos, sin = rope_tables(S, MLA_ROPE)
    q_rope = apply_rope(q_rope, cos[:, None, :], sin[:, None, :])
    k_rope = apply_rope(k_rope, cos, sin)
    scale = (MLA_NOPE + MLA_ROPE) ** -0.5
    nq = S // Q_BLOCK
    qn = q_nope.reshape(B, nq, Q_BLOCK, H, MLA_NOPE).transpose(1, 0, 2, 3, 4)
    qr = q_rope.reshape(B, nq, Q_BLOCK, H, MLA_ROPE).transpose(1, 0, 2, 3, 4)
    kpos = jnp.arange(S)

    def block(args):
        qn_b, qr_b, qb = args
        s = (jnp.einsum('bqhd,bkhd->bhqk', qn_b, k_nope)
             + jnp.einsum('bqhd,bkd->bhqk', qr_b, k_rope)).astype(jnp.float32) * scale
        qpos = qb * Q_BLOCK + jnp.arange(Q_BLOCK)
        s = jnp.where(kpos[None, :] <= qpos[:, None], s, NEG_INF)
        p = jax.nn.softmax(s, axis=-1).astype(v.dtype)
        return jnp.einsum('bhqk,bkhd->bqhd', p, v)

    o = lax.map(block, (qn, qr, jnp.arange(nq)))
    return o.transpose(1, 0, 2, 3, 4).reshape(B, S, H * MLA_V)


def nsa_attention(q, k_c, v_c, k_s, v_s, k_w, v_w, gates,
                  pos_k, pos_v, ck_w1, ck_w2, cv_w1, cv_w2):
    B, S, G, R, dh = q.shape
    dt = q.dtype
    slopes = alibi_slopes(NSA_HEADS).reshape(G, R)
    scale = dh ** -0.5
    t = jnp.arange(S)

    n_cmp = (S - L_CMP) // D_CMP + 1
    tok = jnp.arange(n_cmp)[:, None] * D_CMP + jnp.arange(L_CMP)[None, :]

    def compress(a, pos, w1, w2):
        blocks = a[:, tok] + pos[None, None, :, None, :]
        hid = jax.nn.gelu(jnp.einsum('bnlgd,lde->bnge', blocks, w1))
        return jnp.einsum('bnge,ed->bngd', hid, w2)

    kc = compress(k_c, pos_k, ck_w1, ck_w2)
    vc = compress(v_c, pos_v, cv_w1, cv_w2)
    blk_end = jnp.arange(n_cmp) * D_CMP + L_CMP - 1
    dist_c = t[:, None] - blk_end[None, :]
    mask_c = dist_c >= 0
    s_c = (jnp.einsum('bsgrd,bngd->bgrsn', q, kc).astype(jnp.float32) * scale
           - slopes[:, :, None, None] * dist_c.astype(jnp.float32))
    p_cmp = jax.nn.softmax(jnp.where(mask_c, s_c, NEG_INF), axis=-1) * mask_c
    o_cmp = jnp.einsum('bgrsn,bngd->bsgrd', p_cmp.astype(dt), vc)

    n_blk = S // L_SEL
    cs = jnp.arange(n_cmp) * D_CMP
    ss = jnp.arange(n_blk) * L_SEL
    overlap = ((cs[:, None] < ss[None, :] + L_SEL)
               & (cs[:, None] + L_CMP > ss[None, :])).astype(jnp.float32)
    imp = jnp.einsum('bgrsn,nj->bgsj', p_cmp, overlap)
    cur = (t // L_SEL)[:, None]
    j = jnp.arange(n_blk)[None, :]
    imp = jnp.where(j > cur, -jnp.inf, imp)
    imp = jnp.where((j == 0) | (j == cur) | (j == cur - 1), jnp.inf, imp)
    n_top = min(N_SEL, n_blk)
    _, idx = lax.top_k(imp, n_top)

    kb = k_s.reshape(B, n_blk, L_SEL, G, dh).transpose(0, 3, 1, 2, 4)
    vb = v_s.reshape(B, n_blk, L_SEL, G, dh).transpose(0, 3, 1, 2, 4)
    nq = S // Q_BLOCK
    q_ch = q.reshape(B, nq, Q_BLOCK, G, R, dh).transpose(1, 0, 2, 3, 4, 5)
    idx_ch = idx.reshape(B, G, nq, Q_BLOCK, n_top).transpose(2, 0, 1, 3, 4)
    bi = jnp.arange(B)[:, None, None, None]
    gi = jnp.arange(G)[None, :, None, None]

    def sel_block(args):
        q_b, idx_b, qb = args
        kg = kb[bi, gi, idx_b]
        vg = vb[bi, gi, idx_b]
        tq = qb * Q_BLOCK + jnp.arange(Q_BLOCK)
        kpos = idx_b[..., None] * L_SEL + jnp.arange(L_SEL)
        dist = (tq[None, None, :, None, None] - kpos)[:, :, None]
        s = (jnp.einsum('bqgrd,bgqnld->bgrqnl', q_b, kg).astype(jnp.float32) * scale
             - slopes[None, :, :, None, None, None] * dist.astype(jnp.float32))
        s = jnp.where(dist >= 0, s, NEG_INF).reshape(B, G, R, Q_BLOCK, n_top * L_SEL)
        p = jax.nn.softmax(s, axis=-1).reshape(B, G, R, Q_BLOCK, n_top, L_SEL).astype(dt)
        return jnp.einsum('bgrqnl,bgqnld->bqgrd', p, vg)

    o_sel = lax.map(sel_block, (q_ch, idx_ch, jnp.arange(nq)))
    o_sel = o_sel.transpose(1, 0, 2, 3, 4, 5).reshape(B, S, G, R, dh)

    n_prev = W_WIN // Q_BLOCK

    def bands(a):
        ap = jnp.pad(a, ((0, 0), (W_WIN, 0), (0, 0), (0, 0)))
        ap = ap.reshape(B, nq + n_prev, Q_BLOCK, G, dh)
        return jnp.concatenate([ap[:, i:i + nq] for i in range(n_prev + 1)], axis=2)

    kw = bands(k_w)
    vw = bands(v_w)
    qw = q.reshape(B, nq, Q_BLOCK, G, R, dh)
    tq = t.reshape(nq, Q_BLOCK)
    kpos = jnp.arange(nq)[:, None] * Q_BLOCK - W_WIN + jnp.arange((n_prev + 1) * Q_BLOCK)[None, :]
    dist_w = tq[:, :, None] - kpos[:, None, :]
    mask_w = (dist_w >= 0) & (dist_w < W_WIN) & (kpos[:, None, :] >= 0)
    s_w = (jnp.einsum('bnqgrd,bnkgd->bgrnqk', qw, kw).astype(jnp.float32) * scale
           - slopes[:, :, None, None, None] * dist_w.astype(jnp.float32))
    p_w = jax.nn.softmax(jnp.where(mask_w, s_w, NEG_INF), axis=-1).astype(dt)
    o_win = jnp.einsum('bgrnqk,bnkgd->bnqgrd', p_w, vw).reshape(B, S, G, R, dh)

    g = jax.nn.sigmoid(gates.astype(jnp.float32)).astype(dt)
    o = g[..., 0:1] * o_cmp + g[..., 1:2] * o_sel + g[..., 2:3] * o_win
    return o.reshape(B, S, G * R * dh)


def token_mixer(n, w_in, q_norm, kv_norm, w_uq, w_ukv, pos_k, pos_v,
                ck_w1, ck_w2, cv_w1, cv_w2, w_o):
    B, S, _ = n.shape
    z = n @ w_in
    sizes = [MLA_Q_RANK, MLA_KV_RANK, MLA_ROPE, NSA_HEADS * NSA_DH] + [NSA_KV] * 6 + [3 * NSA_HEADS]
    offs = [int(o) for o in np.cumsum(sizes)[:-1]]
    (c_q, c_kv, k_rope, q_n, kc, vc, ks, vs, kwin, vwin, gt) = jnp.split(z, offs, axis=-1)
    o_mla = mla_attention(c_q, c_kv, k_rope, q_norm, kv_norm, w_uq, w_ukv)
    G, R, dh = NSA_KV_HEADS, NSA_REP, NSA_DH
    kvs = (B, S, G, dh)
    o_nsa = nsa_attention(q_n.reshape(B, S, G, R, dh),
                          kc.reshape(kvs), vc.reshape(kvs), ks.reshape(kvs), vs.reshape(kvs),
                          kwin.reshape(kvs), vwin.reshape(kvs), gt.reshape(B, S, G, R, 3),
                          pos_k, pos_v, ck_w1, ck_w2, cv_w1, cv_w2)
    return jnp.concatenate([o_mla, o_nsa], axis=-1) @ w_o


def conv_ffn(x, norm_g, w_up, conv_w, conv_b, w_down):
    S = x.shape[1]
    h = rmsnorm(x, norm_g) @ w_up
    hp = jnp.pad(h, ((0, 0), (CONV_W - 1, 0), (0, 0)))
    hc = conv_b
    for i in range(CONV_W):
        hc = hc + hp[:, i:i + S] * conv_w[i]
    gate, up = jnp.split(hc, 2, axis=-1)
    return (jax.nn.silu(gate) * up) @ w_down


def setup_inputs(seed: int = 0) -> dict:
    key = jax.random.key(seed)
    ks = jax.random.split(key, 20)
    f32 = jnp.float32
    L = DEPTH

    def nrm(k, shape, fan_in):
        return jax.random.normal(k, shape, f32) * (fan_in ** -0.5)

    def gain(k, shape):
        return 1.0 + 0.01 * jax.random.normal(k, shape, f32)

    return {
        'x': jax.random.normal(ks[0], (BATCH, SEQ, D_MODEL), f32),
        'attn_norm': gain(ks[1], (L, D_MODEL)),
        'w_in': nrm(ks[2], (L, D_MODEL, D_IN), D_MODEL),
        'q_norm': gain(ks[3], (L, MLA_Q_RANK)),
        'kv_norm': gain(ks[4], (L, MLA_KV_RANK)),
        'w_uq': nrm(ks[5], (L, MLA_Q_RANK, MLA_HEADS * (MLA_NOPE + MLA_ROPE)), MLA_Q_RANK),
        'w_ukv': nrm(ks[6], (L, MLA_KV_RANK, MLA_HEADS * (MLA_NOPE + MLA_V)), MLA_KV_RANK),
        'cmp_pos_k': 0.1 * jax.random.normal(ks[7], (L, L_CMP, NSA_DH), f32),
        'cmp_pos_v': 0.1 * jax.random.normal(ks[8], (L, L_CMP, NSA_DH), f32),
        'cmp_k_w1': nrm(ks[9], (L, L_CMP, NSA_DH, CMP_HID), L_CMP * NSA_DH),
        'cmp_k_w2': nrm(ks[10], (L, CMP_HID, NSA_DH), CMP_HID),
        'cmp_v_w1': nrm(ks[11], (L, L_CMP, NSA_DH, CMP_HID), L_CMP * NSA_DH),
        'cmp_v_w2': nrm(ks[12], (L, CMP_HID, NSA_DH), CMP_HID),
        'w_o': nrm(ks[13], (L, D_MIX, D_MODEL), D_MIX),
        'ffn_norm': gain(ks[14], (L, D_MODEL)),
        'w_up': nrm(ks[15], (L, D_MODEL, 2 * D_FF), D_MODEL),
        'conv_w': nrm(ks[16], (L, CONV_W, 2 * D_FF), CONV_W),
        'conv_b': 0.01 * jax.random.normal(ks[17], (L, 2 * D_FF), f32),
        'w_down': nrm(ks[18], (L, D_FF, D_MODEL), D_FF),
        'final_norm': gain(ks[19], (D_MODEL,)),
    }


def reference(x, attn_norm, w_in, q_norm, kv_norm, w_uq, w_ukv, cmp_pos_k, cmp_pos_v,
              cmp_k_w1, cmp_k_w2, cmp_v_w1, cmp_v_w2, w_o, ffn_norm, w_up, conv_w,
              conv_b, w_down, final_norm):
    h = x
    for l in range(DEPTH):
        h = h + token_mixer(rmsnorm(h, attn_norm[l]), w_in[l], q_norm[l], kv_norm[l],
                            w_uq[l], w_ukv[l], cmp_pos_k[l], cmp_pos_v[l],
                            cmp_k_w1[l], cmp_k_w2[l], cmp_v_w1[l], cmp_v_w2[l], w_o[l])
        h = h + conv_ffn(h, ffn_norm[l], w_up[l], conv_w[l], conv_b[l], w_down[l])
    return rmsnorm(h, final_norm)
```

```python
import numpy as np
import ml_dtypes
import concourse.bass as bass
import concourse.mybir as mybir
from concourse.bass_utils import run_bass_kernel_spmd

F32 = mybir.dt.float32
BF16 = mybir.dt.bfloat16
ALU = mybir.AluOpType
AF = mybir.ActivationFunctionType
AX = mybir.AxisListType
NPBF = ml_dtypes.bfloat16

ENGS = ["pe", "act", "dve", "pool", "sp"]
DMA_POOL = 12
S = 4096
DM = 1024
NEG = -30000.0
SC_MLA = 96 ** -0.5
SC_NSA = 0.125
EPS = 1e-6


class Prog:
    def __init__(self, nc):
        self.nc = nc
        self.ops = {e: [] for e in ENGS}
        self.lastw = {}
        self.readers = {}
        self.dma_n = {e: 0 for e in ENGS + ["cc"]}
        self.dma_sem_cnt = {}
        self.last_c = {}
        self.last_d = {}

    def sb(self, name, shape, dt):
        return self.nc.alloc_sbuf_tensor(name, list(shape), dt)

    def ps(self, name, shape, dt=F32):
        return self.nc.alloc_psum_tensor(name, list(shape), dt)

    def _add(self, eng, fn, reads, writes, dma, cc=False):
        op = dict(eng=eng, fn=fn, deps=[], dma=dma, marked=False, inc=(1 if cc else 16))
        deps = []
        for k in reads:
            w = self.lastw.get(k)
            if w is not None:
                deps.append(w)
        for k in writes:
            w = self.lastw.get(k)
            if w is not None:
                deps.append(w)
            deps.extend(self.readers.get(k, ()))
        seen = set()
        for d in deps:
            if id(d) in seen or d is op:
                continue
            seen.add(id(d))
            if (not d["dma"]) and d["eng"] == eng and eng in ("pe", "sp"):
                continue
            op["deps"].append(d)
            d["marked"] = True
        if dma:
            qn = "cc" if cc else eng
            q = self.dma_n[qn]
            self.dma_n[qn] += 1
            semkey = (qn, q % (4 if cc else DMA_POOL))
            m = self.dma_sem_cnt.get(semkey, 0) + 1
            self.dma_sem_cnt[semkey] = m
            op["dsem"] = semkey
            op["dval"] = op["inc"] * m
            op["marked"] = True
            self.last_d[semkey] = op
        else:
            self.last_c[eng] = op
        for k in reads:
            self.readers.setdefault(k, []).append(op)
        for k in writes:
            self.lastw[k] = op
            self.readers[k] = []
        self.ops[eng].append(op)
        return op

    def op(self, eng, fn, reads=(), writes=()):
        return self._add(eng, fn, list(reads), list(writes), False)

    def dma(self, eng, out, in_, reads=(), writes=()):
        return self._add(eng, lambda e: e.dma_start(out=out, in_=in_), list(reads), list(writes), True)

    def cc(self, kind, groups, in_ap, out_ap, reads=(), writes=()):
        return self._add("pool", lambda e: e.collective_compute(kind, ALU.bypass, replica_groups=groups,
                                                                ins=[in_ap], outs=[out_ap]),
                         list(reads), list(writes), True, cc=True)

    def barrier(self):
        deps = list(self.last_c.values()) + list(self.last_d.values())
        for d in deps:
            d["marked"] = True
        for e in ENGS:
            self.ops[e].append(dict(eng=e, fn=None, deps=list(deps), dma=False, marked=False, inc=0))
        self.lastw = {}
        self.readers = {}

    def emit(self):
        nc = self.nc
        csem = {e: nc.alloc_semaphore("c_" + e) for e in ENGS}
        dsem = {}
        for (qn, i) in self.dma_sem_cnt:
            dsem[(qn, i)] = nc.alloc_semaphore("d_%s_%d" % (qn, i))
        for e in ENGS:
            c = 0
            for o in self.ops[e]:
                if o["dma"] or o["fn"] is None:
                    continue
                if o["marked"]:
                    c += 1
                    o["cval"] = c
        all_dma = [o for e in ENGS for o in self.ops[e] if o["dma"]]

        def run(e, eng):
            seen = {}

            def wait(sem_key, sem, val):
                if seen.get(sem_key, 0) >= val:
                    return
                seen[sem_key] = val
                eng.wait_ge(sem, val)

            for o in self.ops[e]:
                for d in o["deps"]:
                    if d["dma"]:
                        wait(d["dsem"], dsem[d["dsem"]], d["dval"])
                    else:
                        wait(("c", d["eng"]), csem[d["eng"]], d["cval"])
                if o["fn"] is None:
                    continue
                if o["dma"]:
                    if o["dval"] > o["inc"]:
                        wait(o["dsem"], dsem[o["dsem"]], o["dval"] - o["inc"])
                    o["fn"](eng).then_inc(dsem[o["dsem"]], o["inc"])
                else:
                    ins = o["fn"](eng)
                    if o["marked"]:
                        ins.then_inc(csem[e], 1)
            if e == "sp":
                last = {}
                for o in all_dma:
                    last[o["dsem"]] = max(last.get(o["dsem"], 0), o["dval"])
                for k, v in last.items():
                    eng.wait_ge(dsem[k], v)

        with nc.Block() as block:
            @block.tensor
            def _(eng):
                run("pe", eng)

            @block.scalar
            def _(eng):
                run("act", eng)

            @block.vector
            def _(eng):
                run("dve", eng)

            @block.gpsimd
            def _(eng):
                run("pool", eng)

            @block.sync
            def _(eng):
                run("sp", eng)


def _nbytes(shape, dt):
    n = 1
    for d in shape[1:]:
        n *= d
    return n * (4 if dt == F32 else 2)


class Ctx:
    def __init__(self, nc):
        self.nc = nc
        self.P = P = Prog(nc)
        self.rots = {}
        self.pname = "g_"
        self.ident = nc.alloc_sbuf_tensor("ident", [128, 128], BF16)
        self.identf = nc.alloc_sbuf_tensor("identf", [128, 128], F32)
        self.onesf = nc.alloc_sbuf_tensor("onesf", [128, 128], F32)
        self.epsn = nc.alloc_sbuf_tensor("epsn", [128, 1], F32)
        self.flags = nc.alloc_sbuf_tensor("flags_sb", [128, 2], F32)
        identf, ident, onesf, epsn = self.identf, self.ident, self.onesf, self.epsn
        P.op("pool", lambda e: e.memset(identf[:], 0.0), writes=["identf"])
        P.op("pool", lambda e: e.affine_select(out=identf[:], in_=identf[:], pattern=[[-1, 128]],
                                                compare_op=ALU.not_equal, fill=1.0, base=0, channel_multiplier=1),
             reads=["identf"], writes=["identf"])
        P.op("dve", lambda e: e.tensor_copy(out=ident[:], in_=identf[:]), reads=["identf"], writes=["ident"])
        P.op("dve", lambda e: e.memset(onesf[:], 1.0), writes=["onesf"])
        P.op("dve", lambda e: e.memset(epsn[:], EPS), writes=["epsn"])
        self.pst = P.ps("pst", [128, 1024], BF16)
        self.banks = [P.ps("bank%d" % i, [128, 512], F32) for i in range(7)]
        self.banks.append(self.pst.bitcast(F32))
        self.base = ((int(nc.sbuf_base) + 63) // 64) * 64
        self.top = int(nc.sbuf_top)
        self.off = self.base

    def begin_phase(self, name):
        self.P.barrier()
        self.pname = name
        self.off = self.base

    def sb(self, name, shape, dt):
        nb = ((_nbytes(shape, dt) + 31) // 32) * 32
        assert self.off + nb <= self.top, ("SBUF overflow", self.pname, name, self.off + nb - self.top)
        t = self.nc.alloc_sbuf_tensor_at(self.pname + name, list(shape), dt, offset=self.off)
        self.off += nb
        return t

    def rot(self, name, n):
        i = self.rots.get(name, 0) % n
        self.rots[name] = (i + 1) % n
        return i

    def dram(self, name, shape, dt, out=False):
        return self.nc.dram_tensor(name, list(shape), dt, kind="ExternalOutput" if out else "ExternalInput").ap()

    def mm(self, out, lhsT, rhs, start, stop, reads, writes):
        return self.P.op("pe", lambda e: e.matmul(out, lhsT=lhsT, rhs=rhs, start=start, stop=stop), reads, writes)

    def mmg(self, out, pairs, reads, writes, start=True, stop=True):
        n = len(pairs)

        def fn(e):
            ins = None
            for i, (l, r) in enumerate(pairs):
                ins = e.matmul(out, lhsT=l, rhs=r, start=(start and i == 0), stop=(stop and i == n - 1))
            return ins
        return self.P.op("pe", fn, reads, writes)

    def mmlist(self, items, reads, writes):
        def fn(e):
            ins = None
            for (o, l, r, st, sp) in items:
                ins = e.matmul(o, lhsT=l, rhs=r, start=st, stop=sp)
            return ins
        return self.P.op("pe", fn, reads, writes)

    def tr(self, out, in_, ident, reads, writes):
        return self.P.op("pe", lambda e: e.transpose(out, in_, ident), reads, writes)

    def act(self, out, in_, func, reads, writes, bias=None, scale=1.0, accum=None):
        kw = {}
        if bias is not None:
            kw["bias"] = bias
        if accum is not None:
            kw["accum_out"] = accum
        return self.P.op("act", lambda e: e.activation(out=out, in_=in_, func=func, scale=scale, **kw), reads, writes)

    def tt(self, eng, out, in0, in1, op, reads, writes):
        return self.P.op(eng, lambda e: e.tensor_tensor(out=out, in0=in0, in1=in1, op=op), reads, writes)

    def ts(self, eng, out, in0, s1, op0, reads, writes, s2=None, op1=None):
        if op1 is None:
            return self.P.op(eng, lambda e: e.tensor_scalar(out=out, in0=in0, scalar1=s1, scalar2=None, op0=op0), reads, writes)
        return self.P.op(eng, lambda e: e.tensor_scalar(out=out, in0=in0, scalar1=s1, scalar2=s2, op0=op0, op1=op1), reads, writes)

    def stt(self, eng, out, in0, scalar, in1, op0, op1, reads, writes):
        return self.P.op(eng, lambda e: e.scalar_tensor_tensor(out=out, in0=in0, scalar=scalar, in1=in1, op0=op0, op1=op1), reads, writes)

    def cp(self, eng, out, in_, reads, writes):
        if eng == "act":
            return self.P.op("act", lambda e: e.copy(out=out, in_=in_), reads, writes)
        return self.P.op(eng, lambda e: e.tensor_copy(out=out, in_=in_), reads, writes)

    def recip(self, out, in_, reads, writes):
        return self.P.op("dve", lambda e: e.reciprocal(out=out, in_=in_), reads, writes)

    def memset(self, eng, ap, val, writes):
        return self.P.op(eng, lambda e: e.memset(ap, val), [], writes)

    def bank(self, grp, idxs):
        i = idxs[self.rot(grp, len(idxs))]
        return self.banks[i], ("pst" if i == 7 else "bank%d" % i)

    def load_w(self, name, w_dram, nk, ncols):
        wsb = self.sb(name, [128, nk, ncols], BF16)
        for k in range(nk):
            self.P.dma("pool", wsb[:, k, :], w_dram[k * 128:(k + 1) * 128, :], writes=[name])
        return wsb

    def load_const(self, name, dram_ap, shape, dt, eng="pool"):
        t = self.sb(name, shape, dt)
        self.P.dma(eng, t[:], dram_ap, writes=[name])
        return t

    def setup_norm(self):
        self.ht = [self.sb("ht%d" % i, [128, DM], F32) for i in range(2)]
        self.junk = self.sb("junk", [128, DM], BF16)
        self.nb = self.sb("nb", [128, DM], BF16)
        self.ss = [self.sb("ss%d" % i, [128, 1], F32) for i in range(2)]

    def norm_T(self, src, srckey, nT, nkey, col0, g_sb, gkey):
        b = self.rot("ss", 2)
        ss = self.ss[b]
        sk = "ss%d" % b
        self.memset("dve", ss[:], 0.0, [sk])
        self.act(self.junk[:], src, AF.Square, [srckey, sk], ["junk", sk], accum=ss[:])
        self.act(ss[:], ss[:], AF.Sqrt, [sk, "epsn"], [sk], bias=self.epsn[:], scale=1.0 / DM)
        self.recip(ss[:], ss[:], [sk], [sk])
        self.ts("dve", self.nb[:], src, ss[:, 0:1], ALU.mult, [srckey, sk], ["nb"])
        pst = self.pst
        nb, ident = self.nb, self.ident

        def fn(e):
            ins = None
            for k in range(8):
                ins = e.transpose(pst[:, k * 128:(k + 1) * 128], nb[:, k * 128:(k + 1) * 128], ident[:])
            return ins
        self.P.op("pe", fn, ["nb", "ident"], ["pst"])
        self.tt("dve", nT[:, :, col0:col0 + 128], pst[:, :].rearrange("p (k t) -> p k t", k=8),
                g_sb[:, :].unsqueeze(2).to_broadcast([128, 8, 128]), ALU.mult, ["pst", gkey], [nkey])


def attn_loop(C, tiles, score_fn, scale, v_fn, po, pok, sbanks, pts, ptname, after_first=None, depth=2):
    n = len(tiles)
    issued = []

    def issue(i):
        sbk, sk = C.bank("s", sbanks)
        pairs, rd = score_fn(tiles[i])
        C.mmg(sbk[:, :], pairs, rd, [sk])
        issued.append((sbk, sk))
    for i in range(min(depth, n)):
        issue(i)
    if after_first is not None:
        after_first()
    for i in range(n):
        sbk, sk = issued[i]
        pi = C.rot(ptname, len(pts))
        C.act(pts[pi][:], sbk[:, :], AF.Exp, [sk], ["%s%d" % (ptname, pi)], scale=scale)
        if i + depth < n:
            issue(i + depth)
        lhsT, rd = v_fn(tiles[i])
        C.mm(po[:, :], lhsT, pts[pi][:], i == 0, i == n - 1, rd + ["%s%d" % (ptname, pi)], [pok])


def phase_mla(C, name, L, G, hsrc, hkey, osink):
    P = C.P
    C.begin_phase(name)
    C.setup_norm()
    gA = C.load_const("gA_sb", L["gAm"], [128, 8], F32)
    gq = C.load_const("gq_sb", L["gq"], [128, 3], F32)
    gkv = C.load_const("gkv_sb", L["gkv"], [128, 2], F32)
    dmask = C.load_const("dmask_sb", G["dmask"], [128, 2048], BF16)
    wA = C.load_w("wA_sb", L["wAm"], 8, 832)
    wq = C.load_w("wq_sb", L["wq"], 3, 768)
    wkv = C.load_w("wkv_sb", L["wkv"], 2, 512)
    ropeC_d, ropeS_d = G["ropeC"], G["ropeS"]

    Kh = C.sb("Kh", [128, 4, S], BF16)
    C.memset("pool", Kh[96:128, :, :], 0.0, ["Kh_pad"])
    Vt = C.sb("Vt", [128, 32, 4, 128], BF16)
    C.memset("pool", Vt[:, :, :, 64:65], 1.0, ["Vt_%d" % c for c in range(8)])
    C.memset("pool", Vt[:, :, :, 65:128], 0.0, ["Vt_pad"])
    nTs = [C.sb("nT%d" % i, [128, 8, 512], BF16) for i in range(2)]
    Qhs = [C.sb("Qh%d" % i, [128, 4, 512], BF16) for i in range(2)]
    for i in range(2):
        C.memset("pool", Qhs[i][96:128, :, :], 0.0, ["Qh_pad"])
    zf = C.sb("zf", [128, 3, 512], F32)
    sq = C.sb("sq", [128, 3, 512], F32)
    rr = C.sb("rr", [128, 512], F32)
    cqn = C.sb("cqn", [128, 3, 512], BF16)
    ckvn = C.sb("ckvn", [128, 2, 512], BF16)
    Ct = C.sb("Ct", [96, 512], F32)
    St = C.sb("St", [96, 512], F32)
    t1 = C.sb("t1", [96, 512], F32)
    t2 = C.sb("t2", [96, 512], F32)
    pts = [C.sb("pt%d" % i, [128, 512], BF16) for i in range(4)]
    rsrow = C.sb("rsrow", [65, 512], F32)
    bcs = C.sb("bcs", [64, 512], F32)
    ots = [C.sb("ot%d" % i, [64, 512], BF16) for i in range(2)]

    PJ = [0, 1]
    SB_ = [2, 3, 4]
    PO = [5, 6]
    pending = []

    def flush():
        while pending:
            pending.pop(0)()

    def latent(c0, nm, dim, dst, dkey, nT, nkeys, gl, glkey):
        for m in range(nm):
            pj, pk = C.bank("pj", PJ)
            C.mmg(pj[:, :], [(wA[:, k, c0 + m * 128:c0 + (m + 1) * 128], nT[:, k, :]) for k in range(8)],
                  ["wA_sb"] + nkeys, [pk])
            C.act(zf[:, m, :], pj[:, :], AF.Copy, [pk], ["zf%d" % m])
            C.act(sq[:, m, :], pj[:, :], AF.Square, [pk], ["sq%d" % m])
        pj, pk = C.bank("pj", PJ)
        C.mmg(pj[:, :], [(C.onesf[:, :], sq[:, m, :]) for m in range(nm)], ["onesf"] + ["sq%d" % m for m in range(nm)], [pk])
        C.act(rr[:], pj[:, :], AF.Sqrt, [pk, "epsn"], ["rr"], bias=C.epsn[:], scale=1.0 / dim)
        C.recip(rr[:], rr[:], ["rr"], ["rr"])
        for m in range(nm):
            C.stt("dve", dst[:, m, :], zf[:, m, :], gl[:, m:m + 1], rr[:], ALU.mult, ALU.mult, ["zf%d" % m, "rr", glkey], [dkey])

    def stage_T(tc):
        nT = nTs[tc % 2]
        nkeys = []
        for ti in range(4):
            hb = C.rot("ht", 2)
            P.dma("sp", C.ht[hb][:], hsrc(4 * tc + ti), reads=[hkey(4 * tc + ti)], writes=["ht%d" % hb])
            nk = "nT%d_%d" % (tc % 2, ti)
            C.norm_T(C.ht[hb][:], "ht%d" % hb, nT, nk, ti * 128, gA, "gA_sb")
            nkeys.append(nk)
        return nT, nkeys

    def stage_P1(tc, nT, nkeys):
        t0 = tc * 512
        latent(0, 3, 384.0, cqn, "cqn", nT, nkeys, gq, "gq_sb")
        latent(384, 2, 256.0, ckvn, "ckvn", nT, nkeys, gkv, "gkv_sb")
        P.dma("sp", Ct[64:96, :], ropeC_d[:, t0:t0 + 512], writes=["Ct"])
        P.dma("sp", St[64:96, :], ropeS_d[:, t0:t0 + 512], writes=["St"])
        pA, pAk = C.bank("pj", PJ)
        C.mmg(pA[0:96, :], [(wA[:, k, 640:736], nT[:, k, :]) for k in range(8)], ["wA_sb"] + nkeys, [pAk])
        C.tt("dve", t1[64:96, :], pA[64:96, :], Ct[64:96, :], ALU.mult, [pAk, "Ct"], ["t1"])
        pB, pBk = C.bank("pj", PJ)
        C.mmg(pB[0:96, :], [(wA[:, k, 736:832], nT[:, k, :]) for k in range(8)], ["wA_sb"] + nkeys, [pBk])
        C.tt("dve", t2[64:96, :], pB[64:96, :], St[64:96, :], ALU.mult, [pBk, "St"], ["t2"])
        for hh in range(4):
            C.tt("pool", Kh[64:96, hh, t0:t0 + 512], t1[64:96, :], t2[64:96, :], ALU.add, ["t1", "t2"], ["Kh_%d" % tc])

    def stage_P2(tc):
        t0 = tc * 512
        Qh = Qhs[tc % 2]
        qk = "Qh%d" % (tc % 2)
        for hh in range(4):
            pA, pAk = C.bank("pj", PJ)
            C.mmg(pA[0:96, :], [(wq[:, m, hh * 192:hh * 192 + 96], cqn[:, m, :]) for m in range(3)], ["wq_sb", "cqn"], [pAk])
            C.cp("act", Qh[0:64, hh, :], pA[0:64, :], [pAk], [qk])
            C.tt("dve", t1[64:96, :], pA[64:96, :], Ct[64:96, :], ALU.mult, [pAk, "Ct"], ["t1"])
            pB, pBk = C.bank("pj", PJ)
            C.mmg(pB[0:96, :], [(wq[:, m, hh * 192 + 96:hh * 192 + 192], cqn[:, m, :]) for m in range(3)], ["wq_sb", "cqn"], [pBk])
            C.tt("dve", t2[64:96, :], pB[64:96, :], St[64:96, :], ALU.mult, [pBk, "St"], ["t2"])
            C.tt("pool", Qh[64:96, hh, :], t1[64:96, :], t2[64:96, :], ALU.add, ["t1", "t2"], [qk])
        for hh in range(4):
            pj, pk = C.bank("pj", PJ)
            C.mmg(pj[0:64, :], [(wkv[:, j, hh * 64:(hh + 1) * 64], ckvn[:, j, :]) for j in range(2)], ["wkv_sb", "ckvn"], [pk])
            C.cp("act", Kh[0:64, hh, t0:t0 + 512], pj[0:64, :], [pk], ["Kh_%d" % tc])
        for ti in range(4):
            pj, pk = C.bank("pj", PJ)
            C.mmg(pj[:, 0:256], [(ckvn[:, j, ti * 128:(ti + 1) * 128], wkv[:, j, 256:512]) for j in range(2)], ["wkv_sb", "ckvn"], [pk])
            C.cp("act", Vt[:, 4 * tc + ti, :, 0:64], pj[:, 0:256].rearrange("p (h d) -> p h d", h=4), [pk], ["Vt_%d" % tc])

    def head(tc, hh):
        Qh = Qhs[tc % 2]
        qk = "Qh%d" % (tc % 2)
        po, pok = C.bank("po", PO)
        nkt = 4 * tc + 4

        def score(j):
            pairs = [(Kh[:, hh, j * 128:(j + 1) * 128], Qh[:, hh, :])]
            rd = [qk, "Kh_%d" % (j // 4), "Kh_pad", "Qh_pad"]
            if j >= 4 * tc:
                m = j - 4 * tc
                pairs.append((C.ident[:, :], dmask[:, m * 512:(m + 1) * 512]))
                rd += ["ident", "dmask_sb"]
            return pairs, rd

        def vfn(j):
            return Vt[:, j, hh, :], ["Vt_%d" % (j // 4), "Vt_pad"]
        attn_loop(C, list(range(nkt)), score, SC_MLA, vfn, po, pok, SB_, pts, "pt", after_first=flush)

        def fin():
            C.ts("dve", rsrow[64:65, :], po[64:65, :], 1e-30, ALU.add, [pok], ["rsrow"])
            C.recip(rsrow[64:65, :], rsrow[64:65, :], ["rsrow"], ["rsrow"])
            pj, pk = C.bank("pj", PJ)
            C.mm(pj[0:64, :], C.onesf[64:65, 0:64], rsrow[64:65, :], True, True, ["onesf", "rsrow"], [pk])
            C.cp("act", bcs[:, :], pj[0:64, :], [pk], ["bcs"])
            oi = C.rot("ot", 2)
            C.tt("dve", ots[oi][:, :], po[0:64, :], bcs[:, :], ALU.mult, [pok, "bcs"], ["ot%d" % oi])
            osink(hh, tc, ots[oi], "ot%d" % oi)
        pending.append(fin)

    st = stage_T(0)
    stage_P1(0, *st)
    stage_P2(0)
    for tc in range(8):
        head(tc, 0)
        if tc < 7:
            st = stage_T(tc + 1)
        head(tc, 1)
        if tc < 7:
            stage_P1(tc + 1, *st)
        head(tc, 2)
        if tc < 7:
            stage_P2(tc + 1)
        head(tc, 3)
    flush()


def phase_nsa(C, name, L, G, hsrc, hkey, osink):
    P = C.P
    C.begin_phase(name)
    NW = 780
    C.setup_norm()
    gA = C.load_const("gA_sb", L["gAn"], [128, 8], F32)
    wA = C.load_w("wA_sb", L["wAn"], 8, NW)
    w1k = C.load_w("w1k_sb", L["w1k"], 16, 128)
    w1v = C.load_w("w1v_sb", L["w1v"], 16, 128)
    w2k = C.load_w("w2k_sb", L["w2k"], 1, 64)
    w2v = C.load_w("w2v_sb", L["w2v"], 1, 64)
    posk = C.load_w("posk_sb", L["posk"], 1, 16)
    posv = C.load_w("posv_sb", L["posv"], 1, 16)
    maskc = C.load_const("maskc_sb", G["maskc"], [128, 2 * S], BF16)
    eall = C.sb("eall_sb", [128, S], BF16)
    C.memset("pool", eall[64:128, :], 0.0, ["eall_sb"])
    P.dma("pool", eall[0:64, :], G["eall"], writes=["eall_sb"])
    selbias = C.load_const("selbias_sb", G["selbias"], [128, 2048], BF16)
    ovl = C.load_const("ovl_sb", G["ovl"], [128, 130], BF16)
    selg = C.load_const("selg_sb", G["selg"], [12, 768], F32)
    dm4 = C.load_const("dm4_sb", G["dm4"], [128, 512], BF16)
    wm4 = C.load_const("wm4_sb", G["wm4"], [128, 512], BF16)

    Qa = C.sb("Qa", [128, 32, 512], BF16)
    Kw = C.sb("Kw", [128, S], BF16)
    Ks = C.sb("Ks", [128, S], BF16)
    Kc = C.sb("Kc", [128, 256], BF16)
    C.memset("pool", Qa[64:128, :, :], 0.0, ["Qa_aug"])
    C.memset("pool", Kw[64:128, :], 0.0, ["Kw_aug"])
    C.memset("pool", Ks[64:128, :], 0.0, ["Ks_aug"])
    C.memset("pool", Kc[64:128, :], 0.0, ["Kc_aug"])
    P.dma("pool", Qa[64:68, :, :], G["qaug"].rearrange("p (a b) -> p a b", a=32), writes=["Qa_aug"])
    P.dma("pool", Kw[64:68, :], G["kaug"], writes=["Kw_aug"])
    P.dma("pool", Ks[64:68, :], G["kaug"], writes=["Ks_aug"])
    P.dma("pool", Kc[64:68, :], G["kaugc"], writes=["Kc_aug"])
    kc2 = C.sb("kc2", [128, S + 32], BF16)
    vc2 = C.sb("vc2", [128, S + 32], BF16)
    C.memset("pool", kc2[:, S:S + 32], 0.0, ["kc2_tail"])
    C.memset("pool", vc2[:, S:S + 32], 0.0, ["vc2_tail"])
    Vs = C.sb("Vs", [128, 32, 128], BF16)
    Vw = C.sb("Vw", [128, 32, 128], BF16)
    Vc = C.sb("Vc", [128, 2, 128], BF16)
    for (vt_, keys_) in ((Vs, ["Vs_%d" % c for c in range(8)]), (Vw, ["Vw_%d" % c for c in range(8)]), (Vc, ["Vc"])):
        C.memset("pool", vt_[:, :, 65:128], 0.0, keys_)
        C.memset("pool", vt_[:, :, 64:65], 1.0, keys_)
    Gs = C.sb("Gs", [12, S], F32)
    nTs = [C.sb("nT%d" % i, [128, 8, 512], BF16) for i in range(2)]

    PJ = [0, 1]
    SB_ = [2, 3]
    POC, POS, POW = 4, 5, 6

    for tc in range(8):
        t0 = tc * 512
        nb_ = C.rot("nT", 2)
        nT = nTs[nb_]
        nkeys = []
        for ti in range(4):
            hb = C.rot("ht", 2)
            P.dma("sp", C.ht[hb][:], hsrc(4 * tc + ti), reads=[hkey(4 * tc + ti)], writes=["ht%d" % hb])
            nk = "nT%d_%d" % (nb_, ti)
            C.norm_T(C.ht[hb][:], "ht%d" % hb, nT, nk, ti * 128, gA, "gA_sb")
            nkeys.append(nk)

        def proj(c0, m, rows=128):
            pj, pk = C.bank("pj", PJ)
            C.mmg(pj[0:rows, :], [(wA[:, k, c0:c0 + m], nT[:, k, :]) for k in range(8)], ["wA_sb"] + nkeys, [pk])
            return pj, pk
        for r in range(4):
            pj, pk = proj(r * 64, 64, 64)
            C.cp("act", Qa[0:64, 4 * tc:4 * tc + 4, r * 128:(r + 1) * 128],
                 pj[0:64, :].rearrange("p (a b) -> p a b", a=4), [pk], ["Qa_%d" % tc])
        for (c0, dst, dk) in ((256, kc2, "kc2"), (384, vc2, "vc2")):
            pj, pk = proj(c0, 128)
            C.cp("act", dst[0:64, t0:t0 + 512], pj[0:64, :], [pk], [dk])
            if tc == 0:
                C.cp("dve", dst[64:128, 0:511], pj[64:128, 1:512], [pk], [dk])
            else:
                C.cp("dve", dst[64:128, t0 - 1:t0 + 511], pj[64:128, :], [pk], [dk])
        pj, pk = proj(512, 64, 64)
        C.cp("act", Ks[0:64, t0:t0 + 512], pj[0:64, :], [pk], ["Ks_%d" % tc])
        pj, pk = proj(576, 64, 64)
        C.cp("act", Kw[0:64, t0:t0 + 512], pj[0:64, :], [pk], ["Kw_%d" % tc])
        pj, pk = proj(768, 12, 12)
        C.act(Gs[0:12, t0:t0 + 512], pj[0:12, :], AF.Sigmoid, [pk], ["Gs_%d" % tc])
        for ti in range(4):
            pj, pk = C.bank("pj", PJ)
            C.mmg(pj[:, 0:128], [(nT[:, k, ti * 128:(ti + 1) * 128], wA[:, k, 640:768]) for k in range(8)],
                  ["wA_sb"] + nkeys, [pk])
            C.cp("act", Vs[:, 4 * tc + ti, 0:64], pj[:, 0:64], [pk], ["Vs_%d" % tc])
            C.cp("dve", Vw[:, 4 * tc + ti, 0:64], pj[:, 64:128], [pk], ["Vw_%d" % tc])

    xs = C.sb("xs", [128, 256], F32)
    x2 = C.sb("x2", [128, 256], F32)
    hid = C.sb("hid", [128, 256], BF16)
    cbias = C.sb("cbias", [128, 1], F32)
    for (src, skey, w1, w1key, pos, poskey, isk) in ((kc2, "kc2", w1k, "w1k_sb", posk, "posk_sb", True),
                                                     (vc2, "vc2", w1v, "w1v_sb", posv, "posv_sb", False)):
        pj, pk = C.bank("pj", PJ)
        C.mmg(pj[:, 0:1], [(w1[:, j, :], pos[:, 0, j:j + 1]) for j in range(16)], [w1key, poskey], [pk])
        C.cp("act", cbias[:], pj[:, 0:1], [pk], ["cbias"])
        pj, pk = C.bank("pj", PJ)
        C.mmg(pj[:, 0:255], [(w1[:, j, :], src[:, 2 * j:2 * j + 16 * 255:16]) for j in range(16)],
              [w1key, skey, skey + "_tail"], [pk])
        C.memset("dve", xs[:, 255:256], 0.0, ["xs"])
        C.act(xs[:, 0:255], pj[:, 0:255], AF.Identity, [pk, "cbias"], ["xs"], bias=cbias[:])
        C.tt("dve", x2[:], xs[:], xs[:], ALU.mult, ["xs"], ["x2"])
        C.ts("dve", x2[:], x2[:], 0.044715, ALU.mult, ["x2"], ["x2"], s2=1.0, op1=ALU.add)
        C.tt("dve", x2[:], x2[:], xs[:], ALU.mult, ["x2", "xs"], ["x2"])
        C.act(x2[:], x2[:], AF.Sigmoid, ["x2"], ["x2"], scale=1.5957691216057308)
        C.tt("dve", hid[:], xs[:], x2[:], ALU.mult, ["x2", "xs"], ["hid"])
        if isk:
            pj, pk = C.bank("pj", PJ)
            C.mm(pj[0:64, 0:256], w2k[:, 0, :], hid[:], True, True, ["w2k_sb", "hid"], [pk])
            C.cp("act", Kc[0:64, :], pj[0:64, 0:256], [pk], ["Kc"])
        else:
            for nt in range(2):
                pj, pk = C.bank("pj", PJ)
                C.mm(pj[:, 0:64], hid[:, nt * 128:(nt + 1) * 128], w2v[:, 0, :], True, True, ["w2v_sb", "hid"], [pk])
                C.cp("act", Vc[:, nt, 0:64], pj[:, 0:64], [pk], ["Vc"])

    ptc = [[C.sb("ptc%d_%d" % (i, j), [128, 512], BF16) for j in range(2)] for i in range(2)]
    pts = [C.sb("pt%d" % i, [128, 512], BF16) for i in range(4)]
    imp = C.sb("imp", [128, 64], F32)
    wk = C.sb("wk", [128, 64], F32)
    m8 = C.sb("m8", [128, 16], F32)
    rci = C.sb("rci", [128, 4], F32)
    selb = C.sb("selb", [128, 64], BF16)
    selT4s = [C.sb("selT4_%d" % i, [128, 512], BF16) for i in range(2)]
    for i in range(2):
        C.memset("pool", selT4s[i][64:128, :], 0.0, ["selT4_%d" % i])
    rsrow = C.sb("rsrow", [65, 512], F32)
    bcs = C.sb("bcs", [64, 512], F32)
    tmpo = C.sb("tmpo", [64, 512], F32)
    oacc = [C.sb("oacc%d" % i, [64, 512], F32) for i in range(2)]
    oaccb = [C.sb("oaccb%d" % i, [64, 512], BF16) for i in range(2)]
    PJ = [0, 7]
    SB_ = [1, 2, 3]
    pending = []

    def flush():
        while pending:
            pending.pop(0)()

    def finalize(po, pok, br, qb, first):
        ai = qb % 2
        acc, ak = oacc[ai], "oacc%d" % ai
        C.ts("dve", rsrow[64:65, :], po[64:65, :], 1e-30, ALU.add, [pok], ["rsrow"])
        C.recip(rsrow[64:65, :], rsrow[64:65, :], ["rsrow"], ["rsrow"])
        pj, pk = C.bank("pj", PJ)
        items = []
        for r in range(4):
            g = r * 3 + br
            items.append((pj[0:64, r * 128:(r + 1) * 128], selg[0:12, g * 64:(g + 1) * 64],
                          Gs[0:12, qb * 128:(qb + 1) * 128], True, True))
        C.mmlist(items, ["selg_sb", "Gs_%d" % (qb // 4)], [pk])
        C.cp("act", bcs[:, :], pj[0:64, :], [pk], ["bcs"])
        C.tt("dve", tmpo[:, :], po[0:64, :], bcs[:, :], ALU.mult, [pok, "bcs"], ["tmpo"])
        pj2, pk2 = C.bank("pj", PJ)
        C.mm(pj2[0:64, :], C.onesf[64:65, 0:64], rsrow[64:65, :], True, True, ["onesf", "rsrow"], [pk2])
        C.cp("act", bcs[:, :], pj2[0:64, :], [pk2], ["bcs"])
        if first:
            C.tt("dve", acc[:, :], tmpo[:, :], bcs[:, :], ALU.mult, ["tmpo", "bcs"], [ak])
        else:
            C.tt("dve", tmpo[:, :], tmpo[:, :], bcs[:, :], ALU.mult, ["tmpo", "bcs"], ["tmpo"])
            C.tt("pool", acc[:, :], acc[:, :], tmpo[:, :], ALU.add, ["tmpo", ak], [ak])

    def qinfo(qb):
        return Qa[:, qb, :], ["Qa_%d" % (qb // 4), "Qa_aug"]

    def cmp_stage(qb):
        q_rhs, qkeys = qinfo(qb)
        pc = ptc[qb % 2]
        selT4, stk = selT4s[qb % 2], "selT4_%d" % (qb % 2)
        ntn = 1 if qb < 16 else 2
        po = C.banks[POC]
        for nt in range(ntn):
            sbk, sk = C.bank("s", SB_)
            items = [(sbk[:, :], Kc[:, nt * 128:(nt + 1) * 128], q_rhs, True, False)]
            for r in range(4):
                items.append((sbk[:, r * 128:(r + 1) * 128], C.ident[:, :],
                              maskc[:, nt * S + qb * 128:nt * S + (qb + 1) * 128], False, r == 3))
            C.mmlist(items, qkeys + ["Kc", "Kc_aug", "ident", "maskc_sb"], [sk])
            C.act(pc[nt][:], sbk[:, :], AF.Exp, [sk], ["ptc%d_%d" % (qb % 2, nt)], scale=SC_NSA)
        for nt in range(ntn):
            C.mm(po[:, :], Vc[:, nt, :], pc[nt][:], nt == 0, nt == ntn - 1,
                 ["Vc", "ptc%d_%d" % (qb % 2, nt)], ["bank%d" % POC])
        pj, pk = C.bank("pj", PJ)
        items = []
        for r in range(4):
            for nt in range(ntn):
                items.append((pj[:, r * 65:(r + 1) * 65], pc[nt][:, r * 128:(r + 1) * 128], ovl[:, nt * 65:(nt + 1) * 65],
                              nt == 0, nt == ntn - 1))
        C.mmlist(items, ["ovl_sb"] + ["ptc%d_%d" % (qb % 2, nt) for nt in range(ntn)], [pk])
        for r in range(4):
            C.ts("dve", rci[:, r:r + 1], pj[:, r * 65 + 64:r * 65 + 65], 1e-30, ALU.add, [pk], ["rci"])
        C.recip(rci[:, 0:4], rci[:, 0:4], ["rci"], ["rci"])
        for r in range(4):
            prev = selbias[:, qb * 64:(qb + 1) * 64] if r == 0 else imp[:]
            C.stt("dve", imp[:], pj[:, r * 65:r * 65 + 64], rci[:, r:r + 1], prev, ALU.mult, ALU.add,
                  [pk, "rci", "imp", "selbias_sb"], ["imp"])
        P.op("dve", lambda e: e.max(out=m8[:, 0:8], in_=imp[:]), ["imp"], ["m8"])
        P.op("dve", lambda e: e.match_replace(out=wk[:], in_to_replace=m8[:, 0:8], in_values=imp[:], imm_value=-1e9),
             ["imp", "m8"], ["wk"])
        P.op("dve", lambda e: e.max(out=m8[:, 8:16], in_=wk[:]), ["wk"], ["m8"])
        C.ts("dve", wk[:], imp[:], m8[:, 15:16], ALU.is_ge, ["imp", "m8"], ["wk"])
        C.ts("dve", selb[:], wk[:], -NEG, ALU.mult, ["wk"], ["selb"], s2=NEG, op1=ALU.add)

        def tail(qb=qb, selT4=selT4, stk=stk, po=po):
            C.tr(C.pst[0:64, 0:128], selb[:, :], C.ident[:, :], ["selb", "ident"], ["pst"])
            for r in range(4):
                C.cp("act" if r % 2 == 0 else "dve", selT4[0:64, r * 128:(r + 1) * 128], C.pst[0:64, 0:128], ["pst"], [stk])
            finalize(po, "bank%d" % POC, 0, qb, True)
        pending.append(tail)

    def sel_stage(qb):
        q_rhs, qkeys = qinfo(qb)
        selT4, stk = selT4s[qb % 2], "selT4_%d" % (qb % 2)
        po = C.banks[POS]

        def score(kt):
            pairs = [(Ks[:, kt * 128:(kt + 1) * 128], q_rhs), (eall[:, kt * 128:(kt + 1) * 128], selT4[:, :])]
            rd = qkeys + ["Ks_%d" % (kt // 4), "Ks_aug", "eall_sb", stk]
            if kt == qb:
                pairs.append((C.ident[:, :], dm4[:, :]))
                rd += ["ident", "dm4_sb"]
            return pairs, rd
        attn_loop(C, list(range(qb + 1)), score, SC_NSA, lambda kt: (Vs[:, kt, :], ["Vs_%d" % (kt // 4)]),
                  po, "bank%d" % POS, SB_, pts, "pt", after_first=None)

        def tail(qb=qb, po=po):
            finalize(po, "bank%d" % POS, 1, qb, False)
            ai = qb % 2
            C.cp("act", oaccb[ai][:, :], oacc[ai][:, :], ["oacc%d" % ai], ["oaccb%d" % ai])
            osink(qb, oaccb[ai], "oaccb%d" % ai)
        pending.append(tail)

    def win_stage(qb):
        q_rhs, qkeys = qinfo(qb)
        po = C.banks[POW]
        k0 = max(0, qb - 4)

        def score(kt):
            pairs = [(Kw[:, kt * 128:(kt + 1) * 128], q_rhs)]
            rd = qkeys + ["Kw_%d" % (kt // 4), "Kw_aug"]
            if kt == qb:
                pairs.append((C.ident[:, :], dm4[:, :]))
                rd += ["ident", "dm4_sb"]
            if kt == qb - 4:
                pairs.append((C.ident[:, :], wm4[:, :]))
                rd += ["ident", "wm4_sb"]
            return pairs, rd
        attn_loop(C, list(range(k0, qb + 1)), score, SC_NSA, lambda kt: (Vw[:, kt, :], ["Vw_%d" % (kt // 4)]),
                  po, "bank%d" % POW, SB_, pts, "pt", after_first=flush)

        def tail(qb=qb, po=po):
            finalize(po, "bank%d" % POW, 2, qb, False)
        pending.append(tail)

    cmp_stage(0)
    for qb in range(32):
        win_stage(qb)
        if qb + 1 < 32:
            cmp_stage(qb + 1)
        sel_stage(qb)
    flush()


def phase_ffn(C, name, L, G, final, hown, hownkey, hhalo, hhalokey, oall, osink):
    P = C.P
    C.begin_phase(name)
    C.setup_norm()
    fl = C.flags
    g2 = C.load_const("g2_sb", L["g2"], [128, 8], F32)
    cw = C.load_const("cw_sb", L["cw"], [128, 176], F32)
    if final:
        gF = C.sb("gF_sb", [128, DM], F32)
        P.dma("pool", gF[:], G["gF"][0:1, :].partition_broadcast(128), writes=["gF_sb"])
    wup = C.load_w("wup_sb", L["wup"], 8, 5632)
    wdn = C.load_w("wdn_sb", L["wdn"], 22, DM)
    wo_d = L["wo"]

    NC_ = 256
    aT = [C.sb("aT%d" % i, [128, NC_], BF16) for i in range(4)]
    hm = C.sb("hm", [128, 2, DM], F32)
    n2T = C.sb("n2T", [128, 8, NC_], BF16)
    oTb = C.sb("oTb", [128, 8, NC_], BF16)
    oa = [C.sb("oa%d" % i, [128, NC_], BF16) for i in range(2)]
    ob = [C.sb("ob%d" % i, [128, NC_], BF16) for i in range(2)]
    wob = [C.sb("wob%d" % i, [128, DM], BF16) for i in range(4)]
    ubuf = [C.sb("ubuf%d" % i, [128, NC_ + 2], F32) for i in range(3)]
    tb = [C.sb("tb%d" % i, [128, NC_], F32) for i in range(4)]
    sg = C.sb("sg", [128, NC_], F32)
    carry = C.sb("carry", [128, 44, 2], F32)
    res = C.sb("res", [128, DM], F32)
    ss2 = C.sb("ss2", [128, 1], F32)

    ACC = [0, 1, 2, 3]
    UP = [4, 5, 6]

    def chunk(ci, halo):
        nt_ = 1 if halo else 2
        ncol = nt_ * 128
        c0 = 1920 if halo else ci * 256
        for k in range(8):
            b = C.rot("oa", 2)
            if halo:
                P.dma("sp", oa[b][:, 0:ncol], oall[0][k * 128:(k + 1) * 128, c0:c0 + ncol], reads=["oall0"], writes=["oa%d" % b])
                C.ts("dve", oTb[:, k, 0:ncol], oa[b][:, 0:ncol], fl[:, 1:2], ALU.mult, ["oa%d" % b, "flags"], ["oTb"])
            else:
                P.dma("sp", oa[b][:, 0:ncol], oall[0][k * 128:(k + 1) * 128, c0:c0 + ncol], reads=["oall0"], writes=["oa%d" % b])
                P.dma("sp", ob[b][:, 0:ncol], oall[1][k * 128:(k + 1) * 128, c0:c0 + ncol], reads=["oall1"], writes=["ob%d" % b])
                C.ts("dve", oa[b][:, 0:ncol], oa[b][:, 0:ncol], fl[:, 0:1], ALU.mult, ["oa%d" % b, "flags"], ["oa%d" % b])
                C.stt("dve", oTb[:, k, 0:ncol], ob[b][:, 0:ncol], fl[:, 1:2], oa[b][:, 0:ncol], ALU.mult, ALU.add,
                      ["oa%d" % b, "ob%d" % b, "flags"], ["oTb"])
        for k in range(8):
            wb = C.rot("wob", 4)
            P.dma("pool", wob[wb][:, :], wo_d[k * 128:(k + 1) * 128, :], writes=["wob%d" % wb])
            items = []
            for ti in range(nt_):
                for hf in range(2):
                    items.append((C.banks[ACC[ti * 2 + hf]][:, :], oTb[:, k, ti * 128:(ti + 1) * 128],
                                  wob[wb][:, hf * 512:(hf + 1) * 512], k == 0, k == 7))
            C.mmlist(items, ["oTb", "wob%d" % wb], ["bank%d" % ACC[i] for i in range(nt_ * 2)])
        for ti in range(nt_):
            hb = C.rot("ht", 2)
            if halo:
                P.dma("sp", C.ht[hb][:], hhalo, reads=[hhalokey], writes=["ht%d" % hb])
                C.ts("dve", C.ht[hb][:], C.ht[hb][:], fl[:, 1:2], ALU.mult, ["ht%d" % hb, "flags"], ["ht%d" % hb])
            else:
                P.dma("sp", C.ht[hb][:], hown(2 * ci + ti), reads=[hownkey(2 * ci + ti)], writes=["ht%d" % hb])
            for hf in range(2):
                C.tt("dve", hm[:, ti, hf * 512:(hf + 1) * 512], C.banks[ACC[ti * 2 + hf]][:, :],
                     C.ht[hb][:, hf * 512:(hf + 1) * 512], ALU.add, ["bank%d" % ACC[ti * 2 + hf], "ht%d" % hb], ["hm%d" % ti])
            C.norm_T(hm[:, ti, :], "hm%d" % ti, n2T, "n2T_%d" % ti, ti * 128, g2, "g2_sb")
        nkeys = ["n2T_%d" % ti for ti in range(nt_)]
        dq = []

        def down(i, ai):
            items = []
            for ti in range(nt_):
                for hf in range(2):
                    items.append((C.banks[ACC[ti * 2 + hf]][:, :], aT[ai][:, ti * 128:(ti + 1) * 128],
                                  wdn[:, i, hf * 512:(hf + 1) * 512], i == 0, i == 21))
            C.mmlist(items, ["aT%d" % ai, "wdn_sb"], ["bank%d" % ACC[q] for q in range(nt_ * 2)])
        for i in range(22):
            tfin = []
            for part in range(2):
                fc = i + 22 * part
                up, upk = C.bank("up", UP)
                C.mmg(up[:, 0:ncol], [(wup[:, k, fc * 128:(fc + 1) * 128], n2T[:, k, 0:ncol]) for k in range(8)],
                      ["wup_sb"] + nkeys, [upk])
                ub = C.rot("ubuf", 3)
                u = ubuf[ub]
                uk = "ubuf%d" % ub
                C.cp("pool", u[:, 0:2], carry[:, fc, :], ["carry%d" % fc], [uk])
                C.cp("act", u[:, 2:2 + ncol], up[:, 0:ncol], [upk], [uk])
                if not halo:
                    ta = C.rot("tb", 4)
                    C.act(tb[ta][:, 0:ncol], up[:, 0:ncol], AF.Identity, [upk, "cw_sb"], ["tb%d" % ta],
                          bias=cw[:, fc * 4 + 3:fc * 4 + 4], scale=cw[:, fc * 4 + 2:fc * 4 + 3])
                    C.stt("dve", tb[ta][:, 0:ncol], u[:, 1:1 + ncol], cw[:, fc * 4 + 1:fc * 4 + 2], tb[ta][:, 0:ncol],
                          ALU.mult, ALU.add, [uk, "cw_sb", "tb%d" % ta], ["tb%d" % ta])
                    C.stt("dve", tb[ta][:, 0:ncol], u[:, 0:ncol], cw[:, fc * 4:fc * 4 + 1], tb[ta][:, 0:ncol],
                          ALU.mult, ALU.add, [uk, "cw_sb", "tb%d" % ta], ["tb%d" % ta])
                    tfin.append(ta)
                C.cp("pool", carry[:, fc, :], u[:, ncol:ncol + 2], [uk], ["carry%d" % fc])
            if not halo:
                C.act(sg[:, 0:ncol], tb[tfin[0]][:, 0:ncol], AF.Silu, ["tb%d" % tfin[0]], ["sg"])
                ai = C.rot("aT", 4)
                C.tt("dve", aT[ai][:, 0:ncol], sg[:, 0:ncol], tb[tfin[1]][:, 0:ncol], ALU.mult,
                     ["sg", "tb%d" % tfin[1]], ["aT%d" % ai])
                dq.append((i, ai))
                if len(dq) > 2:
                    down(*dq.pop(0))
        while dq:
            down(*dq.pop(0))
        if halo:
            return
        for ti in range(nt_):
            for hf in range(2):
                bk = ACC[ti * 2 + hf]
                C.tt("dve", res[:, hf * 512:(hf + 1) * 512], C.banks[bk][:, :], hm[:, ti, hf * 512:(hf + 1) * 512],
                     ALU.add, ["bank%d" % bk, "hm%d" % ti], ["res"])
            if final:
                C.memset("dve", ss2[:], 0.0, ["ss2"])
                C.act(C.junk[:], res[:], AF.Square, ["res", "ss2"], ["junk", "ss2"], accum=ss2[:])
                C.act(ss2[:], ss2[:], AF.Sqrt, ["ss2", "epsn"], ["ss2"], bias=C.epsn[:], scale=1.0 / DM)
                C.recip(ss2[:], ss2[:], ["ss2"], ["ss2"])
                C.stt("dve", res[:], res[:], ss2[:, 0:1], gF[:], ALU.mult, ALU.mult, ["res", "ss2", "gF_sb"], ["res"])
            osink(2 * ci + ti, res, "res")

    C.memset("pool", carry[:], 0.0, ["carry%d" % fc for fc in range(44)])
    chunk(0, True)
    for c in range(8):
        chunk(c, False)


LAYER_IN = [("wAm", [DM, 832]), ("gAm", [128, 8]), ("wq", [384, 768]), ("gq", [128, 3]), ("wkv", [256, 512]),
            ("gkv", [128, 2]), ("wAn", [DM, 780]), ("gAn", [128, 8]), ("w1k", [2048, 128]), ("w1v", [2048, 128]),
            ("w2k", [128, 64]), ("w2v", [128, 64]), ("posk", [128, 16]), ("posv", [128, 16]),
            ("wo", [DM, DM]), ("wup", [DM, 5632]), ("wdn", [2816, DM]), ("g2", [128, 8]), ("cw", [128, 176])]
GLOB_IN = [("ropeC", [32, S], F32), ("ropeS", [32, S], F32), ("dmask", [128, 2048], BF16),
           ("maskc", [128, 2 * S], BF16), ("eall", [64, S], BF16), ("selbias", [128, 2048], BF16),
           ("ovl", [128, 130], BF16), ("selg", [12, 768], F32), ("dm4", [128, 512], BF16), ("wm4", [128, 512], BF16),
           ("qaug", [4, 32 * 512], BF16), ("kaug", [4, S], BF16), ("kaugc", [4, 256], BF16), ("gF", [1, DM], F32)]
GROUPS = [[0, 1], [2, 3], [4, 5], [6, 7]]


def build_fused(nlayers=2):
    nc = bass.Bass("TRN2", target_bir_lowering=False)
    C = Ctx(nc)
    P = C.P
    x_d = C.dram("x", [S, DM], F32)
    xown_d = C.dram("xown", [2048, DM], F32)
    xhalo_d = C.dram("xhalo", [128, DM], F32)
    flags_d = C.dram("flags", [128, 2], F32)
    G = {n: C.dram(n, sh, dt) for (n, sh, dt) in GLOB_IN}
    Ls = [{n: C.dram("%s_%d" % (n, l), sh, F32) for (n, sh) in LAYER_IN} for l in range(nlayers)]
    out_d = C.dram("hout", [2048, DM], F32, out=True)
    P.dma("pool", C.flags[:], flags_d, writes=["flags"])

    omy = [[nc.dram_tensor("omy_%d_%d" % (l, c), [512, 2048], BF16) for c in range(2)] for l in range(nlayers)]
    oall = [[nc.dram_tensor("oall_%d_%d" % (l, c), [1024, 2048], BF16) for c in range(2)] for l in range(nlayers)]
    hmy = [nc.dram_tensor("hmy_%d" % j, [512, DM], F32) for j in range(4)]
    hall = [nc.dram_tensor("hall_%d" % j, [1024, DM], F32) for j in range(4)]

    for l in range(nlayers):
        if l == 0:
            hsrc = lambda g: x_d[g * 128:(g + 1) * 128, :]
            hkey = lambda g: "x"
        else:
            def hsrc(g):
                r, w = g // 16, g % 16
                return hall[w // 4][r * 512 + (w % 4) * 128:r * 512 + (w % 4) * 128 + 128, :]
            hkey = lambda g: "hall%d" % ((g % 16) // 4)

        def osink_mla(hh, tc, ot, otkey, l=l):
            c = tc // 4
            col = (tc % 4) * 512
            P.dma("sp", omy[l][c][hh * 64:(hh + 1) * 64, col:col + 512], ot[:, :], reads=[otkey], writes=["omy%d" % c])

        def osink_nsa(qb, ot, otkey, l=l):
            c = qb // 16
            col = (qb % 16) * 128
            P.dma("sp", omy[l][c][256:512, :].rearrange("(r d) t -> d r t", d=64)[:, :, col:col + 128],
                  ot[:, :].rearrange("p (r t) -> p r t", r=4), reads=[otkey], writes=["omy%d" % c])
            if qb % 16 == 15:
                P.cc("AllGather", GROUPS, omy[l][c].ap().opt(), oall[l][c].ap().opt(), reads=["omy%d" % c], writes=["oall%d" % c])

        phase_mla(C, "m%d_" % l, Ls[l], G, hsrc, hkey, osink_mla)
        phase_nsa(C, "n%d_" % l, Ls[l], G, hsrc, hkey, osink_nsa)
        final = (l == nlayers - 1)
        if l == 0:
            hown = lambda t: xown_d[t * 128:(t + 1) * 128, :]
            hownkey = lambda t: "xown"
            hhalo, hhalokey = xhalo_d, "xhalo"
        else:
            hown = lambda t: hmy[t // 4][(t % 4) * 128:(t % 4) * 128 + 128, :]
            hownkey = lambda t: "hmy%d" % (t // 4)
            hhalo, hhalokey = hall[3][384:512, :], "hall3"
        if final:
            def osink_ffn(t, res, rkey):
                P.dma("sp", out_d[t * 128:(t + 1) * 128, :], res[:], reads=[rkey])
        else:
            def osink_ffn(t, res, rkey):
                P.dma("sp", hmy[t // 4][(t % 4) * 128:(t % 4) * 128 + 128, :], res[:], reads=[rkey], writes=["hmy%d" % (t // 4)])
                if t % 4 == 3:
                    j = t // 4
                    P.cc("AllGather", GROUPS, hmy[j].ap().opt(), hall[j].ap().opt(), reads=["hmy%d" % j], writes=["hall%d" % j])
        phase_ffn(C, "f%d_" % l, Ls[l], G, final, hown, hownkey, hhalo, hhalokey, oall[l], osink_ffn)
    P.emit()
    return nc


def _pk(g, nk):
    return np.ascontiguousarray(np.asarray(g, np.float32).reshape(nk, 128).T)


def _consts():
    c = {}
    p = np.arange(128)[:, None]
    i512 = np.arange(512)[None, :]
    dm = np.zeros((128, 4, 512), np.float32)
    for m in range(4):
        dm[:, m, :] = np.where(128 * m + p <= i512, 0.0, NEG)
    c["dmask"] = dm.reshape(128, 2048).astype(NPBF)
    i128 = np.arange(128)[None, :]
    c["dm4"] = np.tile(np.where(p <= i128, 0.0, NEG), (1, 4)).astype(NPBF)
    c["wm4"] = np.tile(np.where(i128 < p, 0.0, NEG), (1, 4)).astype(NPBF)
    n = np.arange(256)[:, None]
    t = np.arange(S)[None, :]
    mc = np.where((t >= 16 * n + 31) & (n <= 254), 0.0, NEG).astype(np.float32)
    c["maskc"] = np.ascontiguousarray(mc.reshape(2, 128, S).transpose(1, 0, 2).reshape(128, 2 * S)).astype(NPBF)
    j = np.arange(64)[:, None]
    c["eall"] = (np.arange(S)[None, :] // 64 == j).astype(np.float32).astype(NPBF)
    tt_ = np.arange(S)
    cur = (tt_ // 64)[:, None]
    jj = np.arange(64)[None, :]
    sbias = np.zeros((S, 64), np.float32)
    sbias[np.broadcast_to(jj > cur, (S, 64))] = -1e4
    sbias[np.broadcast_to((jj == 0) | (jj == cur) | (jj == cur - 1), (S, 64))] = 1e4
    c["selbias"] = np.ascontiguousarray(sbias.reshape(32, 128, 64).transpose(1, 0, 2).reshape(128, 2048)).astype(NPBF)
    cs = (np.arange(256) * 16)[:, None]
    ss_ = (np.arange(64) * 64)[None, :]
    ov = ((cs < ss_ + 64) & (cs + 32 > ss_)).astype(np.float32)
    ov[255] = 0.0
    ov1 = np.concatenate([ov, np.ones((256, 1), np.float32)], 1)
    c["ovl"] = np.ascontiguousarray(ov1.reshape(2, 128, 65).transpose(1, 0, 2).reshape(128, 130)).astype(NPBF)
    sg = np.zeros((12, 12, 64), np.float32)
    for g in range(12):
        sg[g, g, :] = 1.0
    c["selg"] = sg.reshape(12, 768)
    k = np.arange(S)
    c["kaug"] = np.stack([np.ones(S), np.ones(S), k // 64, k % 64]).astype(np.float32).astype(NPBF)
    e = np.arange(256) * 16 + 31
    c["kaugc"] = np.stack([np.ones(256), np.ones(256), e // 64, e % 64]).astype(np.float32).astype(NPBF)
    inv = 1.0 / (10000.0 ** (np.arange(0, 32, 2, dtype=np.float32) / 32))
    ang = np.arange(S, dtype=np.float32)[:, None] * inv[None, :]
    cos, sin = np.cos(ang).T.astype(np.float32), np.sin(ang).T.astype(np.float32)
    c["ropeC"] = np.ascontiguousarray(np.concatenate([cos, cos], 0))
    c["ropeS"] = np.ascontiguousarray(np.concatenate([-sin, sin], 0))
    return c


def _qaug(group):
    slopes = np.exp2(-8.0 * np.arange(1, 9, dtype=np.float32) / 8)
    t = np.arange(S).reshape(32, 1, 128)
    out = np.zeros((4, 32, 4, 128), np.float32)
    for r in range(4):
        a = slopes[group * 4 + r] / SC_NSA
        out[0, :, r, :] = (-a * 64 * (t // 64))[:, 0, :]
        out[1, :, r, :] = (-a * (t % 64))[:, 0, :]
        out[2, :, r, :] = a * 64
        out[3, :, r, :] = a
    return out.reshape(4, 32 * 512).astype(NPBF)


_PROG = {}


def _prog(nlayers=2):
    if nlayers not in _PROG:
        _PROG[nlayers] = build_fused(nlayers)
    return _PROG[nlayers]


def _layer_maps(l, I, c):
    m = {}
    w_in, w_uq, w_ukv = I["w_in"][l], I["w_uq"][l], I["w_ukv"][l]
    sw = list(range(656, 672)) + list(range(640, 656))
    colsA = list(range(0, 640)) + list(range(0, 64)) + list(range(640, 672)) + list(range(0, 64)) + sw
    m["wAm"] = np.ascontiguousarray(w_in[:, colsA])
    m["gAm"] = _pk(I["attn_norm"][l], 8)
    qc, kc, vc = [], [], []
    for hh in range(4 * c, 4 * c + 4):
        base = 96 * hh
        nope = list(range(base, base + 64))
        rope = list(range(base + 64, base + 96))
        qc += nope + rope + nope + rope[16:] + rope[:16]
        kc += list(range(128 * hh, 128 * hh + 64))
        vc += list(range(128 * hh + 64, 128 * hh + 128))
    m["wq"] = np.ascontiguousarray(w_uq[:, qc])
    m["gq"] = _pk(I["q_norm"][l], 3)
    m["wkv"] = np.ascontiguousarray(w_ukv[:, kc + vc])
    m["gkv"] = _pk(I["kv_norm"][l], 2)
    g = c
    q0 = 672 + 256 * g
    o = 1184
    rng = lambda a: list(range(a, a + 64))
    kcc, vcc, ksc = rng(o + 64 * g), rng(o + 128 + 64 * g), rng(o + 256 + 64 * g)
    vsc, kwc, vwc = rng(o + 384 + 64 * g), rng(o + 512 + 64 * g), rng(o + 640 + 64 * g)
    gtc = list(range(1952 + 12 * g, 1952 + 12 * g + 12))
    cols = list(range(q0, q0 + 256)) + kcc + kcc + vcc + vcc + ksc + kwc + vsc + vwc + gtc
    m["wAn"] = np.ascontiguousarray(w_in[:, cols])
    m["gAn"] = m["gAm"]
    posT = lambda pz: np.ascontiguousarray(np.asarray(pz, np.float32).reshape(16, 128).T)
    m["w1k"] = np.ascontiguousarray(I["cmp_k_w1"][l].reshape(2048, 128))
    m["w1v"] = np.ascontiguousarray(I["cmp_v_w1"][l].reshape(2048, 128))
    m["w2k"] = np.ascontiguousarray(I["cmp_k_w2"][l])
    m["w2v"] = np.ascontiguousarray(I["cmp_v_w2"][l])
    m["posk"] = posT(I["cmp_pos_k"][l])
    m["posv"] = posT(I["cmp_pos_v"][l])
    perm = list(range(0, 256)) + list(range(512, 768)) + list(range(256, 512)) + list(range(768, 1024))
    m["wo"] = np.ascontiguousarray(I["w_o"][l][perm, :])
    m["wup"] = np.ascontiguousarray(I["w_up"][l])
    m["wdn"] = np.ascontiguousarray(I["w_down"][l])
    m["g2"] = _pk(I["ffn_norm"][l], 8)
    cwv = np.stack([I["conv_w"][l][0], I["conv_w"][l][1], I["conv_w"][l][2], I["conv_b"][l]], -1)
    m["cw"] = np.ascontiguousarray(cwv.reshape(44, 128, 4).transpose(1, 0, 2).reshape(128, 176)).astype(np.float32)
    return m


def make_maps(I, nlayers=2):
    cst = _consts()
    lm = [[_layer_maps(l, I, c) for c in range(2)] for l in range(nlayers)]
    maps = []
    for b in range(4):
        for c in range(2):
            m = {"x": np.ascontiguousarray(I["x"][b]),
                 "xown": np.ascontiguousarray(I["x"][b][2048 * c:2048 * c + 2048]),
                 "xhalo": np.ascontiguousarray(I["x"][b][1920:2048]),
                 "flags": np.ascontiguousarray(np.tile(np.array([[1.0 - c, float(c)]], np.float32), (128, 1)))}
            for (n, sh, dt) in GLOB_IN:
                if n == "qaug":
                    m[n] = _qaug(c)
                elif n == "gF":
                    m[n] = np.ascontiguousarray(np.asarray(I["final_norm"], np.float32).reshape(1, DM))
                else:
                    m[n] = cst[n]
            for l in range(nlayers):
                for k, v in lm[l][c].items():
                    m["%s_%d" % (k, l)] = v
            maps.append(m)
    return maps


def kernel(**inputs):
    I = {k: np.asarray(v, dtype=np.float32) for k, v in inputs.items()}
    res = run_bass_kernel_spmd(_prog(2), make_maps(I, 2), core_ids=list(range(8))).results
    out = np.empty((4, S, DM), np.float32)
    for b in range(4):
        for c in range(2):
            out[b, 2048 * c:2048 * c + 2048] = np.asarray(res[2 * b + c]["hout"])
    return out
```

```python
import numpy as np
import ml_dtypes
import concourse.bass as bass
import concourse.mybir as mybir
from concourse.bass_utils import run_bass_kernel_spmd

F32 = mybir.dt.float32
BF16 = mybir.dt.bfloat16
ALU = mybir.AluOpType
AF = mybir.ActivationFunctionType
AX = mybir.AxisListType
NPBF = ml_dtypes.bfloat16

ENGS = ["pe", "act", "dve", "pool", "sp"]
DMA_POOL = 12
S = 4096
DM = 1024
NEG = -30000.0
SC_MLA = 96 ** -0.5
SC_NSA = 0.125
EPS = 1e-6


class Prog:
    def __init__(self, nc):
        self.nc = nc
        self.ops = {e: [] for e in ENGS}
        self.lastw = {}
        self.readers = {}
        self.dma_n = {e: 0 for e in ENGS + ["cc"]}
        self.dma_sem_cnt = {}
        self.last_c = {}
        self.last_d = {}

    def sb(self, name, shape, dt):
        return self.nc.alloc_sbuf_tensor(name, list(shape), dt)

    def ps(self, name, shape, dt=F32):
        return self.nc.alloc_psum_tensor(name, list(shape), dt)

    def _add(self, eng, fn, reads, writes, dma, cc=False):
        op = dict(eng=eng, fn=fn, deps=[], dma=dma, marked=False, inc=(1 if cc else 16))
        deps = []
        for k in reads:
            w = self.lastw.get(k)
            if w is not None:
                deps.append(w)
        for k in writes:
            w = self.lastw.get(k)
            if w is not None:
                deps.append(w)
            deps.extend(self.readers.get(k, ()))
        seen = set()
        for d in deps:
            if id(d) in seen or d is op:
                continue
            seen.add(id(d))
            if (not d["dma"]) and d["eng"] == eng and eng in ("pe", "sp"):
                continue
            op["deps"].append(d)
            d["marked"] = True
        if dma:
            qn = "cc" if cc else eng
            q = self.dma_n[qn]
            self.dma_n[qn] += 1
            semkey = (qn, q % (4 if cc else DMA_POOL))
            m = self.dma_sem_cnt.get(semkey, 0) + 1
            self.dma_sem_cnt[semkey] = m
            op["dsem"] = semkey
            op["dval"] = op["inc"] * m
            op["marked"] = True
            self.last_d[semkey] = op
        else:
            self.last_c[eng] = op
        for k in reads:
            self.readers.setdefault(k, []).append(op)
        for k in writes:
            self.lastw[k] = op
            self.readers[k] = []
        self.ops[eng].append(op)
        return op

    def op(self, eng, fn, reads=(), writes=()):
        return self._add(eng, fn, list(reads), list(writes), False)

    def dma(self, eng, out, in_, reads=(), writes=()):
        return self._add(eng, lambda e: e.dma_start(out=out, in_=in_), list(reads), list(writes), True)

    def cc(self, kind, groups, in_ap, out_ap, reads=(), writes=()):
        return self._add("pool", lambda e: e.collective_compute(kind, ALU.bypass, replica_groups=groups,
                                                                ins=[in_ap], outs=[out_ap]),
                         list(reads), list(writes), True, cc=True)

    def barrier(self):
        deps = list(self.last_c.values()) + list(self.last_d.values())
        for d in deps:
            d["marked"] = True
        for e in ENGS:
            self.ops[e].append(dict(eng=e, fn=None, deps=list(deps), dma=False, marked=False, inc=0))
        self.lastw = {}
        self.readers = {}

    def emit(self):
        nc = self.nc
        csem = {e: nc.alloc_semaphore("c_" + e) for e in ENGS}
        dsem = {}
        for (qn, i) in self.dma_sem_cnt:
            dsem[(qn, i)] = nc.alloc_semaphore("d_%s_%d" % (qn, i))
        for e in ENGS:
            c = 0
            for o in self.ops[e]:
                if o["dma"] or o["fn"] is None:
                    continue
                if o["marked"]:
                    c += 1
                    o["cval"] = c
        all_dma = [o for e in ENGS for o in self.ops[e] if o["dma"]]

        def run(e, eng):
            seen = {}

            def wait(sem_key, sem, val):
                if seen.get(sem_key, 0) >= val:
                    return
                seen[sem_key] = val
                eng.wait_ge(sem, val)

            for o in self.ops[e]:
                for d in o["deps"]:
                    if d["dma"]:
                        wait(d["dsem"], dsem[d["dsem"]], d["dval"])
                    else:
                        wait(("c", d["eng"]), csem[d["eng"]], d["cval"])
                if o["fn"] is None:
                    continue
                if o["dma"]:
                    if o["dval"] > o["inc"]:
                        wait(o["dsem"], dsem[o["dsem"]], o["dval"] - o["inc"])
                    o["fn"](eng).then_inc(dsem[o["dsem"]], o["inc"])
                else:
                    ins = o["fn"](eng)
                    if o["marked"]:
                        ins.then_inc(csem[e], 1)
            if e == "sp":
                last = {}
                for o in all_dma:
                    last[o["dsem"]] = max(last.get(o["dsem"], 0), o["dval"])
                for k, v in last.items():
                    eng.wait_ge(dsem[k], v)

        with nc.Block() as block:
            @block.tensor
            def _(eng):
                run("pe", eng)

            @block.scalar
            def _(eng):
                run("act", eng)

            @block.vector
            def _(eng):
                run("dve", eng)

            @block.gpsimd
            def _(eng):
                run("pool", eng)

            @block.sync
            def _(eng):
                run("sp", eng)


def _nbytes(shape, dt):
    n = 1
    for d in shape[1:]:
        n *= d
    return n * (4 if dt == F32 else 2)


class Ctx:
    def __init__(self, nc):
        self.nc = nc
        self.P = P = Prog(nc)
        self.rots = {}
        self.pname = "g_"
        self.ident = nc.alloc_sbuf_tensor("ident", [128, 128], BF16)
        self.identf = nc.alloc_sbuf_tensor("identf", [128, 128], F32)
        self.onesf = nc.alloc_sbuf_tensor("onesf", [128, 128], F32)
        self.epsn = nc.alloc_sbuf_tensor("epsn", [128, 1], F32)
        self.flags = nc.alloc_sbuf_tensor("flags_sb", [128, 2], F32)
        identf, ident, onesf, epsn = self.identf, self.ident, self.onesf, self.epsn
        P.op("pool", lambda e: e.memset(identf[:], 0.0), writes=["identf"])
        P.op("pool", lambda e: e.affine_select(out=identf[:], in_=identf[:], pattern=[[-1, 128]],
                                                compare_op=ALU.not_equal, fill=1.0, base=0, channel_multiplier=1),
             reads=["identf"], writes=["identf"])
        P.op("dve", lambda e: e.tensor_copy(out=ident[:], in_=identf[:]), reads=["identf"], writes=["ident"])
        P.op("dve", lambda e: e.memset(onesf[:], 1.0), writes=["onesf"])
        P.op("dve", lambda e: e.memset(epsn[:], EPS), writes=["epsn"])
        self.pst = P.ps("pst", [128, 1024], BF16)
        self.banks = [P.ps("bank%d" % i, [128, 512], F32) for i in range(7)]
        self.banks.append(self.pst.bitcast(F32))
        self.base = ((int(nc.sbuf_base) + 63) // 64) * 64
        self.top = int(nc.sbuf_top)
        self.off = self.base

    def begin_phase(self, name):
        self.P.barrier()
        self.pname = name
        self.off = self.base

    def sb(self, name, shape, dt):
        nb = ((_nbytes(shape, dt) + 31) // 32) * 32
        assert self.off + nb <= self.top, ("SBUF overflow", self.pname, name, self.off + nb - self.top)
        t = self.nc.alloc_sbuf_tensor_at(self.pname + name, list(shape), dt, offset=self.off)
        self.off += nb
        return t

    def rot(self, name, n):
        i = self.rots.get(name, 0) % n
        self.rots[name] = (i + 1) % n
        return i

    def dram(self, name, shape, dt, out=False):
        return self.nc.dram_tensor(name, list(shape), dt, kind="ExternalOutput" if out else "ExternalInput").ap()

    def mm(self, out, lhsT, rhs, start, stop, reads, writes):
        return self.P.op("pe", lambda e: e.matmul(out, lhsT=lhsT, rhs=rhs, start=start, stop=stop), reads, writes)

    def mmg(self, out, pairs, reads, writes, start=True, stop=True):
        n = len(pairs)

        def fn(e):
            ins = None
            for i, (l, r) in enumerate(pairs):
                ins = e.matmul(out, lhsT=l, rhs=r, start=(start and i == 0), stop=(stop and i == n - 1))
            return ins
        return self.P.op("pe", fn, reads, writes)

    def mmlist(self, items, reads, writes):
        def fn(e):
            ins = None
            for (o, l, r, st, sp) in items:
                ins = e.matmul(o, lhsT=l, rhs=r, start=st, stop=sp)
            return ins
        return self.P.op("pe", fn, reads, writes)

    def tr(self, out, in_, ident, reads, writes):
        return self.P.op("pe", lambda e: e.transpose(out, in_, ident), reads, writes)

    def act(self, out, in_, func, reads, writes, bias=None, scale=1.0, accum=None):
        kw = {}
        if bias is not None:
            kw["bias"] = bias
        if accum is not None:
            kw["accum_out"] = accum
        return self.P.op("act", lambda e: e.activation(out=out, in_=in_, func=func, scale=scale, **kw), reads, writes)

    def tt(self, eng, out, in0, in1, op, reads, writes):
        return self.P.op(eng, lambda e: e.tensor_tensor(out=out, in0=in0, in1=in1, op=op), reads, writes)

    def ts(self, eng, out, in0, s1, op0, reads, writes, s2=None, op1=None):
        if op1 is None:
            return self.P.op(eng, lambda e: e.tensor_scalar(out=out, in0=in0, scalar1=s1, scalar2=None, op0=op0), reads, writes)
        return self.P.op(eng, lambda e: e.tensor_scalar(out=out, in0=in0, scalar1=s1, scalar2=s2, op0=op0, op1=op1), reads, writes)

    def stt(self, eng, out, in0, scalar, in1, op0, op1, reads, writes):
        return self.P.op(eng, lambda e: e.scalar_tensor_tensor(out=out, in0=in0, scalar=scalar, in1=in1, op0=op0, op1=op1), reads, writes)

    def cp(self, eng, out, in_, reads, writes):
        if eng == "act":
            return self.P.op("act", lambda e: e.copy(out=out, in_=in_), reads, writes)
        return self.P.op(eng, lambda e: e.tensor_copy(out=out, in_=in_), reads, writes)

    def recip(self, out, in_, reads, writes):
        return self.P.op("dve", lambda e: e.reciprocal(out=out, in_=in_), reads, writes)

    def memset(self, eng, ap, val, writes):
        return self.P.op(eng, lambda e: e.memset(ap, val), [], writes)

    def bank(self, grp, idxs):
        i = idxs[self.rot(grp, len(idxs))]
        return self.banks[i], ("pst" if i == 7 else "bank%d" % i)

    def load_w(self, name, w_dram, nk, ncols):
        wsb = self.sb(name, [128, nk, ncols], BF16)
        for k in range(nk):
            self.P.dma("pool", wsb[:, k, :], w_dram[k * 128:(k + 1) * 128, :], writes=[name])
        return wsb

    def load_const(self, name, dram_ap, shape, dt, eng="pool"):
        t = self.sb(name, shape, dt)
        self.P.dma(eng, t[:], dram_ap, writes=[name])
        return t

    def setup_norm(self):
        self.ht = [self.sb("ht%d" % i, [128, DM], F32) for i in range(2)]
        self.junk = self.sb("junk", [128, DM], BF16)
        self.nb = self.sb("nb", [128, DM], BF16)
        self.ss = [self.sb("ss%d" % i, [128, 1], F32) for i in range(2)]

    def norm_T(self, src, srckey, nT, nkey, col0, g_sb, gkey):
        b = self.rot("ss", 2)
        ss = self.ss[b]
        sk = "ss%d" % b
        self.memset("dve", ss[:], 0.0, [sk])
        self.act(self.junk[:], src, AF.Square, [srckey, sk], ["junk", sk], accum=ss[:])
        self.act(ss[:], ss[:], AF.Sqrt, [sk, "epsn"], [sk], bias=self.epsn[:], scale=1.0 / DM)
        self.recip(ss[:], ss[:], [sk], [sk])
        self.ts("dve", self.nb[:], src, ss[:, 0:1], ALU.mult, [srckey, sk], ["nb"])
        pst = self.pst
        nb, ident = self.nb, self.ident

        def fn(e):
            ins = None
            for k in range(8):
                ins = e.transpose(pst[:, k * 128:(k + 1) * 128], nb[:, k * 128:(k + 1) * 128], ident[:])
            return ins
        self.P.op("pe", fn, ["nb", "ident"], ["pst"])
        self.tt("dve", nT[:, :, col0:col0 + 128], pst[:, :].rearrange("p (k t) -> p k t", k=8),
                g_sb[:, :].unsqueeze(2).to_broadcast([128, 8, 128]), ALU.mult, ["pst", gkey], [nkey])


def attn_loop(C, tiles, score_fn, scale, v_fn, po, pok, sbanks, pts, ptname, after_first=None, depth=2):
    n = len(tiles)
    issued = []

    def issue(i):
        sbk, sk = C.bank("s", sbanks)
        pairs, rd = score_fn(tiles[i])
        C.mmg(sbk[:, :], pairs, rd, [sk])
        issued.append((sbk, sk))
    for i in range(min(depth, n)):
        issue(i)
    if after_first is not None:
        after_first()
    for i in range(n):
        sbk, sk = issued[i]
        pi = C.rot(ptname, len(pts))
        C.act(pts[pi][:], sbk[:, :], AF.Exp, [sk], ["%s%d" % (ptname, pi)], scale=scale)
        if i + depth < n:
            issue(i + depth)
        lhsT, rd = v_fn(tiles[i])
        C.mm(po[:, :], lhsT, pts[pi][:], i == 0, i == n - 1, rd + ["%s%d" % (ptname, pi)], [pok])


def phase_mla(C, name, L, G, hsrc, hkey, osink):
    P = C.P
    C.begin_phase(name)
    C.setup_norm()
    gA = C.load_const("gA_sb", L["gAm"], [128, 8], F32)
    gq = C.load_const("gq_sb", L["gq"], [128, 3], F32)
    gkv = C.load_const("gkv_sb", L["gkv"], [128, 2], F32)
    dmask = C.load_const("dmask_sb", G["dmask"], [128, 2048], BF16)
    wA = C.load_w("wA_sb", L["wAm"], 8, 832)
    wq = C.load_w("wq_sb", L["wq"], 3, 768)
    wkv = C.load_w("wkv_sb", L["wkv"], 2, 512)
    ropeC_d, ropeS_d = G["ropeC"], G["ropeS"]

    Kh = C.sb("Kh", [128, 4, S], BF16)
    C.memset("pool", Kh[96:128, :, :], 0.0, ["Kh_pad"])
    Vt = C.sb("Vt", [128, 32, 4, 128], BF16)
    C.memset("pool", Vt[:, :, :, 64:65], 1.0, ["Vt_%d" % c for c in range(8)])
    C.memset("pool", Vt[:, :, :, 65:128], 0.0, ["Vt_pad"])
    nTs = [C.sb("nT%d" % i, [128, 8, 512], BF16) for i in range(2)]
    Qhs = [C.sb("Qh%d" % i, [128, 4, 512], BF16) for i in range(2)]
    for i in range(2):
        C.memset("pool", Qhs[i][96:128, :, :], 0.0, ["Qh_pad"])
    zf = C.sb("zf", [128, 3, 512], F32)
    sq = C.sb("sq", [128, 3, 512], F32)
    rr = C.sb("rr", [128, 512], F32)
    cqn = C.sb("cqn", [128, 3, 512], BF16)
    ckvn = C.sb("ckvn", [128, 2, 512], BF16)
    Ct = C.sb("Ct", [96, 512], F32)
    St = C.sb("St", [96, 512], F32)
    t1 = C.sb("t1", [96, 512], F32)
    t2 = C.sb("t2", [96, 512], F32)
    pts = [C.sb("pt%d" % i, [128, 512], BF16) for i in range(4)]
    rsrow = C.sb("rsrow", [65, 512], F32)
    bcs = C.sb("bcs", [64, 512], F32)
    ots = [C.sb("ot%d" % i, [64, 512], BF16) for i in range(2)]

    PJ = [0, 1]
    SB_ = [2, 3, 4]
    PO = [5, 6]
    pending = []

    def flush():
        while pending:
            pending.pop(0)()

    def latent(c0, nm, dim, dst, dkey, nT, nkeys, gl, glkey):
        for m in range(nm):
            pj, pk = C.bank("pj", PJ)
            C.mmg(pj[:, :], [(wA[:, k, c0 + m * 128:c0 + (m + 1) * 128], nT[:, k, :]) for k in range(8)],
                  ["wA_sb"] + nkeys, [pk])
            C.act(zf[:, m, :], pj[:, :], AF.Copy, [pk], ["zf%d" % m])
            C.act(sq[:, m, :], pj[:, :], AF.Square, [pk], ["sq%d" % m])
        pj, pk = C.bank("pj", PJ)
        C.mmg(pj[:, :], [(C.onesf[:, :], sq[:, m, :]) for m in range(nm)], ["onesf"] + ["sq%d" % m for m in range(nm)], [pk])
        C.act(rr[:], pj[:, :], AF.Sqrt, [pk, "epsn"], ["rr"], bias=C.epsn[:], scale=1.0 / dim)
        C.recip(rr[:], rr[:], ["rr"], ["rr"])
        for m in range(nm):
            C.stt("dve", dst[:, m, :], zf[:, m, :], gl[:, m:m + 1], rr[:], ALU.mult, ALU.mult, ["zf%d" % m, "rr", glkey], [dkey])

    def stage_T(tc):
        nT = nTs[tc % 2]
        nkeys = []
        for ti in range(4):
            hb = C.rot("ht", 2)
            P.dma("sp", C.ht[hb][:], hsrc(4 * tc + ti), reads=[hkey(4 * tc + ti)], writes=["ht%d" % hb])
            nk = "nT%d_%d" % (tc % 2, ti)
            C.norm_T(C.ht[hb][:], "ht%d" % hb, nT, nk, ti * 128, gA, "gA_sb")
            nkeys.append(nk)
        return nT, nkeys

    def stage_P1(tc, nT, nkeys):
        t0 = tc * 512
        latent(0, 3, 384.0, cqn, "cqn", nT, nkeys, gq, "gq_sb")
        latent(384, 2, 256.0, ckvn, "ckvn", nT, nkeys, gkv, "gkv_sb")
        P.dma("sp", Ct[64:96, :], ropeC_d[:, t0:t0 + 512], writes=["Ct"])
        P.dma("sp", St[64:96, :], ropeS_d[:, t0:t0 + 512], writes=["St"])
        pA, pAk = C.bank("pj", PJ)
        C.mmg(pA[0:96, :], [(wA[:, k, 640:736], nT[:, k, :]) for k in range(8)], ["wA_sb"] + nkeys, [pAk])
        C.tt("dve", t1[64:96, :], pA[64:96, :], Ct[64:96, :], ALU.mult, [pAk, "Ct"], ["t1"])
        pB, pBk = C.bank("pj", PJ)
        C.mmg(pB[0:96, :], [(wA[:, k, 736:832], nT[:, k, :]) for k in range(8)], ["wA_sb"] + nkeys, [pBk])
        C.tt("dve", t2[64:96, :], pB[64:96, :], St[64:96, :], ALU.mult, [pBk, "St"], ["t2"])
        for hh in range(4):
            C.tt("pool", Kh[64:96, hh, t0:t0 + 512], t1[64:96, :], t2[64:96, :], ALU.add, ["t1", "t2"], ["Kh_%d" % tc])

    def stage_P2(tc):
        t0 = tc * 512
        Qh = Qhs[tc % 2]
        qk = "Qh%d" % (tc % 2)
        for hh in range(4):
            pA, pAk = C.bank("pj", PJ)
            C.mmg(pA[0:96, :], [(wq[:, m, hh * 192:hh * 192 + 96], cqn[:, m, :]) for m in range(3)], ["wq_sb", "cqn"], [pAk])
            C.cp("act", Qh[0:64, hh, :], pA[0:64, :], [pAk], [qk])
            C.tt("dve", t1[64:96, :], pA[64:96, :], Ct[64:96, :], ALU.mult, [pAk, "Ct"], ["t1"])
            pB, pBk = C.bank("pj", PJ)
            C.mmg(pB[0:96, :], [(wq[:, m, hh * 192 + 96:hh * 192 + 192], cqn[:, m, :]) for m in range(3)], ["wq_sb", "cqn"], [pBk])
            C.tt("dve", t2[64:96, :], pB[64:96, :], St[64:96, :], ALU.mult, [pBk, "St"], ["t2"])
            C.tt("pool", Qh[64:96, hh, :], t1[64:96, :], t2[64:96, :], ALU.add, ["t1", "t2"], [qk])
        for hh in range(4):
            pj, pk = C.bank("pj", PJ)
            C.mmg(pj[0:64, :], [(wkv[:, j, hh * 64:(hh + 1) * 64], ckvn[:, j, :]) for j in range(2)], ["wkv_sb", "ckvn"], [pk])
            C.cp("act", Kh[0:64, hh, t0:t0 + 512], pj[0:64, :], [pk], ["Kh_%d" % tc])
        for ti in range(4):
            pj, pk = C.bank("pj", PJ)
            C.mmg(pj[:, 0:256], [(ckvn[:, j, ti * 128:(ti + 1) * 128], wkv[:, j, 256:512]) for j in range(2)], ["wkv_sb", "ckvn"], [pk])
            C.cp("act", Vt[:, 4 * tc + ti, :, 0:64], pj[:, 0:256].rearrange("p (h d) -> p h d", h=4), [pk], ["Vt_%d" % tc])

    def head(tc, hh):
        Qh = Qhs[tc % 2]
        qk = "Qh%d" % (tc % 2)
        po, pok = C.bank("po", PO)
        nkt = 4 * tc + 4

        def score(j):
            pairs = [(Kh[:, hh, j * 128:(j + 1) * 128], Qh[:, hh, :])]
            rd = [qk, "Kh_%d" % (j // 4), "Kh_pad", "Qh_pad"]
            if j >= 4 * tc:
                m = j - 4 * tc
                pairs.append((C.ident[:, :], dmask[:, m * 512:(m + 1) * 512]))
                rd += ["ident", "dmask_sb"]
            return pairs, rd

        def vfn(j):
            return Vt[:, j, hh, :], ["Vt_%d" % (j // 4), "Vt_pad"]
        attn_loop(C, list(range(nkt)), score, SC_MLA, vfn, po, pok, SB_, pts, "pt", after_first=flush)

        def fin():
            C.ts("dve", rsrow[64:65, :], po[64:65, :], 1e-30, ALU.add, [pok], ["rsrow"])
            C.recip(rsrow[64:65, :], rsrow[64:65, :], ["rsrow"], ["rsrow"])
            pj, pk = C.bank("pj", PJ)
            C.mm(pj[0:64, :], C.onesf[64:65, 0:64], rsrow[64:65, :], True, True, ["onesf", "rsrow"], [pk])
            C.cp("act", bcs[:, :], pj[0:64, :], [pk], ["bcs"])
            oi = C.rot("ot", 2)
            C.tt("dve", ots[oi][:, :], po[0:64, :], bcs[:, :], ALU.mult, [pok, "bcs"], ["ot%d" % oi])
            osink(hh, tc, ots[oi], "ot%d" % oi)
        pending.append(fin)

    st = stage_T(0)
    stage_P1(0, *st)
    stage_P2(0)
    for tc in range(8):
        head(tc, 0)
        if tc < 7:
            st = stage_T(tc + 1)
        head(tc, 1)
        if tc < 7:
            stage_P1(tc + 1, *st)
        head(tc, 2)
        if tc < 7:
            stage_P2(tc + 1)
        head(tc, 3)
    flush()


def phase_nsa(C, name, L, G, hsrc, hkey, osink):
    P = C.P
    C.begin_phase(name)
    NW = 780
    C.setup_norm()
    gA = C.load_const("gA_sb", L["gAn"], [128, 8], F32)
    wA = C.load_w("wA_sb", L["wAn"], 8, NW)
    w1k = C.load_w("w1k_sb", L["w1k"], 16, 128)
    w1v = C.load_w("w1v_sb", L["w1v"], 16, 128)
    w2k = C.load_w("w2k_sb", L["w2k"], 1, 64)
    w2v = C.load_w("w2v_sb", L["w2v"], 1, 64)
    posk = C.load_w("posk_sb", L["posk"], 1, 16)
    posv = C.load_w("posv_sb", L["posv"], 1, 16)
    maskc = C.load_const("maskc_sb", G["maskc"], [128, 2 * S], BF16)
    eall = C.sb("eall_sb", [128, S], BF16)
    C.memset("pool", eall[64:128, :], 0.0, ["eall_sb"])
    P.dma("pool", eall[0:64, :], G["eall"], writes=["eall_sb"])
    selbias = C.load_const("selbias_sb", G["selbias"], [128, 2048], BF16)
    ovl = C.load_const("ovl_sb", G["ovl"], [128, 130], BF16)
    selg = C.load_const("selg_sb", G["selg"], [12, 768], F32)
    dm4 = C.load_const("dm4_sb", G["dm4"], [128, 512], BF16)
    wm4 = C.load_const("wm4_sb", G["wm4"], [128, 512], BF16)

    Qa = C.sb("Qa", [128, 32, 512], BF16)
    Kw = C.sb("Kw", [128, S], BF16)
    Ks = C.sb("Ks", [128, S], BF16)
    Kc = C.sb("Kc", [128, 256], BF16)
    C.memset("pool", Qa[64:128, :, :], 0.0, ["Qa_aug"])
    C.memset("pool", Kw[64:128, :], 0.0, ["Kw_aug"])
    C.memset("pool", Ks[64:128, :], 0.0, ["Ks_aug"])
    C.memset("pool", Kc[64:128, :], 0.0, ["Kc_aug"])
    P.dma("pool", Qa[64:68, :, :], G["qaug"].rearrange("p (a b) -> p a b", a=32), writes=["Qa_aug"])
    P.dma("pool", Kw[64:68, :], G["kaug"], writes=["Kw_aug"])
    P.dma("pool", Ks[64:68, :], G["kaug"], writes=["Ks_aug"])
    P.dma("pool", Kc[64:68, :], G["kaugc"], writes=["Kc_aug"])
    kc2 = C.sb("kc2", [128, S + 32], BF16)
    vc2 = C.sb("vc2", [128, S + 32], BF16)
    C.memset("pool", kc2[:, S:S + 32], 0.0, ["kc2_tail"])
    C.memset("pool", vc2[:, S:S + 32], 0.0, ["vc2_tail"])
    Vs = C.sb("Vs", [128, 32, 128], BF16)
    Vw = C.sb("Vw", [128, 32, 128], BF16)
    Vc = C.sb("Vc", [128, 2, 128], BF16)
    for (vt_, keys_) in ((Vs, ["Vs_%d" % c for c in range(8)]), (Vw, ["Vw_%d" % c for c in range(8)]), (Vc, ["Vc"])):
        C.memset("pool", vt_[:, :, 65:128], 0.0, keys_)
        C.memset("pool", vt_[:, :, 64:65], 1.0, keys_)
    Gs = C.sb("Gs", [12, S], F32)
    nTs = [C.sb("nT%d" % i, [128, 8, 512], BF16) for i in range(2)]

    PJ = [0, 1]
    SB_ = [2, 3]
    POC, POS, POW = 4, 5, 6

    for tc in range(8):
        t0 = tc * 512
        nb_ = C.rot("nT", 2)
        nT = nTs[nb_]
        nkeys = []
        for ti in range(4):
            hb = C.rot("ht", 2)
            P.dma("sp", C.ht[hb][:], hsrc(4 * tc + ti), reads=[hkey(4 * tc + ti)], writes=["ht%d" % hb])
            nk = "nT%d_%d" % (nb_, ti)
            C.norm_T(C.ht[hb][:], "ht%d" % hb, nT, nk, ti * 128, gA, "gA_sb")
            nkeys.append(nk)

        def proj(c0, m, rows=128):
            pj, pk = C.bank("pj", PJ)
            C.mmg(pj[0:rows, :], [(wA[:, k, c0:c0 + m], nT[:, k, :]) for k in range(8)], ["wA_sb"] + nkeys, [pk])
            return pj, pk
        for r in range(4):
            pj, pk = proj(r * 64, 64, 64)
            C.cp("act", Qa[0:64, 4 * tc:4 * tc + 4, r * 128:(r + 1) * 128],
                 pj[0:64, :].rearrange("p (a b) -> p a b", a=4), [pk], ["Qa_%d" % tc])
        for (c0, dst, dk) in ((256, kc2, "kc2"), (384, vc2, "vc2")):
            pj, pk = proj(c0, 128)
            C.cp("act", dst[0:64, t0:t0 + 512], pj[0:64, :], [pk], [dk])
            if tc == 0:
                C.cp("dve", dst[64:128, 0:511], pj[64:128, 1:512], [pk], [dk])
            else:
                C.cp("dve", dst[64:128, t0 - 1:t0 + 511], pj[64:128, :], [pk], [dk])
        pj, pk = proj(512, 64, 64)
        C.cp("act", Ks[0:64, t0:t0 + 512], pj[0:64, :], [pk], ["Ks_%d" % tc])
        pj, pk = proj(576, 64, 64)
        C.cp("act", Kw[0:64, t0:t0 + 512], pj[0:64, :], [pk], ["Kw_%d" % tc])
        pj, pk = proj(768, 12, 12)
        C.act(Gs[0:12, t0:t0 + 512], pj[0:12, :], AF.Sigmoid, [pk], ["Gs_%d" % tc])
        for ti in range(4):
            pj, pk = C.bank("pj", PJ)
            C.mmg(pj[:, 0:128], [(nT[:, k, ti * 128:(ti + 1) * 128], wA[:, k, 640:768]) for k in range(8)],
                  ["wA_sb"] + nkeys, [pk])
            C.cp("act", Vs[:, 4 * tc + ti, 0:64], pj[:, 0:64], [pk], ["Vs_%d" % tc])
            C.cp("dve", Vw[:, 4 * tc + ti, 0:64], pj[:, 64:128], [pk], ["Vw_%d" % tc])

    xs = C.sb("xs", [128, 256], F32)
    x2 = C.sb("x2", [128, 256], F32)
    hid = C.sb("hid", [128, 256], BF16)
    cbias = C.sb("cbias", [128, 1], F32)
    for (src, skey, w1, w1key, pos, poskey, isk) in ((kc2, "kc2", w1k, "w1k_sb", posk, "posk_sb", True),
                                                     (vc2, "vc2", w1v, "w1v_sb", posv, "posv_sb", False)):
        pj, pk = C.bank("pj", PJ)
        C.mmg(pj[:, 0:1], [(w1[:, j, :], pos[:, 0, j:j + 1]) for j in range(16)], [w1key, poskey], [pk])
        C.cp("act", cbias[:], pj[:, 0:1], [pk], ["cbias"])
        pj, pk = C.bank("pj", PJ)
        C.mmg(pj[:, 0:255], [(w1[:, j, :], src[:, 2 * j:2 * j + 16 * 255:16]) for j in range(16)],
              [w1key, skey, skey + "_tail"], [pk])
        C.memset("dve", xs[:, 255:256], 0.0, ["xs"])
        C.act(xs[:, 0:255], pj[:, 0:255], AF.Identity, [pk, "cbias"], ["xs"], bias=cbias[:])
        C.tt("dve", x2[:], xs[:], xs[:], ALU.mult, ["xs"], ["x2"])
        C.ts("dve", x2[:], x2[:], 0.044715, ALU.mult, ["x2"], ["x2"], s2=1.0, op1=ALU.add)
        C.tt("dve", x2[:], x2[:], xs[:], ALU.mult, ["x2", "xs"], ["x2"])
        C.act(x2[:], x2[:], AF.Sigmoid, ["x2"], ["x2"], scale=1.5957691216057308)
        C.tt("dve", hid[:], xs[:], x2[:], ALU.mult, ["x2", "xs"], ["hid"])
        if isk:
            pj, pk = C.bank("pj", PJ)
            C.mm(pj[0:64, 0:256], w2k[:, 0, :], hid[:], True, True, ["w2k_sb", "hid"], [pk])
            C.cp("act", Kc[0:64, :], pj[0:64, 0:256], [pk], ["Kc"])
        else:
            for nt in range(2):
                pj, pk = C.bank("pj", PJ)
                C.mm(pj[:, 0:64], hid[:, nt * 128:(nt + 1) * 128], w2v[:, 0, :], True, True, ["w2v_sb", "hid"], [pk])
                C.cp("act", Vc[:, nt, 0:64], pj[:, 0:64], [pk], ["Vc"])

    ptc = [[C.sb("ptc%d_%d" % (i, j), [128, 512], BF16) for j in range(2)] for i in range(2)]
    pts = [C.sb("pt%d" % i, [128, 512], BF16) for i in range(4)]
    imp = C.sb("imp", [128, 64], F32)
    wk = C.sb("wk", [128, 64], F32)
    m8 = C.sb("m8", [128, 16], F32)
    rci = C.sb("rci", [128, 4], F32)
    selb = C.sb("selb", [128, 64], BF16)
    selT4s = [C.sb("selT4_%d" % i, [128, 512], BF16) for i in range(2)]
    for i in range(2):
        C.memset("pool", selT4s[i][64:128, :], 0.0, ["selT4_%d" % i])
    rsrow = C.sb("rsrow", [65, 512], F32)
    bcs = C.sb("bcs", [64, 512], F32)
    tmpo = C.sb("tmpo", [64, 512], F32)
    oacc = [C.sb("oacc%d" % i, [64, 512], F32) for i in range(2)]
    oaccb = [C.sb("oaccb%d" % i, [64, 512], BF16) for i in range(2)]
    PJ = [0, 7]
    SB_ = [1, 2, 3]
    pending = []

    def flush():
        while pending:
            pending.pop(0)()

    def finalize(po, pok, br, qb, first):
        ai = qb % 2
        acc, ak = oacc[ai], "oacc%d" % ai
        C.ts("dve", rsrow[64:65, :], po[64:65, :], 1e-30, ALU.add, [pok], ["rsrow"])
        C.recip(rsrow[64:65, :], rsrow[64:65, :], ["rsrow"], ["rsrow"])
        pj, pk = C.bank("pj", PJ)
        items = []
        for r in range(4):
            g = r * 3 + br
            items.append((pj[0:64, r * 128:(r + 1) * 128], selg[0:12, g * 64:(g + 1) * 64],
                          Gs[0:12, qb * 128:(qb + 1) * 128], True, True))
        C.mmlist(items, ["selg_sb", "Gs_%d" % (qb // 4)], [pk])
        C.cp("act", bcs[:, :], pj[0:64, :], [pk], ["bcs"])
        C.tt("dve", tmpo[:, :], po[0:64, :], bcs[:, :], ALU.mult, [pok, "bcs"], ["tmpo"])
        pj2, pk2 = C.bank("pj", PJ)
        C.mm(pj2[0:64, :], C.onesf[64:65, 0:64], rsrow[64:65, :], True, True, ["onesf", "rsrow"], [pk2])
        C.cp("act", bcs[:, :], pj2[0:64, :], [pk2], ["bcs"])
        if first:
            C.tt("dve", acc[:, :], tmpo[:, :], bcs[:, :], ALU.mult, ["tmpo", "bcs"], [ak])
        else:
            C.tt("dve", tmpo[:, :], tmpo[:, :], bcs[:, :], ALU.mult, ["tmpo", "bcs"], ["tmpo"])
            C.tt("pool", acc[:, :], acc[:, :], tmpo[:, :], ALU.add, ["tmpo", ak], [ak])

    def qinfo(qb):
        return Qa[:, qb, :], ["Qa_%d" % (qb // 4), "Qa_aug"]

    def cmp_stage(qb):
        q_rhs, qkeys = qinfo(qb)
        pc = ptc[qb % 2]
        selT4, stk = selT4s[qb % 2], "selT4_%d" % (qb % 2)
        ntn = 1 if qb < 16 else 2
        po = C.banks[POC]
        for nt in range(ntn):
            sbk, sk = C.bank("s", SB_)
            items = [(sbk[:, :], Kc[:, nt * 128:(nt + 1) * 128], q_rhs, True, False)]
            for r in range(4):
                items.append((sbk[:, r * 128:(r + 1) * 128], C.ident[:, :],
                              maskc[:, nt * S + qb * 128:nt * S + (qb + 1) * 128], False, r == 3))
            C.mmlist(items, qkeys + ["Kc", "Kc_aug", "ident", "maskc_sb"], [sk])
            C.act(pc[nt][:], sbk[:, :], AF.Exp, [sk], ["ptc%d_%d" % (qb % 2, nt)], scale=SC_NSA)
        for nt in range(ntn):
            C.mm(po[:, :], Vc[:, nt, :], pc[nt][:], nt == 0, nt == ntn - 1,
                 ["Vc", "ptc%d_%d" % (qb % 2, nt)], ["bank%d" % POC])
        pj, pk = C.bank("pj", PJ)
        items = []
        for r in range(4):
            for nt in range(ntn):
                items.append((pj[:, r * 65:(r + 1) * 65], pc[nt][:, r * 128:(r + 1) * 128], ovl[:, nt * 65:(nt + 1) * 65],
                              nt == 0, nt == ntn - 1))
        C.mmlist(items, ["ovl_sb"] + ["ptc%d_%d" % (qb % 2, nt) for nt in range(ntn)], [pk])
        for r in range(4):
            C.ts("dve", rci[:, r:r + 1], pj[:, r * 65 + 64:r * 65 + 65], 1e-30, ALU.add, [pk], ["rci"])
        C.recip(rci[:, 0:4], rci[:, 0:4], ["rci"], ["rci"])
        for r in range(4):
            prev = selbias[:, qb * 64:(qb + 1) * 64] if r == 0 else imp[:]
            C.stt("dve", imp[:], pj[:, r * 65:r * 65 + 64], rci[:, r:r + 1], prev, ALU.mult, ALU.add,
                  [pk, "rci", "imp", "selbias_sb"], ["imp"])
        P.op("dve", lambda e: e.max(out=m8[:, 0:8], in_=imp[:]), ["imp"], ["m8"])
        P.op("dve", lambda e: e.match_replace(out=wk[:], in_to_replace=m8[:, 0:8], in_values=imp[:], imm_value=-1e9),
             ["imp", "m8"], ["wk"])
        P.op("dve", lambda e: e.max(out=m8[:, 8:16], in_=wk[:]), ["wk"], ["m8"])
        C.ts("dve", wk[:], imp[:], m8[:, 15:16], ALU.is_ge, ["imp", "m8"], ["wk"])
        C.ts("dve", selb[:], wk[:], -NEG, ALU.mult, ["wk"], ["selb"], s2=NEG, op1=ALU.add)

        def tail(qb=qb, selT4=selT4, stk=stk, po=po):
            C.tr(C.pst[0:64, 0:128], selb[:, :], C.ident[:, :], ["selb", "ident"], ["pst"])
            for r in range(4):
                C.cp("act" if r % 2 == 0 else "dve", selT4[0:64, r * 128:(r + 1) * 128], C.pst[0:64, 0:128], ["pst"], [stk])
            finalize(po, "bank%d" % POC, 0, qb, True)
        pending.append(tail)

    def sel_stage(qb):
        q_rhs, qkeys = qinfo(qb)
        selT4, stk = selT4s[qb % 2], "selT4_%d" % (qb % 2)
        po = C.banks[POS]

        def score(kt):
            pairs = [(Ks[:, kt * 128:(kt + 1) * 128], q_rhs), (eall[:, kt * 128:(kt + 1) * 128], selT4[:, :])]
            rd = qkeys + ["Ks_%d" % (kt // 4), "Ks_aug", "eall_sb", stk]
            if kt == qb:
                pairs.append((C.ident[:, :], dm4[:, :]))
                rd += ["ident", "dm4_sb"]
            return pairs, rd
        attn_loop(C, list(range(qb + 1)), score, SC_NSA, lambda kt: (Vs[:, kt, :], ["Vs_%d" % (kt // 4)]),
                  po, "bank%d" % POS, SB_, pts, "pt", after_first=None)

        def tail(qb=qb, po=po):
            finalize(po, "bank%d" % POS, 1, qb, False)
            ai = qb % 2
            C.cp("act", oaccb[ai][:, :], oacc[ai][:, :], ["oacc%d" % ai], ["oaccb%d" % ai])
            osink(qb, oaccb[ai], "oaccb%d" % ai)
        pending.append(tail)

    def win_stage(qb):
        q_rhs, qkeys = qinfo(qb)
        po = C.banks[POW]
        k0 = max(0, qb - 4)

        def score(kt):
            pairs = [(Kw[:, kt * 128:(kt + 1) * 128], q_rhs)]
            rd = qkeys + ["Kw_%d" % (kt // 4), "Kw_aug"]
            if kt == qb:
                pairs.append((C.ident[:, :], dm4[:, :]))
                rd += ["ident", "dm4_sb"]
            if kt == qb - 4:
                pairs.append((C.ident[:, :], wm4[:, :]))
                rd += ["ident", "wm4_sb"]
            return pairs, rd
        attn_loop(C, list(range(k0, qb + 1)), score, SC_NSA, lambda kt: (Vw[:, kt, :], ["Vw_%d" % (kt // 4)]),
                  po, "bank%d" % POW, SB_, pts, "pt", after_first=flush)

        def tail(qb=qb, po=po):
            finalize(po, "bank%d" % POW, 2, qb, False)
        pending.append(tail)

    cmp_stage(0)
    for qb in range(32):
        win_stage(qb)
        if qb + 1 < 32:
            cmp_stage(qb + 1)
        sel_stage(qb)
    flush()


def phase_ffn(C, name, L, G, final, hown, hownkey, hhalo, hhalokey, oall, osink):
    P = C.P
    C.begin_phase(name)
    C.setup_norm()
    fl = C.flags
    g2 = C.load_const("g2_sb", L["g2"], [128, 8], F32)
    cw = C.load_const("cw_sb", L["cw"], [128, 176], F32)
    if final:
        gF = C.sb("gF_sb", [128, DM], F32)
        P.dma("pool", gF[:], G["gF"][0:1, :].partition_broadcast(128), writes=["gF_sb"])
    wup = C.load_w("wup_sb", L["wup"], 8, 5632)
    wdn = C.sb("wdn_sb", [128, 22, DM], BF16)

    def load_wdn():
        for k in range(22):
            P.dma("pool", wdn[:, k, :], L["wdn"][k * 128:(k + 1) * 128, :], writes=["wdn_sb"])
    wo_d = L["wo"]

    NC_ = 256
    aT = [C.sb("aT%d" % i, [128, NC_], BF16) for i in range(4)]
    hm = C.sb("hm", [128, 2, DM], F32)
    n2T = C.sb("n2T", [128, 8, NC_], BF16)
    oTb = C.sb("oTb", [128, 8, NC_], BF16)
    oa = [C.sb("oa%d" % i, [128, NC_], BF16) for i in range(2)]
    ob = [C.sb("ob%d" % i, [128, NC_], BF16) for i in range(2)]
    wob = [C.sb("wob%d" % i, [128, DM], BF16) for i in range(4)]
    ubuf = [C.sb("ubuf%d" % i, [128, NC_ + 2], F32) for i in range(3)]
    tb = [C.sb("tb%d" % i, [128, NC_], F32) for i in range(4)]
    sg = C.sb("sg", [128, NC_], F32)
    carry = C.sb("carry", [128, 44, 2], F32)
    res = C.sb("res", [128, DM], F32)
    ss2 = C.sb("ss2", [128, 1], F32)

    ACC = [0, 1, 2, 3]
    UP = [4, 5, 6]

    def chunk(ci, halo):
        nt_ = 1 if halo else 2
        ncol = nt_ * 128
        c0 = 1920 if halo else ci * 256
        for k in range(8):
            b = C.rot("oa", 2)
            if halo:
                P.dma("sp", oa[b][:, 0:ncol], oall[0][k * 128:(k + 1) * 128, c0:c0 + ncol], reads=["oall0"], writes=["oa%d" % b])
                C.ts("dve", oTb[:, k, 0:ncol], oa[b][:, 0:ncol], fl[:, 1:2], ALU.mult, ["oa%d" % b, "flags"], ["oTb"])
            else:
                P.dma("sp", oa[b][:, 0:ncol], oall[0][k * 128:(k + 1) * 128, c0:c0 + ncol], reads=["oall0"], writes=["oa%d" % b])
                P.dma("sp", ob[b][:, 0:ncol], oall[1][k * 128:(k + 1) * 128, c0:c0 + ncol], reads=["oall1"], writes=["ob%d" % b])
                C.ts("dve", oa[b][:, 0:ncol], oa[b][:, 0:ncol], fl[:, 0:1], ALU.mult, ["oa%d" % b, "flags"], ["oa%d" % b])
                C.stt("dve", oTb[:, k, 0:ncol], ob[b][:, 0:ncol], fl[:, 1:2], oa[b][:, 0:ncol], ALU.mult, ALU.add,
                      ["oa%d" % b, "ob%d" % b, "flags"], ["oTb"])
        for k in range(8):
            wb = C.rot("wob", 4)
            P.dma("pool", wob[wb][:, :], wo_d[k * 128:(k + 1) * 128, :], writes=["wob%d" % wb])
            items = []
            for ti in range(nt_):
                for hf in range(2):
                    items.append((C.banks[ACC[ti * 2 + hf]][:, :], oTb[:, k, ti * 128:(ti + 1) * 128],
                                  wob[wb][:, hf * 512:(hf + 1) * 512], k == 0, k == 7))
            C.mmlist(items, ["oTb", "wob%d" % wb], ["bank%d" % ACC[i] for i in range(nt_ * 2)])
        if ci == 0 and not halo:
            load_wdn()
        for ti in range(nt_):
            hb = C.rot("ht", 2)
            if halo:
                P.dma("sp", C.ht[hb][:], hhalo, reads=[hhalokey], writes=["ht%d" % hb])
                C.ts("dve", C.ht[hb][:], C.ht[hb][:], fl[:, 1:2], ALU.mult, ["ht%d" % hb, "flags"], ["ht%d" % hb])
            else:
                P.dma("sp", C.ht[hb][:], hown(2 * ci + ti), reads=[hownkey(2 * ci + ti)], writes=["ht%d" % hb])
            for hf in range(2):
                C.tt("dve", hm[:, ti, hf * 512:(hf + 1) * 512], C.banks[ACC[ti * 2 + hf]][:, :],
                     C.ht[hb][:, hf * 512:(hf + 1) * 512], ALU.add, ["bank%d" % ACC[ti * 2 + hf], "ht%d" % hb], ["hm%d" % ti])
            C.norm_T(hm[:, ti, :], "hm%d" % ti, n2T, "n2T_%d" % ti, ti * 128, g2, "g2_sb")
        nkeys = ["n2T_%d" % ti for ti in range(nt_)]
        dq = []

        def down(i, ai):
            items = []
            for ti in range(nt_):
                for hf in range(2):
                    items.append((C.banks[ACC[ti * 2 + hf]][:, :], aT[ai][:, ti * 128:(ti + 1) * 128],
                                  wdn[:, i, hf * 512:(hf + 1) * 512], i == 0, i == 21))
            C.mmlist(items, ["aT%d" % ai, "wdn_sb"], ["bank%d" % ACC[q] for q in range(nt_ * 2)])
        for i in range(22):
            tfin = []
            for part in range(2):
                fc = i + 22 * part
                up, upk = C.bank("up", UP)
                C.mmg(up[:, 0:ncol], [(wup[:, k, fc * 128:(fc + 1) * 128], n2T[:, k, 0:ncol]) for k in range(8)],
                      ["wup_sb"] + nkeys, [upk])
                ub = C.rot("ubuf", 3)
                u = ubuf[ub]
                uk = "ubuf%d" % ub
                C.cp("pool", u[:, 0:2], carry[:, fc, :], ["carry%d" % fc], [uk])
                C.cp("act", u[:, 2:2 + ncol], up[:, 0:ncol], [upk], [uk])
                if not halo:
                    ta = C.rot("tb", 4)
                    C.act(tb[ta][:, 0:ncol], up[:, 0:ncol], AF.Identity, [upk, "cw_sb"], ["tb%d" % ta],
                          bias=cw[:, fc * 4 + 3:fc * 4 + 4], scale=cw[:, fc * 4 + 2:fc * 4 + 3])
                    C.stt("dve", tb[ta][:, 0:ncol], u[:, 1:1 + ncol], cw[:, fc * 4 + 1:fc * 4 + 2], tb[ta][:, 0:ncol],
                          ALU.mult, ALU.add, [uk, "cw_sb", "tb%d" % ta], ["tb%d" % ta])
                    C.stt("dve", tb[ta][:, 0:ncol], u[:, 0:ncol], cw[:, fc * 4:fc * 4 + 1], tb[ta][:, 0:ncol],
                          ALU.mult, ALU.add, [uk, "cw_sb", "tb%d" % ta], ["tb%d" % ta])
                    tfin.append(ta)
                C.cp("pool", carry[:, fc, :], u[:, ncol:ncol + 2], [uk], ["carry%d" % fc])
            if not halo:
                C.act(sg[:, 0:ncol], tb[tfin[0]][:, 0:ncol], AF.Silu, ["tb%d" % tfin[0]], ["sg"])
                ai = C.rot("aT", 4)
                C.tt("dve", aT[ai][:, 0:ncol], sg[:, 0:ncol], tb[tfin[1]][:, 0:ncol], ALU.mult,
                     ["sg", "tb%d" % tfin[1]], ["aT%d" % ai])
                dq.append((i, ai))
                if len(dq) > 2:
                    down(*dq.pop(0))
        while dq:
            down(*dq.pop(0))
        if halo:
            return
        for ti in range(nt_):
            for hf in range(2):
                bk = ACC[ti * 2 + hf]
                C.tt("dve", res[:, hf * 512:(hf + 1) * 512], C.banks[bk][:, :], hm[:, ti, hf * 512:(hf + 1) * 512],
                     ALU.add, ["bank%d" % bk, "hm%d" % ti], ["res"])
            if final:
                C.memset("dve", ss2[:], 0.0, ["ss2"])
                C.act(C.junk[:], res[:], AF.Square, ["res", "ss2"], ["junk", "ss2"], accum=ss2[:])
                C.act(ss2[:], ss2[:], AF.Sqrt, ["ss2", "epsn"], ["ss2"], bias=C.epsn[:], scale=1.0 / DM)
                C.recip(ss2[:], ss2[:], ["ss2"], ["ss2"])
                C.stt("dve", res[:], res[:], ss2[:, 0:1], gF[:], ALU.mult, ALU.mult, ["res", "ss2", "gF_sb"], ["res"])
            osink(2 * ci + ti, res, "res")

    C.memset("pool", carry[:], 0.0, ["carry%d" % fc for fc in range(44)])
    chunk(0, True)
    for c in range(8):
        chunk(c, False)


LAYER_IN = [("wAm", [DM, 832]), ("gAm", [128, 8]), ("wq", [384, 768]), ("gq", [128, 3]), ("wkv", [256, 512]),
            ("gkv", [128, 2]), ("wAn", [DM, 780]), ("gAn", [128, 8]), ("w1k", [2048, 128]), ("w1v", [2048, 128]),
            ("w2k", [128, 64]), ("w2v", [128, 64]), ("posk", [128, 16]), ("posv", [128, 16]),
            ("wo", [DM, DM]), ("wup", [DM, 5632]), ("wdn", [2816, DM]), ("g2", [128, 8]), ("cw", [128, 176])]
GLOB_IN = [("ropeC", [32, S], F32), ("ropeS", [32, S], F32), ("dmask", [128, 2048], BF16),
           ("maskc", [128, 2 * S], BF16), ("eall", [64, S], BF16), ("selbias", [128, 2048], BF16),
           ("ovl", [128, 130], BF16), ("selg", [12, 768], F32), ("dm4", [128, 512], BF16), ("wm4", [128, 512], BF16),
           ("qaug", [4, 32 * 512], BF16), ("kaug", [4, S], BF16), ("kaugc", [4, 256], BF16), ("gF", [1, DM], F32)]
GROUPS = [[0, 1], [2, 3], [4, 5], [6, 7]]


def build_fused(nlayers=2):
    nc = bass.Bass("TRN2", target_bir_lowering=False)
    C = Ctx(nc)
    P = C.P
    x_d = C.dram("x", [S, DM], F32)
    xown_d = C.dram("xown", [2048, DM], F32)
    xhalo_d = C.dram("xhalo", [128, DM], F32)
    flags_d = C.dram("flags", [128, 2], F32)
    G = {n: C.dram(n, sh, dt) for (n, sh, dt) in GLOB_IN}
    Ls = [{n: C.dram("%s_%d" % (n, l), sh, F32) for (n, sh) in LAYER_IN} for l in range(nlayers)]
    out_d = C.dram("hout", [2048, DM], F32, out=True)
    P.dma("pool", C.flags[:], flags_d, writes=["flags"])

    omy = [[nc.dram_tensor("omy_%d_%d" % (l, c), [512, 2048], BF16) for c in range(2)] for l in range(nlayers)]
    oall = [[nc.dram_tensor("oall_%d_%d" % (l, c), [1024, 2048], BF16) for c in range(2)] for l in range(nlayers)]
    hmy = [nc.dram_tensor("hmy_%d" % j, [512, DM], F32) for j in range(4)]
    hall = [nc.dram_tensor("hall_%d" % j, [1024, DM], F32) for j in range(4)]

    for l in range(nlayers):
        if l == 0:
            hsrc = lambda g: x_d[g * 128:(g + 1) * 128, :]
            hkey = lambda g: "x"
        else:
            def hsrc(g):
                r, w = g // 16, g % 16
                return hall[w // 4][r * 512 + (w % 4) * 128:r * 512 + (w % 4) * 128 + 128, :]
            hkey = lambda g: "hall%d" % ((g % 16) // 4)

        def osink_mla(hh, tc, ot, otkey, l=l):
            c = tc // 4
            col = (tc % 4) * 512
            P.dma("sp", omy[l][c][hh * 64:(hh + 1) * 64, col:col + 512], ot[:, :], reads=[otkey], writes=["omy%d" % c])

        def osink_nsa(qb, ot, otkey, l=l):
            c = qb // 16
            col = (qb % 16) * 128
            P.dma("sp", omy[l][c][256:512, :].rearrange("(r d) t -> d r t", d=64)[:, :, col:col + 128],
                  ot[:, :].rearrange("p (r t) -> p r t", r=4), reads=[otkey], writes=["omy%d" % c])
            if qb % 16 == 15:
                P.cc("AllGather", GROUPS, omy[l][c].ap().opt(), oall[l][c].ap().opt(), reads=["omy%d" % c], writes=["oall%d" % c])

        phase_mla(C, "m%d_" % l, Ls[l], G, hsrc, hkey, osink_mla)
        phase_nsa(C, "n%d_" % l, Ls[l], G, hsrc, hkey, osink_nsa)
        final = (l == nlayers - 1)
        if l == 0:
            hown = lambda t: xown_d[t * 128:(t + 1) * 128, :]
            hownkey = lambda t: "xown"
            hhalo, hhalokey = xhalo_d, "xhalo"
        else:
            hown = lambda t: hmy[t // 4][(t % 4) * 128:(t % 4) * 128 + 128, :]
            hownkey = lambda t: "hmy%d" % (t // 4)
            hhalo, hhalokey = hall[3][384:512, :], "hall3"
        if final:
            def osink_ffn(t, res, rkey):
                P.dma("sp", out_d[t * 128:(t + 1) * 128, :], res[:], reads=[rkey])
        else:
            def osink_ffn(t, res, rkey):
                P.dma("sp", hmy[t // 4][(t % 4) * 128:(t % 4) * 128 + 128, :], res[:], reads=[rkey], writes=["hmy%d" % (t // 4)])
                if t % 4 == 3:
                    j = t // 4
                    P.cc("AllGather", GROUPS, hmy[j].ap().opt(), hall[j].ap().opt(), reads=["hmy%d" % j], writes=["hall%d" % j])
        phase_ffn(C, "f%d_" % l, Ls[l], G, final, hown, hownkey, hhalo, hhalokey, oall[l], osink_ffn)
    P.emit()
    return nc


def _pk(g, nk):
    return np.ascontiguousarray(np.asarray(g, np.float32).reshape(nk, 128).T)


def _consts():
    c = {}
    p = np.arange(128)[:, None]
    i512 = np.arange(512)[None, :]
    dm = np.zeros((128, 4, 512), np.float32)
    for m in range(4):
        dm[:, m, :] = np.where(128 * m + p <= i512, 0.0, NEG)
    c["dmask"] = dm.reshape(128, 2048).astype(NPBF)
    i128 = np.arange(128)[None, :]
    c["dm4"] = np.tile(np.where(p <= i128, 0.0, NEG), (1, 4)).astype(NPBF)
    c["wm4"] = np.tile(np.where(i128 < p, 0.0, NEG), (1, 4)).astype(NPBF)
    n = np.arange(256)[:, None]
    t = np.arange(S)[None, :]
    mc = np.where((t >= 16 * n + 31) & (n <= 254), 0.0, NEG).astype(np.float32)
    c["maskc"] = np.ascontiguousarray(mc.reshape(2, 128, S).transpose(1, 0, 2).reshape(128, 2 * S)).astype(NPBF)
    j = np.arange(64)[:, None]
    c["eall"] = (np.arange(S)[None, :] // 64 == j).astype(np.float32).astype(NPBF)
    tt_ = np.arange(S)
    cur = (tt_ // 64)[:, None]
    jj = np.arange(64)[None, :]
    sbias = np.zeros((S, 64), np.float32)
    sbias[np.broadcast_to(jj > cur, (S, 64))] = -1e4
    sbias[np.broadcast_to((jj == 0) | (jj == cur) | (jj == cur - 1), (S, 64))] = 1e4
    c["selbias"] = np.ascontiguousarray(sbias.reshape(32, 128, 64).transpose(1, 0, 2).reshape(128, 2048)).astype(NPBF)
    cs = (np.arange(256) * 16)[:, None]
    ss_ = (np.arange(64) * 64)[None, :]
    ov = ((cs < ss_ + 64) & (cs + 32 > ss_)).astype(np.float32)
    ov[255] = 0.0
    ov1 = np.concatenate([ov, np.ones((256, 1), np.float32)], 1)
    c["ovl"] = np.ascontiguousarray(ov1.reshape(2, 128, 65).transpose(1, 0, 2).reshape(128, 130)).astype(NPBF)
    sg = np.zeros((12, 12, 64), np.float32)
    for g in range(12):
        sg[g, g, :] = 1.0
    c["selg"] = sg.reshape(12, 768)
    k = np.arange(S)
    c["kaug"] = np.stack([np.ones(S), np.ones(S), k // 64, k % 64]).astype(np.float32).astype(NPBF)
    e = np.arange(256) * 16 + 31
    c["kaugc"] = np.stack([np.ones(256), np.ones(256), e // 64, e % 64]).astype(np.float32).astype(NPBF)
    inv = 1.0 / (10000.0 ** (np.arange(0, 32, 2, dtype=np.float32) / 32))
    ang = np.arange(S, dtype=np.float32)[:, None] * inv[None, :]
    cos, sin = np.cos(ang).T.astype(np.float32), np.sin(ang).T.astype(np.float32)
    c["ropeC"] = np.ascontiguousarray(np.concatenate([cos, cos], 0))
    c["ropeS"] = np.ascontiguousarray(np.concatenate([-sin, sin], 0))
    return c


def _qaug(group):
    slopes = np.exp2(-8.0 * np.arange(1, 9, dtype=np.float32) / 8)
    t = np.arange(S).reshape(32, 1, 128)
    out = np.zeros((4, 32, 4, 128), np.float32)
    for r in range(4):
        a = slopes[group * 4 + r] / SC_NSA
        out[0, :, r, :] = (-a * 64 * (t // 64))[:, 0, :]
        out[1, :, r, :] = (-a * (t % 64))[:, 0, :]
        out[2, :, r, :] = a * 64
        out[3, :, r, :] = a
    return out.reshape(4, 32 * 512).astype(NPBF)


_PROG = {}


def _prog(nlayers=2):
    if nlayers not in _PROG:
        _PROG[nlayers] = build_fused(nlayers)
    return _PROG[nlayers]


def _layer_maps(l, I, c):
    m = {}
    w_in, w_uq, w_ukv = I["w_in"][l], I["w_uq"][l], I["w_ukv"][l]
    sw = list(range(656, 672)) + list(range(640, 656))
    colsA = list(range(0, 640)) + list(range(0, 64)) + list(range(640, 672)) + list(range(0, 64)) + sw
    m["wAm"] = np.ascontiguousarray(w_in[:, colsA])
    m["gAm"] = _pk(I["attn_norm"][l], 8)
    qc, kc, vc = [], [], []
    for hh in range(4 * c, 4 * c + 4):
        base = 96 * hh
        nope = list(range(base, base + 64))
        rope = list(range(base + 64, base + 96))
        qc += nope + rope + nope + rope[16:] + rope[:16]
        kc += list(range(128 * hh, 128 * hh + 64))
        vc += list(range(128 * hh + 64, 128 * hh + 128))
    m["wq"] = np.ascontiguousarray(w_uq[:, qc])
    m["gq"] = _pk(I["q_norm"][l], 3)
    m["wkv"] = np.ascontiguousarray(w_ukv[:, kc + vc])
    m["gkv"] = _pk(I["kv_norm"][l], 2)
    g = c
    q0 = 672 + 256 * g
    o = 1184
    rng = lambda a: list(range(a, a + 64))
    kcc, vcc, ksc = rng(o + 64 * g), rng(o + 128 + 64 * g), rng(o + 256 + 64 * g)
    vsc, kwc, vwc = rng(o + 384 + 64 * g), rng(o + 512 + 64 * g), rng(o + 640 + 64 * g)
    gtc = list(range(1952 + 12 * g, 1952 + 12 * g + 12))
    cols = list(range(q0, q0 + 256)) + kcc + kcc + vcc + vcc + ksc + kwc + vsc + vwc + gtc
    m["wAn"] = np.ascontiguousarray(w_in[:, cols])
    m["gAn"] = m["gAm"]
    posT = lambda pz: np.ascontiguousarray(np.asarray(pz, np.float32).reshape(16, 128).T)
    m["w1k"] = np.ascontiguousarray(I["cmp_k_w1"][l].reshape(2048, 128))
    m["w1v"] = np.ascontiguousarray(I["cmp_v_w1"][l].reshape(2048, 128))
    m["w2k"] = np.ascontiguousarray(I["cmp_k_w2"][l])
    m["w2v"] = np.ascontiguousarray(I["cmp_v_w2"][l])
    m["posk"] = posT(I["cmp_pos_k"][l])
    m["posv"] = posT(I["cmp_pos_v"][l])
    perm = list(range(0, 256)) + list(range(512, 768)) + list(range(256, 512)) + list(range(768, 1024))
    m["wo"] = np.ascontiguousarray(I["w_o"][l][perm, :])
    m["wup"] = np.ascontiguousarray(I["w_up"][l])
    m["wdn"] = np.ascontiguousarray(I["w_down"][l])
    m["g2"] = _pk(I["ffn_norm"][l], 8)
    cwv = np.stack([I["conv_w"][l][0], I["conv_w"][l][1], I["conv_w"][l][2], I["conv_b"][l]], -1)
    m["cw"] = np.ascontiguousarray(cwv.reshape(44, 128, 4).transpose(1, 0, 2).reshape(128, 176)).astype(np.float32)
    return m


def make_maps(I, nlayers=2):
    cst = _consts()
    lm = [[_layer_maps(l, I, c) for c in range(2)] for l in range(nlayers)]
    maps = []
    for b in range(4):
        for c in range(2):
            m = {"x": np.ascontiguousarray(I["x"][b]),
                 "xown": np.ascontiguousarray(I["x"][b][2048 * c:2048 * c + 2048]),
                 "xhalo": np.ascontiguousarray(I["x"][b][1920:2048]),
                 "flags": np.ascontiguousarray(np.tile(np.array([[1.0 - c, float(c)]], np.float32), (128, 1)))}
            for (n, sh, dt) in GLOB_IN:
                if n == "qaug":
                    m[n] = _qaug(c)
                elif n == "gF":
                    m[n] = np.ascontiguousarray(np.asarray(I["final_norm"], np.float32).reshape(1, DM))
                else:
                    m[n] = cst[n]
            for l in range(nlayers):
                for k, v in lm[l][c].items():
                    m["%s_%d" % (k, l)] = v
            maps.append(m)
    return maps


def kernel(**inputs):
    I = {k: np.asarray(v, dtype=np.float32) for k, v in inputs.items()}
    res = run_bass_kernel_spmd(_prog(2), make_maps(I, 2), core_ids=list(range(8))).results
    out = np.empty((4, S, DM), np.float32)
    for b in range(4):
        for c in range(2):
            out[b, 2048 * c:2048 * c + 2048] = np.asarray(res[2 * b + c]["hout"])
    return out
```

```python
import numpy as np
import ml_dtypes
import concourse.bass as bass
import concourse.mybir as mybir
from concourse.bass_utils import run_bass_kernel_spmd

F32 = mybir.dt.float32
BF16 = mybir.dt.bfloat16
ALU = mybir.AluOpType
AF = mybir.ActivationFunctionType
AX = mybir.AxisListType
NPBF = ml_dtypes.bfloat16

ENGS = ["pe", "act", "dve", "pool", "sp"]
DMA_POOL = 12
S = 4096
DM = 1024
NEG = -30000.0
SC_MLA = 96 ** -0.5
SC_NSA = 0.125
EPS = 1e-6


class Prog:
    def __init__(self, nc):
        self.nc = nc
        self.ops = {e: [] for e in ENGS}
        self.lastw = {}
        self.readers = {}
        self.dma_n = {e: 0 for e in ENGS + ["cc"]}
        self.dma_sem_cnt = {}
        self.last_c = {}
        self.last_d = {}

    def sb(self, name, shape, dt):
        return self.nc.alloc_sbuf_tensor(name, list(shape), dt)

    def ps(self, name, shape, dt=F32):
        return self.nc.alloc_psum_tensor(name, list(shape), dt)

    def _add(self, eng, fn, reads, writes, dma, cc=False):
        op = dict(eng=eng, fn=fn, deps=[], dma=dma, marked=False, inc=(1 if cc else 16))
        deps = []
        for k in reads:
            w = self.lastw.get(k)
            if w is not None:
                deps.append(w)
        for k in writes:
            w = self.lastw.get(k)
            if w is not None:
                deps.append(w)
            deps.extend(self.readers.get(k, ()))
        seen = set()
        for d in deps:
            if id(d) in seen or d is op:
                continue
            seen.add(id(d))
            if (not d["dma"]) and d["eng"] == eng and eng in ("pe", "sp"):
                continue
            op["deps"].append(d)
            d["marked"] = True
        if dma:
            qn = "cc" if cc else eng
            q = self.dma_n[qn]
            self.dma_n[qn] += 1
            semkey = (qn, q % (4 if cc else DMA_POOL))
            m = self.dma_sem_cnt.get(semkey, 0) + 1
            self.dma_sem_cnt[semkey] = m
            op["dsem"] = semkey
            op["dval"] = op["inc"] * m
            op["marked"] = True
            self.last_d[semkey] = op
        else:
            self.last_c[eng] = op
        for k in reads:
            self.readers.setdefault(k, []).append(op)
        for k in writes:
            self.lastw[k] = op
            self.readers[k] = []
        self.ops[eng].append(op)
        return op

    def op(self, eng, fn, reads=(), writes=()):
        return self._add(eng, fn, list(reads), list(writes), False)

    def dma(self, eng, out, in_, reads=(), writes=()):
        return self._add(eng, lambda e: e.dma_start(out=out, in_=in_), list(reads), list(writes), True)

    def cc(self, kind, groups, in_ap, out_ap, reads=(), writes=()):
        return self._add("pool", lambda e: e.collective_compute(kind, ALU.bypass, replica_groups=groups,
                                                                ins=[in_ap], outs=[out_ap]),
                         list(reads), list(writes), True, cc=True)

    def barrier(self):
        deps = list(self.last_c.values()) + list(self.last_d.values())
        for d in deps:
            d["marked"] = True
        for e in ENGS:
            self.ops[e].append(dict(eng=e, fn=None, deps=list(deps), dma=False, marked=False, inc=0))
        self.lastw = {}
        self.readers = {}

    def emit(self):
        nc = self.nc
        csem = {e: nc.alloc_semaphore("c_" + e) for e in ENGS}
        dsem = {}
        for (qn, i) in self.dma_sem_cnt:
            dsem[(qn, i)] = nc.alloc_semaphore("d_%s_%d" % (qn, i))
        for e in ENGS:
            c = 0
            for o in self.ops[e]:
                if o["dma"] or o["fn"] is None:
                    continue
                if o["marked"]:
                    c += 1
                    o["cval"] = c
        all_dma = [o for e in ENGS for o in self.ops[e] if o["dma"]]

        def run(e, eng):
            seen = {}

            def wait(sem_key, sem, val):
                if seen.get(sem_key, 0) >= val:
                    return
                seen[sem_key] = val
                eng.wait_ge(sem, val)

            for o in self.ops[e]:
                for d in o["deps"]:
                    if d["dma"]:
                        wait(d["dsem"], dsem[d["dsem"]], d["dval"])
                    else:
                        wait(("c", d["eng"]), csem[d["eng"]], d["cval"])
                if o["fn"] is None:
                    continue
                if o["dma"]:
                    if o["dval"] > o["inc"]:
                        wait(o["dsem"], dsem[o["dsem"]], o["dval"] - o["inc"])
                    o["fn"](eng).then_inc(dsem[o["dsem"]], o["inc"])
                else:
                    ins = o["fn"](eng)
                    if o["marked"]:
                        ins.then_inc(csem[e], 1)
            if e == "sp":
                last = {}
                for o in all_dma:
                    last[o["dsem"]] = max(last.get(o["dsem"], 0), o["dval"])
                for k, v in last.items():
                    eng.wait_ge(dsem[k], v)

        with nc.Block() as block:
            @block.tensor
            def _(eng):
                run("pe", eng)

            @block.scalar
            def _(eng):
                run("act", eng)

            @block.vector
            def _(eng):
                run("dve", eng)

            @block.gpsimd
            def _(eng):
                run("pool", eng)

            @block.sync
            def _(eng):
                run("sp", eng)


def _nbytes(shape, dt):
    n = 1
    for d in shape[1:]:
        n *= d
    return n * (4 if dt == F32 else 2)


class Ctx:
    def __init__(self, nc):
        self.nc = nc
        self.P = P = Prog(nc)
        self.rots = {}
        self.pname = "g_"
        self.ident = nc.alloc_sbuf_tensor("ident", [128, 128], BF16)
        self.identf = nc.alloc_sbuf_tensor("identf", [128, 128], F32)
        self.onesf = nc.alloc_sbuf_tensor("onesf", [128, 128], F32)
        self.epsn = nc.alloc_sbuf_tensor("epsn", [128, 1], F32)
        self.flags = nc.alloc_sbuf_tensor("flags_sb", [128, 2], F32)
        identf, ident, onesf, epsn = self.identf, self.ident, self.onesf, self.epsn
        P.op("pool", lambda e: e.memset(identf[:], 0.0), writes=["identf"])
        P.op("pool", lambda e: e.affine_select(out=identf[:], in_=identf[:], pattern=[[-1, 128]],
                                                compare_op=ALU.not_equal, fill=1.0, base=0, channel_multiplier=1),
             reads=["identf"], writes=["identf"])
        P.op("dve", lambda e: e.tensor_copy(out=ident[:], in_=identf[:]), reads=["identf"], writes=["ident"])
        P.op("dve", lambda e: e.memset(onesf[:], 1.0), writes=["onesf"])
        P.op("dve", lambda e: e.memset(epsn[:], EPS), writes=["epsn"])
        self.pst = P.ps("pst", [128, 1024], BF16)
        self.banks = [P.ps("bank%d" % i, [128, 512], F32) for i in range(7)]
        self.banks.append(self.pst.bitcast(F32))
        self.base = ((int(nc.sbuf_base) + 63) // 64) * 64
        self.top = int(nc.sbuf_top)
        self.off = self.base

    def begin_phase(self, name):
        self.P.barrier()
        self.pname = name
        self.off = self.base

    def sb(self, name, shape, dt):
        nb = ((_nbytes(shape, dt) + 31) // 32) * 32
        assert self.off + nb <= self.top, ("SBUF overflow", self.pname, name, self.off + nb - self.top)
        t = self.nc.alloc_sbuf_tensor_at(self.pname + name, list(shape), dt, offset=self.off)
        self.off += nb
        return t

    def rot(self, name, n):
        i = self.rots.get(name, 0) % n
        self.rots[name] = (i + 1) % n
        return i

    def dram(self, name, shape, dt, out=False):
        return self.nc.dram_tensor(name, list(shape), dt, kind="ExternalOutput" if out else "ExternalInput").ap()

    def mm(self, out, lhsT, rhs, start, stop, reads, writes):
        return self.P.op("pe", lambda e: e.matmul(out, lhsT=lhsT, rhs=rhs, start=start, stop=stop), reads, writes)

    def mmg(self, out, pairs, reads, writes, start=True, stop=True):
        n = len(pairs)

        def fn(e):
            ins = None
            for i, (l, r) in enumerate(pairs):
                ins = e.matmul(out, lhsT=l, rhs=r, start=(start and i == 0), stop=(stop and i == n - 1))
            return ins
        return self.P.op("pe", fn, reads, writes)

    def mmlist(self, items, reads, writes):
        def fn(e):
            ins = None
            for (o, l, r, st, sp) in items:
                ins = e.matmul(o, lhsT=l, rhs=r, start=st, stop=sp)
            return ins
        return self.P.op("pe", fn, reads, writes)

    def tr(self, out, in_, ident, reads, writes):
        return self.P.op("pe", lambda e: e.transpose(out, in_, ident), reads, writes)

    def act(self, out, in_, func, reads, writes, bias=None, scale=1.0, accum=None):
        kw = {}
        if bias is not None:
            kw["bias"] = bias
        if accum is not None:
            kw["accum_out"] = accum
        return self.P.op("act", lambda e: e.activation(out=out, in_=in_, func=func, scale=scale, **kw), reads, writes)

    def tt(self, eng, out, in0, in1, op, reads, writes):
        return self.P.op(eng, lambda e: e.tensor_tensor(out=out, in0=in0, in1=in1, op=op), reads, writes)

    def ts(self, eng, out, in0, s1, op0, reads, writes, s2=None, op1=None):
        if op1 is None:
            return self.P.op(eng, lambda e: e.tensor_scalar(out=out, in0=in0, scalar1=s1, scalar2=None, op0=op0), reads, writes)
        return self.P.op(eng, lambda e: e.tensor_scalar(out=out, in0=in0, scalar1=s1, scalar2=s2, op0=op0, op1=op1), reads, writes)

    def stt(self, eng, out, in0, scalar, in1, op0, op1, reads, writes):
        return self.P.op(eng, lambda e: e.scalar_tensor_tensor(out=out, in0=in0, scalar=scalar, in1=in1, op0=op0, op1=op1), reads, writes)

    def cp(self, eng, out, in_, reads, writes):
        if eng == "act":
            return self.P.op("act", lambda e: e.copy(out=out, in_=in_), reads, writes)
        return self.P.op(eng, lambda e: e.tensor_copy(out=out, in_=in_), reads, writes)

    def recip(self, out, in_, reads, writes):
        return self.P.op("dve", lambda e: e.reciprocal(out=out, in_=in_), reads, writes)

    def memset(self, eng, ap, val, writes):
        return self.P.op(eng, lambda e: e.memset(ap, val), [], writes)

    def bank(self, grp, idxs):
        i = idxs[self.rot(grp, len(idxs))]
        return self.banks[i], ("pst" if i == 7 else "bank%d" % i)

    def load_w(self, name, w_dram, nk, ncols):
        wsb = self.sb(name, [128, nk, ncols], BF16)
        for k in range(nk):
            self.P.dma("pool", wsb[:, k, :], w_dram[k * 128:(k + 1) * 128, :], writes=[name])
        return wsb

    def load_const(self, name, dram_ap, shape, dt, eng="pool"):
        t = self.sb(name, shape, dt)
        self.P.dma(eng, t[:], dram_ap, writes=[name])
        return t

    def setup_norm(self):
        self.ht = [self.sb("ht%d" % i, [128, DM], F32) for i in range(2)]
        self.junk = self.sb("junk", [128, DM], BF16)
        self.nb = self.sb("nb", [128, DM], BF16)
        self.ss = [self.sb("ss%d" % i, [128, 1], F32) for i in range(2)]

    def norm_T(self, src, srckey, nT, nkey, col0, g_sb, gkey):
        b = self.rot("ss", 2)
        ss = self.ss[b]
        sk = "ss%d" % b
        self.memset("dve", ss[:], 0.0, [sk])
        self.act(self.junk[:], src, AF.Square, [srckey, sk], ["junk", sk], accum=ss[:])
        self.act(ss[:], ss[:], AF.Sqrt, [sk, "epsn"], [sk], bias=self.epsn[:], scale=1.0 / DM)
        self.recip(ss[:], ss[:], [sk], [sk])
        self.ts("dve", self.nb[:], src, ss[:, 0:1], ALU.mult, [srckey, sk], ["nb"])
        pst = self.pst
        nb, ident = self.nb, self.ident

        def fn(e):
            ins = None
            for k in range(8):
                ins = e.transpose(pst[:, k * 128:(k + 1) * 128], nb[:, k * 128:(k + 1) * 128], ident[:])
            return ins
        self.P.op("pe", fn, ["nb", "ident"], ["pst"])
        self.tt("dve", nT[:, :, col0:col0 + 128], pst[:, :].rearrange("p (k t) -> p k t", k=8),
                g_sb[:, :].unsqueeze(2).to_broadcast([128, 8, 128]), ALU.mult, ["pst", gkey], [nkey])


def attn_loop(C, tiles, score_fn, scale, v_fn, po, pok, sbanks, pts, ptname, after_first=None, depth=2):
    n = len(tiles)
    issued = []

    def issue(i):
        sbk, sk = C.bank("s", sbanks)
        pairs, rd = score_fn(tiles[i])
        C.mmg(sbk[:, :], pairs, rd, [sk])
        issued.append((sbk, sk))
    for i in range(min(depth, n)):
        issue(i)
    if after_first is not None:
        after_first()
    for i in range(n):
        sbk, sk = issued[i]
        pi = C.rot(ptname, len(pts))
        C.act(pts[pi][:], sbk[:, :], AF.Exp, [sk], ["%s%d" % (ptname, pi)], scale=scale)
        if i + depth < n:
            issue(i + depth)
        lhsT, rd = v_fn(tiles[i])
        C.mm(po[:, :], lhsT, pts[pi][:], i == 0, i == n - 1, rd + ["%s%d" % (ptname, pi)], [pok])


def phase_mla(C, name, L, G, hsrc, hkey, osink):
    P = C.P
    C.begin_phase(name)
    C.setup_norm()
    gA = C.load_const("gA_sb", L["gAm"], [128, 8], F32)
    gq = C.load_const("gq_sb", L["gq"], [128, 3], F32)
    gkv = C.load_const("gkv_sb", L["gkv"], [128, 2], F32)
    dmask = C.load_const("dmask_sb", G["dmask"], [128, 2048], BF16)
    wA = C.load_w("wA_sb", L["wAm"], 8, 832)
    wq = C.load_w("wq_sb", L["wq"], 3, 768)
    wkv = C.load_w("wkv_sb", L["wkv"], 2, 512)
    ropeC_d, ropeS_d = G["ropeC"], G["ropeS"]

    Kh = C.sb("Kh", [128, 4, S], BF16)
    C.memset("pool", Kh[96:128, :, :], 0.0, ["Kh_pad"])
    Vt = C.sb("Vt", [128, 32, 4, 128], BF16)
    C.memset("pool", Vt[:, :, :, 64:65], 1.0, ["Vt_%d" % c for c in range(8)])
    C.memset("pool", Vt[:, :, :, 65:128], 0.0, ["Vt_pad"])
    nTs = [C.sb("nT%d" % i, [128, 8, 512], BF16) for i in range(2)]
    Qhs = [C.sb("Qh%d" % i, [128, 4, 512], BF16) for i in range(2)]
    for i in range(2):
        C.memset("pool", Qhs[i][96:128, :, :], 0.0, ["Qh_pad"])
    zf = C.sb("zf", [128, 3, 512], F32)
    sq = C.sb("sq", [128, 3, 512], F32)
    rr = C.sb("rr", [128, 512], F32)
    cqn = C.sb("cqn", [128, 3, 512], BF16)
    ckvn = C.sb("ckvn", [128, 2, 512], BF16)
    Ct = C.sb("Ct", [96, 512], F32)
    St = C.sb("St", [96, 512], F32)
    t1 = C.sb("t1", [96, 512], F32)
    t2 = C.sb("t2", [96, 512], F32)
    pts = [C.sb("pt%d" % i, [128, 512], BF16) for i in range(4)]
    rsrow = C.sb("rsrow", [65, 512], F32)
    rec4 = C.sb("rec4", [128, 4], F32)
    wb = C.sb("wb", [128, 4, 64], F32)
    bcs = C.sb("bcs", [64, 512], F32)
    ots = [C.sb("ot%d" % i, [64, 512], BF16) for i in range(2)]

    PJ = [0, 1]
    SB_ = [2, 3, 4]
    PO = [5, 6]
    pending = []

    def flush():
        while pending:
            pending.pop(0)()

    def latent(c0, nm, dim, dst, dkey, nT, nkeys, gl, glkey):
        for m in range(nm):
            pj, pk = C.bank("pj", PJ)
            C.mmg(pj[:, :], [(wA[:, k, c0 + m * 128:c0 + (m + 1) * 128], nT[:, k, :]) for k in range(8)],
                  ["wA_sb"] + nkeys, [pk])
            C.act(zf[:, m, :], pj[:, :], AF.Copy, [pk], ["zf%d" % m])
            C.act(sq[:, m, :], pj[:, :], AF.Square, [pk], ["sq%d" % m])
        pj, pk = C.bank("pj", PJ)
        C.mmg(pj[:, :], [(C.onesf[:, :], sq[:, m, :]) for m in range(nm)], ["onesf"] + ["sq%d" % m for m in range(nm)], [pk])
        C.act(rr[:], pj[:, :], AF.Sqrt, [pk, "epsn"], ["rr"], bias=C.epsn[:], scale=1.0 / dim)
        C.recip(rr[:], rr[:], ["rr"], ["rr"])
        for m in range(nm):
            C.stt("dve", dst[:, m, :], zf[:, m, :], gl[:, m:m + 1], rr[:], ALU.mult, ALU.mult, ["zf%d" % m, "rr", glkey], [dkey])

    def stage_T(tc):
        nT = nTs[tc % 2]
        nkeys = []
        for ti in range(4):
            hb = C.rot("ht", 2)
            P.dma("sp", C.ht[hb][:], hsrc(4 * tc + ti), reads=[hkey(4 * tc + ti)], writes=["ht%d" % hb])
            nk = "nT%d_%d" % (tc % 2, ti)
            C.norm_T(C.ht[hb][:], "ht%d" % hb, nT, nk, ti * 128, gA, "gA_sb")
            nkeys.append(nk)
        return nT, nkeys

    def stage_P1(tc, nT, nkeys):
        t0 = tc * 512
        latent(0, 3, 384.0, cqn, "cqn", nT, nkeys, gq, "gq_sb")
        latent(384, 2, 256.0, ckvn, "ckvn", nT, nkeys, gkv, "gkv_sb")
        P.dma("sp", Ct[64:96, :], ropeC_d[:, t0:t0 + 512], writes=["Ct"])
        P.dma("sp", St[64:96, :], ropeS_d[:, t0:t0 + 512], writes=["St"])
        pA, pAk = C.bank("pj", PJ)
        C.mmg(pA[0:96, :], [(wA[:, k, 640:736], nT[:, k, :]) for k in range(8)], ["wA_sb"] + nkeys, [pAk])
        C.tt("dve", t1[64:96, :], pA[64:96, :], Ct[64:96, :], ALU.mult, [pAk, "Ct"], ["t1"])
        pB, pBk = C.bank("pj", PJ)
        C.mmg(pB[0:96, :], [(wA[:, k, 736:832], nT[:, k, :]) for k in range(8)], ["wA_sb"] + nkeys, [pBk])
        C.tt("dve", t2[64:96, :], pB[64:96, :], St[64:96, :], ALU.mult, [pBk, "St"], ["t2"])
        for hh in range(4):
            C.tt("pool", Kh[64:96, hh, t0:t0 + 512], t1[64:96, :], t2[64:96, :], ALU.add, ["t1", "t2"], ["Kh_%d" % tc])

    def stage_P2(tc):
        t0 = tc * 512
        Qh = Qhs[tc % 2]
        qk = "Qh%d" % (tc % 2)
        for hh in range(4):
            pA, pAk = C.bank("pj", PJ)
            C.mmg(pA[0:96, :], [(wq[:, m, hh * 192:hh * 192 + 96], cqn[:, m, :]) for m in range(3)], ["wq_sb", "cqn"], [pAk])
            C.cp("act", Qh[0:64, hh, :], pA[0:64, :], [pAk], [qk])
            C.tt("dve", t1[64:96, :], pA[64:96, :], Ct[64:96, :], ALU.mult, [pAk, "Ct"], ["t1"])
            pB, pBk = C.bank("pj", PJ)
            C.mmg(pB[0:96, :], [(wq[:, m, hh * 192 + 96:hh * 192 + 192], cqn[:, m, :]) for m in range(3)], ["wq_sb", "cqn"], [pBk])
            C.tt("dve", t2[64:96, :], pB[64:96, :], St[64:96, :], ALU.mult, [pBk, "St"], ["t2"])
            C.tt("pool", Qh[64:96, hh, :], t1[64:96, :], t2[64:96, :], ALU.add, ["t1", "t2"], [qk])
        for hh in range(4):
            pj, pk = C.bank("pj", PJ)
            C.mmg(pj[0:64, :], [(wkv[:, j, hh * 64:(hh + 1) * 64], ckvn[:, j, :]) for j in range(2)], ["wkv_sb", "ckvn"], [pk])
            C.cp("act", Kh[0:64, hh, t0:t0 + 512], pj[0:64, :], [pk], ["Kh_%d" % tc])
        for ti in range(4):
            pj, pk = C.bank("pj", PJ)
            C.mmg(pj[:, 0:256], [(ckvn[:, j, ti * 128:(ti + 1) * 128], wkv[:, j, 256:512]) for j in range(2)], ["wkv_sb", "ckvn"], [pk])
            C.cp("act", Vt[:, 4 * tc + ti, :, 0:64], pj[:, 0:256].rearrange("p (h d) -> p h d", h=4), [pk], ["Vt_%d" % tc])

    def head(tc, hh):
        Qh = Qhs[tc % 2]
        qk = "Qh%d" % (tc % 2)
        po, pok = C.bank("po", PO)
        nkt = 4 * tc + 4

        def score(j):
            pairs = [(Kh[:, hh, j * 128:(j + 1) * 128], Qh[:, hh, :])]
            rd = [qk, "Kh_%d" % (j // 4), "Kh_pad", "Qh_pad"]
            if j >= 4 * tc:
                m = j - 4 * tc
                pairs.append((C.ident[:, :], dmask[:, m * 512:(m + 1) * 512]))
                rd += ["ident", "dmask_sb"]
            return pairs, rd

        def vfn(j):
            return Vt[:, j, hh, :], ["Vt_%d" % (j // 4), "Vt_pad"]
        attn_loop(C, list(range(nkt)), score, SC_MLA, vfn, po, pok, SB_, pts, "pt", after_first=flush)

        def fin():
            C.cp("dve", rsrow[64:65, :], po[64:65, :], [pok], ["rsrow"])
            pj, pk = C.bank("pj", PJ)
            C.mmlist([(pj[:, r:r + 1], rsrow[64:65, r * 128:(r + 1) * 128], C.onesf[64:65, 0:1], True, True) for r in range(4)],
                     ["rsrow", "onesf"], [pk])
            C.ts("dve", rec4[:, :], pj[:, 0:4], 1e-30, ALU.add, [pk], ["rec4"])
            C.recip(rec4[:, :], rec4[:, :], ["rec4"], ["rec4"])
            C.cp("dve", wb[:, :, :], rec4[:, 0:4].unsqueeze(2).to_broadcast([128, 4, 64]), ["rec4"], ["wb"])
            pj2, pk2 = C.bank("pj", PJ)
            C.mmlist([(pj2[0:64, r * 128:(r + 1) * 128], wb[:, r, :], C.identf[:, :], True, True) for r in range(4)],
                     ["wb", "identf"], [pk2])
            C.cp("act", bcs[:, :], pj2[0:64, :], [pk2], ["bcs"])
            oi = C.rot("ot", 2)
            C.tt("dve", ots[oi][:, :], po[0:64, :], bcs[:, :], ALU.mult, [pok, "bcs"], ["ot%d" % oi])
            osink(hh, tc, ots[oi], "ot%d" % oi)
        pending.append(fin)

    st = stage_T(0)
    stage_P1(0, *st)
    stage_P2(0)
    for tc in range(8):
        head(tc, 0)
        if tc < 7:
            st = stage_T(tc + 1)
        head(tc, 1)
        if tc < 7:
            stage_P1(tc + 1, *st)
        head(tc, 2)
        if tc < 7:
            stage_P2(tc + 1)
        head(tc, 3)
    flush()


def phase_nsa(C, name, L, G, hsrc, hkey, osink):
    P = C.P
    C.begin_phase(name)
    NW = 780
    C.setup_norm()
    gA = C.load_const("gA_sb", L["gAn"], [128, 8], F32)
    wA = C.load_w("wA_sb", L["wAn"], 8, NW)
    w1k = C.load_w("w1k_sb", L["w1k"], 16, 128)
    w1v = C.load_w("w1v_sb", L["w1v"], 16, 128)
    w2k = C.load_w("w2k_sb", L["w2k"], 1, 64)
    w2v = C.load_w("w2v_sb", L["w2v"], 1, 64)
    posk = C.load_w("posk_sb", L["posk"], 1, 16)
    posv = C.load_w("posv_sb", L["posv"], 1, 16)
    maskc = C.load_const("maskc_sb", G["maskc"], [128, 2 * S], BF16)
    eall = C.sb("eall_sb", [128, S], BF16)
    C.memset("pool", eall[64:128, :], 0.0, ["eall_sb"])
    P.dma("pool", eall[0:64, :], G["eall"], writes=["eall_sb"])
    selbias = C.load_const("selbias_sb", G["selbias"], [128, 2048], BF16)
    ovl = C.load_const("ovl_sb", G["ovl"], [128, 130], BF16)
    selg = C.load_const("selg_sb", G["selg"], [12, 768], F32)
    dm4 = C.load_const("dm4_sb", G["dm4"], [128, 512], BF16)
    wm4 = C.load_const("wm4_sb", G["wm4"], [128, 512], BF16)

    Qa = C.sb("Qa", [128, 32, 512], BF16)
    Kw = C.sb("Kw", [128, S], BF16)
    Ks = C.sb("Ks", [128, S], BF16)
    Kc = C.sb("Kc", [128, 256], BF16)
    C.memset("pool", Qa[64:128, :, :], 0.0, ["Qa_aug"])
    C.memset("pool", Kw[64:128, :], 0.0, ["Kw_aug"])
    C.memset("pool", Ks[64:128, :], 0.0, ["Ks_aug"])
    C.memset("pool", Kc[64:128, :], 0.0, ["Kc_aug"])
    P.dma("pool", Qa[64:68, :, :], G["qaug"].rearrange("p (a b) -> p a b", a=32), writes=["Qa_aug"])
    P.dma("pool", Kw[64:68, :], G["kaug"], writes=["Kw_aug"])
    P.dma("pool", Ks[64:68, :], G["kaug"], writes=["Ks_aug"])
    P.dma("pool", Kc[64:68, :], G["kaugc"], writes=["Kc_aug"])
    kc2 = C.sb("kc2", [128, S + 32], BF16)
    vc2 = C.sb("vc2", [128, S + 32], BF16)
    C.memset("pool", kc2[:, S:S + 32], 0.0, ["kc2_tail"])
    C.memset("pool", vc2[:, S:S + 32], 0.0, ["vc2_tail"])
    Vs = C.sb("Vs", [128, 32, 128], BF16)
    Vw = C.sb("Vw", [128, 32, 128], BF16)
    Vc = C.sb("Vc", [128, 2, 128], BF16)
    for (vt_, keys_) in ((Vs, ["Vs_%d" % c for c in range(8)]), (Vw, ["Vw_%d" % c for c in range(8)]), (Vc, ["Vc"])):
        C.memset("pool", vt_[:, :, 65:128], 0.0, keys_)
        C.memset("pool", vt_[:, :, 64:65], 1.0, keys_)
    Gtm = C.sb("Gtm", [128, 32, 12], F32)
    nTs = [C.sb("nT%d" % i, [128, 8, 512], BF16) for i in range(2)]

    PJ = [0, 1]
    SB_ = [2, 3]
    POC, POS, POW = 4, 5, 6

    for tc in range(8):
        t0 = tc * 512
        nb_ = C.rot("nT", 2)
        nT = nTs[nb_]
        nkeys = []
        for ti in range(4):
            hb = C.rot("ht", 2)
            P.dma("sp", C.ht[hb][:], hsrc(4 * tc + ti), reads=[hkey(4 * tc + ti)], writes=["ht%d" % hb])
            nk = "nT%d_%d" % (nb_, ti)
            C.norm_T(C.ht[hb][:], "ht%d" % hb, nT, nk, ti * 128, gA, "gA_sb")
            nkeys.append(nk)

        def proj(c0, m, rows=128):
            pj, pk = C.bank("pj", PJ)
            C.mmg(pj[0:rows, :], [(wA[:, k, c0:c0 + m], nT[:, k, :]) for k in range(8)], ["wA_sb"] + nkeys, [pk])
            return pj, pk
        for r in range(4):
            pj, pk = proj(r * 64, 64, 64)
            C.cp("act", Qa[0:64, 4 * tc:4 * tc + 4, r * 128:(r + 1) * 128],
                 pj[0:64, :].rearrange("p (a b) -> p a b", a=4), [pk], ["Qa_%d" % tc])
        for (c0, dst, dk) in ((256, kc2, "kc2"), (384, vc2, "vc2")):
            pj, pk = proj(c0, 128)
            C.cp("act", dst[0:64, t0:t0 + 512], pj[0:64, :], [pk], [dk])
            if tc == 0:
                C.cp("dve", dst[64:128, 0:511], pj[64:128, 1:512], [pk], [dk])
            else:
                C.cp("dve", dst[64:128, t0 - 1:t0 + 511], pj[64:128, :], [pk], [dk])
        pj, pk = proj(512, 64, 64)
        C.cp("act", Ks[0:64, t0:t0 + 512], pj[0:64, :], [pk], ["Ks_%d" % tc])
        pj, pk = proj(576, 64, 64)
        C.cp("act", Kw[0:64, t0:t0 + 512], pj[0:64, :], [pk], ["Kw_%d" % tc])
        for ti in range(4):
            pj, pk = C.bank("pj", PJ)
            C.mmg(pj[:, 0:128], [(nT[:, k, ti * 128:(ti + 1) * 128], wA[:, k, 640:768]) for k in range(8)],
                  ["wA_sb"] + nkeys, [pk])
            pg, pgk = C.bank("pj", PJ)
            C.mmg(pg[:, 0:12], [(nT[:, k, ti * 128:(ti + 1) * 128], wA[:, k, 768:780]) for k in range(8)],
                  ["wA_sb"] + nkeys, [pgk])
            C.cp("dve", Gtm[:, 4 * tc + ti, :], pg[:, 0:12], [pgk], ["Gtm_%d" % tc])
            C.cp("act", Vs[:, 4 * tc + ti, 0:64], pj[:, 0:64], [pk], ["Vs_%d" % tc])
            C.cp("dve", Vw[:, 4 * tc + ti, 0:64], pj[:, 64:128], [pk], ["Vw_%d" % tc])

    C.act(Gtm[:, :, :], Gtm[:, :, :], AF.Sigmoid, ["Gtm_%d" % c for c in range(8)], ["Gtm_%d" % c for c in range(8)])
    xs = C.sb("xs", [128, 256], F32)
    x2 = C.sb("x2", [128, 256], F32)
    hid = C.sb("hid", [128, 256], BF16)
    cbias = C.sb("cbias", [128, 1], F32)
    for (src, skey, w1, w1key, pos, poskey, isk) in ((kc2, "kc2", w1k, "w1k_sb", posk, "posk_sb", True),
                                                     (vc2, "vc2", w1v, "w1v_sb", posv, "posv_sb", False)):
        pj, pk = C.bank("pj", PJ)
        C.mmg(pj[:, 0:1], [(w1[:, j, :], pos[:, 0, j:j + 1]) for j in range(16)], [w1key, poskey], [pk])
        C.cp("act", cbias[:], pj[:, 0:1], [pk], ["cbias"])
        pj, pk = C.bank("pj", PJ)
        C.mmg(pj[:, 0:255], [(w1[:, j, :], src[:, 2 * j:2 * j + 16 * 255:16]) for j in range(16)],
              [w1key, skey, skey + "_tail"], [pk])
        C.memset("dve", xs[:, 255:256], 0.0, ["xs"])
        C.act(xs[:, 0:255], pj[:, 0:255], AF.Identity, [pk, "cbias"], ["xs"], bias=cbias[:])
        C.tt("dve", x2[:], xs[:], xs[:], ALU.mult, ["xs"], ["x2"])
        C.ts("dve", x2[:], x2[:], 0.044715, ALU.mult, ["x2"], ["x2"], s2=1.0, op1=ALU.add)
        C.tt("dve", x2[:], x2[:], xs[:], ALU.mult, ["x2", "xs"], ["x2"])
        C.act(x2[:], x2[:], AF.Sigmoid, ["x2"], ["x2"], scale=1.5957691216057308)
        C.tt("dve", hid[:], xs[:], x2[:], ALU.mult, ["x2", "xs"], ["hid"])
        if isk:
            pj, pk = C.bank("pj", PJ)
            C.mm(pj[0:64, 0:256], w2k[:, 0, :], hid[:], True, True, ["w2k_sb", "hid"], [pk])
            C.cp("act", Kc[0:64, :], pj[0:64, 0:256], [pk], ["Kc"])
        else:
            for nt in range(2):
                pj, pk = C.bank("pj", PJ)
                C.mm(pj[:, 0:64], hid[:, nt * 128:(nt + 1) * 128], w2v[:, 0, :], True, True, ["w2v_sb", "hid"], [pk])
                C.cp("act", Vc[:, nt, 0:64], pj[:, 0:64], [pk], ["Vc"])

    ptc = [[C.sb("ptc%d_%d" % (i, j), [128, 512], BF16) for j in range(2)] for i in range(2)]
    pts = [C.sb("pt%d" % i, [128, 512], BF16) for i in range(4)]
    imp = C.sb("imp", [128, 64], F32)
    wk = C.sb("wk", [128, 64], F32)
    m8 = C.sb("m8", [128, 16], F32)
    rci = C.sb("rci", [128, 4], F32)
    rec4 = C.sb("rec4", [128, 4], F32)
    wb = C.sb("wb", [128, 4, 64], F32)
    selb = C.sb("selb", [128, 64], BF16)
    selT4s = [C.sb("selT4_%d" % i, [128, 512], BF16) for i in range(2)]
    for i in range(2):
        C.memset("pool", selT4s[i][64:128, :], 0.0, ["selT4_%d" % i])
    rsrow = C.sb("rsrow", [65, 512], F32)
    bcs = C.sb("bcs", [64, 512], F32)
    tmpo = C.sb("tmpo", [64, 512], F32)
    oacc = [C.sb("oacc%d" % i, [64, 512], F32) for i in range(2)]
    oaccb = [C.sb("oaccb%d" % i, [64, 512], BF16) for i in range(2)]
    PJ = [0, 7]
    SB_ = [1, 2, 3]
    pending = []

    def flush():
        while pending:
            pending.pop(0)()

    def finalize(po, pok, br, qb, first, rec_src=None):
        ai = qb % 2
        acc, ak = oacc[ai], "oacc%d" % ai
        if rec_src is None:
            C.cp("dve", rsrow[64:65, :], po[64:65, :], [pok], ["rsrow"])
            pj, pk = C.bank("pj", PJ)
            C.mmlist([(pj[:, r:r + 1], rsrow[64:65, r * 128:(r + 1) * 128], C.onesf[64:65, 0:1], True, True) for r in range(4)],
                     ["rsrow", "onesf"], [pk])
            C.ts("dve", rec4[:, :], pj[:, 0:4], 1e-30, ALU.add, [pk], ["rec4"])
            C.recip(rec4[:, :], rec4[:, :], ["rec4"], ["rec4"])
            recap, reckey = rec4, "rec4"
        else:
            recap, reckey = rec_src
        C.tt("dve", wb[:, :, :], recap[:, 0:4].unsqueeze(2).to_broadcast([128, 4, 64]),
             Gtm[:, qb, br:12:3].unsqueeze(2).to_broadcast([128, 4, 64]), ALU.mult, [reckey, "Gtm_%d" % (qb // 4)], ["wb"])
        pj2, pk2 = C.bank("pj", PJ)
        C.mmlist([(pj2[0:64, r * 128:(r + 1) * 128], wb[:, r, :], C.identf[:, :], True, True) for r in range(4)],
                 ["wb", "identf"], [pk2])
        C.cp("act", bcs[:, :], pj2[0:64, :], [pk2], ["bcs"])
        if first:
            C.tt("dve", acc[:, :], po[0:64, :], bcs[:, :], ALU.mult, [pok, "bcs"], [ak])
        else:
            C.tt("dve", tmpo[:, :], po[0:64, :], bcs[:, :], ALU.mult, [pok, "bcs"], ["tmpo"])
            C.tt("dve", acc[:, :], acc[:, :], tmpo[:, :], ALU.add, ["tmpo", ak], [ak])

    def qinfo(qb):
        return Qa[:, qb, :], ["Qa_%d" % (qb // 4), "Qa_aug"]

    def cmp_stage(qb):
        q_rhs, qkeys = qinfo(qb)
        pc = ptc[qb % 2]
        selT4, stk = selT4s[qb % 2], "selT4_%d" % (qb % 2)
        ntn = 1 if qb < 16 else 2
        po = C.banks[POC]
        for nt in range(ntn):
            sbk, sk = C.bank("s", SB_)
            items = [(sbk[:, :], Kc[:, nt * 128:(nt + 1) * 128], q_rhs, True, False)]
            for r in range(4):
                items.append((sbk[:, r * 128:(r + 1) * 128], C.ident[:, :],
                              maskc[:, nt * S + qb * 128:nt * S + (qb + 1) * 128], False, r == 3))
            C.mmlist(items, qkeys + ["Kc", "Kc_aug", "ident", "maskc_sb"], [sk])
            C.act(pc[nt][:], sbk[:, :], AF.Exp, [sk], ["ptc%d_%d" % (qb % 2, nt)], scale=SC_NSA)
        for nt in range(ntn):
            C.mm(po[:, :], Vc[:, nt, :], pc[nt][:], nt == 0, nt == ntn - 1,
                 ["Vc", "ptc%d_%d" % (qb % 2, nt)], ["bank%d" % POC])
        pj, pk = C.bank("pj", PJ)
        items = []
        for r in range(4):
            for nt in range(ntn):
                items.append((pj[:, r * 65:(r + 1) * 65], pc[nt][:, r * 128:(r + 1) * 128], ovl[:, nt * 65:(nt + 1) * 65],
                              nt == 0, nt == ntn - 1))
        C.mmlist(items, ["ovl_sb"] + ["ptc%d_%d" % (qb % 2, nt) for nt in range(ntn)], [pk])
        for r in range(4):
            C.ts("dve", rci[:, r:r + 1], pj[:, r * 65 + 64:r * 65 + 65], 1e-30, ALU.add, [pk], ["rci"])
        C.recip(rci[:, 0:4], rci[:, 0:4], ["rci"], ["rci"])
        for r in range(4):
            prev = selbias[:, qb * 64:(qb + 1) * 64] if r == 0 else imp[:]
            C.stt("dve", imp[:], pj[:, r * 65:r * 65 + 64], rci[:, r:r + 1], prev, ALU.mult, ALU.add,
                  [pk, "rci", "imp", "selbias_sb"], ["imp"])
        P.op("dve", lambda e: e.max(out=m8[:, 0:8], in_=imp[:]), ["imp"], ["m8"])
        P.op("dve", lambda e: e.match_replace(out=wk[:], in_to_replace=m8[:, 0:8], in_values=imp[:], imm_value=-1e9),
             ["imp", "m8"], ["wk"])
        P.op("dve", lambda e: e.max(out=m8[:, 8:16], in_=wk[:]), ["wk"], ["m8"])
        C.ts("dve", wk[:], imp[:], m8[:, 15:16], ALU.is_ge, ["imp", "m8"], ["wk"])
        C.ts("dve", selb[:], wk[:], -NEG, ALU.mult, ["wk"], ["selb"], s2=NEG, op1=ALU.add)

        def tail(qb=qb, selT4=selT4, stk=stk, po=po):
            C.tr(C.pst[0:64, 0:128], selb[:, :], C.ident[:, :], ["selb", "ident"], ["pst"])
            for r in range(4):
                C.cp("act" if r % 2 == 0 else "dve", selT4[0:64, r * 128:(r + 1) * 128], C.pst[0:64, 0:128], ["pst"], [stk])
            finalize(po, "bank%d" % POC, 0, qb, True, rec_src=(rci, "rci"))
        pending.append(tail)

    def sel_stage(qb):
        q_rhs, qkeys = qinfo(qb)
        selT4, stk = selT4s[qb % 2], "selT4_%d" % (qb % 2)
        po = C.banks[POS]

        def score(kt):
            pairs = [(Ks[:, kt * 128:(kt + 1) * 128], q_rhs), (eall[:, kt * 128:(kt + 1) * 128], selT4[:, :])]
            rd = qkeys + ["Ks_%d" % (kt // 4), "Ks_aug", "eall_sb", stk]
            if kt == qb:
                pairs.append((C.ident[:, :], dm4[:, :]))
                rd += ["ident", "dm4_sb"]
            return pairs, rd
        attn_loop(C, list(range(qb + 1)), score, SC_NSA, lambda kt: (Vs[:, kt, :], ["Vs_%d" % (kt // 4)]),
                  po, "bank%d" % POS, SB_, pts, "pt", after_first=None)

        def tail(qb=qb, po=po):
            finalize(po, "bank%d" % POS, 1, qb, False)
            ai = qb % 2
            C.cp("act", oaccb[ai][:, :], oacc[ai][:, :], ["oacc%d" % ai], ["oaccb%d" % ai])
            osink(qb, oaccb[ai], "oaccb%d" % ai)
        pending.append(tail)

    def win_stage(qb):
        q_rhs, qkeys = qinfo(qb)
        po = C.banks[POW]
        k0 = max(0, qb - 4)

        def score(kt):
            pairs = [(Kw[:, kt * 128:(kt + 1) * 128], q_rhs)]
            rd = qkeys + ["Kw_%d" % (kt // 4), "Kw_aug"]
            if kt == qb:
                pairs.append((C.ident[:, :], dm4[:, :]))
                rd += ["ident", "dm4_sb"]
            if kt == qb - 4:
                pairs.append((C.ident[:, :], wm4[:, :]))
                rd += ["ident", "wm4_sb"]
            return pairs, rd
        attn_loop(C, list(range(k0, qb + 1)), score, SC_NSA, lambda kt: (Vw[:, kt, :], ["Vw_%d" % (kt // 4)]),
                  po, "bank%d" % POW, SB_, pts, "pt", after_first=flush)

        def tail(qb=qb, po=po):
            finalize(po, "bank%d" % POW, 2, qb, False)
        pending.append(tail)

    cmp_stage(0)
    for qb in range(32):
        win_stage(qb)
        if qb + 1 < 32:
            cmp_stage(qb + 1)
        sel_stage(qb)
    flush()


def phase_ffn(C, name, L, G, final, hown, hownkey, hhalo, hhalokey, oall, osink):
    P = C.P
    C.begin_phase(name)
    C.setup_norm()
    fl = C.flags
    g2 = C.load_const("g2_sb", L["g2"], [128, 8], F32)
    cw = C.load_const("cw_sb", L["cw"], [128, 176], F32)
    if final:
        gF = C.sb("gF_sb", [128, DM], F32)
        P.dma("pool", gF[:], G["gF"][0:1, :].partition_broadcast(128), writes=["gF_sb"])
    wup = C.load_w("wup_sb", L["wup"], 8, 5632)
    wdn = C.sb("wdn_sb", [128, 22, DM], BF16)

    def load_wdn():
        for k in range(22):
            P.dma("pool", wdn[:, k, :], L["wdn"][k * 128:(k + 1) * 128, :], writes=["wdn_sb"])
    wo_d = L["wo"]

    NC_ = 256
    aT = [C.sb("aT%d" % i, [128, NC_], BF16) for i in range(4)]
    hm = C.sb("hm", [128, 2, DM], F32)
    n2T = C.sb("n2T", [128, 8, NC_], BF16)
    oTb = C.sb("oTb", [128, 8, NC_], BF16)
    oa = [C.sb("oa%d" % i, [128, NC_], BF16) for i in range(2)]
    ob = [C.sb("ob%d" % i, [128, NC_], BF16) for i in range(2)]
    wob = [C.sb("wob%d" % i, [128, DM], BF16) for i in range(4)]
    ubuf = [C.sb("ubuf%d" % i, [128, NC_ + 2], F32) for i in range(3)]
    tb = [C.sb("tb%d" % i, [128, NC_], F32) for i in range(4)]
    sg = C.sb("sg", [128, NC_], F32)
    carry = C.sb("carry", [128, 44, 2], F32)
    res = C.sb("res", [128, DM], F32)
    ss2 = C.sb("ss2", [128, 1], F32)

    ACC = [0, 1, 2, 3]
    UP = [4, 5, 6]

    def chunk(ci, halo):
        nt_ = 1 if halo else 2
        ncol = nt_ * 128
        c0 = 1920 if halo else ci * 256
        for k in range(8):
            b = C.rot("oa", 2)
            if halo:
                P.dma("sp", oa[b][:, 0:ncol], oall[0][k * 128:(k + 1) * 128, c0:c0 + ncol], reads=["oall0"], writes=["oa%d" % b])
                C.ts("dve", oTb[:, k, 0:ncol], oa[b][:, 0:ncol], fl[:, 1:2], ALU.mult, ["oa%d" % b, "flags"], ["oTb"])
            else:
                P.dma("sp", oa[b][:, 0:ncol], oall[0][k * 128:(k + 1) * 128, c0:c0 + ncol], reads=["oall0"], writes=["oa%d" % b])
                P.dma("sp", ob[b][:, 0:ncol], oall[1][k * 128:(k + 1) * 128, c0:c0 + ncol], reads=["oall1"], writes=["ob%d" % b])
                C.ts("dve", oa[b][:, 0:ncol], oa[b][:, 0:ncol], fl[:, 0:1], ALU.mult, ["oa%d" % b, "flags"], ["oa%d" % b])
                C.stt("dve", oTb[:, k, 0:ncol], ob[b][:, 0:ncol], fl[:, 1:2], oa[b][:, 0:ncol], ALU.mult, ALU.add,
                      ["oa%d" % b, "ob%d" % b, "flags"], ["oTb"])
        for k in range(8):
            wb = C.rot("wob", 4)
            P.dma("pool", wob[wb][:, :], wo_d[k * 128:(k + 1) * 128, :], writes=["wob%d" % wb])
            items = []
            for ti in range(nt_):
                for hf in range(2):
                    items.append((C.banks[ACC[ti * 2 + hf]][:, :], oTb[:, k, ti * 128:(ti + 1) * 128],
                                  wob[wb][:, hf * 512:(hf + 1) * 512], k == 0, k == 7))
            C.mmlist(items, ["oTb", "wob%d" % wb], ["bank%d" % ACC[i] for i in range(nt_ * 2)])
        if ci == 0 and not halo:
            load_wdn()
        for ti in range(nt_):
            hb = C.rot("ht", 2)
            if halo:
                P.dma("sp", C.ht[hb][:], hhalo, reads=[hhalokey], writes=["ht%d" % hb])
                C.ts("dve", C.ht[hb][:], C.ht[hb][:], fl[:, 1:2], ALU.mult, ["ht%d" % hb, "flags"], ["ht%d" % hb])
            else:
                P.dma("sp", C.ht[hb][:], hown(2 * ci + ti), reads=[hownkey(2 * ci + ti)], writes=["ht%d" % hb])
            for hf in range(2):
                C.tt("dve", hm[:, ti, hf * 512:(hf + 1) * 512], C.banks[ACC[ti * 2 + hf]][:, :],
                     C.ht[hb][:, hf * 512:(hf + 1) * 512], ALU.add, ["bank%d" % ACC[ti * 2 + hf], "ht%d" % hb], ["hm%d" % ti])
            C.norm_T(hm[:, ti, :], "hm%d" % ti, n2T, "n2T_%d" % ti, ti * 128, g2, "g2_sb")
        nkeys = ["n2T_%d" % ti for ti in range(nt_)]
        dq = []

        def down(i, ai):
            items = []
            for ti in range(nt_):
                for hf in range(2):
                    items.append((C.banks[ACC[ti * 2 + hf]][:, :], aT[ai][:, ti * 128:(ti + 1) * 128],
                                  wdn[:, i, hf * 512:(hf + 1) * 512], i == 0, i == 21))
            C.mmlist(items, ["aT%d" % ai, "wdn_sb"], ["bank%d" % ACC[q] for q in range(nt_ * 2)])
        for i in range(22):
            tfin = []
            for part in range(2):
                fc = i + 22 * part
                up, upk = C.bank("up", UP)
                C.mmg(up[:, 0:ncol], [(wup[:, k, fc * 128:(fc + 1) * 128], n2T[:, k, 0:ncol]) for k in range(8)],
                      ["wup_sb"] + nkeys, [upk])
                ub = C.rot("ubuf", 3)
                u = ubuf[ub]
                uk = "ubuf%d" % ub
                C.cp("pool", u[:, 0:2], carry[:, fc, :], ["carry%d" % fc], [uk])
                C.cp("act", u[:, 2:2 + ncol], up[:, 0:ncol], [upk], [uk])
                if not halo:
                    ta = C.rot("tb", 4)
                    C.act(tb[ta][:, 0:ncol], up[:, 0:ncol], AF.Identity, [upk, "cw_sb"], ["tb%d" % ta],
                          bias=cw[:, fc * 4 + 3:fc * 4 + 4], scale=cw[:, fc * 4 + 2:fc * 4 + 3])
                    C.stt("dve", tb[ta][:, 0:ncol], u[:, 1:1 + ncol], cw[:, fc * 4 + 1:fc * 4 + 2], tb[ta][:, 0:ncol],
                          ALU.mult, ALU.add, [uk, "cw_sb", "tb%d" % ta], ["tb%d" % ta])
                    C.stt("dve", tb[ta][:, 0:ncol], u[:, 0:ncol], cw[:, fc * 4:fc * 4 + 1], tb[ta][:, 0:ncol],
                          ALU.mult, ALU.add, [uk, "cw_sb", "tb%d" % ta], ["tb%d" % ta])
                    tfin.append(ta)
                C.cp("pool", carry[:, fc, :], u[:, ncol:ncol + 2], [uk], ["carry%d" % fc])
            if not halo:
                C.act(sg[:, 0:ncol], tb[tfin[0]][:, 0:ncol], AF.Silu, ["tb%d" % tfin[0]], ["sg"])
                ai = C.rot("aT", 4)
                C.tt("dve", aT[ai][:, 0:ncol], sg[:, 0:ncol], tb[tfin[1]][:, 0:ncol], ALU.mult,
                     ["sg", "tb%d" % tfin[1]], ["aT%d" % ai])
                dq.append((i, ai))
                if len(dq) > 2:
                    down(*dq.pop(0))
        while dq:
            down(*dq.pop(0))
        if halo:
            return
        for ti in range(nt_):
            for hf in range(2):
                bk = ACC[ti * 2 + hf]
                C.tt("dve", res[:, hf * 512:(hf + 1) * 512], C.banks[bk][:, :], hm[:, ti, hf * 512:(hf + 1) * 512],
                     ALU.add, ["bank%d" % bk, "hm%d" % ti], ["res"])
            if final:
                C.memset("dve", ss2[:], 0.0, ["ss2"])
                C.act(C.junk[:], res[:], AF.Square, ["res", "ss2"], ["junk", "ss2"], accum=ss2[:])
                C.act(ss2[:], ss2[:], AF.Sqrt, ["ss2", "epsn"], ["ss2"], bias=C.epsn[:], scale=1.0 / DM)
                C.recip(ss2[:], ss2[:], ["ss2"], ["ss2"])
                C.stt("dve", res[:], res[:], ss2[:, 0:1], gF[:], ALU.mult, ALU.mult, ["res", "ss2", "gF_sb"], ["res"])
            osink(2 * ci + ti, res, "res")

    C.memset("pool", carry[:], 0.0, ["carry%d" % fc for fc in range(44)])
    chunk(0, True)
    for c in range(8):
        chunk(c, False)


LAYER_IN = [("wAm", [DM, 832]), ("gAm", [128, 8]), ("wq", [384, 768]), ("gq", [128, 3]), ("wkv", [256, 512]),
            ("gkv", [128, 2]), ("wAn", [DM, 780]), ("gAn", [128, 8]), ("w1k", [2048, 128]), ("w1v", [2048, 128]),
            ("w2k", [128, 64]), ("w2v", [128, 64]), ("posk", [128, 16]), ("posv", [128, 16]),
            ("wo", [DM, DM]), ("wup", [DM, 5632]), ("wdn", [2816, DM]), ("g2", [128, 8]), ("cw", [128, 176])]
GLOB_IN = [("ropeC", [32, S], F32), ("ropeS", [32, S], F32), ("dmask", [128, 2048], BF16),
           ("maskc", [128, 2 * S], BF16), ("eall", [64, S], BF16), ("selbias", [128, 2048], BF16),
           ("ovl", [128, 130], BF16), ("selg", [12, 768], F32), ("dm4", [128, 512], BF16), ("wm4", [128, 512], BF16),
           ("qaug", [4, 32 * 512], BF16), ("kaug", [4, S], BF16), ("kaugc", [4, 256], BF16), ("gF", [1, DM], F32)]
GROUPS = [[0, 1], [2, 3], [4, 5], [6, 7]]


def build_fused(nlayers=2):
    nc = bass.Bass("TRN2", target_bir_lowering=False)
    C = Ctx(nc)
    P = C.P
    x_d = C.dram("x", [S, DM], F32)
    xown_d = C.dram("xown", [2048, DM], F32)
    xhalo_d = C.dram("xhalo", [128, DM], F32)
    flags_d = C.dram("flags", [128, 2], F32)
    G = {n: C.dram(n, sh, dt) for (n, sh, dt) in GLOB_IN}
    Ls = [{n: C.dram("%s_%d" % (n, l), sh, F32) for (n, sh) in LAYER_IN} for l in range(nlayers)]
    out_d = C.dram("hout", [2048, DM], F32, out=True)
    P.dma("pool", C.flags[:], flags_d, writes=["flags"])

    omy = [[nc.dram_tensor("omy_%d_%d" % (l, c), [512, 2048], BF16) for c in range(2)] for l in range(nlayers)]
    oall = [[nc.dram_tensor("oall_%d_%d" % (l, c), [1024, 2048], BF16) for c in range(2)] for l in range(nlayers)]
    hmy = [nc.dram_tensor("hmy_%d" % j, [512, DM], F32) for j in range(4)]
    hall = [nc.dram_tensor("hall_%d" % j, [1024, DM], F32) for j in range(4)]

    for l in range(nlayers):
        if l == 0:
            hsrc = lambda g: x_d[g * 128:(g + 1) * 128, :]
            hkey = lambda g: "x"
        else:
            def hsrc(g):
                r, w = g // 16, g % 16
                return hall[w // 4][r * 512 + (w % 4) * 128:r * 512 + (w % 4) * 128 + 128, :]
            hkey = lambda g: "hall%d" % ((g % 16) // 4)

        def osink_mla(hh, tc, ot, otkey, l=l):
            c = tc // 4
            col = (tc % 4) * 512
            P.dma("sp", omy[l][c][hh * 64:(hh + 1) * 64, col:col + 512], ot[:, :], reads=[otkey], writes=["omy%d" % c])

        def osink_nsa(qb, ot, otkey, l=l):
            c = qb // 16
            col = (qb % 16) * 128
            P.dma("sp", omy[l][c][256:512, :].rearrange("(r d) t -> d r t", d=64)[:, :, col:col + 128],
                  ot[:, :].rearrange("p (r t) -> p r t", r=4), reads=[otkey], writes=["omy%d" % c])
            if qb % 16 == 15:
                P.cc("AllGather", GROUPS, omy[l][c].ap().opt(), oall[l][c].ap().opt(), reads=["omy%d" % c], writes=["oall%d" % c])

        phase_mla(C, "m%d_" % l, Ls[l], G, hsrc, hkey, osink_mla)
        phase_nsa(C, "n%d_" % l, Ls[l], G, hsrc, hkey, osink_nsa)
        final = (l == nlayers - 1)
        if l == 0:
            hown = lambda t: xown_d[t * 128:(t + 1) * 128, :]
            hownkey = lambda t: "xown"
            hhalo, hhalokey = xhalo_d, "xhalo"
        else:
            hown = lambda t: hmy[t // 4][(t % 4) * 128:(t % 4) * 128 + 128, :]
            hownkey = lambda t: "hmy%d" % (t // 4)
            hhalo, hhalokey = hall[3][384:512, :], "hall3"
        if final:
            def osink_ffn(t, res, rkey):
                P.dma("sp", out_d[t * 128:(t + 1) * 128, :], res[:], reads=[rkey])
        else:
            def osink_ffn(t, res, rkey):
                P.dma("sp", hmy[t // 4][(t % 4) * 128:(t % 4) * 128 + 128, :], res[:], reads=[rkey], writes=["hmy%d" % (t // 4)])
                if t % 4 == 3:
                    j = t // 4
                    P.cc("AllGather", GROUPS, hmy[j].ap().opt(), hall[j].ap().opt(), reads=["hmy%d" % j], writes=["hall%d" % j])
        phase_ffn(C, "f%d_" % l, Ls[l], G, final, hown, hownkey, hhalo, hhalokey, oall[l], osink_ffn)
    P.emit()
    return nc


def _pk(g, nk):
    return np.ascontiguousarray(np.asarray(g, np.float32).reshape(nk, 128).T)


def _consts():
    c = {}
    p = np.arange(128)[:, None]
    i512 = np.arange(512)[None, :]
    dm = np.zeros((128, 4, 512), np.float32)
    for m in range(4):
        dm[:, m, :] = np.where(128 * m + p <= i512, 0.0, NEG)
    c["dmask"] = dm.reshape(128, 2048).astype(NPBF)
    i128 = np.arange(128)[None, :]
    c["dm4"] = np.tile(np.where(p <= i128, 0.0, NEG), (1, 4)).astype(NPBF)
    c["wm4"] = np.tile(np.where(i128 < p, 0.0, NEG), (1, 4)).astype(NPBF)
    n = np.arange(256)[:, None]
    t = np.arange(S)[None, :]
    mc = np.where((t >= 16 * n + 31) & (n <= 254), 0.0, NEG).astype(np.float32)
    c["maskc"] = np.ascontiguousarray(mc.reshape(2, 128, S).transpose(1, 0, 2).reshape(128, 2 * S)).astype(NPBF)
    j = np.arange(64)[:, None]
    c["eall"] = (np.arange(S)[None, :] // 64 == j).astype(np.float32).astype(NPBF)
    tt_ = np.arange(S)
    cur = (tt_ // 64)[:, None]
    jj = np.arange(64)[None, :]
    sbias = np.zeros((S, 64), np.float32)
    sbias[np.broadcast_to(jj > cur, (S, 64))] = -1e4
    sbias[np.broadcast_to((jj == 0) | (jj == cur) | (jj == cur - 1), (S, 64))] = 1e4
    c["selbias"] = np.ascontiguousarray(sbias.reshape(32, 128, 64).transpose(1, 0, 2).reshape(128, 2048)).astype(NPBF)
    cs = (np.arange(256) * 16)[:, None]
    ss_ = (np.arange(64) * 64)[None, :]
    ov = ((cs < ss_ + 64) & (cs + 32 > ss_)).astype(np.float32)
    ov[255] = 0.0
    ov1 = np.concatenate([ov, np.ones((256, 1), np.float32)], 1)
    c["ovl"] = np.ascontiguousarray(ov1.reshape(2, 128, 65).transpose(1, 0, 2).reshape(128, 130)).astype(NPBF)
    sg = np.zeros((12, 12, 64), np.float32)
    for g in range(12):
        sg[g, g, :] = 1.0
    c["selg"] = sg.reshape(12, 768)
    k = np.arange(S)
    c["kaug"] = np.stack([np.ones(S), np.ones(S), k // 64, k % 64]).astype(np.float32).astype(NPBF)
    e = np.arange(256) * 16 + 31
    c["kaugc"] = np.stack([np.ones(256), np.ones(256), e // 64, e % 64]).astype(np.float32).astype(NPBF)
    inv = 1.0 / (10000.0 ** (np.arange(0, 32, 2, dtype=np.float32) / 32))
    ang = np.arange(S, dtype=np.float32)[:, None] * inv[None, :]
    cos, sin = np.cos(ang).T.astype(np.float32), np.sin(ang).T.astype(np.float32)
    c["ropeC"] = np.ascontiguousarray(np.concatenate([cos, cos], 0))
    c["ropeS"] = np.ascontiguousarray(np.concatenate([-sin, sin], 0))
    return c


def _qaug(group):
    slopes = np.exp2(-8.0 * np.arange(1, 9, dtype=np.float32) / 8)
    t = np.arange(S).reshape(32, 1, 128)
    out = np.zeros((4, 32, 4, 128), np.float32)
    for r in range(4):
        a = slopes[group * 4 + r] / SC_NSA
        out[0, :, r, :] = (-a * 64 * (t // 64))[:, 0, :]
        out[1, :, r, :] = (-a * (t % 64))[:, 0, :]
        out[2, :, r, :] = a * 64
        out[3, :, r, :] = a
    return out.reshape(4, 32 * 512).astype(NPBF)


_PROG = {}


def _prog(nlayers=2):
    if nlayers not in _PROG:
        _PROG[nlayers] = build_fused(nlayers)
    return _PROG[nlayers]


def _layer_maps(l, I, c):
    m = {}
    w_in, w_uq, w_ukv = I["w_in"][l], I["w_uq"][l], I["w_ukv"][l]
    sw = list(range(656, 672)) + list(range(640, 656))
    colsA = list(range(0, 640)) + list(range(0, 64)) + list(range(640, 672)) + list(range(0, 64)) + sw
    m["wAm"] = np.ascontiguousarray(w_in[:, colsA])
    m["gAm"] = _pk(I["attn_norm"][l], 8)
    qc, kc, vc = [], [], []
    for hh in range(4 * c, 4 * c + 4):
        base = 96 * hh
        nope = list(range(base, base + 64))
        rope = list(range(base + 64, base + 96))
        qc += nope + rope + nope + rope[16:] + rope[:16]
        kc += list(range(128 * hh, 128 * hh + 64))
        vc += list(range(128 * hh + 64, 128 * hh + 128))
    m["wq"] = np.ascontiguousarray(w_uq[:, qc])
    m["gq"] = _pk(I["q_norm"][l], 3)
    m["wkv"] = np.ascontiguousarray(w_ukv[:, kc + vc])
    m["gkv"] = _pk(I["kv_norm"][l], 2)
    g = c
    q0 = 672 + 256 * g
    o = 1184
    rng = lambda a: list(range(a, a + 64))
    kcc, vcc, ksc = rng(o + 64 * g), rng(o + 128 + 64 * g), rng(o + 256 + 64 * g)
    vsc, kwc, vwc = rng(o + 384 + 64 * g), rng(o + 512 + 64 * g), rng(o + 640 + 64 * g)
    gtc = list(range(1952 + 12 * g, 1952 + 12 * g + 12))
    cols = list(range(q0, q0 + 256)) + kcc + kcc + vcc + vcc + ksc + kwc + vsc + vwc + gtc
    m["wAn"] = np.ascontiguousarray(w_in[:, cols])
    m["gAn"] = m["gAm"]
    posT = lambda pz: np.ascontiguousarray(np.asarray(pz, np.float32).reshape(16, 128).T)
    m["w1k"] = np.ascontiguousarray(I["cmp_k_w1"][l].reshape(2048, 128))
    m["w1v"] = np.ascontiguousarray(I["cmp_v_w1"][l].reshape(2048, 128))
    m["w2k"] = np.ascontiguousarray(I["cmp_k_w2"][l])
    m["w2v"] = np.ascontiguousarray(I["cmp_v_w2"][l])
    m["posk"] = posT(I["cmp_pos_k"][l])
    m["posv"] = posT(I["cmp_pos_v"][l])
    perm = list(range(0, 256)) + list(range(512, 768)) + list(range(256, 512)) + list(range(768, 1024))
    m["wo"] = np.ascontiguousarray(I["w_o"][l][perm, :])
    m["wup"] = np.ascontiguousarray(I["w_up"][l])
    m["wdn"] = np.ascontiguousarray(I["w_down"][l])
    m["g2"] = _pk(I["ffn_norm"][l], 8)
    cwv = np.stack([I["conv_w"][l][0], I["conv_w"][l][1], I["conv_w"][l][2], I["conv_b"][l]], -1)
    m["cw"] = np.ascontiguousarray(cwv.reshape(44, 128, 4).transpose(1, 0, 2).reshape(128, 176)).astype(np.float32)
    return m


def make_maps(I, nlayers=2):
    cst = _consts()
    lm = [[_layer_maps(l, I, c) for c in range(2)] for l in range(nlayers)]
    maps = []
    for b in range(4):
        for c in range(2):
            m = {"x": np.ascontiguousarray(I["x"][b]),
                 "xown": np.ascontiguousarray(I["x"][b][2048 * c:2048 * c + 2048]),
                 "xhalo": np.ascontiguousarray(I["x"][b][1920:2048]),
                 "flags": np.ascontiguousarray(np.tile(np.array([[1.0 - c, float(c)]], np.float32), (128, 1)))}
            for (n, sh, dt) in GLOB_IN:
                if n == "qaug":
                    m[n] = _qaug(c)
                elif n == "gF":
                    m[n] = np.ascontiguousarray(np.asarray(I["final_norm"], np.float32).reshape(1, DM))
                else:
                    m[n] = cst[n]
            for l in range(nlayers):
                for k, v in lm[l][c].items():
                    m["%s_%d" % (k, l)] = v
            maps.append(m)
    return maps


def kernel(**inputs):
    I = {k: np.asarray(v, dtype=np.float32) for k, v in inputs.items()}
    res = run_bass_kernel_spmd(_prog(2), make_maps(I, 2), core_ids=list(range(8))).results
    out = np.empty((4, S, DM), np.float32)
    for b in range(4):
        for c in range(2):
            out[b, 2048 * c:2048 * c + 2048] = np.asarray(res[2 * b + c]["hout"])
    return out
```

```python
import numpy as np
import ml_dtypes
import concourse.bass as bass
import concourse.mybir as mybir
from concourse.bass_utils import run_bass_kernel_spmd

F32 = mybir.dt.float32
BF16 = mybir.dt.bfloat16
ALU = mybir.AluOpType
AF = mybir.ActivationFunctionType
AX = mybir.AxisListType
NPBF = ml_dtypes.bfloat16

ENGS = ["pe", "act", "dve", "pool", "sp"]
DMA_POOL = 12
S = 4096
DM = 1024
NEG = -30000.0
SC_MLA = 96 ** -0.5
SC_NSA = 0.125
EPS = 1e-6


class Prog:
    def __init__(self, nc):
        self.nc = nc
        self.ops = {e: [] for e in ENGS}
        self.lastw = {}
        self.readers = {}
        self.dma_n = {e: 0 for e in ENGS + ["cc"]}
        self.dma_sem_cnt = {}
        self.last_c = {}
        self.last_d = {}

    def sb(self, name, shape, dt):
        return self.nc.alloc_sbuf_tensor(name, list(shape), dt)

    def ps(self, name, shape, dt=F32):
        return self.nc.alloc_psum_tensor(name, list(shape), dt)

    def _add(self, eng, fn, reads, writes, dma, cc=False):
        op = dict(eng=eng, fn=fn, deps=[], dma=dma, marked=False, inc=(1 if cc else 16))
        deps = []
        for k in reads:
            w = self.lastw.get(k)
            if w is not None:
                deps.append(w)
        for k in writes:
            w = self.lastw.get(k)
            if w is not None:
                deps.append(w)
            deps.extend(self.readers.get(k, ()))
        seen = set()
        for d in deps:
            if id(d) in seen or d is op:
                continue
            seen.add(id(d))
            if (not d["dma"]) and d["eng"] == eng and eng in ("pe", "sp"):
                continue
            op["deps"].append(d)
            d["marked"] = True
        if dma:
            qn = "cc" if cc else eng
            q = self.dma_n[qn]
            self.dma_n[qn] += 1
            semkey = (qn, q % (4 if cc else DMA_POOL))
            m = self.dma_sem_cnt.get(semkey, 0) + 1
            self.dma_sem_cnt[semkey] = m
            op["dsem"] = semkey
            op["dval"] = op["inc"] * m
            op["marked"] = True
            self.last_d[semkey] = op
        else:
            self.last_c[eng] = op
        for k in reads:
            self.readers.setdefault(k, []).append(op)
        for k in writes:
            self.lastw[k] = op
            self.readers[k] = []
        self.ops[eng].append(op)
        return op

    def op(self, eng, fn, reads=(), writes=()):
        return self._add(eng, fn, list(reads), list(writes), False)

    def dma(self, eng, out, in_, reads=(), writes=()):
        return self._add(eng, lambda e: e.dma_start(out=out, in_=in_), list(reads), list(writes), True)

    def cc(self, kind, groups, in_ap, out_ap, reads=(), writes=()):
        return self._add("pool", lambda e: e.collective_compute(kind, ALU.bypass, replica_groups=groups,
                                                                ins=[in_ap], outs=[out_ap]),
                         list(reads), list(writes), True, cc=True)

    def barrier(self):
        deps = list(self.last_c.values()) + list(self.last_d.values())
        for d in deps:
            d["marked"] = True
        for e in ENGS:
            self.ops[e].append(dict(eng=e, fn=None, deps=list(deps), dma=False, marked=False, inc=0))
        self.lastw = {}
        self.readers = {}

    def emit(self):
        nc = self.nc
        csem = {e: nc.alloc_semaphore("c_" + e) for e in ENGS}
        dsem = {}
        for (qn, i) in self.dma_sem_cnt:
            dsem[(qn, i)] = nc.alloc_semaphore("d_%s_%d" % (qn, i))
        for e in ENGS:
            c = 0
            for o in self.ops[e]:
                if o["dma"] or o["fn"] is None:
                    continue
                if o["marked"]:
                    c += 1
                    o["cval"] = c
        all_dma = [o for e in ENGS for o in self.ops[e] if o["dma"]]

        def run(e, eng):
            seen = {}

            def wait(sem_key, sem, val):
                if seen.get(sem_key, 0) >= val:
                    return
                seen[sem_key] = val
                eng.wait_ge(sem, val)

            for o in self.ops[e]:
                for d in o["deps"]:
                    if d["dma"]:
                        wait(d["dsem"], dsem[d["dsem"]], d["dval"])
                    else:
                        wait(("c", d["eng"]), csem[d["eng"]], d["cval"])
                if o["fn"] is None:
                    continue
                if o["dma"]:
                    if o["dval"] > o["inc"]:
                        wait(o["dsem"], dsem[o["dsem"]], o["dval"] - o["inc"])
                    o["fn"](eng).then_inc(dsem[o["dsem"]], o["inc"])
                else:
                    ins = o["fn"](eng)
                    if o["marked"]:
                        ins.then_inc(csem[e], 1)
            if e == "sp":
                last = {}
                for o in all_dma:
                    last[o["dsem"]] = max(last.get(o["dsem"], 0), o["dval"])
                for k, v in last.items():
                    eng.wait_ge(dsem[k], v)

        with nc.Block() as block:
            @block.tensor
            def _(eng):
                run("pe", eng)

            @block.scalar
            def _(eng):
                run("act", eng)

            @block.vector
            def _(eng):
                run("dve", eng)

            @block.gpsimd
            def _(eng):
                run("pool", eng)

            @block.sync
            def _(eng):
                run("sp", eng)


def _nbytes(shape, dt):
    n = 1
    for d in shape[1:]:
        n *= d
    return n * (4 if dt == F32 else 2)


class Ctx:
    def __init__(self, nc):
        self.nc = nc
        self.P = P = Prog(nc)
        self.rots = {}
        self.pname = "g_"
        self.ident = nc.alloc_sbuf_tensor("ident", [128, 128], BF16)
        self.identf = nc.alloc_sbuf_tensor("identf", [128, 128], F32)
        self.onesf = nc.alloc_sbuf_tensor("onesf", [128, 128], F32)
        self.epsn = nc.alloc_sbuf_tensor("epsn", [128, 1], F32)
        self.flags = nc.alloc_sbuf_tensor("flags_sb", [128, 2], F32)
        identf, ident, onesf, epsn = self.identf, self.ident, self.onesf, self.epsn
        P.op("pool", lambda e: e.memset(identf[:], 0.0), writes=["identf"])
        P.op("pool", lambda e: e.affine_select(out=identf[:], in_=identf[:], pattern=[[-1, 128]],
                                                compare_op=ALU.not_equal, fill=1.0, base=0, channel_multiplier=1),
             reads=["identf"], writes=["identf"])
        P.op("dve", lambda e: e.tensor_copy(out=ident[:], in_=identf[:]), reads=["identf"], writes=["ident"])
        P.op("dve", lambda e: e.memset(onesf[:], 1.0), writes=["onesf"])
        P.op("dve", lambda e: e.memset(epsn[:], EPS), writes=["epsn"])
        self.pst = P.ps("pst", [128, 1024], BF16)
        self.banks = [P.ps("bank%d" % i, [128, 512], F32) for i in range(7)]
        self.banks.append(self.pst.bitcast(F32))
        self.base = ((int(nc.sbuf_base) + 63) // 64) * 64
        self.top = int(nc.sbuf_top)
        self.off = self.base

    def begin_phase(self, name):
        self.P.barrier()
        self.pname = name
        self.off = self.base

    def sb(self, name, shape, dt):
        nb = ((_nbytes(shape, dt) + 31) // 32) * 32
        assert self.off + nb <= self.top, ("SBUF overflow", self.pname, name, self.off + nb - self.top)
        t = self.nc.alloc_sbuf_tensor_at(self.pname + name, list(shape), dt, offset=self.off)
        self.off += nb
        return t

    def rot(self, name, n):
        i = self.rots.get(name, 0) % n
        self.rots[name] = (i + 1) % n
        return i

    def dram(self, name, shape, dt, out=False):
        return self.nc.dram_tensor(name, list(shape), dt, kind="ExternalOutput" if out else "ExternalInput").ap()

    def mm(self, out, lhsT, rhs, start, stop, reads, writes):
        return self.P.op("pe", lambda e: e.matmul(out, lhsT=lhsT, rhs=rhs, start=start, stop=stop), reads, writes)

    def mmg(self, out, pairs, reads, writes, start=True, stop=True):
        n = len(pairs)

        def fn(e):
            ins = None
            for i, (l, r) in enumerate(pairs):
                ins = e.matmul(out, lhsT=l, rhs=r, start=(start and i == 0), stop=(stop and i == n - 1))
            return ins
        return self.P.op("pe", fn, reads, writes)

    def mmlist(self, items, reads, writes):
        def fn(e):
            ins = None
            for (o, l, r, st, sp) in items:
                ins = e.matmul(o, lhsT=l, rhs=r, start=st, stop=sp)
            return ins
        return self.P.op("pe", fn, reads, writes)

    def tr(self, out, in_, ident, reads, writes):
        return self.P.op("pe", lambda e: e.transpose(out, in_, ident), reads, writes)

    def act(self, out, in_, func, reads, writes, bias=None, scale=1.0, accum=None):
        kw = {}
        if bias is not None:
            kw["bias"] = bias
        if accum is not None:
            kw["accum_out"] = accum
        return self.P.op("act", lambda e: e.activation(out=out, in_=in_, func=func, scale=scale, **kw), reads, writes)

    def tt(self, eng, out, in0, in1, op, reads, writes):
        return self.P.op(eng, lambda e: e.tensor_tensor(out=out, in0=in0, in1=in1, op=op), reads, writes)

    def ts(self, eng, out, in0, s1, op0, reads, writes, s2=None, op1=None):
        if op1 is None:
            return self.P.op(eng, lambda e: e.tensor_scalar(out=out, in0=in0, scalar1=s1, scalar2=None, op0=op0), reads, writes)
        return self.P.op(eng, lambda e: e.tensor_scalar(out=out, in0=in0, scalar1=s1, scalar2=s2, op0=op0, op1=op1), reads, writes)

    def stt(self, eng, out, in0, scalar, in1, op0, op1, reads, writes):
        return self.P.op(eng, lambda e: e.scalar_tensor_tensor(out=out, in0=in0, scalar=scalar, in1=in1, op0=op0, op1=op1), reads, writes)

    def cp(self, eng, out, in_, reads, writes):
        if eng == "act":
            return self.P.op("act", lambda e: e.copy(out=out, in_=in_), reads, writes)
        return self.P.op(eng, lambda e: e.tensor_copy(out=out, in_=in_), reads, writes)

    def recip(self, out, in_, reads, writes):
        return self.P.op("dve", lambda e: e.reciprocal(out=out, in_=in_), reads, writes)

    def memset(self, eng, ap, val, writes):
        return self.P.op(eng, lambda e: e.memset(ap, val), [], writes)

    def bank(self, grp, idxs):
        i = idxs[self.rot(grp, len(idxs))]
        return self.banks[i], ("pst" if i == 7 else "bank%d" % i)

    def load_w(self, name, w_dram, nk, ncols):
        wsb = self.sb(name, [128, nk, ncols], BF16)
        for k in range(nk):
            self.P.dma("pool", wsb[:, k, :], w_dram[k * 128:(k + 1) * 128, :], writes=[name])
        return wsb

    def load_const(self, name, dram_ap, shape, dt, eng="pool"):
        t = self.sb(name, shape, dt)
        self.P.dma(eng, t[:], dram_ap, writes=[name])
        return t

    def setup_norm(self):
        self.ht = [self.sb("ht%d" % i, [128, DM], F32) for i in range(2)]
        self.junk = self.sb("junk", [128, DM], BF16)
        self.nb = self.sb("nb", [128, DM], BF16)
        self.ss = [self.sb("ss%d" % i, [128, 1], F32) for i in range(2)]

    def norm_T(self, src, srckey, nT, nkey, col0, g_sb, gkey):
        b = self.rot("ss", 2)
        ss = self.ss[b]
        sk = "ss%d" % b
        self.memset("dve", ss[:], 0.0, [sk])
        self.act(self.junk[:], src, AF.Square, [srckey, sk], ["junk", sk], accum=ss[:])
        self.act(ss[:], ss[:], AF.Sqrt, [sk, "epsn"], [sk], bias=self.epsn[:], scale=1.0 / DM)
        self.recip(ss[:], ss[:], [sk], [sk])
        self.ts("dve", self.nb[:], src, ss[:, 0:1], ALU.mult, [srckey, sk], ["nb"])
        pst = self.pst
        nb, ident = self.nb, self.ident

        def fn(e):
            ins = None
            for k in range(8):
                ins = e.transpose(pst[:, k * 128:(k + 1) * 128], nb[:, k * 128:(k + 1) * 128], ident[:])
            return ins
        self.P.op("pe", fn, ["nb", "ident"], ["pst"])
        self.tt("dve", nT[:, :, col0:col0 + 128], pst[:, :].rearrange("p (k t) -> p k t", k=8),
                g_sb[:, :].unsqueeze(2).to_broadcast([128, 8, 128]), ALU.mult, ["pst", gkey], [nkey])


def attn_loop(C, tiles, score_fn, scale, v_fn, po, pok, sbanks, pts, ptname, after_first=None, depth=2):
    n = len(tiles)
    issued = []

    def issue(i):
        sbk, sk = C.bank("s", sbanks)
        pairs, rd = score_fn(tiles[i])
        C.mmg(sbk[:, :], pairs, rd, [sk])
        issued.append((sbk, sk))
    for i in range(min(depth, n)):
        issue(i)
    if after_first is not None:
        after_first()
    for i in range(n):
        sbk, sk = issued[i]
        pi = C.rot(ptname, len(pts))
        C.act(pts[pi][:], sbk[:, :], AF.Exp, [sk], ["%s%d" % (ptname, pi)], scale=scale)
        if i + depth < n:
            issue(i + depth)
        lhsT, rd = v_fn(tiles[i])
        C.mm(po[:, :], lhsT, pts[pi][:], i == 0, i == n - 1, rd + ["%s%d" % (ptname, pi)], [pok])


def phase_mla(C, name, L, G, hsrc, hkey, osink):
    P = C.P
    C.begin_phase(name)
    C.setup_norm()
    gA = C.load_const("gA_sb", L["gAm"], [128, 8], F32)
    gq = C.load_const("gq_sb", L["gq"], [128, 3], F32)
    gkv = C.load_const("gkv_sb", L["gkv"], [128, 2], F32)
    dmask = C.load_const("dmask_sb", G["dmask"], [128, 2048], BF16)
    wA = C.load_w("wA_sb", L["wAm"], 8, 832)
    wq = C.load_w("wq_sb", L["wq"], 3, 768)
    wkv = C.load_w("wkv_sb", L["wkv"], 2, 512)
    ropeC_d, ropeS_d = G["ropeC"], G["ropeS"]

    Kh = C.sb("Kh", [128, 4, S], BF16)
    C.memset("pool", Kh[96:128, :, :], 0.0, ["Kh_pad"])
    Vt = C.sb("Vt", [128, 32, 4, 128], BF16)
    C.memset("pool", Vt[:, :, :, 64:65], 1.0, ["Vt_%d" % c for c in range(8)])
    C.memset("pool", Vt[:, :, :, 65:128], 0.0, ["Vt_pad"])
    nTs = [C.sb("nT%d" % i, [128, 8, 512], BF16) for i in range(2)]
    Qhs = [C.sb("Qh%d" % i, [128, 4, 512], BF16) for i in range(2)]
    for i in range(2):
        C.memset("pool", Qhs[i][96:128, :, :], 0.0, ["Qh_pad"])
    zf = C.sb("zf", [128, 3, 512], F32)
    sq = C.sb("sq", [128, 3, 512], F32)
    rr = C.sb("rr", [128, 512], F32)
    cqn = C.sb("cqn", [128, 3, 512], BF16)
    ckvn = C.sb("ckvn", [128, 2, 512], BF16)
    Ct = C.sb("Ct", [96, 512], F32)
    St = C.sb("St", [96, 512], F32)
    t1 = C.sb("t1", [96, 512], F32)
    t2 = C.sb("t2", [96, 512], F32)
    pts = [C.sb("pt%d" % i, [128, 512], BF16) for i in range(4)]
    rsrow = C.sb("rsrow", [65, 512], F32)
    rec4 = C.sb("rec4", [128, 4], F32)
    wb = C.sb("wb", [128, 4, 64], F32)
    bcs = C.sb("bcs", [64, 512], F32)
    ots = [C.sb("ot%d" % i, [64, 512], BF16) for i in range(2)]

    PJ = [0, 1]
    SB_ = [2, 3, 4]
    PO = [5, 6]
    pending = []

    def flush():
        while pending:
            pending.pop(0)()

    def latent(c0, nm, dim, dst, dkey, nT, nkeys, gl, glkey):
        for m in range(nm):
            pj, pk = C.bank("pj", PJ)
            C.mmg(pj[:, :], [(wA[:, k, c0 + m * 128:c0 + (m + 1) * 128], nT[:, k, :]) for k in range(8)],
                  ["wA_sb"] + nkeys, [pk])
            C.act(zf[:, m, :], pj[:, :], AF.Copy, [pk], ["zf%d" % m])
            C.act(sq[:, m, :], pj[:, :], AF.Square, [pk], ["sq%d" % m])
        pj, pk = C.bank("pj", PJ)
        C.mmg(pj[:, :], [(C.onesf[:, :], sq[:, m, :]) for m in range(nm)], ["onesf"] + ["sq%d" % m for m in range(nm)], [pk])
        C.act(rr[:], pj[:, :], AF.Sqrt, [pk, "epsn"], ["rr"], bias=C.epsn[:], scale=1.0 / dim)
        C.recip(rr[:], rr[:], ["rr"], ["rr"])
        for m in range(nm):
            C.stt("dve", dst[:, m, :], zf[:, m, :], gl[:, m:m + 1], rr[:], ALU.mult, ALU.mult, ["zf%d" % m, "rr", glkey], [dkey])

    def stage_T(tc):
        nT = nTs[tc % 2]
        nkeys = []
        for ti in range(4):
            hb = C.rot("ht", 2)
            P.dma("sp", C.ht[hb][:], hsrc(4 * tc + ti), reads=[hkey(4 * tc + ti)], writes=["ht%d" % hb])
            nk = "nT%d_%d" % (tc % 2, ti)
            C.norm_T(C.ht[hb][:], "ht%d" % hb, nT, nk, ti * 128, gA, "gA_sb")
            nkeys.append(nk)
        return nT, nkeys

    def stage_P1(tc, nT, nkeys):
        t0 = tc * 512
        latent(0, 3, 384.0, cqn, "cqn", nT, nkeys, gq, "gq_sb")
        latent(384, 2, 256.0, ckvn, "ckvn", nT, nkeys, gkv, "gkv_sb")
        P.dma("sp", Ct[64:96, :], ropeC_d[:, t0:t0 + 512], writes=["Ct"])
        P.dma("sp", St[64:96, :], ropeS_d[:, t0:t0 + 512], writes=["St"])
        pA, pAk = C.bank("pj", PJ)
        C.mmg(pA[0:96, :], [(wA[:, k, 640:736], nT[:, k, :]) for k in range(8)], ["wA_sb"] + nkeys, [pAk])
        C.tt("dve", t1[64:96, :], pA[64:96, :], Ct[64:96, :], ALU.mult, [pAk, "Ct"], ["t1"])
        pB, pBk = C.bank("pj", PJ)
        C.mmg(pB[0:96, :], [(wA[:, k, 736:832], nT[:, k, :]) for k in range(8)], ["wA_sb"] + nkeys, [pBk])
        C.tt("dve", t2[64:96, :], pB[64:96, :], St[64:96, :], ALU.mult, [pBk, "St"], ["t2"])
        for hh in range(4):
            C.tt("pool", Kh[64:96, hh, t0:t0 + 512], t1[64:96, :], t2[64:96, :], ALU.add, ["t1", "t2"], ["Kh_%d" % tc])

    def stage_P2(tc):
        t0 = tc * 512
        Qh = Qhs[tc % 2]
        qk = "Qh%d" % (tc % 2)
        for hh in range(4):
            pA, pAk = C.bank("pj", PJ)
            C.mmg(pA[0:96, :], [(wq[:, m, hh * 192:hh * 192 + 96], cqn[:, m, :]) for m in range(3)], ["wq_sb", "cqn"], [pAk])
            C.cp("act", Qh[0:64, hh, :], pA[0:64, :], [pAk], [qk])
            C.tt("dve", t1[64:96, :], pA[64:96, :], Ct[64:96, :], ALU.mult, [pAk, "Ct"], ["t1"])
            pB, pBk = C.bank("pj", PJ)
            C.mmg(pB[0:96, :], [(wq[:, m, hh * 192 + 96:hh * 192 + 192], cqn[:, m, :]) for m in range(3)], ["wq_sb", "cqn"], [pBk])
            C.tt("dve", t2[64:96, :], pB[64:96, :], St[64:96, :], ALU.mult, [pBk, "St"], ["t2"])
            C.tt("pool", Qh[64:96, hh, :], t1[64:96, :], t2[64:96, :], ALU.add, ["t1", "t2"], [qk])
        for hh in range(4):
            pj, pk = C.bank("pj", PJ)
            C.mmg(pj[0:64, :], [(wkv[:, j, hh * 64:(hh + 1) * 64], ckvn[:, j, :]) for j in range(2)], ["wkv_sb", "ckvn"], [pk])
            C.cp("act", Kh[0:64, hh, t0:t0 + 512], pj[0:64, :], [pk], ["Kh_%d" % tc])
        for ti in range(4):
            pj, pk = C.bank("pj", PJ)
            C.mmg(pj[:, 0:256], [(ckvn[:, j, ti * 128:(ti + 1) * 128], wkv[:, j, 256:512]) for j in range(2)], ["wkv_sb", "ckvn"], [pk])
            C.cp("act", Vt[:, 4 * tc + ti, :, 0:64], pj[:, 0:256].rearrange("p (h d) -> p h d", h=4), [pk], ["Vt_%d" % tc])

    def head(tc, hh):
        Qh = Qhs[tc % 2]
        qk = "Qh%d" % (tc % 2)
        po, pok = C.bank("po", PO)
        nkt = 4 * tc + 4

        def score(j):
            pairs = [(Kh[:, hh, j * 128:(j + 1) * 128], Qh[:, hh, :])]
            rd = [qk, "Kh_%d" % (j // 4), "Kh_pad", "Qh_pad"]
            if j >= 4 * tc:
                m = j - 4 * tc
                pairs.append((C.ident[:, :], dmask[:, m * 512:(m + 1) * 512]))
                rd += ["ident", "dmask_sb"]
            return pairs, rd

        def vfn(j):
            return Vt[:, j, hh, :], ["Vt_%d" % (j // 4), "Vt_pad"]
        attn_loop(C, list(range(nkt)), score, SC_MLA, vfn, po, pok, SB_, pts, "pt", after_first=flush)

        def fin():
            C.cp("dve", rsrow[64:65, :], po[64:65, :], [pok], ["rsrow"])
            pj, pk = C.bank("pj", PJ)
            C.mmlist([(pj[:, r:r + 1], rsrow[64:65, r * 128:(r + 1) * 128], C.onesf[64:65, 0:1], True, True) for r in range(4)],
                     ["rsrow", "onesf"], [pk])
            C.ts("dve", rec4[:, :], pj[:, 0:4], 1e-30, ALU.add, [pk], ["rec4"])
            C.recip(rec4[:, :], rec4[:, :], ["rec4"], ["rec4"])
            C.cp("dve", wb[:, :, :], rec4[:, 0:4].unsqueeze(2).to_broadcast([128, 4, 64]), ["rec4"], ["wb"])
            pj2, pk2 = C.bank("pj", PJ)
            C.mmlist([(pj2[0:64, r * 128:(r + 1) * 128], wb[:, r, :], C.identf[:, :], True, True) for r in range(4)],
                     ["wb", "identf"], [pk2])
            C.cp("act", bcs[:, :], pj2[0:64, :], [pk2], ["bcs"])
            oi = C.rot("ot", 2)
            C.tt("dve", ots[oi][:, :], po[0:64, :], bcs[:, :], ALU.mult, [pok, "bcs"], ["ot%d" % oi])
            osink(hh, tc, ots[oi], "ot%d" % oi)
        pending.append(fin)

    st = stage_T(0)
    stage_P1(0, *st)
    stage_P2(0)
    for tc in range(8):
        head(tc, 0)
        if tc < 7:
            st = stage_T(tc + 1)
        head(tc, 1)
        if tc < 7:
            stage_P1(tc + 1, *st)
        head(tc, 2)
        if tc < 7:
            stage_P2(tc + 1)
        head(tc, 3)
    flush()


def phase_nsa(C, name, L, G, hsrc, hkey, osink):
    P = C.P
    C.begin_phase(name)
    NW = 780
    C.setup_norm()
    gA = C.load_const("gA_sb", L["gAn"], [128, 8], F32)
    wA = C.load_w("wA_sb", L["wAn"], 8, NW)
    w1k = C.load_w("w1k_sb", L["w1k"], 16, 128)
    w1v = C.load_w("w1v_sb", L["w1v"], 16, 128)
    w2k = C.load_w("w2k_sb", L["w2k"], 1, 64)
    w2v = C.load_w("w2v_sb", L["w2v"], 1, 64)
    posk = C.load_w("posk_sb", L["posk"], 1, 16)
    posv = C.load_w("posv_sb", L["posv"], 1, 16)
    maskc = C.load_const("maskc_sb", G["maskc"], [128, 2 * S], BF16)
    eall = C.sb("eall_sb", [128, S], BF16)
    C.memset("pool", eall[64:128, :], 0.0, ["eall_sb"])
    P.dma("pool", eall[0:64, :], G["eall"], writes=["eall_sb"])
    selbias = C.load_const("selbias_sb", G["selbias"], [128, 2048], BF16)
    ovl = C.load_const("ovl_sb", G["ovl"], [128, 130], BF16)
    selg = C.load_const("selg_sb", G["selg"], [12, 768], F32)
    dm4 = C.load_const("dm4_sb", G["dm4"], [128, 512], BF16)
    wm4 = C.load_const("wm4_sb", G["wm4"], [128, 512], BF16)

    Qa = C.sb("Qa", [128, 32, 512], BF16)
    Kw = C.sb("Kw", [128, S], BF16)
    Ks = C.sb("Ks", [128, S], BF16)
    Kc = C.sb("Kc", [128, 256], BF16)
    C.memset("pool", Qa[64:128, :, :], 0.0, ["Qa_aug"])
    C.memset("pool", Kw[64:128, :], 0.0, ["Kw_aug"])
    C.memset("pool", Ks[64:128, :], 0.0, ["Ks_aug"])
    C.memset("pool", Kc[64:128, :], 0.0, ["Kc_aug"])
    P.dma("pool", Qa[64:68, :, :], G["qaug"].rearrange("p (a b) -> p a b", a=32), writes=["Qa_aug"])
    P.dma("pool", Kw[64:68, :], G["kaug"], writes=["Kw_aug"])
    P.dma("pool", Ks[64:68, :], G["kaug"], writes=["Ks_aug"])
    P.dma("pool", Kc[64:68, :], G["kaugc"], writes=["Kc_aug"])
    kc2 = C.sb("kc2", [128, S + 32], BF16)
    vc2 = C.sb("vc2", [128, S + 32], BF16)
    C.memset("pool", kc2[:, S:S + 32], 0.0, ["kc2_tail"])
    C.memset("pool", vc2[:, S:S + 32], 0.0, ["vc2_tail"])
    Vs = C.sb("Vs", [128, 32, 128], BF16)
    Vw = C.sb("Vw", [128, 32, 128], BF16)
    Vc = C.sb("Vc", [128, 2, 128], BF16)
    for (vt_, keys_) in ((Vs, ["Vs_%d" % c for c in range(8)]), (Vw, ["Vw_%d" % c for c in range(8)]), (Vc, ["Vc"])):
        C.memset("pool", vt_[:, :, 65:128], 0.0, keys_)
        C.memset("pool", vt_[:, :, 64:65], 1.0, keys_)
    Gtm = C.sb("Gtm", [128, 32, 12], F32)
    nTs = [C.sb("nT%d" % i, [128, 8, 512], BF16) for i in range(2)]

    PJ = [0, 1]
    SB_ = [2, 3]
    POC, POS, POW = 4, 5, 6

    for tc in range(8):
        t0 = tc * 512
        nb_ = C.rot("nT", 2)
        nT = nTs[nb_]
        nkeys = []
        for ti in range(4):
            hb = C.rot("ht", 2)
            P.dma("sp", C.ht[hb][:], hsrc(4 * tc + ti), reads=[hkey(4 * tc + ti)], writes=["ht%d" % hb])
            nk = "nT%d_%d" % (nb_, ti)
            C.norm_T(C.ht[hb][:], "ht%d" % hb, nT, nk, ti * 128, gA, "gA_sb")
            nkeys.append(nk)

        def proj(c0, m, rows=128):
            pj, pk = C.bank("pj", PJ)
            C.mmg(pj[0:rows, :], [(wA[:, k, c0:c0 + m], nT[:, k, :]) for k in range(8)], ["wA_sb"] + nkeys, [pk])
            return pj, pk
        for r in range(4):
            pj, pk = proj(r * 64, 64, 64)
            C.cp("act", Qa[0:64, 4 * tc:4 * tc + 4, r * 128:(r + 1) * 128],
                 pj[0:64, :].rearrange("p (a b) -> p a b", a=4), [pk], ["Qa_%d" % tc])
        for (c0, dst, dk) in ((256, kc2, "kc2"), (384, vc2, "vc2")):
            pj, pk = proj(c0, 128)
            C.cp("act", dst[0:64, t0:t0 + 512], pj[0:64, :], [pk], [dk])
            if tc == 0:
                C.cp("dve", dst[64:128, 0:511], pj[64:128, 1:512], [pk], [dk])
            else:
                C.cp("dve", dst[64:128, t0 - 1:t0 + 511], pj[64:128, :], [pk], [dk])
        pj, pk = proj(512, 64, 64)
        C.cp("act", Ks[0:64, t0:t0 + 512], pj[0:64, :], [pk], ["Ks_%d" % tc])
        pj, pk = proj(576, 64, 64)
        C.cp("act", Kw[0:64, t0:t0 + 512], pj[0:64, :], [pk], ["Kw_%d" % tc])
        for ti in range(4):
            pj, pk = C.bank("pj", PJ)
            C.mmg(pj[:, 0:128], [(nT[:, k, ti * 128:(ti + 1) * 128], wA[:, k, 640:768]) for k in range(8)],
                  ["wA_sb"] + nkeys, [pk])
            pg, pgk = C.bank("pj", PJ)
            C.mmg(pg[:, 0:12], [(nT[:, k, ti * 128:(ti + 1) * 128], wA[:, k, 768:780]) for k in range(8)],
                  ["wA_sb"] + nkeys, [pgk])
            C.cp("dve", Gtm[:, 4 * tc + ti, :], pg[:, 0:12], [pgk], ["Gtm_%d" % tc])
            C.cp("act", Vs[:, 4 * tc + ti, 0:64], pj[:, 0:64], [pk], ["Vs_%d" % tc])
            C.cp("dve", Vw[:, 4 * tc + ti, 0:64], pj[:, 64:128], [pk], ["Vw_%d" % tc])

    C.act(Gtm[:, :, :], Gtm[:, :, :], AF.Sigmoid, ["Gtm_%d" % c for c in range(8)], ["Gtm_%d" % c for c in range(8)])
    xs = C.sb("xs", [128, 256], F32)
    x2 = C.sb("x2", [128, 256], F32)
    hid = C.sb("hid", [128, 256], BF16)
    cbias = C.sb("cbias", [128, 1], F32)
    for (src, skey, w1, w1key, pos, poskey, isk) in ((kc2, "kc2", w1k, "w1k_sb", posk, "posk_sb", True),
                                                     (vc2, "vc2", w1v, "w1v_sb", posv, "posv_sb", False)):
        pj, pk = C.bank("pj", PJ)
        C.mmg(pj[:, 0:1], [(w1[:, j, :], pos[:, 0, j:j + 1]) for j in range(16)], [w1key, poskey], [pk])
        C.cp("act", cbias[:], pj[:, 0:1], [pk], ["cbias"])
        pj, pk = C.bank("pj", PJ)
        C.mmg(pj[:, 0:255], [(w1[:, j, :], src[:, 2 * j:2 * j + 16 * 255:16]) for j in range(16)],
              [w1key, skey, skey + "_tail"], [pk])
        C.memset("dve", xs[:, 255:256], 0.0, ["xs"])
        C.act(xs[:, 0:255], pj[:, 0:255], AF.Identity, [pk, "cbias"], ["xs"], bias=cbias[:])
        C.tt("dve", x2[:], xs[:], xs[:], ALU.mult, ["xs"], ["x2"])
        C.ts("dve", x2[:], x2[:], 0.044715, ALU.mult, ["x2"], ["x2"], s2=1.0, op1=ALU.add)
        C.tt("dve", x2[:], x2[:], xs[:], ALU.mult, ["x2", "xs"], ["x2"])
        C.act(x2[:], x2[:], AF.Sigmoid, ["x2"], ["x2"], scale=1.5957691216057308)
        C.tt("dve", hid[:], xs[:], x2[:], ALU.mult, ["x2", "xs"], ["hid"])
        if isk:
            pj, pk = C.bank("pj", PJ)
            C.mm(pj[0:64, 0:256], w2k[:, 0, :], hid[:], True, True, ["w2k_sb", "hid"], [pk])
            C.cp("act", Kc[0:64, :], pj[0:64, 0:256], [pk], ["Kc"])
        else:
            for nt in range(2):
                pj, pk = C.bank("pj", PJ)
                C.mm(pj[:, 0:64], hid[:, nt * 128:(nt + 1) * 128], w2v[:, 0, :], True, True, ["w2v_sb", "hid"], [pk])
                C.cp("act", Vc[:, nt, 0:64], pj[:, 0:64], [pk], ["Vc"])

    ptc = [[C.sb("ptc%d_%d" % (i, j), [128, 512], BF16) for j in range(2)] for i in range(2)]
    pts = [C.sb("pt%d" % i, [128, 512], BF16) for i in range(4)]
    imp = C.sb("imp", [128, 64], F32)
    wk = C.sb("wk", [128, 64], F32)
    m8 = C.sb("m8", [128, 16], F32)
    rci = C.sb("rci", [128, 4], F32)
    selb = C.sb("selb", [128, 64], BF16)
    selT4s = [C.sb("selT4_%d" % i, [128, 512], BF16) for i in range(2)]
    for i in range(2):
        C.memset("pool", selT4s[i][64:128, :], 0.0, ["selT4_%d" % i])
    rsrows = [C.sb("rsrow%d" % i, [65, 512], F32) for i in range(3)]
    bcss = [C.sb("bcs%d" % i, [64, 512], F32) for i in range(3)]
    tmpos = [C.sb("tmpo%d" % i, [64, 512], F32) for i in range(3)]
    rec4s = [C.sb("rec4_%d" % i, [128, 4], F32) for i in range(3)]
    wbs = [C.sb("wb%d" % i, [128, 4, 64], F32) for i in range(3)]
    oacc = [C.sb("oacc%d" % i, [64, 512], F32) for i in range(2)]
    oaccb = [C.sb("oaccb%d" % i, [64, 512], BF16) for i in range(2)]
    PJ = [0, 7]
    SB_ = [1, 2, 3]
    pending = []

    def flush():
        tl = list(pending)
        del pending[:]
        while any(tl):
            for t in tl:
                if t:
                    t.pop(0)()

    def finalize_steps(po, pok, br, qb, first, rec_src=None):
        ai = qb % 2
        acc, ak = oacc[ai], "oacc%d" % ai
        rsrow, bcs, tmpo, rec4, wb = rsrows[br], bcss[br], tmpos[br], rec4s[br], wbs[br]
        rk, bk, tk, r4k, wk_ = "rsrow%d" % br, "bcs%d" % br, "tmpo%d" % br, "rec4_%d" % br, "wb%d" % br

        def s1():
            if rec_src is None:
                C.cp("dve", rsrow[64:65, :], po[64:65, :], [pok], [rk])
                pj, pk = C.bank("pj", PJ)
                C.mmlist([(pj[:, r:r + 1], rsrow[64:65, r * 128:(r + 1) * 128], C.onesf[64:65, 0:1], True, True) for r in range(4)],
                         [rk, "onesf"], [pk])
                C.ts("dve", rec4[:, :], pj[:, 0:4], 1e-30, ALU.add, [pk], [r4k])

        def s2():
            if rec_src is None:
                C.recip(rec4[:, :], rec4[:, :], [r4k], [r4k])
                recap, reckey = rec4, r4k
            else:
                recap, reckey = rec_src
            C.tt("dve", wb[:, :, :], recap[:, 0:4].unsqueeze(2).to_broadcast([128, 4, 64]),
                 Gtm[:, qb, br:12:3].unsqueeze(2).to_broadcast([128, 4, 64]), ALU.mult, [reckey, "Gtm_%d" % (qb // 4)], [wk_])

        def s3():
            pj2, pk2 = C.bank("pj", PJ)
            C.mmlist([(pj2[0:64, r * 128:(r + 1) * 128], wb[:, r, :], C.identf[:, :], True, True) for r in range(4)],
                     [wk_, "identf"], [pk2])
            C.cp("act", bcs[:, :], pj2[0:64, :], [pk2], [bk])

        def s4():
            if first:
                C.tt("dve", acc[:, :], po[0:64, :], bcs[:, :], ALU.mult, [pok, bk], [ak])
            else:
                C.tt("dve", tmpo[:, :], po[0:64, :], bcs[:, :], ALU.mult, [pok, bk], [tk])
                C.tt("dve", acc[:, :], acc[:, :], tmpo[:, :], ALU.add, [tk, ak], [ak])
        return [s1, s2, s3, s4]

    def qinfo(qb):
        return Qa[:, qb, :], ["Qa_%d" % (qb // 4), "Qa_aug"]

    def cmp_stage(qb):
        q_rhs, qkeys = qinfo(qb)
        pc = ptc[qb % 2]
        selT4, stk = selT4s[qb % 2], "selT4_%d" % (qb % 2)
        ntn = 1 if qb < 16 else 2
        po = C.banks[POC]
        for nt in range(ntn):
            sbk, sk = C.bank("s", SB_)
            items = [(sbk[:, :], Kc[:, nt * 128:(nt + 1) * 128], q_rhs, True, False)]
            for r in range(4):
                items.append((sbk[:, r * 128:(r + 1) * 128], C.ident[:, :],
                              maskc[:, nt * S + qb * 128:nt * S + (qb + 1) * 128], False, r == 3))
            C.mmlist(items, qkeys + ["Kc", "Kc_aug", "ident", "maskc_sb"], [sk])
            C.act(pc[nt][:], sbk[:, :], AF.Exp, [sk], ["ptc%d_%d" % (qb % 2, nt)], scale=SC_NSA)
        for nt in range(ntn):
            C.mm(po[:, :], Vc[:, nt, :], pc[nt][:], nt == 0, nt == ntn - 1,
                 ["Vc", "ptc%d_%d" % (qb % 2, nt)], ["bank%d" % POC])
        pj, pk = C.bank("pj", PJ)
        items = []
        for r in range(4):
            for nt in range(ntn):
                items.append((pj[:, r * 65:(r + 1) * 65], pc[nt][:, r * 128:(r + 1) * 128], ovl[:, nt * 65:(nt + 1) * 65],
                              nt == 0, nt == ntn - 1))
        C.mmlist(items, ["ovl_sb"] + ["ptc%d_%d" % (qb % 2, nt) for nt in range(ntn)], [pk])
        for r in range(4):
            C.ts("dve", rci[:, r:r + 1], pj[:, r * 65 + 64:r * 65 + 65], 1e-30, ALU.add, [pk], ["rci"])
        C.recip(rci[:, 0:4], rci[:, 0:4], ["rci"], ["rci"])
        for r in range(4):
            prev = selbias[:, qb * 64:(qb + 1) * 64] if r == 0 else imp[:]
            C.stt("dve", imp[:], pj[:, r * 65:r * 65 + 64], rci[:, r:r + 1], prev, ALU.mult, ALU.add,
                  [pk, "rci", "imp", "selbias_sb"], ["imp"])
        P.op("dve", lambda e: e.max(out=m8[:, 0:8], in_=imp[:]), ["imp"], ["m8"])
        P.op("dve", lambda e: e.match_replace(out=wk[:], in_to_replace=m8[:, 0:8], in_values=imp[:], imm_value=-1e9),
             ["imp", "m8"], ["wk"])
        P.op("dve", lambda e: e.max(out=m8[:, 8:16], in_=wk[:]), ["wk"], ["m8"])
        C.ts("dve", wk[:], imp[:], m8[:, 15:16], ALU.is_ge, ["imp", "m8"], ["wk"])
        C.ts("dve", selb[:], wk[:], -NEG, ALU.mult, ["wk"], ["selb"], s2=NEG, op1=ALU.add)

        def s0(qb=qb, selT4=selT4, stk=stk):
            C.tr(C.pst[0:64, 0:128], selb[:, :], C.ident[:, :], ["selb", "ident"], ["pst"])
            for r in range(4):
                C.cp("act" if r % 2 == 0 else "dve", selT4[0:64, r * 128:(r + 1) * 128], C.pst[0:64, 0:128], ["pst"], [stk])
        C.cp("dve", rec4s[0][:, :], rci[:, 0:4], ["rci"], ["rec4_0"])
        pending.append([s0] + finalize_steps(po, "bank%d" % POC, 0, qb, True, rec_src=(rec4s[0], "rec4_0")))

    def sel_stage(qb):
        q_rhs, qkeys = qinfo(qb)
        selT4, stk = selT4s[qb % 2], "selT4_%d" % (qb % 2)
        po = C.banks[POS]

        def score(kt):
            pairs = [(Ks[:, kt * 128:(kt + 1) * 128], q_rhs), (eall[:, kt * 128:(kt + 1) * 128], selT4[:, :])]
            rd = qkeys + ["Ks_%d" % (kt // 4), "Ks_aug", "eall_sb", stk]
            if kt == qb:
                pairs.append((C.ident[:, :], dm4[:, :]))
                rd += ["ident", "dm4_sb"]
            return pairs, rd
        attn_loop(C, list(range(qb + 1)), score, SC_NSA, lambda kt: (Vs[:, kt, :], ["Vs_%d" % (kt // 4)]),
                  po, "bank%d" % POS, SB_, pts, "pt", after_first=None)

        def s5(qb=qb):
            ai = qb % 2
            C.cp("act", oaccb[ai][:, :], oacc[ai][:, :], ["oacc%d" % ai], ["oaccb%d" % ai])
            osink(qb, oaccb[ai], "oaccb%d" % ai)
        pending.append(finalize_steps(po, "bank%d" % POS, 1, qb, False) + [s5])

    def win_stage(qb):
        q_rhs, qkeys = qinfo(qb)
        po = C.banks[POW]
        k0 = max(0, qb - 4)

        def score(kt):
            pairs = [(Kw[:, kt * 128:(kt + 1) * 128], q_rhs)]
            rd = qkeys + ["Kw_%d" % (kt // 4), "Kw_aug"]
            if kt == qb:
                pairs.append((C.ident[:, :], dm4[:, :]))
                rd += ["ident", "dm4_sb"]
            if kt == qb - 4:
                pairs.append((C.ident[:, :], wm4[:, :]))
                rd += ["ident", "wm4_sb"]
            return pairs, rd
        attn_loop(C, list(range(k0, qb + 1)), score, SC_NSA, lambda kt: (Vw[:, kt, :], ["Vw_%d" % (kt // 4)]),
                  po, "bank%d" % POW, SB_, pts, "pt", after_first=flush)

        pending.append(finalize_steps(po, "bank%d" % POW, 2, qb, False))

    cmp_stage(0)
    for qb in range(32):
        win_stage(qb)
        if qb + 1 < 32:
            cmp_stage(qb + 1)
        sel_stage(qb)
    flush()


def phase_ffn(C, name, L, G, final, hown, hownkey, hhalo, hhalokey, oall, osink):
    P = C.P
    C.begin_phase(name)
    C.setup_norm()
    fl = C.flags
    g2 = C.load_const("g2_sb", L["g2"], [128, 8], F32)
    cw = C.load_const("cw_sb", L["cw"], [128, 176], F32)
    if final:
        gF = C.sb("gF_sb", [128, DM], F32)
        P.dma("pool", gF[:], G["gF"][0:1, :].partition_broadcast(128), writes=["gF_sb"])
    wup = C.load_w("wup_sb", L["wup"], 8, 5632)
    wdn = C.sb("wdn_sb", [128, 22, DM], BF16)

    def load_wdn():
        for k in range(22):
            P.dma("pool", wdn[:, k, :], L["wdn"][k * 128:(k + 1) * 128, :], writes=["wdn_sb"])
    wo_d = L["wo"]

    NC_ = 256
    aT = [C.sb("aT%d" % i, [128, NC_], BF16) for i in range(4)]
    hm = C.sb("hm", [128, 2, DM], F32)
    n2T = C.sb("n2T", [128, 8, NC_], BF16)
    oTb = C.sb("oTb", [128, 8, NC_], BF16)
    oa = [C.sb("oa%d" % i, [128, NC_], BF16) for i in range(2)]
    ob = [C.sb("ob%d" % i, [128, NC_], BF16) for i in range(2)]
    wob = [C.sb("wob%d" % i, [128, DM], BF16) for i in range(4)]
    ubuf = [C.sb("ubuf%d" % i, [128, NC_ + 2], F32) for i in range(3)]
    tb = [C.sb("tb%d" % i, [128, NC_], F32) for i in range(4)]
    sg = C.sb("sg", [128, NC_], F32)
    carry = C.sb("carry", [128, 44, 2], F32)
    res = C.sb("res", [128, DM], F32)
    ss2 = C.sb("ss2", [128, 1], F32)

    ACC = [0, 1, 2, 3]
    UP = [4, 5, 6]

    def chunk(ci, halo):
        nt_ = 1 if halo else 2
        ncol = nt_ * 128
        c0 = 1920 if halo else ci * 256
        for k in range(8):
            b = C.rot("oa", 2)
            if halo:
                P.dma("sp", oa[b][:, 0:ncol], oall[0][k * 128:(k + 1) * 128, c0:c0 + ncol], reads=["oall0"], writes=["oa%d" % b])
                C.ts("dve", oTb[:, k, 0:ncol], oa[b][:, 0:ncol], fl[:, 1:2], ALU.mult, ["oa%d" % b, "flags"], ["oTb"])
            else:
                P.dma("sp", oa[b][:, 0:ncol], oall[0][k * 128:(k + 1) * 128, c0:c0 + ncol], reads=["oall0"], writes=["oa%d" % b])
                P.dma("sp", ob[b][:, 0:ncol], oall[1][k * 128:(k + 1) * 128, c0:c0 + ncol], reads=["oall1"], writes=["ob%d" % b])
                C.ts("dve", oa[b][:, 0:ncol], oa[b][:, 0:ncol], fl[:, 0:1], ALU.mult, ["oa%d" % b, "flags"], ["oa%d" % b])
                C.stt("dve", oTb[:, k, 0:ncol], ob[b][:, 0:ncol], fl[:, 1:2], oa[b][:, 0:ncol], ALU.mult, ALU.add,
                      ["oa%d" % b, "ob%d" % b, "flags"], ["oTb"])
        for k in range(8):
            wb = C.rot("wob", 4)
            P.dma("pool", wob[wb][:, :], wo_d[k * 128:(k + 1) * 128, :], writes=["wob%d" % wb])
            items = []
            for ti in range(nt_):
                for hf in range(2):
                    items.append((C.banks[ACC[ti * 2 + hf]][:, :], oTb[:, k, ti * 128:(ti + 1) * 128],
                                  wob[wb][:, hf * 512:(hf + 1) * 512], k == 0, k == 7))
            C.mmlist(items, ["oTb", "wob%d" % wb], ["bank%d" % ACC[i] for i in range(nt_ * 2)])
        if ci == 0 and not halo:
            load_wdn()
        for ti in range(nt_):
            hb = C.rot("ht", 2)
            if halo:
                P.dma("sp", C.ht[hb][:], hhalo, reads=[hhalokey], writes=["ht%d" % hb])
                C.ts("dve", C.ht[hb][:], C.ht[hb][:], fl[:, 1:2], ALU.mult, ["ht%d" % hb, "flags"], ["ht%d" % hb])
            else:
                P.dma("sp", C.ht[hb][:], hown(2 * ci + ti), reads=[hownkey(2 * ci + ti)], writes=["ht%d" % hb])
            for hf in range(2):
                C.tt("dve", hm[:, ti, hf * 512:(hf + 1) * 512], C.banks[ACC[ti * 2 + hf]][:, :],
                     C.ht[hb][:, hf * 512:(hf + 1) * 512], ALU.add, ["bank%d" % ACC[ti * 2 + hf], "ht%d" % hb], ["hm%d" % ti])
            C.norm_T(hm[:, ti, :], "hm%d" % ti, n2T, "n2T_%d" % ti, ti * 128, g2, "g2_sb")
        nkeys = ["n2T_%d" % ti for ti in range(nt_)]
        dq = []

        def down(i, ai):
            items = []
            for ti in range(nt_):
                for hf in range(2):
                    items.append((C.banks[ACC[ti * 2 + hf]][:, :], aT[ai][:, ti * 128:(ti + 1) * 128],
                                  wdn[:, i, hf * 512:(hf + 1) * 512], i == 0, i == 21))
            C.mmlist(items, ["aT%d" % ai, "wdn_sb"], ["bank%d" % ACC[q] for q in range(nt_ * 2)])
        for i in range(22):
            tfin = []
            for part in range(2):
                fc = i + 22 * part
                up, upk = C.bank("up", UP)
                C.mmg(up[:, 0:ncol], [(wup[:, k, fc * 128:(fc + 1) * 128], n2T[:, k, 0:ncol]) for k in range(8)],
                      ["wup_sb"] + nkeys, [upk])
                ub = C.rot("ubuf", 3)
                u = ubuf[ub]
                uk = "ubuf%d" % ub
                C.cp("pool", u[:, 0:2], carry[:, fc, :], ["carry%d" % fc], [uk])
                C.cp("act", u[:, 2:2 + ncol], up[:, 0:ncol], [upk], [uk])
                if not halo:
                    ta = C.rot("tb", 4)
                    C.act(tb[ta][:, 0:ncol], up[:, 0:ncol], AF.Identity, [upk, "cw_sb"], ["tb%d" % ta],
                          bias=cw[:, fc * 4 + 3:fc * 4 + 4], scale=cw[:, fc * 4 + 2:fc * 4 + 3])
                    C.stt("dve", tb[ta][:, 0:ncol], u[:, 1:1 + ncol], cw[:, fc * 4 + 1:fc * 4 + 2], tb[ta][:, 0:ncol],
                          ALU.mult, ALU.add, [uk, "cw_sb", "tb%d" % ta], ["tb%d" % ta])
                    C.stt("dve", tb[ta][:, 0:ncol], u[:, 0:ncol], cw[:, fc * 4:fc * 4 + 1], tb[ta][:, 0:ncol],
                          ALU.mult, ALU.add, [uk, "cw_sb", "tb%d" % ta], ["tb%d" % ta])
                    tfin.append(ta)
                C.cp("pool", carry[:, fc, :], u[:, ncol:ncol + 2], [uk], ["carry%d" % fc])
            if not halo:
                C.act(sg[:, 0:ncol], tb[tfin[0]][:, 0:ncol], AF.Silu, ["tb%d" % tfin[0]], ["sg"])
                ai = C.rot("aT", 4)
                C.tt("dve", aT[ai][:, 0:ncol], sg[:, 0:ncol], tb[tfin[1]][:, 0:ncol], ALU.mult,
                     ["sg", "tb%d" % tfin[1]], ["aT%d" % ai])
                dq.append((i, ai))
                if len(dq) > 2:
                    down(*dq.pop(0))
        while dq:
            down(*dq.pop(0))
        if halo:
            return
        for ti in range(nt_):
            for hf in range(2):
                bk = ACC[ti * 2 + hf]
                C.tt("dve", res[:, hf * 512:(hf + 1) * 512], C.banks[bk][:, :], hm[:, ti, hf * 512:(hf + 1) * 512],
                     ALU.add, ["bank%d" % bk, "hm%d" % ti], ["res"])
            if final:
                C.memset("dve", ss2[:], 0.0, ["ss2"])
                C.act(C.junk[:], res[:], AF.Square, ["res", "ss2"], ["junk", "ss2"], accum=ss2[:])
                C.act(ss2[:], ss2[:], AF.Sqrt, ["ss2", "epsn"], ["ss2"], bias=C.epsn[:], scale=1.0 / DM)
                C.recip(ss2[:], ss2[:], ["ss2"], ["ss2"])
                C.stt("dve", res[:], res[:], ss2[:, 0:1], gF[:], ALU.mult, ALU.mult, ["res", "ss2", "gF_sb"], ["res"])
            osink(2 * ci + ti, res, "res")

    C.memset("pool", carry[:], 0.0, ["carry%d" % fc for fc in range(44)])
    chunk(0, True)
    for c in range(8):
        chunk(c, False)


LAYER_IN = [("wAm", [DM, 832]), ("gAm", [128, 8]), ("wq", [384, 768]), ("gq", [128, 3]), ("wkv", [256, 512]),
            ("gkv", [128, 2]), ("wAn", [DM, 780]), ("gAn", [128, 8]), ("w1k", [2048, 128]), ("w1v", [2048, 128]),
            ("w2k", [128, 64]), ("w2v", [128, 64]), ("posk", [128, 16]), ("posv", [128, 16]),
            ("wo", [DM, DM]), ("wup", [DM, 5632]), ("wdn", [2816, DM]), ("g2", [128, 8]), ("cw", [128, 176])]
GLOB_IN = [("ropeC", [32, S], F32), ("ropeS", [32, S], F32), ("dmask", [128, 2048], BF16),
           ("maskc", [128, 2 * S], BF16), ("eall", [64, S], BF16), ("selbias", [128, 2048], BF16),
           ("ovl", [128, 130], BF16), ("selg", [12, 768], F32), ("dm4", [128, 512], BF16), ("wm4", [128, 512], BF16),
           ("qaug", [4, 32 * 512], BF16), ("kaug", [4, S], BF16), ("kaugc", [4, 256], BF16), ("gF", [1, DM], F32)]
GROUPS = [[0, 1], [2, 3], [4, 5], [6, 7]]


def build_fused(nlayers=2):
    nc = bass.Bass("TRN2", target_bir_lowering=False)
    C = Ctx(nc)
    P = C.P
    x_d = C.dram("x", [S, DM], F32)
    xown_d = C.dram("xown", [2048, DM], F32)
    xhalo_d = C.dram("xhalo", [128, DM], F32)
    flags_d = C.dram("flags", [128, 2], F32)
    G = {n: C.dram(n, sh, dt) for (n, sh, dt) in GLOB_IN}
    Ls = [{n: C.dram("%s_%d" % (n, l), sh, F32) for (n, sh) in LAYER_IN} for l in range(nlayers)]
    out_d = C.dram("hout", [2048, DM], F32, out=True)
    P.dma("pool", C.flags[:], flags_d, writes=["flags"])

    omy = [[nc.dram_tensor("omy_%d_%d" % (l, c), [512, 2048], BF16) for c in range(2)] for l in range(nlayers)]
    oall = [[nc.dram_tensor("oall_%d_%d" % (l, c), [1024, 2048], BF16) for c in range(2)] for l in range(nlayers)]
    hmy = [nc.dram_tensor("hmy_%d" % j, [512, DM], F32) for j in range(4)]
    hall = [nc.dram_tensor("hall_%d" % j, [1024, DM], F32) for j in range(4)]

    for l in range(nlayers):
        if l == 0:
            hsrc = lambda g: x_d[g * 128:(g + 1) * 128, :]
            hkey = lambda g: "x"
        else:
            def hsrc(g):
                r, w = g // 16, g % 16
                return hall[w // 4][r * 512 + (w % 4) * 128:r * 512 + (w % 4) * 128 + 128, :]
            hkey = lambda g: "hall%d" % ((g % 16) // 4)

        def osink_mla(hh, tc, ot, otkey, l=l):
            c = tc // 4
            col = (tc % 4) * 512
            P.dma("sp", omy[l][c][hh * 64:(hh + 1) * 64, col:col + 512], ot[:, :], reads=[otkey], writes=["omy%d" % c])

        def osink_nsa(qb, ot, otkey, l=l):
            c = qb // 16
            col = (qb % 16) * 128
            P.dma("sp", omy[l][c][256:512, :].rearrange("(r d) t -> d r t", d=64)[:, :, col:col + 128],
                  ot[:, :].rearrange("p (r t) -> p r t", r=4), reads=[otkey], writes=["omy%d" % c])
            if qb % 16 == 15:
                P.cc("AllGather", GROUPS, omy[l][c].ap().opt(), oall[l][c].ap().opt(), reads=["omy%d" % c], writes=["oall%d" % c])

        phase_mla(C, "m%d_" % l, Ls[l], G, hsrc, hkey, osink_mla)
        phase_nsa(C, "n%d_" % l, Ls[l], G, hsrc, hkey, osink_nsa)
        final = (l == nlayers - 1)
        if l == 0:
            hown = lambda t: xown_d[t * 128:(t + 1) * 128, :]
            hownkey = lambda t: "xown"
            hhalo, hhalokey = xhalo_d, "xhalo"
        else:
            hown = lambda t: hmy[t // 4][(t % 4) * 128:(t % 4) * 128 + 128, :]
            hownkey = lambda t: "hmy%d" % (t // 4)
            hhalo, hhalokey = hall[3][384:512, :], "hall3"
        if final:
            def osink_ffn(t, res, rkey):
                P.dma("sp", out_d[t * 128:(t + 1) * 128, :], res[:], reads=[rkey])
        else:
            def osink_ffn(t, res, rkey):
                P.dma("sp", hmy[t // 4][(t % 4) * 128:(t % 4) * 128 + 128, :], res[:], reads=[rkey], writes=["hmy%d" % (t // 4)])
                if t % 4 == 3:
                    j = t // 4
                    P.cc("AllGather", GROUPS, hmy[j].ap().opt(), hall[j].ap().opt(), reads=["hmy%d" % j], writes=["hall%d" % j])
        phase_ffn(C, "f%d_" % l, Ls[l], G, final, hown, hownkey, hhalo, hhalokey, oall[l], osink_ffn)
    P.emit()
    return nc


def _pk(g, nk):
    return np.ascontiguousarray(np.asarray(g, np.float32).reshape(nk, 128).T)


def _consts():
    c = {}
    p = np.arange(128)[:, None]
    i512 = np.arange(512)[None, :]
    dm = np.zeros((128, 4, 512), np.float32)
    for m in range(4):
        dm[:, m, :] = np.where(128 * m + p <= i512, 0.0, NEG)
    c["dmask"] = dm.reshape(128, 2048).astype(NPBF)
    i128 = np.arange(128)[None, :]
    c["dm4"] = np.tile(np.where(p <= i128, 0.0, NEG), (1, 4)).astype(NPBF)
    c["wm4"] = np.tile(np.where(i128 < p, 0.0, NEG), (1, 4)).astype(NPBF)
    n = np.arange(256)[:, None]
    t = np.arange(S)[None, :]
    mc = np.where((t >= 16 * n + 31) & (n <= 254), 0.0, NEG).astype(np.float32)
    c["maskc"] = np.ascontiguousarray(mc.reshape(2, 128, S).transpose(1, 0, 2).reshape(128, 2 * S)).astype(NPBF)
    j = np.arange(64)[:, None]
    c["eall"] = (np.arange(S)[None, :] // 64 == j).astype(np.float32).astype(NPBF)
    tt_ = np.arange(S)
    cur = (tt_ // 64)[:, None]
    jj = np.arange(64)[None, :]
    sbias = np.zeros((S, 64), np.float32)
    sbias[np.broadcast_to(jj > cur, (S, 64))] = -1e4
    sbias[np.broadcast_to((jj == 0) | (jj == cur) | (jj == cur - 1), (S, 64))] = 1e4
    c["selbias"] = np.ascontiguousarray(sbias.reshape(32, 128, 64).transpose(1, 0, 2).reshape(128, 2048)).astype(NPBF)
    cs = (np.arange(256) * 16)[:, None]
    ss_ = (np.arange(64) * 64)[None, :]
    ov = ((cs < ss_ + 64) & (cs + 32 > ss_)).astype(np.float32)
    ov[255] = 0.0
    ov1 = np.concatenate([ov, np.ones((256, 1), np.float32)], 1)
    c["ovl"] = np.ascontiguousarray(ov1.reshape(2, 128, 65).transpose(1, 0, 2).reshape(128, 130)).astype(NPBF)
    sg = np.zeros((12, 12, 64), np.float32)
    for g in range(12):
        sg[g, g, :] = 1.0
    c["selg"] = sg.reshape(12, 768)
    k = np.arange(S)
    c["kaug"] = np.stack([np.ones(S), np.ones(S), k // 64, k % 64]).astype(np.float32).astype(NPBF)
    e = np.arange(256) * 16 + 31
    c["kaugc"] = np.stack([np.ones(256), np.ones(256), e // 64, e % 64]).astype(np.float32).astype(NPBF)
    inv = 1.0 / (10000.0 ** (np.arange(0, 32, 2, dtype=np.float32) / 32))
    ang = np.arange(S, dtype=np.float32)[:, None] * inv[None, :]
    cos, sin = np.cos(ang).T.astype(np.float32), np.sin(ang).T.astype(np.float32)
    c["ropeC"] = np.ascontiguousarray(np.concatenate([cos, cos], 0))
    c["ropeS"] = np.ascontiguousarray(np.concatenate([-sin, sin], 0))
    return c


def _qaug(group):
    slopes = np.exp2(-8.0 * np.arange(1, 9, dtype=np.float32) / 8)
    t = np.arange(S).reshape(32, 1, 128)
    out = np.zeros((4, 32, 4, 128), np.float32)
    for r in range(4):
        a = slopes[group * 4 + r] / SC_NSA
        out[0, :, r, :] = (-a * 64 * (t // 64))[:, 0, :]
        out[1, :, r, :] = (-a * (t % 64))[:, 0, :]
        out[2, :, r, :] = a * 64
        out[3, :, r, :] = a
    return out.reshape(4, 32 * 512).astype(NPBF)


_PROG = {}


def _prog(nlayers=2):
    if nlayers not in _PROG:
        _PROG[nlayers] = build_fused(nlayers)
    return _PROG[nlayers]


def _layer_maps(l, I, c):
    m = {}
    w_in, w_uq, w_ukv = I["w_in"][l], I["w_uq"][l], I["w_ukv"][l]
    sw = list(range(656, 672)) + list(range(640, 656))
    colsA = list(range(0, 640)) + list(range(0, 64)) + list(range(640, 672)) + list(range(0, 64)) + sw
    m["wAm"] = np.ascontiguousarray(w_in[:, colsA])
    m["gAm"] = _pk(I["attn_norm"][l], 8)
    qc, kc, vc = [], [], []
    for hh in range(4 * c, 4 * c + 4):
        base = 96 * hh
        nope = list(range(base, base + 64))
        rope = list(range(base + 64, base + 96))
        qc += nope + rope + nope + rope[16:] + rope[:16]
        kc += list(range(128 * hh, 128 * hh + 64))
        vc += list(range(128 * hh + 64, 128 * hh + 128))
    m["wq"] = np.ascontiguousarray(w_uq[:, qc])
    m["gq"] = _pk(I["q_norm"][l], 3)
    m["wkv"] = np.ascontiguousarray(w_ukv[:, kc + vc])
    m["gkv"] = _pk(I["kv_norm"][l], 2)
    g = c
    q0 = 672 + 256 * g
    o = 1184
    rng = lambda a: list(range(a, a + 64))
    kcc, vcc, ksc = rng(o + 64 * g), rng(o + 128 + 64 * g), rng(o + 256 + 64 * g)
    vsc, kwc, vwc = rng(o + 384 + 64 * g), rng(o + 512 + 64 * g), rng(o + 640 + 64 * g)
    gtc = list(range(1952 + 12 * g, 1952 + 12 * g + 12))
    cols = list(range(q0, q0 + 256)) + kcc + kcc + vcc + vcc + ksc + kwc + vsc + vwc + gtc
    m["wAn"] = np.ascontiguousarray(w_in[:, cols])
    m["gAn"] = m["gAm"]
    posT = lambda pz: np.ascontiguousarray(np.asarray(pz, np.float32).reshape(16, 128).T)
    m["w1k"] = np.ascontiguousarray(I["cmp_k_w1"][l].reshape(2048, 128))
    m["w1v"] = np.ascontiguousarray(I["cmp_v_w1"][l].reshape(2048, 128))
    m["w2k"] = np.ascontiguousarray(I["cmp_k_w2"][l])
    m["w2v"] = np.ascontiguousarray(I["cmp_v_w2"][l])
    m["posk"] = posT(I["cmp_pos_k"][l])
    m["posv"] = posT(I["cmp_pos_v"][l])
    perm = list(range(0, 256)) + list(range(512, 768)) + list(range(256, 512)) + list(range(768, 1024))
    m["wo"] = np.ascontiguousarray(I["w_o"][l][perm, :])
    m["wup"] = np.ascontiguousarray(I["w_up"][l])
    m["wdn"] = np.ascontiguousarray(I["w_down"][l])
    m["g2"] = _pk(I["ffn_norm"][l], 8)
    cwv = np.stack([I["conv_w"][l][0], I["conv_w"][l][1], I["conv_w"][l][2], I["conv_b"][l]], -1)
    m["cw"] = np.ascontiguousarray(cwv.reshape(44, 128, 4).transpose(1, 0, 2).reshape(128, 176)).astype(np.float32)
    return m


def make_maps(I, nlayers=2):
    cst = _consts()
    lm = [[_layer_maps(l, I, c) for c in range(2)] for l in range(nlayers)]
    maps = []
    for b in range(4):
        for c in range(2):
            m = {"x": np.ascontiguousarray(I["x"][b]),
                 "xown": np.ascontiguousarray(I["x"][b][2048 * c:2048 * c + 2048]),
                 "xhalo": np.ascontiguousarray(I["x"][b][1920:2048]),
                 "flags": np.ascontiguousarray(np.tile(np.array([[1.0 - c, float(c)]], np.float32), (128, 1)))}
            for (n, sh, dt) in GLOB_IN:
                if n == "qaug":
                    m[n] = _qaug(c)
                elif n == "gF":
                    m[n] = np.ascontiguousarray(np.asarray(I["final_norm"], np.float32).reshape(1, DM))
                else:
                    m[n] = cst[n]
            for l in range(nlayers):
                for k, v in lm[l][c].items():
                    m["%s_%d" % (k, l)] = v
            maps.append(m)
    return maps


def kernel(**inputs):
    I = {k: np.asarray(v, dtype=np.float32) for k, v in inputs.items()}
    res = run_bass_kernel_spmd(_prog(2), make_maps(I, 2), core_ids=list(range(8))).results
    out = np.empty((4, S, DM), np.float32)
    for b in range(4):
        for c in range(2):
            out[b, 2048 * c:2048 * c + 2048] = np.asarray(res[2 * b + c]["hout"])
    return out
```

```python
import numpy as np
import ml_dtypes
import concourse.bass as bass
import concourse.mybir as mybir
from concourse.bass_utils import run_bass_kernel_spmd

F32 = mybir.dt.float32
BF16 = mybir.dt.bfloat16
ALU = mybir.AluOpType
AF = mybir.ActivationFunctionType
AX = mybir.AxisListType
NPBF = ml_dtypes.bfloat16

ENGS = ["pe", "act", "dve", "pool", "sp"]
DMA_POOL = 12
S = 4096
DM = 1024
NEG = -30000.0
SC_MLA = 96 ** -0.5
SC_NSA = 0.125
EPS = 1e-6


class Prog:
    def __init__(self, nc):
        self.nc = nc
        self.ops = {e: [] for e in ENGS}
        self.lastw = {}
        self.readers = {}
        self.dma_n = {e: 0 for e in ENGS + ["cc"]}
        self.dma_sem_cnt = {}
        self.last_c = {}
        self.last_d = {}

    def sb(self, name, shape, dt):
        return self.nc.alloc_sbuf_tensor(name, list(shape), dt)

    def ps(self, name, shape, dt=F32):
        return self.nc.alloc_psum_tensor(name, list(shape), dt)

    def _add(self, eng, fn, reads, writes, dma, cc=False):
        op = dict(eng=eng, fn=fn, deps=[], dma=dma, marked=False, inc=(1 if cc else 16))
        deps = []
        for k in reads:
            w = self.lastw.get(k)
            if w is not None:
                deps.append(w)
        for k in writes:
            w = self.lastw.get(k)
            if w is not None:
                deps.append(w)
            deps.extend(self.readers.get(k, ()))
        seen = set()
        for d in deps:
            if id(d) in seen or d is op:
                continue
            seen.add(id(d))
            if (not d["dma"]) and d["eng"] == eng and eng in ("pe", "sp"):
                continue
            op["deps"].append(d)
            d["marked"] = True
        if dma:
            qn = "cc" if cc else eng
            q = self.dma_n[qn]
            self.dma_n[qn] += 1
            semkey = (qn, q % (4 if cc else DMA_POOL))
            m = self.dma_sem_cnt.get(semkey, 0) + 1
            self.dma_sem_cnt[semkey] = m
            op["dsem"] = semkey
            op["dval"] = op["inc"] * m
            op["marked"] = True
            self.last_d[semkey] = op
        else:
            self.last_c[eng] = op
        for k in reads:
            self.readers.setdefault(k, []).append(op)
        for k in writes:
            self.lastw[k] = op
            self.readers[k] = []
        self.ops[eng].append(op)
        return op

    def op(self, eng, fn, reads=(), writes=()):
        return self._add(eng, fn, list(reads), list(writes), False)

    def dma(self, eng, out, in_, reads=(), writes=()):
        return self._add(eng, lambda e: e.dma_start(out=out, in_=in_), list(reads), list(writes), True)

    def cc(self, kind, groups, in_ap, out_ap, reads=(), writes=()):
        return self._add("pool", lambda e: e.collective_compute(kind, ALU.bypass, replica_groups=groups,
                                                                ins=[in_ap], outs=[out_ap]),
                         list(reads), list(writes), True, cc=True)

    def barrier(self):
        deps = list(self.last_c.values()) + list(self.last_d.values())
        for d in deps:
            d["marked"] = True
        for e in ENGS:
            self.ops[e].append(dict(eng=e, fn=None, deps=list(deps), dma=False, marked=False, inc=0))
        self.lastw = {}
        self.readers = {}

    def emit(self):
        nc = self.nc
        csem = {e: nc.alloc_semaphore("c_" + e) for e in ENGS}
        dsem = {}
        for (qn, i) in self.dma_sem_cnt:
            dsem[(qn, i)] = nc.alloc_semaphore("d_%s_%d" % (qn, i))
        for e in ENGS:
            c = 0
            for o in self.ops[e]:
                if o["dma"] or o["fn"] is None:
                    continue
                if o["marked"]:
                    c += 1
                    o["cval"] = c
        all_dma = [o for e in ENGS for o in self.ops[e] if o["dma"]]

        def run(e, eng):
            seen = {}

            def wait(sem_key, sem, val):
                if seen.get(sem_key, 0) >= val:
                    return
                seen[sem_key] = val
                eng.wait_ge(sem, val)

            for o in self.ops[e]:
                for d in o["deps"]:
                    if d["dma"]:
                        wait(d["dsem"], dsem[d["dsem"]], d["dval"])
                    else:
                        wait(("c", d["eng"]), csem[d["eng"]], d["cval"])
                if o["fn"] is None:
                    continue
                if o["dma"]:
                    if o["dval"] > o["inc"]:
                        wait(o["dsem"], dsem[o["dsem"]], o["dval"] - o["inc"])
                    o["fn"](eng).then_inc(dsem[o["dsem"]], o["inc"])
                else:
                    ins = o["fn"](eng)
                    if o["marked"]:
                        ins.then_inc(csem[e], 1)
            if e == "sp":
                last = {}
                for o in all_dma:
                    last[o["dsem"]] = max(last.get(o["dsem"], 0), o["dval"])
                for k, v in last.items():
                    eng.wait_ge(dsem[k], v)

        with nc.Block() as block:
            @block.tensor
            def _(eng):
                run("pe", eng)

            @block.scalar
            def _(eng):
                run("act", eng)

            @block.vector
            def _(eng):
                run("dve", eng)

            @block.gpsimd
            def _(eng):
                run("pool", eng)

            @block.sync
            def _(eng):
                run("sp", eng)


def _nbytes(shape, dt):
    n = 1
    for d in shape[1:]:
        n *= d
    return n * (4 if dt == F32 else 2)


class Ctx:
    def __init__(self, nc):
        self.nc = nc
        self.P = P = Prog(nc)
        self.rots = {}
        self.pname = "g_"
        self.ident = nc.alloc_sbuf_tensor("ident", [128, 128], BF16)
        self.identf = nc.alloc_sbuf_tensor("identf", [128, 128], F32)
        self.onesf = nc.alloc_sbuf_tensor("onesf", [128, 128], F32)
        self.epsn = nc.alloc_sbuf_tensor("epsn", [128, 1], F32)
        self.flags = nc.alloc_sbuf_tensor("flags_sb", [128, 2], F32)
        identf, ident, onesf, epsn = self.identf, self.ident, self.onesf, self.epsn
        P.op("pool", lambda e: e.memset(identf[:], 0.0), writes=["identf"])
        P.op("pool", lambda e: e.affine_select(out=identf[:], in_=identf[:], pattern=[[-1, 128]],
                                                compare_op=ALU.not_equal, fill=1.0, base=0, channel_multiplier=1),
             reads=["identf"], writes=["identf"])
        P.op("dve", lambda e: e.tensor_copy(out=ident[:], in_=identf[:]), reads=["identf"], writes=["ident"])
        P.op("dve", lambda e: e.memset(onesf[:], 1.0), writes=["onesf"])
        P.op("dve", lambda e: e.memset(epsn[:], EPS), writes=["epsn"])
        self.pst = P.ps("pst", [128, 1024], BF16)
        self.banks = [P.ps("bank%d" % i, [128, 512], F32) for i in range(7)]
        self.banks.append(self.pst.bitcast(F32))
        self.base = ((int(nc.sbuf_base) + 63) // 64) * 64
        self.top = int(nc.sbuf_top)
        self.off = self.base
        self.top_off = self.top

    def begin_phase(self, name, keep_top=False):
        self.P.barrier()
        self.pname = name
        self.off = self.base
        if not keep_top:
            self.top_off = self.top

    def sb_top(self, name, shape, dt):
        nb = ((_nbytes(shape, dt) + 31) // 32) * 32
        self.top_off -= nb
        assert self.top_off >= self.off, ("SBUF overflow (top)", name)
        return self.nc.alloc_sbuf_tensor_at(name, list(shape), dt, offset=self.top_off)

    def sb(self, name, shape, dt):
        nb = ((_nbytes(shape, dt) + 31) // 32) * 32
        assert self.off + nb <= self.top_off, ("SBUF overflow", self.pname, name, self.off + nb - self.top_off)
        t = self.nc.alloc_sbuf_tensor_at(self.pname + name, list(shape), dt, offset=self.off)
        self.off += nb
        return t

    def rot(self, name, n):
        i = self.rots.get(name, 0) % n
        self.rots[name] = (i + 1) % n
        return i

    def dram(self, name, shape, dt, out=False):
        return self.nc.dram_tensor(name, list(shape), dt, kind="ExternalOutput" if out else "ExternalInput").ap()

    def mm(self, out, lhsT, rhs, start, stop, reads, writes):
        return self.P.op("pe", lambda e: e.matmul(out, lhsT=lhsT, rhs=rhs, start=start, stop=stop), reads, writes)

    def mmg(self, out, pairs, reads, writes, start=True, stop=True):
        n = len(pairs)

        def fn(e):
            ins = None
            for i, (l, r) in enumerate(pairs):
                ins = e.matmul(out, lhsT=l, rhs=r, start=(start and i == 0), stop=(stop and i == n - 1))
            return ins
        return self.P.op("pe", fn, reads, writes)

    def mmlist(self, items, reads, writes):
        def fn(e):
            ins = None
            for (o, l, r, st, sp) in items:
                ins = e.matmul(o, lhsT=l, rhs=r, start=st, stop=sp)
            return ins
        return self.P.op("pe", fn, reads, writes)

    def tr(self, out, in_, ident, reads, writes):
        return self.P.op("pe", lambda e: e.transpose(out, in_, ident), reads, writes)

    def act(self, out, in_, func, reads, writes, bias=None, scale=1.0, accum=None):
        kw = {}
        if bias is not None:
            kw["bias"] = bias
        if accum is not None:
            kw["accum_out"] = accum
        return self.P.op("act", lambda e: e.activation(out=out, in_=in_, func=func, scale=scale, **kw), reads, writes)

    def tt(self, eng, out, in0, in1, op, reads, writes):
        return self.P.op(eng, lambda e: e.tensor_tensor(out=out, in0=in0, in1=in1, op=op), reads, writes)

    def ts(self, eng, out, in0, s1, op0, reads, writes, s2=None, op1=None):
        if op1 is None:
            return self.P.op(eng, lambda e: e.tensor_scalar(out=out, in0=in0, scalar1=s1, scalar2=None, op0=op0), reads, writes)
        return self.P.op(eng, lambda e: e.tensor_scalar(out=out, in0=in0, scalar1=s1, scalar2=s2, op0=op0, op1=op1), reads, writes)

    def stt(self, eng, out, in0, scalar, in1, op0, op1, reads, writes):
        return self.P.op(eng, lambda e: e.scalar_tensor_tensor(out=out, in0=in0, scalar=scalar, in1=in1, op0=op0, op1=op1), reads, writes)

    def cp(self, eng, out, in_, reads, writes):
        if eng == "act":
            return self.P.op("act", lambda e: e.copy(out=out, in_=in_), reads, writes)
        return self.P.op(eng, lambda e: e.tensor_copy(out=out, in_=in_), reads, writes)

    def recip(self, out, in_, reads, writes):
        return self.P.op("dve", lambda e: e.reciprocal(out=out, in_=in_), reads, writes)

    def memset(self, eng, ap, val, writes):
        return self.P.op(eng, lambda e: e.memset(ap, val), [], writes)

    def bank(self, grp, idxs):
        i = idxs[self.rot(grp, len(idxs))]
        return self.banks[i], ("pst" if i == 7 else "bank%d" % i)

    def load_w(self, name, w_dram, nk, ncols):
        wsb = self.sb(name, [128, nk, ncols], BF16)
        for k in range(nk):
            self.P.dma("pool", wsb[:, k, :], w_dram[k * 128:(k + 1) * 128, :], writes=[name])
        return wsb

    def load_const(self, name, dram_ap, shape, dt, eng="pool"):
        t = self.sb(name, shape, dt)
        self.P.dma(eng, t[:], dram_ap, writes=[name])
        return t

    def setup_norm(self):
        self.ht = [self.sb("ht%d" % i, [128, DM], F32) for i in range(2)]
        self.junk = self.sb("junk", [128, DM], BF16)
        self.nb = self.sb("nb", [128, DM], BF16)
        self.ss = [self.sb("ss%d" % i, [128, 1], F32) for i in range(2)]

    def norm_T(self, src, srckey, nT, nkey, col0, g_sb, gkey):
        b = self.rot("ss", 2)
        ss = self.ss[b]
        sk = "ss%d" % b
        self.memset("dve", ss[:], 0.0, [sk])
        self.act(self.junk[:], src, AF.Square, [srckey, sk], ["junk", sk], accum=ss[:])
        self.act(ss[:], ss[:], AF.Sqrt, [sk, "epsn"], [sk], bias=self.epsn[:], scale=1.0 / DM)
        self.recip(ss[:], ss[:], [sk], [sk])
        self.ts("dve", self.nb[:], src, ss[:, 0:1], ALU.mult, [srckey, sk], ["nb"])
        pst = self.pst
        nb, ident = self.nb, self.ident

        def fn(e):
            ins = None
            for k in range(8):
                ins = e.transpose(pst[:, k * 128:(k + 1) * 128], nb[:, k * 128:(k + 1) * 128], ident[:])
            return ins
        self.P.op("pe", fn, ["nb", "ident"], ["pst"])
        self.tt("dve", nT[:, :, col0:col0 + 128], pst[:, :].rearrange("p (k t) -> p k t", k=8),
                g_sb[:, :].unsqueeze(2).to_broadcast([128, 8, 128]), ALU.mult, ["pst", gkey], [nkey])


def attn_loop(C, tiles, score_fn, scale, v_fn, po, pok, sbanks, pts, ptname, after_first=None, depth=2, hooks=None):
    n = len(tiles)
    issued = []

    def issue(i):
        sbk, sk = C.bank("s", sbanks)
        pairs, rd = score_fn(tiles[i])
        C.mmg(sbk[:, :], pairs, rd, [sk])
        issued.append((sbk, sk))
    for i in range(min(depth, n)):
        issue(i)
    if after_first is not None:
        after_first()
    for i in range(n):
        sbk, sk = issued[i]
        pi = C.rot(ptname, len(pts))
        C.act(pts[pi][:], sbk[:, :], AF.Exp, [sk], ["%s%d" % (ptname, pi)], scale=scale)
        if i + depth < n:
            issue(i + depth)
        lhsT, rd = v_fn(tiles[i])
        C.mm(po[:, :], lhsT, pts[pi][:], i == 0, i == n - 1, rd + ["%s%d" % (ptname, pi)], [pok])
        if hooks and i in hooks:
            hooks[i]()


def phase_mla(C, name, L, G, hsrc, hkey, osink, after_weights=None):
    P = C.P
    C.begin_phase(name)
    C.setup_norm()
    gA = C.load_const("gA_sb", L["gAm"], [128, 8], F32)
    gq = C.load_const("gq_sb", L["gq"], [128, 3], F32)
    gkv = C.load_const("gkv_sb", L["gkv"], [128, 2], F32)
    dmask = C.load_const("dmask_sb", G["dmask"], [128, 2048], BF16)
    wA = C.load_w("wA_sb", L["wAm"], 8, 832)
    wq = C.load_w("wq_sb", L["wq"], 3, 768)
    wkv = C.load_w("wkv_sb", L["wkv"], 2, 512)
    if after_weights is not None:
        after_weights()
    ropeC_d, ropeS_d = G["ropeC"], G["ropeS"]

    Kh = C.sb("Kh", [128, 4, S], BF16)
    C.memset("pool", Kh[96:128, :, :], 0.0, ["Kh_pad"])
    Vt = C.sb("Vt", [128, 32, 4, 128], BF16)
    C.memset("pool", Vt[:, :, :, 64:65], 1.0, ["Vt_%d" % c for c in range(8)])
    C.memset("pool", Vt[:, :, :, 65:128], 0.0, ["Vt_pad"])
    nTs = [C.sb("nT%d" % i, [128, 8, 512], BF16) for i in range(2)]
    Qhs = [C.sb("Qh%d" % i, [128, 4, 512], BF16) for i in range(2)]
    for i in range(2):
        C.memset("pool", Qhs[i][96:128, :, :], 0.0, ["Qh_pad"])
    zf = C.sb("zf", [128, 3, 512], F32)
    sq = C.sb("sq", [128, 3, 512], F32)
    rr = C.sb("rr", [128, 512], F32)
    cqn = C.sb("cqn", [128, 3, 512], BF16)
    ckvn = C.sb("ckvn", [128, 2, 512], BF16)
    Ct = C.sb("Ct", [96, 512], F32)
    St = C.sb("St", [96, 512], F32)
    t1 = C.sb("t1", [96, 512], F32)
    t2 = C.sb("t2", [96, 512], F32)
    pts = [C.sb("pt%d" % i, [128, 512], BF16) for i in range(4)]
    rsrow = C.sb("rsrow", [65, 512], F32)
    rec4 = C.sb("rec4", [128, 4], F32)
    wb = C.sb("wb", [128, 4, 64], F32)
    bcs = C.sb("bcs", [64, 512], F32)
    ots = [C.sb("ot%d" % i, [64, 512], BF16) for i in range(2)]

    PJ = [0, 1]
    SB_ = [2, 3, 4]
    PO = [5, 6]
    pendA = []
    pendB = []

    def flushA():
        while pendA:
            pendA.pop(0)()

    def flushB():
        flushA()
        while pendB:
            pendB.pop(0)()

    def flush():
        flushB()

    def latent(c0, nm, dim, dst, dkey, nT, nkeys, gl, glkey):
        for m in range(nm):
            pj, pk = C.bank("pj", PJ)
            C.mmg(pj[:, :], [(wA[:, k, c0 + m * 128:c0 + (m + 1) * 128], nT[:, k, :]) for k in range(8)],
                  ["wA_sb"] + nkeys, [pk])
            C.act(zf[:, m, :], pj[:, :], AF.Copy, [pk], ["zf%d" % m])
            C.act(sq[:, m, :], pj[:, :], AF.Square, [pk], ["sq%d" % m])
        pj, pk = C.bank("pj", PJ)
        C.mmg(pj[:, :], [(C.onesf[:, :], sq[:, m, :]) for m in range(nm)], ["onesf"] + ["sq%d" % m for m in range(nm)], [pk])
        C.act(rr[:], pj[:, :], AF.Sqrt, [pk, "epsn"], ["rr"], bias=C.epsn[:], scale=1.0 / dim)
        C.recip(rr[:], rr[:], ["rr"], ["rr"])
        for m in range(nm):
            C.stt("dve", dst[:, m, :], zf[:, m, :], gl[:, m:m + 1], rr[:], ALU.mult, ALU.mult, ["zf%d" % m, "rr", glkey], [dkey])

    def stage_T(tc):
        nT = nTs[tc % 2]
        nkeys = []
        for ti in range(4):
            hb = C.rot("ht", 2)
            P.dma("sp", C.ht[hb][:], hsrc(4 * tc + ti), reads=[hkey(4 * tc + ti)], writes=["ht%d" % hb])
            nk = "nT%d_%d" % (tc % 2, ti)
            C.norm_T(C.ht[hb][:], "ht%d" % hb, nT, nk, ti * 128, gA, "gA_sb")
            nkeys.append(nk)
        return nT, nkeys

    def stage_P1(tc, nT, nkeys):
        t0 = tc * 512
        latent(0, 3, 384.0, cqn, "cqn", nT, nkeys, gq, "gq_sb")
        latent(384, 2, 256.0, ckvn, "ckvn", nT, nkeys, gkv, "gkv_sb")
        P.dma("sp", Ct[64:96, :], ropeC_d[:, t0:t0 + 512], writes=["Ct"])
        P.dma("sp", St[64:96, :], ropeS_d[:, t0:t0 + 512], writes=["St"])
        pA, pAk = C.bank("pj", PJ)
        C.mmg(pA[0:96, :], [(wA[:, k, 640:736], nT[:, k, :]) for k in range(8)], ["wA_sb"] + nkeys, [pAk])
        C.tt("dve", t1[64:96, :], pA[64:96, :], Ct[64:96, :], ALU.mult, [pAk, "Ct"], ["t1"])
        pB, pBk = C.bank("pj", PJ)
        C.mmg(pB[0:96, :], [(wA[:, k, 736:832], nT[:, k, :]) for k in range(8)], ["wA_sb"] + nkeys, [pBk])
        C.tt("dve", t2[64:96, :], pB[64:96, :], St[64:96, :], ALU.mult, [pBk, "St"], ["t2"])
        for hh in range(4):
            C.tt("pool", Kh[64:96, hh, t0:t0 + 512], t1[64:96, :], t2[64:96, :], ALU.add, ["t1", "t2"], ["Kh_%d" % tc])

    def stage_P2(tc):
        t0 = tc * 512
        Qh = Qhs[tc % 2]
        qk = "Qh%d" % (tc % 2)
        for hh in range(4):
            pA, pAk = C.bank("pj", PJ)
            C.mmg(pA[0:96, :], [(wq[:, m, hh * 192:hh * 192 + 96], cqn[:, m, :]) for m in range(3)], ["wq_sb", "cqn"], [pAk])
            C.cp("act", Qh[0:64, hh, :], pA[0:64, :], [pAk], [qk])
            C.tt("dve", t1[64:96, :], pA[64:96, :], Ct[64:96, :], ALU.mult, [pAk, "Ct"], ["t1"])
            pB, pBk = C.bank("pj", PJ)
            C.mmg(pB[0:96, :], [(wq[:, m, hh * 192 + 96:hh * 192 + 192], cqn[:, m, :]) for m in range(3)], ["wq_sb", "cqn"], [pBk])
            C.tt("dve", t2[64:96, :], pB[64:96, :], St[64:96, :], ALU.mult, [pBk, "St"], ["t2"])
            C.tt("pool", Qh[64:96, hh, :], t1[64:96, :], t2[64:96, :], ALU.add, ["t1", "t2"], [qk])
        for hh in range(4):
            pj, pk = C.bank("pj", PJ)
            C.mmg(pj[0:64, :], [(wkv[:, j, hh * 64:(hh + 1) * 64], ckvn[:, j, :]) for j in range(2)], ["wkv_sb", "ckvn"], [pk])
            C.cp("act", Kh[0:64, hh, t0:t0 + 512], pj[0:64, :], [pk], ["Kh_%d" % tc])
        for ti in range(4):
            pj, pk = C.bank("pj", PJ)
            C.mmg(pj[:, 0:256], [(ckvn[:, j, ti * 128:(ti + 1) * 128], wkv[:, j, 256:512]) for j in range(2)], ["wkv_sb", "ckvn"], [pk])
            C.cp("act", Vt[:, 4 * tc + ti, :, 0:64], pj[:, 0:256].rearrange("p (h d) -> p h d", h=4), [pk], ["Vt_%d" % tc])

    def head(tc, hh):
        Qh = Qhs[tc % 2]
        qk = "Qh%d" % (tc % 2)
        po, pok = C.bank("po", PO)
        nkt = 4 * tc + 4

        def score(j):
            pairs = [(Kh[:, hh, j * 128:(j + 1) * 128], Qh[:, hh, :])]
            rd = [qk, "Kh_%d" % (j // 4), "Kh_pad", "Qh_pad"]
            if j >= 4 * tc:
                m = j - 4 * tc
                pairs.append((C.ident[:, :], dmask[:, m * 512:(m + 1) * 512]))
                rd += ["ident", "dmask_sb"]
            return pairs, rd

        def vfn(j):
            return Vt[:, j, hh, :], ["Vt_%d" % (j // 4), "Vt_pad"]
        attn_loop(C, list(range(nkt)), score, SC_MLA, vfn, po, pok, SB_, pts, "pt", after_first=flushA, hooks={2: flushB})

        def finA():
            C.cp("dve", rsrow[64:65, :], po[64:65, :], [pok], ["rsrow"])
            pj, pk = C.bank("pj", PJ)
            C.mmlist([(pj[:, r:r + 1], rsrow[64:65, r * 128:(r + 1) * 128], C.onesf[64:65, 0:1], True, True) for r in range(4)],
                     ["rsrow", "onesf"], [pk])
            C.ts("dve", rec4[:, :], pj[:, 0:4], 1e-30, ALU.add, [pk], ["rec4"])
            C.recip(rec4[:, :], rec4[:, :], ["rec4"], ["rec4"])
            C.cp("dve", wb[:, :, :], rec4[:, 0:4].unsqueeze(2).to_broadcast([128, 4, 64]), ["rec4"], ["wb"])

        def finB():
            pj2, pk2 = C.bank("pj", PJ)
            C.mmlist([(pj2[0:64, r * 128:(r + 1) * 128], wb[:, r, :], C.identf[:, :], True, True) for r in range(4)],
                     ["wb", "identf"], [pk2])
            C.cp("act", bcs[:, :], pj2[0:64, :], [pk2], ["bcs"])
            oi = C.rot("ot", 2)
            C.tt("dve", ots[oi][:, :], po[0:64, :], bcs[:, :], ALU.mult, [pok, "bcs"], ["ot%d" % oi])
            osink(hh, tc, ots[oi], "ot%d" % oi)
        pendA.append(finA)
        pendB.append(finB)

    st = stage_T(0)
    stage_P1(0, *st)
    stage_P2(0)
    for tc in range(8):
        head(tc, 0)
        if tc < 7:
            st = stage_T(tc + 1)
        head(tc, 1)
        if tc < 7:
            stage_P1(tc + 1, *st)
        head(tc, 2)
        if tc < 7:
            stage_P2(tc + 1)
        head(tc, 3)
    flush()


def phase_nsa(C, name, L, G, hsrc, hkey, osink, pre=None):
    P = C.P
    pre = pre or {}
    C.begin_phase(name, keep_top=bool(pre))
    NW = 780
    C.setup_norm()
    gA = C.load_const("gA_sb", L["gAn"], [128, 8], F32)
    wA = pre["wA_sb"] if "wA_sb" in pre else C.load_w("wA_sb", L["wAn"], 8, NW)
    w1k = pre["w1k_sb"] if "w1k_sb" in pre else C.load_w("w1k_sb", L["w1k"], 16, 128)
    w1v = pre["w1v_sb"] if "w1v_sb" in pre else C.load_w("w1v_sb", L["w1v"], 16, 128)
    w2k = C.load_w("w2k_sb", L["w2k"], 1, 64)
    w2v = C.load_w("w2v_sb", L["w2v"], 1, 64)
    posk = C.load_w("posk_sb", L["posk"], 1, 16)
    posv = C.load_w("posv_sb", L["posv"], 1, 16)
    maskc = pre["maskc_sb"] if "maskc_sb" in pre else C.load_const("maskc_sb", G["maskc"], [128, 2 * S], BF16)
    eall = C.sb("eall_sb", [128, S], BF16)
    C.memset("pool", eall[64:128, :], 0.0, ["eall_sb"])
    P.dma("pool", eall[0:64, :], G["eall"], writes=["eall_sb"])
    selbias = pre["selbias_sb"] if "selbias_sb" in pre else C.load_const("selbias_sb", G["selbias"], [128, 2048], BF16)
    ovl = C.load_const("ovl_sb", G["ovl"], [128, 130], BF16)
    selg = C.load_const("selg_sb", G["selg"], [12, 768], F32)
    dm4 = C.load_const("dm4_sb", G["dm4"], [128, 512], BF16)
    wm4 = C.load_const("wm4_sb", G["wm4"], [128, 512], BF16)

    Qa = C.sb("Qa", [128, 32, 512], BF16)
    Kw = C.sb("Kw", [128, S], BF16)
    Ks = C.sb("Ks", [128, S], BF16)
    Kc = C.sb("Kc", [128, 256], BF16)
    C.memset("pool", Qa[64:128, :, :], 0.0, ["Qa_aug"])
    C.memset("pool", Kw[64:128, :], 0.0, ["Kw_aug"])
    C.memset("pool", Ks[64:128, :], 0.0, ["Ks_aug"])
    C.memset("pool", Kc[64:128, :], 0.0, ["Kc_aug"])
    P.dma("pool", Qa[64:68, :, :], G["qaug"].rearrange("p (a b) -> p a b", a=32), writes=["Qa_aug"])
    P.dma("pool", Kw[64:68, :], G["kaug"], writes=["Kw_aug"])
    P.dma("pool", Ks[64:68, :], G["kaug"], writes=["Ks_aug"])
    P.dma("pool", Kc[64:68, :], G["kaugc"], writes=["Kc_aug"])
    kc2 = C.sb("kc2", [128, S + 32], BF16)
    vc2 = C.sb("vc2", [128, S + 32], BF16)
    C.memset("pool", kc2[:, S:S + 32], 0.0, ["kc2_tail"])
    C.memset("pool", vc2[:, S:S + 32], 0.0, ["vc2_tail"])
    Vs = C.sb("Vs", [128, 32, 128], BF16)
    Vw = C.sb("Vw", [128, 32, 128], BF16)
    Vc = C.sb("Vc", [128, 2, 128], BF16)
    for (vt_, keys_) in ((Vs, ["Vs_%d" % c for c in range(8)]), (Vw, ["Vw_%d" % c for c in range(8)]), (Vc, ["Vc"])):
        C.memset("pool", vt_[:, :, 65:128], 0.0, keys_)
        C.memset("pool", vt_[:, :, 64:65], 1.0, keys_)
    Gtm = C.sb("Gtm", [128, 32, 12], F32)
    nTs = [C.sb("nT%d" % i, [128, 8, 512], BF16) for i in range(2)]

    PJ = [0, 1]
    SB_ = [2, 3]
    POC, POS, POW = 4, 5, 6

    for tc in range(8):
        t0 = tc * 512
        nb_ = C.rot("nT", 2)
        nT = nTs[nb_]
        nkeys = []
        for ti in range(4):
            hb = C.rot("ht", 2)
            P.dma("sp", C.ht[hb][:], hsrc(4 * tc + ti), reads=[hkey(4 * tc + ti)], writes=["ht%d" % hb])
            nk = "nT%d_%d" % (nb_, ti)
            C.norm_T(C.ht[hb][:], "ht%d" % hb, nT, nk, ti * 128, gA, "gA_sb")
            nkeys.append(nk)

        def proj(c0, m, rows=128):
            pj, pk = C.bank("pj", PJ)
            C.mmg(pj[0:rows, :], [(wA[:, k, c0:c0 + m], nT[:, k, :]) for k in range(8)], ["wA_sb"] + nkeys, [pk])
            return pj, pk
        for r in range(4):
            pj, pk = proj(r * 64, 64, 64)
            C.cp("act", Qa[0:64, 4 * tc:4 * tc + 4, r * 128:(r + 1) * 128],
                 pj[0:64, :].rearrange("p (a b) -> p a b", a=4), [pk], ["Qa_%d" % tc])
        for (c0, dst, dk) in ((256, kc2, "kc2"), (384, vc2, "vc2")):
            pj, pk = proj(c0, 128)
            C.cp("act", dst[0:64, t0:t0 + 512], pj[0:64, :], [pk], [dk])
            if tc == 0:
                C.cp("dve", dst[64:128, 0:511], pj[64:128, 1:512], [pk], [dk])
            else:
                C.cp("dve", dst[64:128, t0 - 1:t0 + 511], pj[64:128, :], [pk], [dk])
        pj, pk = proj(512, 64, 64)
        C.cp("act", Ks[0:64, t0:t0 + 512], pj[0:64, :], [pk], ["Ks_%d" % tc])
        pj, pk = proj(576, 64, 64)
        C.cp("act", Kw[0:64, t0:t0 + 512], pj[0:64, :], [pk], ["Kw_%d" % tc])
        for ti in range(4):
            pj, pk = C.bank("pj", PJ)
            C.mmg(pj[:, 0:128], [(nT[:, k, ti * 128:(ti + 1) * 128], wA[:, k, 640:768]) for k in range(8)],
                  ["wA_sb"] + nkeys, [pk])
            pg, pgk = C.bank("pj", PJ)
            C.mmg(pg[:, 0:12], [(nT[:, k, ti * 128:(ti + 1) * 128], wA[:, k, 768:780]) for k in range(8)],
                  ["wA_sb"] + nkeys, [pgk])
            C.cp("dve", Gtm[:, 4 * tc + ti, :], pg[:, 0:12], [pgk], ["Gtm_%d" % tc])
            C.cp("act", Vs[:, 4 * tc + ti, 0:64], pj[:, 0:64], [pk], ["Vs_%d" % tc])
            C.cp("dve", Vw[:, 4 * tc + ti, 0:64], pj[:, 64:128], [pk], ["Vw_%d" % tc])

    C.act(Gtm[:, :, :], Gtm[:, :, :], AF.Sigmoid, ["Gtm_%d" % c for c in range(8)], ["Gtm_%d" % c for c in range(8)])
    xs = C.sb("xs", [128, 256], F32)
    x2 = C.sb("x2", [128, 256], F32)
    hid = C.sb("hid", [128, 256], BF16)
    cbias = C.sb("cbias", [128, 1], F32)
    for (src, skey, w1, w1key, pos, poskey, isk) in ((kc2, "kc2", w1k, "w1k_sb", posk, "posk_sb", True),
                                                     (vc2, "vc2", w1v, "w1v_sb", posv, "posv_sb", False)):
        pj, pk = C.bank("pj", PJ)
        C.mmg(pj[:, 0:1], [(w1[:, j, :], pos[:, 0, j:j + 1]) for j in range(16)], [w1key, poskey], [pk])
        C.cp("act", cbias[:], pj[:, 0:1], [pk], ["cbias"])
        pj, pk = C.bank("pj", PJ)
        C.mmg(pj[:, 0:255], [(w1[:, j, :], src[:, 2 * j:2 * j + 16 * 255:16]) for j in range(16)],
              [w1key, skey, skey + "_tail"], [pk])
        C.memset("dve", xs[:, 255:256], 0.0, ["xs"])
        C.act(xs[:, 0:255], pj[:, 0:255], AF.Identity, [pk, "cbias"], ["xs"], bias=cbias[:])
        C.tt("dve", x2[:], xs[:], xs[:], ALU.mult, ["xs"], ["x2"])
        C.ts("dve", x2[:], x2[:], 0.044715, ALU.mult, ["x2"], ["x2"], s2=1.0, op1=ALU.add)
        C.tt("dve", x2[:], x2[:], xs[:], ALU.mult, ["x2", "xs"], ["x2"])
        C.act(x2[:], x2[:], AF.Sigmoid, ["x2"], ["x2"], scale=1.5957691216057308)
        C.tt("dve", hid[:], xs[:], x2[:], ALU.mult, ["x2", "xs"], ["hid"])
        if isk:
            pj, pk = C.bank("pj", PJ)
            C.mm(pj[0:64, 0:256], w2k[:, 0, :], hid[:], True, True, ["w2k_sb", "hid"], [pk])
            C.cp("act", Kc[0:64, :], pj[0:64, 0:256], [pk], ["Kc"])
        else:
            for nt in range(2):
                pj, pk = C.bank("pj", PJ)
                C.mm(pj[:, 0:64], hid[:, nt * 128:(nt + 1) * 128], w2v[:, 0, :], True, True, ["w2v_sb", "hid"], [pk])
                C.cp("act", Vc[:, nt, 0:64], pj[:, 0:64], [pk], ["Vc"])

    ptc = [[C.sb("ptc%d_%d" % (i, j), [128, 512], BF16) for j in range(2)] for i in range(2)]
    pts = [C.sb("pt%d" % i, [128, 512], BF16) for i in range(4)]
    imp = C.sb("imp", [128, 64], F32)
    wk = C.sb("wk", [128, 64], F32)
    m8 = C.sb("m8", [128, 16], F32)
    rci = C.sb("rci", [128, 4], F32)
    selb = C.sb("selb", [128, 64], BF16)
    selT4s = [C.sb("selT4_%d" % i, [128, 512], BF16) for i in range(2)]
    for i in range(2):
        C.memset("pool", selT4s[i][64:128, :], 0.0, ["selT4_%d" % i])
    rsrows = [C.sb("rsrow%d" % i, [65, 512], F32) for i in range(3)]
    bcss = [C.sb("bcs%d" % i, [64, 512], F32) for i in range(3)]
    tmpos = [C.sb("tmpo%d" % i, [64, 512], F32) for i in range(3)]
    rec4s = [C.sb("rec4_%d" % i, [128, 4], F32) for i in range(3)]
    wbs = [C.sb("wb%d" % i, [128, 4, 64], F32) for i in range(3)]
    oacc = [C.sb("oacc%d" % i, [64, 512], F32) for i in range(2)]
    oaccb = [C.sb("oaccb%d" % i, [64, 512], BF16) for i in range(2)]
    PJ = [0, 7]
    SB_ = [1, 2, 3]
    pending = []

    def flush():
        tl = list(pending)
        del pending[:]
        while any(tl):
            for t in tl:
                if t:
                    t.pop(0)()

    def finalize_steps(po, pok, br, qb, first, rec_src=None):
        ai = qb % 2
        acc, ak = oacc[ai], "oacc%d" % ai
        rsrow, bcs, tmpo, rec4, wb = rsrows[br], bcss[br], tmpos[br], rec4s[br], wbs[br]
        rk, bk, tk, r4k, wk_ = "rsrow%d" % br, "bcs%d" % br, "tmpo%d" % br, "rec4_%d" % br, "wb%d" % br

        def s1():
            if rec_src is None:
                C.cp("dve", rsrow[64:65, :], po[64:65, :], [pok], [rk])
                pj, pk = C.bank("pj", PJ)
                C.mmlist([(pj[:, r:r + 1], rsrow[64:65, r * 128:(r + 1) * 128], C.onesf[64:65, 0:1], True, True) for r in range(4)],
                         [rk, "onesf"], [pk])
                C.ts("dve", rec4[:, :], pj[:, 0:4], 1e-30, ALU.add, [pk], [r4k])

        def s2():
            if rec_src is None:
                C.recip(rec4[:, :], rec4[:, :], [r4k], [r4k])
                recap, reckey = rec4, r4k
            else:
                recap, reckey = rec_src
            C.tt("dve", wb[:, :, :], recap[:, 0:4].unsqueeze(2).to_broadcast([128, 4, 64]),
                 Gtm[:, qb, br:12:3].unsqueeze(2).to_broadcast([128, 4, 64]), ALU.mult, [reckey, "Gtm_%d" % (qb // 4)], [wk_])

        def s3():
            pj2, pk2 = C.bank("pj", PJ)
            C.mmlist([(pj2[0:64, r * 128:(r + 1) * 128], wb[:, r, :], C.identf[:, :], True, True) for r in range(4)],
                     [wk_, "identf"], [pk2])
            C.cp("act", bcs[:, :], pj2[0:64, :], [pk2], [bk])

        def s4():
            if first:
                C.tt("dve", acc[:, :], po[0:64, :], bcs[:, :], ALU.mult, [pok, bk], [ak])
            else:
                C.tt("dve", tmpo[:, :], po[0:64, :], bcs[:, :], ALU.mult, [pok, bk], [tk])
                C.tt("dve", acc[:, :], acc[:, :], tmpo[:, :], ALU.add, [tk, ak], [ak])
        return [s1, s2, s3, s4]

    def qinfo(qb):
        return Qa[:, qb, :], ["Qa_%d" % (qb // 4), "Qa_aug"]

    def cmp_stage(qb):
        q_rhs, qkeys = qinfo(qb)
        pc = ptc[qb % 2]
        selT4, stk = selT4s[qb % 2], "selT4_%d" % (qb % 2)
        ntn = 1 if qb < 16 else 2
        po = C.banks[POC]
        for nt in range(ntn):
            sbk, sk = C.bank("s", SB_)
            items = [(sbk[:, :], Kc[:, nt * 128:(nt + 1) * 128], q_rhs, True, False)]
            for r in range(4):
                items.append((sbk[:, r * 128:(r + 1) * 128], C.ident[:, :],
                              maskc[:, nt * S + qb * 128:nt * S + (qb + 1) * 128], False, r == 3))
            C.mmlist(items, qkeys + ["Kc", "Kc_aug", "ident", "maskc_sb"], [sk])
            C.act(pc[nt][:], sbk[:, :], AF.Exp, [sk], ["ptc%d_%d" % (qb % 2, nt)], scale=SC_NSA)
        for nt in range(ntn):
            C.mm(po[:, :], Vc[:, nt, :], pc[nt][:], nt == 0, nt == ntn - 1,
                 ["Vc", "ptc%d_%d" % (qb % 2, nt)], ["bank%d" % POC])
        pj, pk = C.bank("pj", PJ)
        items = []
        for r in range(4):
            for nt in range(ntn):
                items.append((pj[:, r * 65:(r + 1) * 65], pc[nt][:, r * 128:(r + 1) * 128], ovl[:, nt * 65:(nt + 1) * 65],
                              nt == 0, nt == ntn - 1))
        C.mmlist(items, ["ovl_sb"] + ["ptc%d_%d" % (qb % 2, nt) for nt in range(ntn)], [pk])
        for r in range(4):
            C.ts("dve", rci[:, r:r + 1], pj[:, r * 65 + 64:r * 65 + 65], 1e-30, ALU.add, [pk], ["rci"])
        C.recip(rci[:, 0:4], rci[:, 0:4], ["rci"], ["rci"])
        for r in range(4):
            prev = selbias[:, qb * 64:(qb + 1) * 64] if r == 0 else imp[:]
            C.stt("dve", imp[:], pj[:, r * 65:r * 65 + 64], rci[:, r:r + 1], prev, ALU.mult, ALU.add,
                  [pk, "rci", "imp", "selbias_sb"], ["imp"])
        P.op("dve", lambda e: e.max(out=m8[:, 0:8], in_=imp[:]), ["imp"], ["m8"])
        P.op("dve", lambda e: e.match_replace(out=wk[:], in_to_replace=m8[:, 0:8], in_values=imp[:], imm_value=-1e9),
             ["imp", "m8"], ["wk"])
        P.op("dve", lambda e: e.max(out=m8[:, 8:16], in_=wk[:]), ["wk"], ["m8"])
        C.ts("dve", wk[:], imp[:], m8[:, 15:16], ALU.is_ge, ["imp", "m8"], ["wk"])
        C.ts("dve", selb[:], wk[:], -NEG, ALU.mult, ["wk"], ["selb"], s2=NEG, op1=ALU.add)

        def s0(qb=qb, selT4=selT4, stk=stk):
            C.tr(C.pst[0:64, 0:128], selb[:, :], C.ident[:, :], ["selb", "ident"], ["pst"])
            for r in range(4):
                C.cp("act" if r % 2 == 0 else "dve", selT4[0:64, r * 128:(r + 1) * 128], C.pst[0:64, 0:128], ["pst"], [stk])
        C.cp("dve", rec4s[0][:, :], rci[:, 0:4], ["rci"], ["rec4_0"])
        pending.append([s0] + finalize_steps(po, "bank%d" % POC, 0, qb, True, rec_src=(rec4s[0], "rec4_0")))

    def sel_stage(qb):
        q_rhs, qkeys = qinfo(qb)
        selT4, stk = selT4s[qb % 2], "selT4_%d" % (qb % 2)
        po = C.banks[POS]

        def score(kt):
            pairs = [(Ks[:, kt * 128:(kt + 1) * 128], q_rhs), (eall[:, kt * 128:(kt + 1) * 128], selT4[:, :])]
            rd = qkeys + ["Ks_%d" % (kt // 4), "Ks_aug", "eall_sb", stk]
            if kt == qb:
                pairs.append((C.ident[:, :], dm4[:, :]))
                rd += ["ident", "dm4_sb"]
            return pairs, rd
        attn_loop(C, list(range(qb + 1)), score, SC_NSA, lambda kt: (Vs[:, kt, :], ["Vs_%d" % (kt // 4)]),
                  po, "bank%d" % POS, SB_, pts, "pt", after_first=None)

        def s5(qb=qb):
            ai = qb % 2
            C.cp("act", oaccb[ai][:, :], oacc[ai][:, :], ["oacc%d" % ai], ["oaccb%d" % ai])
            osink(qb, oaccb[ai], "oaccb%d" % ai)
        pending.append(finalize_steps(po, "bank%d" % POS, 1, qb, False) + [s5])

    def win_stage(qb):
        q_rhs, qkeys = qinfo(qb)
        po = C.banks[POW]
        k0 = max(0, qb - 4)

        def score(kt):
            pairs = [(Kw[:, kt * 128:(kt + 1) * 128], q_rhs)]
            rd = qkeys + ["Kw_%d" % (kt // 4), "Kw_aug"]
            if kt == qb:
                pairs.append((C.ident[:, :], dm4[:, :]))
                rd += ["ident", "dm4_sb"]
            if kt == qb - 4:
                pairs.append((C.ident[:, :], wm4[:, :]))
                rd += ["ident", "wm4_sb"]
            return pairs, rd
        attn_loop(C, list(range(k0, qb + 1)), score, SC_NSA, lambda kt: (Vw[:, kt, :], ["Vw_%d" % (kt // 4)]),
                  po, "bank%d" % POW, SB_, pts, "pt", after_first=flush)

        pending.append(finalize_steps(po, "bank%d" % POW, 2, qb, False))

    cmp_stage(0)
    for qb in range(32):
        win_stage(qb)
        if qb + 1 < 32:
            cmp_stage(qb + 1)
        sel_stage(qb)
    flush()


def phase_ffn(C, name, L, G, final, hown, hownkey, hhalo, hhalokey, oall, osink):
    P = C.P
    C.begin_phase(name)
    C.setup_norm()
    fl = C.flags
    g2 = C.load_const("g2_sb", L["g2"], [128, 8], F32)
    cw = C.load_const("cw_sb", L["cw"], [128, 176], F32)
    if final:
        gF = C.sb("gF_sb", [128, DM], F32)
        P.dma("pool", gF[:], G["gF"][0:1, :].partition_broadcast(128), writes=["gF_sb"])
    wup = C.load_w("wup_sb", L["wup"], 8, 5632)
    wdn = C.sb("wdn_sb", [128, 22, DM], BF16)

    def load_wdn():
        for k in range(22):
            P.dma("pool", wdn[:, k, :], L["wdn"][k * 128:(k + 1) * 128, :], writes=["wdn_sb"])
    wo_d = L["wo"]

    NC_ = 256
    aT = [C.sb("aT%d" % i, [128, NC_], BF16) for i in range(4)]
    hm = C.sb("hm", [128, 2, DM], F32)
    n2T = C.sb("n2T", [128, 8, NC_], BF16)
    oTb = C.sb("oTb", [128, 8, NC_], BF16)
    oa = [C.sb("oa%d" % i, [128, NC_], BF16) for i in range(2)]
    ob = [C.sb("ob%d" % i, [128, NC_], BF16) for i in range(2)]
    wob = [C.sb("wob%d" % i, [128, DM], BF16) for i in range(4)]
    ubuf = [C.sb("ubuf%d" % i, [128, NC_ + 2], F32) for i in range(3)]
    tb = [C.sb("tb%d" % i, [128, NC_], F32) for i in range(4)]
    sg = C.sb("sg", [128, NC_], F32)
    carry = C.sb("carry", [128, 44, 2], F32)
    res = C.sb("res", [128, DM], F32)
    ss2 = C.sb("ss2", [128, 1], F32)

    ACC = [0, 1, 2, 3]
    UP = [4, 5, 6]

    def chunk(ci, halo):
        nt_ = 1 if halo else 2
        ncol = nt_ * 128
        c0 = 1920 if halo else ci * 256
        for k in range(8):
            b = C.rot("oa", 2)
            if halo:
                P.dma("sp", oa[b][:, 0:ncol], oall[0][k * 128:(k + 1) * 128, c0:c0 + ncol], reads=["oall0"], writes=["oa%d" % b])
                C.ts("dve", oTb[:, k, 0:ncol], oa[b][:, 0:ncol], fl[:, 1:2], ALU.mult, ["oa%d" % b, "flags"], ["oTb"])
            else:
                P.dma("sp", oa[b][:, 0:ncol], oall[0][k * 128:(k + 1) * 128, c0:c0 + ncol], reads=["oall0"], writes=["oa%d" % b])
                P.dma("sp", ob[b][:, 0:ncol], oall[1][k * 128:(k + 1) * 128, c0:c0 + ncol], reads=["oall1"], writes=["ob%d" % b])
                C.ts("dve", oa[b][:, 0:ncol], oa[b][:, 0:ncol], fl[:, 0:1], ALU.mult, ["oa%d" % b, "flags"], ["oa%d" % b])
                C.stt("dve", oTb[:, k, 0:ncol], ob[b][:, 0:ncol], fl[:, 1:2], oa[b][:, 0:ncol], ALU.mult, ALU.add,
                      ["oa%d" % b, "ob%d" % b, "flags"], ["oTb"])
        for k in range(8):
            wb = C.rot("wob", 4)
            P.dma("pool", wob[wb][:, :], wo_d[k * 128:(k + 1) * 128, :], writes=["wob%d" % wb])
            items = []
            for ti in range(nt_):
                for hf in range(2):
                    items.append((C.banks[ACC[ti * 2 + hf]][:, :], oTb[:, k, ti * 128:(ti + 1) * 128],
                                  wob[wb][:, hf * 512:(hf + 1) * 512], k == 0, k == 7))
            C.mmlist(items, ["oTb", "wob%d" % wb], ["bank%d" % ACC[i] for i in range(nt_ * 2)])
        if ci == 0 and not halo:
            load_wdn()
        for ti in range(nt_):
            hb = C.rot("ht", 2)
            if halo:
                P.dma("sp", C.ht[hb][:], hhalo, reads=[hhalokey], writes=["ht%d" % hb])
                C.ts("dve", C.ht[hb][:], C.ht[hb][:], fl[:, 1:2], ALU.mult, ["ht%d" % hb, "flags"], ["ht%d" % hb])
            else:
                P.dma("sp", C.ht[hb][:], hown(2 * ci + ti), reads=[hownkey(2 * ci + ti)], writes=["ht%d" % hb])
            for hf in range(2):
                C.tt("dve", hm[:, ti, hf * 512:(hf + 1) * 512], C.banks[ACC[ti * 2 + hf]][:, :],
                     C.ht[hb][:, hf * 512:(hf + 1) * 512], ALU.add, ["bank%d" % ACC[ti * 2 + hf], "ht%d" % hb], ["hm%d" % ti])
            C.norm_T(hm[:, ti, :], "hm%d" % ti, n2T, "n2T_%d" % ti, ti * 128, g2, "g2_sb")
        nkeys = ["n2T_%d" % ti for ti in range(nt_)]
        dq = []

        def down(i, ai):
            items = []
            for ti in range(nt_):
                for hf in range(2):
                    items.append((C.banks[ACC[ti * 2 + hf]][:, :], aT[ai][:, ti * 128:(ti + 1) * 128],
                                  wdn[:, i, hf * 512:(hf + 1) * 512], i == 0, i == 21))
            C.mmlist(items, ["aT%d" % ai, "wdn_sb"], ["bank%d" % ACC[q] for q in range(nt_ * 2)])
        for i in range(22):
            tfin = []
            for part in range(2):
                fc = i + 22 * part
                up, upk = C.bank("up", UP)
                C.mmg(up[:, 0:ncol], [(wup[:, k, fc * 128:(fc + 1) * 128], n2T[:, k, 0:ncol]) for k in range(8)],
                      ["wup_sb"] + nkeys, [upk])
                ub = C.rot("ubuf", 3)
                u = ubuf[ub]
                uk = "ubuf%d" % ub
                C.cp("pool", u[:, 0:2], carry[:, fc, :], ["carry%d" % fc], [uk])
                C.cp("act", u[:, 2:2 + ncol], up[:, 0:ncol], [upk], [uk])
                if not halo:
                    ta = C.rot("tb", 4)
                    C.act(tb[ta][:, 0:ncol], up[:, 0:ncol], AF.Identity, [upk, "cw_sb"], ["tb%d" % ta],
                          bias=cw[:, fc * 4 + 3:fc * 4 + 4], scale=cw[:, fc * 4 + 2:fc * 4 + 3])
                    C.stt("dve", tb[ta][:, 0:ncol], u[:, 1:1 + ncol], cw[:, fc * 4 + 1:fc * 4 + 2], tb[ta][:, 0:ncol],
                          ALU.mult, ALU.add, [uk, "cw_sb", "tb%d" % ta], ["tb%d" % ta])
                    C.stt("dve", tb[ta][:, 0:ncol], u[:, 0:ncol], cw[:, fc * 4:fc * 4 + 1], tb[ta][:, 0:ncol],
                          ALU.mult, ALU.add, [uk, "cw_sb", "tb%d" % ta], ["tb%d" % ta])
                    tfin.append(ta)
                C.cp("pool", carry[:, fc, :], u[:, ncol:ncol + 2], [uk], ["carry%d" % fc])
            if not halo:
                C.act(sg[:, 0:ncol], tb[tfin[0]][:, 0:ncol], AF.Silu, ["tb%d" % tfin[0]], ["sg"])
                ai = C.rot("aT", 4)
                C.tt("dve", aT[ai][:, 0:ncol], sg[:, 0:ncol], tb[tfin[1]][:, 0:ncol], ALU.mult,
                     ["sg", "tb%d" % tfin[1]], ["aT%d" % ai])
                dq.append((i, ai))
                if len(dq) > 2:
                    down(*dq.pop(0))
        while dq:
            down(*dq.pop(0))
        if halo:
            return
        for ti in range(nt_):
            for hf in range(2):
                bk = ACC[ti * 2 + hf]
                C.tt("dve", res[:, hf * 512:(hf + 1) * 512], C.banks[bk][:, :], hm[:, ti, hf * 512:(hf + 1) * 512],
                     ALU.add, ["bank%d" % bk, "hm%d" % ti], ["res"])
            if final:
                C.memset("dve", ss2[:], 0.0, ["ss2"])
                C.act(C.junk[:], res[:], AF.Square, ["res", "ss2"], ["junk", "ss2"], accum=ss2[:])
                C.act(ss2[:], ss2[:], AF.Sqrt, ["ss2", "epsn"], ["ss2"], bias=C.epsn[:], scale=1.0 / DM)
                C.recip(ss2[:], ss2[:], ["ss2"], ["ss2"])
                C.stt("dve", res[:], res[:], ss2[:, 0:1], gF[:], ALU.mult, ALU.mult, ["res", "ss2", "gF_sb"], ["res"])
            osink(2 * ci + ti, res, "res")

    C.memset("pool", carry[:], 0.0, ["carry%d" % fc for fc in range(44)])
    chunk(0, True)
    for c in range(8):
        chunk(c, False)


LAYER_IN = [("wAm", [DM, 832]), ("gAm", [128, 8]), ("wq", [384, 768]), ("gq", [128, 3]), ("wkv", [256, 512]),
            ("gkv", [128, 2]), ("wAn", [DM, 780]), ("gAn", [128, 8]), ("w1k", [2048, 128]), ("w1v", [2048, 128]),
            ("w2k", [128, 64]), ("w2v", [128, 64]), ("posk", [128, 16]), ("posv", [128, 16]),
            ("wo", [DM, DM]), ("wup", [DM, 5632]), ("wdn", [2816, DM]), ("g2", [128, 8]), ("cw", [128, 176])]
GLOB_IN = [("ropeC", [32, S], F32), ("ropeS", [32, S], F32), ("dmask", [128, 2048], BF16),
           ("maskc", [128, 2 * S], BF16), ("eall", [64, S], BF16), ("selbias", [128, 2048], BF16),
           ("ovl", [128, 130], BF16), ("selg", [12, 768], F32), ("dm4", [128, 512], BF16), ("wm4", [128, 512], BF16),
           ("qaug", [4, 32 * 512], BF16), ("kaug", [4, S], BF16), ("kaugc", [4, 256], BF16), ("gF", [1, DM], F32)]
GROUPS = [[0, 1], [2, 3], [4, 5], [6, 7]]


def build_fused(nlayers=2):
    nc = bass.Bass("TRN2", target_bir_lowering=False)
    C = Ctx(nc)
    P = C.P
    x_d = C.dram("x", [S, DM], F32)
    xown_d = C.dram("xown", [2048, DM], F32)
    xhalo_d = C.dram("xhalo", [128, DM], F32)
    flags_d = C.dram("flags", [128, 2], F32)
    G = {n: C.dram(n, sh, dt) for (n, sh, dt) in GLOB_IN}
    Ls = [{n: C.dram("%s_%d" % (n, l), sh, F32) for (n, sh) in LAYER_IN} for l in range(nlayers)]
    out_d = C.dram("hout", [2048, DM], F32, out=True)
    P.dma("pool", C.flags[:], flags_d, writes=["flags"])

    omy = [[nc.dram_tensor("omy_%d_%d" % (l, c), [512, 2048], BF16) for c in range(2)] for l in range(nlayers)]
    oall = [[nc.dram_tensor("oall_%d_%d" % (l, c), [1024, 2048], BF16) for c in range(2)] for l in range(nlayers)]
    hmy = [nc.dram_tensor("hmy_%d" % j, [512, DM], F32) for j in range(4)]
    hall = [nc.dram_tensor("hall_%d" % j, [1024, DM], F32) for j in range(4)]

    for l in range(nlayers):
        if l == 0:
            hsrc = lambda g: x_d[g * 128:(g + 1) * 128, :]
            hkey = lambda g: "x"
        else:
            def hsrc(g):
                r, w = g // 16, g % 16
                return hall[w // 4][r * 512 + (w % 4) * 128:r * 512 + (w % 4) * 128 + 128, :]
            hkey = lambda g: "hall%d" % ((g % 16) // 4)

        def osink_mla(hh, tc, ot, otkey, l=l):
            c = tc // 4
            col = (tc % 4) * 512
            P.dma("sp", omy[l][c][hh * 64:(hh + 1) * 64, col:col + 512], ot[:, :], reads=[otkey], writes=["omy%d" % c])

        def osink_nsa(qb, ot, otkey, l=l):
            c = qb // 16
            col = (qb % 16) * 128
            P.dma("sp", omy[l][c][256:512, :].rearrange("(r d) t -> d r t", d=64)[:, :, col:col + 128],
                  ot[:, :].rearrange("p (r t) -> p r t", r=4), reads=[otkey], writes=["omy%d" % c])
            if qb % 16 == 15:
                P.cc("AllGather", GROUPS, omy[l][c].ap().opt(), oall[l][c].ap().opt(), reads=["omy%d" % c], writes=["oall%d" % c])

        pre = {}

        def prefetch_nsa(l=l, pre=pre):
            def ld(key, name, dram, nk, ncols):
                t = C.sb_top("np%d_%s" % (l, name), [128, nk, ncols], BF16)
                for k in range(nk):
                    P.dma("pool", t[:, k, :], dram[k * 128:(k + 1) * 128, :], writes=[key])
                pre[key] = t
            ld("wA_sb", "wA", Ls[l]["wAn"], 8, 780)
            ld("w1k_sb", "w1k", Ls[l]["w1k"], 16, 128)
            ld("w1v_sb", "w1v", Ls[l]["w1v"], 16, 128)
            t = C.sb_top("np%d_maskc" % l, [128, 2 * S], BF16)
            P.dma("pool", t[:], G["maskc"], writes=["maskc_sb"])
            pre["maskc_sb"] = t
            t = C.sb_top("np%d_selbias" % l, [128, 2048], BF16)
            P.dma("pool", t[:], G["selbias"], writes=["selbias_sb"])
            pre["selbias_sb"] = t
        phase_mla(C, "m%d_" % l, Ls[l], G, hsrc, hkey, osink_mla, after_weights=prefetch_nsa)
        phase_nsa(C, "n%d_" % l, Ls[l], G, hsrc, hkey, osink_nsa, pre=pre)
        final = (l == nlayers - 1)
        if l == 0:
            hown = lambda t: xown_d[t * 128:(t + 1) * 128, :]
            hownkey = lambda t: "xown"
            hhalo, hhalokey = xhalo_d, "xhalo"
        else:
            hown = lambda t: hmy[t // 4][(t % 4) * 128:(t % 4) * 128 + 128, :]
            hownkey = lambda t: "hmy%d" % (t // 4)
            hhalo, hhalokey = hall[3][384:512, :], "hall3"
        if final:
            def osink_ffn(t, res, rkey):
                P.dma("sp", out_d[t * 128:(t + 1) * 128, :], res[:], reads=[rkey])
        else:
            def osink_ffn(t, res, rkey):
                P.dma("sp", hmy[t // 4][(t % 4) * 128:(t % 4) * 128 + 128, :], res[:], reads=[rkey], writes=["hmy%d" % (t // 4)])
                if t % 4 == 3:
                    j = t // 4
                    P.cc("AllGather", GROUPS, hmy[j].ap().opt(), hall[j].ap().opt(), reads=["hmy%d" % j], writes=["hall%d" % j])
        phase_ffn(C, "f%d_" % l, Ls[l], G, final, hown, hownkey, hhalo, hhalokey, oall[l], osink_ffn)
    P.emit()
    return nc


def _pk(g, nk):
    return np.ascontiguousarray(np.asarray(g, np.float32).reshape(nk, 128).T)


def _consts():
    c = {}
    p = np.arange(128)[:, None]
    i512 = np.arange(512)[None, :]
    dm = np.zeros((128, 4, 512), np.float32)
    for m in range(4):
        dm[:, m, :] = np.where(128 * m + p <= i512, 0.0, NEG)
    c["dmask"] = dm.reshape(128, 2048).astype(NPBF)
    i128 = np.arange(128)[None, :]
    c["dm4"] = np.tile(np.where(p <= i128, 0.0, NEG), (1, 4)).astype(NPBF)
    c["wm4"] = np.tile(np.where(i128 < p, 0.0, NEG), (1, 4)).astype(NPBF)
    n = np.arange(256)[:, None]
    t = np.arange(S)[None, :]
    mc = np.where((t >= 16 * n + 31) & (n <= 254), 0.0, NEG).astype(np.float32)
    c["maskc"] = np.ascontiguousarray(mc.reshape(2, 128, S).transpose(1, 0, 2).reshape(128, 2 * S)).astype(NPBF)
    j = np.arange(64)[:, None]
    c["eall"] = (np.arange(S)[None, :] // 64 == j).astype(np.float32).astype(NPBF)
    tt_ = np.arange(S)
    cur = (tt_ // 64)[:, None]
    jj = np.arange(64)[None, :]
    sbias = np.zeros((S, 64), np.float32)
    sbias[np.broadcast_to(jj > cur, (S, 64))] = -1e4
    sbias[np.broadcast_to((jj == 0) | (jj == cur) | (jj == cur - 1), (S, 64))] = 1e4
    c["selbias"] = np.ascontiguousarray(sbias.reshape(32, 128, 64).transpose(1, 0, 2).reshape(128, 2048)).astype(NPBF)
    cs = (np.arange(256) * 16)[:, None]
    ss_ = (np.arange(64) * 64)[None, :]
    ov = ((cs < ss_ + 64) & (cs + 32 > ss_)).astype(np.float32)
    ov[255] = 0.0
    ov1 = np.concatenate([ov, np.ones((256, 1), np.float32)], 1)
    c["ovl"] = np.ascontiguousarray(ov1.reshape(2, 128, 65).transpose(1, 0, 2).reshape(128, 130)).astype(NPBF)
    sg = np.zeros((12, 12, 64), np.float32)
    for g in range(12):
        sg[g, g, :] = 1.0
    c["selg"] = sg.reshape(12, 768)
    k = np.arange(S)
    c["kaug"] = np.stack([np.ones(S), np.ones(S), k // 64, k % 64]).astype(np.float32).astype(NPBF)
    e = np.arange(256) * 16 + 31
    c["kaugc"] = np.stack([np.ones(256), np.ones(256), e // 64, e % 64]).astype(np.float32).astype(NPBF)
    inv = 1.0 / (10000.0 ** (np.arange(0, 32, 2, dtype=np.float32) / 32))
    ang = np.arange(S, dtype=np.float32)[:, None] * inv[None, :]
    cos, sin = np.cos(ang).T.astype(np.float32), np.sin(ang).T.astype(np.float32)
    c["ropeC"] = np.ascontiguousarray(np.concatenate([cos, cos], 0))
    c["ropeS"] = np.ascontiguousarray(np.concatenate([-sin, sin], 0))
    return c


def _qaug(group):
    slopes = np.exp2(-8.0 * np.arange(1, 9, dtype=np.float32) / 8)
    t = np.arange(S).reshape(32, 1, 128)
    out = np.zeros((4, 32, 4, 128), np.float32)
    for r in range(4):
        a = slopes[group * 4 + r] / SC_NSA
        out[0, :, r, :] = (-a * 64 * (t // 64))[:, 0, :]
        out[1, :, r, :] = (-a * (t % 64))[:, 0, :]
        out[2, :, r, :] = a * 64
        out[3, :, r, :] = a
    return out.reshape(4, 32 * 512).astype(NPBF)


_PROG = {}


def _prog(nlayers=2):
    if nlayers not in _PROG:
        _PROG[nlayers] = build_fused(nlayers)
    return _PROG[nlayers]


def _layer_maps(l, I, c):
    m = {}
    w_in, w_uq, w_ukv = I["w_in"][l], I["w_uq"][l], I["w_ukv"][l]
    sw = list(range(656, 672)) + list(range(640, 656))
    colsA = list(range(0, 640)) + list(range(0, 64)) + list(range(640, 672)) + list(range(0, 64)) + sw
    m["wAm"] = np.ascontiguousarray(w_in[:, colsA])
    m["gAm"] = _pk(I["attn_norm"][l], 8)
    qc, kc, vc = [], [], []
    for hh in range(4 * c, 4 * c + 4):
        base = 96 * hh
        nope = list(range(base, base + 64))
        rope = list(range(base + 64, base + 96))
        qc += nope + rope + nope + rope[16:] + rope[:16]
        kc += list(range(128 * hh, 128 * hh + 64))
        vc += list(range(128 * hh + 64, 128 * hh + 128))
    m["wq"] = np.ascontiguousarray(w_uq[:, qc])
    m["gq"] = _pk(I["q_norm"][l], 3)
    m["wkv"] = np.ascontiguousarray(w_ukv[:, kc + vc])
    m["gkv"] = _pk(I["kv_norm"][l], 2)
    g = c
    q0 = 672 + 256 * g
    o = 1184
    rng = lambda a: list(range(a, a + 64))
    kcc, vcc, ksc = rng(o + 64 * g), rng(o + 128 + 64 * g), rng(o + 256 + 64 * g)
    vsc, kwc, vwc = rng(o + 384 + 64 * g), rng(o + 512 + 64 * g), rng(o + 640 + 64 * g)
    gtc = list(range(1952 + 12 * g, 1952 + 12 * g + 12))
    cols = list(range(q0, q0 + 256)) + kcc + kcc + vcc + vcc + ksc + kwc + vsc + vwc + gtc
    m["wAn"] = np.ascontiguousarray(w_in[:, cols])
    m["gAn"] = m["gAm"]
    posT = lambda pz: np.ascontiguousarray(np.asarray(pz, np.float32).reshape(16, 128).T)
    m["w1k"] = np.ascontiguousarray(I["cmp_k_w1"][l].reshape(2048, 128))
    m["w1v"] = np.ascontiguousarray(I["cmp_v_w1"][l].reshape(2048, 128))
    m["w2k"] = np.ascontiguousarray(I["cmp_k_w2"][l])
    m["w2v"] = np.ascontiguousarray(I["cmp_v_w2"][l])
    m["posk"] = posT(I["cmp_pos_k"][l])
    m["posv"] = posT(I["cmp_pos_v"][l])
    perm = list(range(0, 256)) + list(range(512, 768)) + list(range(256, 512)) + list(range(768, 1024))
    m["wo"] = np.ascontiguousarray(I["w_o"][l][perm, :])
    m["wup"] = np.ascontiguousarray(I["w_up"][l])
    m["wdn"] = np.ascontiguousarray(I["w_down"][l])
    m["g2"] = _pk(I["ffn_norm"][l], 8)
    cwv = np.stack([I["conv_w"][l][0], I["conv_w"][l][1], I["conv_w"][l][2], I["conv_b"][l]], -1)
    m["cw"] = np.ascontiguousarray(cwv.reshape(44, 128, 4).transpose(1, 0, 2).reshape(128, 176)).astype(np.float32)
    return m


def make_maps(I, nlayers=2):
    cst = _consts()
    lm = [[_layer_maps(l, I, c) for c in range(2)] for l in range(nlayers)]
    maps = []
    for b in range(4):
        for c in range(2):
            m = {"x": np.ascontiguousarray(I["x"][b]),
                 "xown": np.ascontiguousarray(I["x"][b][2048 * c:2048 * c + 2048]),
                 "xhalo": np.ascontiguousarray(I["x"][b][1920:2048]),
                 "flags": np.ascontiguousarray(np.tile(np.array([[1.0 - c, float(c)]], np.float32), (128, 1)))}
            for (n, sh, dt) in GLOB_IN:
                if n == "qaug":
                    m[n] = _qaug(c)
                elif n == "gF":
                    m[n] = np.ascontiguousarray(np.asarray(I["final_norm"], np.float32).reshape(1, DM))
                else:
                    m[n] = cst[n]
            for l in range(nlayers):
                for k, v in lm[l][c].items():
                    m["%s_%d" % (k, l)] = v
            maps.append(m)
    return maps


def kernel(**inputs):
    I = {k: np.asarray(v, dtype=np.float32) for k, v in inputs.items()}
    res = run_bass_kernel_spmd(_prog(2), make_maps(I, 2), core_ids=list(range(8))).results
    out = np.empty((4, S, DM), np.float32)
    for b in range(4):
        for c in range(2):
            out[b, 2048 * c:2048 * c + 2048] = np.asarray(res[2 * b + c]["hout"])
    return out
```

```python
import numpy as np
import ml_dtypes
import concourse.bass as bass
import concourse.mybir as mybir
from concourse.bass_utils import run_bass_kernel_spmd

F32 = mybir.dt.float32
BF16 = mybir.dt.bfloat16
ALU = mybir.AluOpType
AF = mybir.ActivationFunctionType
AX = mybir.AxisListType
NPBF = ml_dtypes.bfloat16

ENGS = ["pe", "act", "dve", "pool", "sp"]
DMA_POOL = 12
S = 4096
DM = 1024
NEG = -30000.0
SC_MLA = 96 ** -0.5
SC_NSA = 0.125
EPS = 1e-6


class Prog:
    def __init__(self, nc):
        self.nc = nc
        self.ops = {e: [] for e in ENGS}
        self.lastw = {}
        self.readers = {}
        self.dma_n = {e: 0 for e in ENGS + ["cc"]}
        self.dma_sem_cnt = {}
        self.last_c = {}
        self.last_d = {}

    def sb(self, name, shape, dt):
        return self.nc.alloc_sbuf_tensor(name, list(shape), dt)

    def ps(self, name, shape, dt=F32):
        return self.nc.alloc_psum_tensor(name, list(shape), dt)

    def _add(self, eng, fn, reads, writes, dma, cc=False):
        op = dict(eng=eng, fn=fn, deps=[], dma=dma, marked=False, inc=(1 if cc else 16))
        deps = []
        for k in reads:
            w = self.lastw.get(k)
            if w is not None:
                deps.append(w)
        for k in writes:
            w = self.lastw.get(k)
            if w is not None:
                deps.append(w)
            deps.extend(self.readers.get(k, ()))
        seen = set()
        for d in deps:
            if id(d) in seen or d is op:
                continue
            seen.add(id(d))
            if (not d["dma"]) and d["eng"] == eng and eng in ("pe", "sp"):
                continue
            op["deps"].append(d)
            d["marked"] = True
        if dma:
            qn = "cc" if cc else eng
            q = self.dma_n[qn]
            self.dma_n[qn] += 1
            semkey = (qn, q % (4 if cc else DMA_POOL))
            m = self.dma_sem_cnt.get(semkey, 0) + 1
            self.dma_sem_cnt[semkey] = m
            op["dsem"] = semkey
            op["dval"] = op["inc"] * m
            op["marked"] = True
            self.last_d[semkey] = op
        else:
            self.last_c[eng] = op
        for k in reads:
            self.readers.setdefault(k, []).append(op)
        for k in writes:
            self.lastw[k] = op
            self.readers[k] = []
        self.ops[eng].append(op)
        return op

    def op(self, eng, fn, reads=(), writes=()):
        return self._add(eng, fn, list(reads), list(writes), False)

    def dma(self, eng, out, in_, reads=(), writes=()):
        return self._add(eng, lambda e: e.dma_start(out=out, in_=in_), list(reads), list(writes), True)

    def cc(self, kind, groups, in_ap, out_ap, reads=(), writes=()):
        return self._add("pool", lambda e: e.collective_compute(kind, ALU.bypass, replica_groups=groups,
                                                                ins=[in_ap], outs=[out_ap]),
                         list(reads), list(writes), True, cc=True)

    def barrier(self):
        deps = list(self.last_c.values()) + list(self.last_d.values())
        for d in deps:
            d["marked"] = True
        for e in ENGS:
            self.ops[e].append(dict(eng=e, fn=None, deps=list(deps), dma=False, marked=False, inc=0))
        self.lastw = {}
        self.readers = {}

    def emit(self):
        nc = self.nc
        csem = {e: nc.alloc_semaphore("c_" + e) for e in ENGS}
        dsem = {}
        for (qn, i) in self.dma_sem_cnt:
            dsem[(qn, i)] = nc.alloc_semaphore("d_%s_%d" % (qn, i))
        for e in ENGS:
            c = 0
            for o in self.ops[e]:
                if o["dma"] or o["fn"] is None:
                    continue
                if o["marked"]:
                    c += 1
                    o["cval"] = c
        all_dma = [o for e in ENGS for o in self.ops[e] if o["dma"]]

        def run(e, eng):
            seen = {}

            def wait(sem_key, sem, val):
                if seen.get(sem_key, 0) >= val:
                    return
                seen[sem_key] = val
                eng.wait_ge(sem, val)

            for o in self.ops[e]:
                for d in o["deps"]:
                    if d["dma"]:
                        wait(d["dsem"], dsem[d["dsem"]], d["dval"])
                    else:
                        wait(("c", d["eng"]), csem[d["eng"]], d["cval"])
                if o["fn"] is None:
                    continue
                if o["dma"]:
                    if o["dval"] > o["inc"]:
                        wait(o["dsem"], dsem[o["dsem"]], o["dval"] - o["inc"])
                    o["fn"](eng).then_inc(dsem[o["dsem"]], o["inc"])
                else:
                    ins = o["fn"](eng)
                    if o["marked"]:
                        ins.then_inc(csem[e], 1)
            if e == "sp":
                last = {}
                for o in all_dma:
                    last[o["dsem"]] = max(last.get(o["dsem"], 0), o["dval"])
                for k, v in last.items():
                    eng.wait_ge(dsem[k], v)

        with nc.Block() as block:
            @block.tensor
            def _(eng):
                run("pe", eng)

            @block.scalar
            def _(eng):
                run("act", eng)

            @block.vector
            def _(eng):
                run("dve", eng)

            @block.gpsimd
            def _(eng):
                run("pool", eng)

            @block.sync
            def _(eng):
                run("sp", eng)


def _nbytes(shape, dt):
    n = 1
    for d in shape[1:]:
        n *= d
    return n * (4 if dt == F32 else 2)


class Ctx:
    def __init__(self, nc):
        self.nc = nc
        self.P = P = Prog(nc)
        self.rots = {}
        self.pname = "g_"
        self.ident = nc.alloc_sbuf_tensor("ident", [128, 128], BF16)
        self.identf = nc.alloc_sbuf_tensor("identf", [128, 128], F32)
        self.onesf = nc.alloc_sbuf_tensor("onesf", [128, 128], F32)
        self.epsn = nc.alloc_sbuf_tensor("epsn", [128, 1], F32)
        self.flags = nc.alloc_sbuf_tensor("flags_sb", [128, 2], F32)
        identf, ident, onesf, epsn = self.identf, self.ident, self.onesf, self.epsn
        P.op("pool", lambda e: e.memset(identf[:], 0.0), writes=["identf"])
        P.op("pool", lambda e: e.affine_select(out=identf[:], in_=identf[:], pattern=[[-1, 128]],
                                                compare_op=ALU.not_equal, fill=1.0, base=0, channel_multiplier=1),
             reads=["identf"], writes=["identf"])
        P.op("dve", lambda e: e.tensor_copy(out=ident[:], in_=identf[:]), reads=["identf"], writes=["ident"])
        P.op("dve", lambda e: e.memset(onesf[:], 1.0), writes=["onesf"])
        P.op("dve", lambda e: e.memset(epsn[:], EPS), writes=["epsn"])
        self.pst = P.ps("pst", [128, 1024], BF16)
        self.banks = [P.ps("bank%d" % i, [128, 512], F32) for i in range(7)]
        self.banks.append(self.pst.bitcast(F32))
        self.base = ((int(nc.sbuf_base) + 63) // 64) * 64
        self.top = int(nc.sbuf_top)
        self.off = self.base
        self.top_off = self.top

    def begin_phase(self, name, keep_top=False):
        self.P.barrier()
        self.pname = name
        self.off = self.base
        if not keep_top:
            self.top_off = self.top

    def sb_top(self, name, shape, dt):
        nb = ((_nbytes(shape, dt) + 31) // 32) * 32
        self.top_off -= nb
        assert self.top_off >= self.off, ("SBUF overflow (top)", name)
        return self.nc.alloc_sbuf_tensor_at(name, list(shape), dt, offset=self.top_off)

    def sb(self, name, shape, dt):
        nb = ((_nbytes(shape, dt) + 31) // 32) * 32
        assert self.off + nb <= self.top_off, ("SBUF overflow", self.pname, name, self.off + nb - self.top_off)
        t = self.nc.alloc_sbuf_tensor_at(self.pname + name, list(shape), dt, offset=self.off)
        self.off += nb
        return t

    def rot(self, name, n):
        i = self.rots.get(name, 0) % n
        self.rots[name] = (i + 1) % n
        return i

    def dram(self, name, shape, dt, out=False):
        return self.nc.dram_tensor(name, list(shape), dt, kind="ExternalOutput" if out else "ExternalInput").ap()

    def mm(self, out, lhsT, rhs, start, stop, reads, writes):
        return self.P.op("pe", lambda e: e.matmul(out, lhsT=lhsT, rhs=rhs, start=start, stop=stop), reads, writes)

    def mmg(self, out, pairs, reads, writes, start=True, stop=True):
        n = len(pairs)

        def fn(e):
            ins = None
            for i, (l, r) in enumerate(pairs):
                ins = e.matmul(out, lhsT=l, rhs=r, start=(start and i == 0), stop=(stop and i == n - 1))
            return ins
        return self.P.op("pe", fn, reads, writes)

    def mmlist(self, items, reads, writes):
        def fn(e):
            ins = None
            for (o, l, r, st, sp) in items:
                ins = e.matmul(o, lhsT=l, rhs=r, start=st, stop=sp)
            return ins
        return self.P.op("pe", fn, reads, writes)

    def tr(self, out, in_, ident, reads, writes):
        return self.P.op("pe", lambda e: e.transpose(out, in_, ident), reads, writes)

    def act(self, out, in_, func, reads, writes, bias=None, scale=1.0, accum=None):
        kw = {}
        if bias is not None:
            kw["bias"] = bias
        if accum is not None:
            kw["accum_out"] = accum
        return self.P.op("act", lambda e: e.activation(out=out, in_=in_, func=func, scale=scale, **kw), reads, writes)

    def tt(self, eng, out, in0, in1, op, reads, writes):
        return self.P.op(eng, lambda e: e.tensor_tensor(out=out, in0=in0, in1=in1, op=op), reads, writes)

    def ts(self, eng, out, in0, s1, op0, reads, writes, s2=None, op1=None):
        if op1 is None:
            return self.P.op(eng, lambda e: e.tensor_scalar(out=out, in0=in0, scalar1=s1, scalar2=None, op0=op0), reads, writes)
        return self.P.op(eng, lambda e: e.tensor_scalar(out=out, in0=in0, scalar1=s1, scalar2=s2, op0=op0, op1=op1), reads, writes)

    def stt(self, eng, out, in0, scalar, in1, op0, op1, reads, writes):
        return self.P.op(eng, lambda e: e.scalar_tensor_tensor(out=out, in0=in0, scalar=scalar, in1=in1, op0=op0, op1=op1), reads, writes)

    def cp(self, eng, out, in_, reads, writes):
        if eng == "act":
            return self.P.op("act", lambda e: e.copy(out=out, in_=in_), reads, writes)
        return self.P.op(eng, lambda e: e.tensor_copy(out=out, in_=in_), reads, writes)

    def recip(self, out, in_, reads, writes):
        return self.P.op("dve", lambda e: e.reciprocal(out=out, in_=in_), reads, writes)

    def memset(self, eng, ap, val, writes):
        return self.P.op(eng, lambda e: e.memset(ap, val), [], writes)

    def bank(self, grp, idxs):
        i = idxs[self.rot(grp, len(idxs))]
        return self.banks[i], ("pst" if i == 7 else "bank%d" % i)

    def load_w(self, name, w_dram, nk, ncols):
        wsb = self.sb(name, [128, nk, ncols], BF16)
        for k in range(nk):
            self.P.dma("pool", wsb[:, k, :], w_dram[k * 128:(k + 1) * 128, :], writes=[name])
        return wsb

    def load_const(self, name, dram_ap, shape, dt, eng="pool"):
        t = self.sb(name, shape, dt)
        self.P.dma(eng, t[:], dram_ap, writes=[name])
        return t

    def setup_norm(self):
        self.ht = [self.sb("ht%d" % i, [128, DM], F32) for i in range(2)]
        self.junk = self.sb("junk", [128, DM], BF16)
        self.nb = self.sb("nb", [128, DM], BF16)
        self.ss = [self.sb("ss%d" % i, [128, 1], F32) for i in range(2)]

    def norm_T(self, src, srckey, nT, nkey, col0, g_sb, gkey):
        b = self.rot("ss", 2)
        ss = self.ss[b]
        sk = "ss%d" % b
        self.memset("dve", ss[:], 0.0, [sk])
        self.act(self.junk[:], src, AF.Square, [srckey, sk], ["junk", sk], accum=ss[:])
        self.act(ss[:], ss[:], AF.Sqrt, [sk, "epsn"], [sk], bias=self.epsn[:], scale=1.0 / DM)
        self.recip(ss[:], ss[:], [sk], [sk])
        self.ts("dve", self.nb[:], src, ss[:, 0:1], ALU.mult, [srckey, sk], ["nb"])
        pst = self.pst
        nb, ident = self.nb, self.ident

        def fn(e):
            ins = None
            for k in range(8):
                ins = e.transpose(pst[:, k * 128:(k + 1) * 128], nb[:, k * 128:(k + 1) * 128], ident[:])
            return ins
        self.P.op("pe", fn, ["nb", "ident"], ["pst"])
        self.tt("dve", nT[:, :, col0:col0 + 128], pst[:, :].rearrange("p (k t) -> p k t", k=8),
                g_sb[:, :].unsqueeze(2).to_broadcast([128, 8, 128]), ALU.mult, ["pst", gkey], [nkey])


def attn_loop(C, tiles, score_fn, scale, v_fn, po, pok, sbanks, pts, ptname, after_first=None, depth=2, hooks=None):
    n = len(tiles)
    issued = []

    def issue(i):
        sbk, sk = C.bank("s", sbanks)
        pairs, rd = score_fn(tiles[i])
        C.mmg(sbk[:, :], pairs, rd, [sk])
        issued.append((sbk, sk))
    for i in range(min(depth, n)):
        issue(i)
    if after_first is not None:
        after_first()
    for i in range(n):
        sbk, sk = issued[i]
        pi = C.rot(ptname, len(pts))
        C.act(pts[pi][:], sbk[:, :], AF.Exp, [sk], ["%s%d" % (ptname, pi)], scale=scale)
        if i + depth < n:
            issue(i + depth)
        lhsT, rd = v_fn(tiles[i])
        C.mm(po[:, :], lhsT, pts[pi][:], i == 0, i == n - 1, rd + ["%s%d" % (ptname, pi)], [pok])
        if hooks and i in hooks:
            hooks[i]()


def phase_mla(C, name, L, G, hsrc, hkey, osink, after_weights=None):
    P = C.P
    C.begin_phase(name)
    C.setup_norm()
    gA = C.load_const("gA_sb", L["gAm"], [128, 8], F32)
    gq = C.load_const("gq_sb", L["gq"], [128, 3], F32)
    gkv = C.load_const("gkv_sb", L["gkv"], [128, 2], F32)
    dmask = C.load_const("dmask_sb", G["dmask"], [128, 2048], BF16)
    wA = C.load_w("wA_sb", L["wAm"], 8, 832)
    wq = C.load_w("wq_sb", L["wq"], 3, 768)
    wkv = C.load_w("wkv_sb", L["wkv"], 2, 512)
    if after_weights is not None:
        after_weights()
    ropeC_d, ropeS_d = G["ropeC"], G["ropeS"]

    Kh = C.sb("Kh", [128, 4, S], BF16)
    C.memset("pool", Kh[96:128, :, :], 0.0, ["Kh_pad"])
    Vt = C.sb("Vt", [128, 32, 4, 128], BF16)
    C.memset("pool", Vt[:, :, :, 64:65], 1.0, ["Vt_%d" % c for c in range(8)])
    C.memset("pool", Vt[:, :, :, 65:128], 0.0, ["Vt_pad"])
    nTs = [C.sb("nT%d" % i, [128, 8, 512], BF16) for i in range(2)]
    Qhs = [C.sb("Qh%d" % i, [128, 4, 512], BF16) for i in range(2)]
    for i in range(2):
        C.memset("pool", Qhs[i][96:128, :, :], 0.0, ["Qh_pad"])
    zf = C.sb("zf", [128, 3, 512], F32)
    sq = C.sb("sq", [128, 3, 512], F32)
    rr = C.sb("rr", [128, 512], F32)
    cqn = C.sb("cqn", [128, 3, 512], BF16)
    ckvn = C.sb("ckvn", [128, 2, 512], BF16)
    Ct = C.sb("Ct", [96, 512], F32)
    St = C.sb("St", [96, 512], F32)
    t1 = C.sb("t1", [96, 512], F32)
    t2 = C.sb("t2", [96, 512], F32)
    pts = [C.sb("pt%d" % i, [128, 512], BF16) for i in range(4)]
    rsrow = C.sb("rsrow", [65, 512], F32)
    rec4 = C.sb("rec4", [128, 4], F32)
    wb = C.sb("wb", [128, 4, 64], F32)
    bcs = C.sb("bcs", [64, 512], F32)
    ots = [C.sb("ot%d" % i, [64, 512], BF16) for i in range(2)]

    PJ = [0, 1]
    SB_ = [2, 3, 4]
    PO = [5, 6]
    pendA = []
    pendB = []

    def flushA():
        while pendA:
            pendA.pop(0)()

    def flushB():
        flushA()
        while pendB:
            pendB.pop(0)()

    def flush():
        flushB()

    def latent(c0, nm, dim, dst, dkey, nT, nkeys, gl, glkey):
        for m in range(nm):
            pj, pk = C.bank("pj", PJ)
            C.mmg(pj[:, :], [(wA[:, k, c0 + m * 128:c0 + (m + 1) * 128], nT[:, k, :]) for k in range(8)],
                  ["wA_sb"] + nkeys, [pk])
            C.act(zf[:, m, :], pj[:, :], AF.Copy, [pk], ["zf%d" % m])
            C.act(sq[:, m, :], pj[:, :], AF.Square, [pk], ["sq%d" % m])
        pj, pk = C.bank("pj", PJ)
        C.mmg(pj[:, :], [(C.onesf[:, :], sq[:, m, :]) for m in range(nm)], ["onesf"] + ["sq%d" % m for m in range(nm)], [pk])
        C.act(rr[:], pj[:, :], AF.Sqrt, [pk, "epsn"], ["rr"], bias=C.epsn[:], scale=1.0 / dim)
        C.recip(rr[:], rr[:], ["rr"], ["rr"])
        for m in range(nm):
            C.stt("dve", dst[:, m, :], zf[:, m, :], gl[:, m:m + 1], rr[:], ALU.mult, ALU.mult, ["zf%d" % m, "rr", glkey], [dkey])

    def stage_T(tc):
        nT = nTs[tc % 2]
        nkeys = []
        for ti in range(4):
            hb = C.rot("ht", 2)
            P.dma("sp", C.ht[hb][:], hsrc(4 * tc + ti), reads=[hkey(4 * tc + ti)], writes=["ht%d" % hb])
            nk = "nT%d_%d" % (tc % 2, ti)
            C.norm_T(C.ht[hb][:], "ht%d" % hb, nT, nk, ti * 128, gA, "gA_sb")
            nkeys.append(nk)
        return nT, nkeys

    def stage_P1(tc, nT, nkeys):
        t0 = tc * 512
        latent(0, 3, 384.0, cqn, "cqn", nT, nkeys, gq, "gq_sb")
        latent(384, 2, 256.0, ckvn, "ckvn", nT, nkeys, gkv, "gkv_sb")
        P.dma("sp", Ct[64:96, :], ropeC_d[:, t0:t0 + 512], writes=["Ct"])
        P.dma("sp", St[64:96, :], ropeS_d[:, t0:t0 + 512], writes=["St"])
        pA, pAk = C.bank("pj", PJ)
        C.mmg(pA[0:96, :], [(wA[:, k, 640:736], nT[:, k, :]) for k in range(8)], ["wA_sb"] + nkeys, [pAk])
        C.tt("dve", t1[64:96, :], pA[64:96, :], Ct[64:96, :], ALU.mult, [pAk, "Ct"], ["t1"])
        pB, pBk = C.bank("pj", PJ)
        C.mmg(pB[0:96, :], [(wA[:, k, 736:832], nT[:, k, :]) for k in range(8)], ["wA_sb"] + nkeys, [pBk])
        C.tt("dve", t2[64:96, :], pB[64:96, :], St[64:96, :], ALU.mult, [pBk, "St"], ["t2"])
        for hh in range(4):
            C.tt("pool", Kh[64:96, hh, t0:t0 + 512], t1[64:96, :], t2[64:96, :], ALU.add, ["t1", "t2"], ["Kh_%d" % tc])

    def stage_P2(tc):
        t0 = tc * 512
        Qh = Qhs[tc % 2]
        qk = "Qh%d" % (tc % 2)
        for hh in range(4):
            pA, pAk = C.bank("pj", PJ)
            C.mmg(pA[0:96, :], [(wq[:, m, hh * 192:hh * 192 + 96], cqn[:, m, :]) for m in range(3)], ["wq_sb", "cqn"], [pAk])
            C.cp("act", Qh[0:64, hh, :], pA[0:64, :], [pAk], [qk])
            C.tt("dve", t1[64:96, :], pA[64:96, :], Ct[64:96, :], ALU.mult, [pAk, "Ct"], ["t1"])
            pB, pBk = C.bank("pj", PJ)
            C.mmg(pB[0:96, :], [(wq[:, m, hh * 192 + 96:hh * 192 + 192], cqn[:, m, :]) for m in range(3)], ["wq_sb", "cqn"], [pBk])
            C.tt("dve", t2[64:96, :], pB[64:96, :], St[64:96, :], ALU.mult, [pBk, "St"], ["t2"])
            C.tt("pool", Qh[64:96, hh, :], t1[64:96, :], t2[64:96, :], ALU.add, ["t1", "t2"], [qk])
        for hh in range(4):
            pj, pk = C.bank("pj", PJ)
            C.mmg(pj[0:64, :], [(wkv[:, j, hh * 64:(hh + 1) * 64], ckvn[:, j, :]) for j in range(2)], ["wkv_sb", "ckvn"], [pk])
            C.cp("act", Kh[0:64, hh, t0:t0 + 512], pj[0:64, :], [pk], ["Kh_%d" % tc])
        for ti in range(4):
            pj, pk = C.bank("pj", PJ)
            C.mmg(pj[:, 0:256], [(ckvn[:, j, ti * 128:(ti + 1) * 128], wkv[:, j, 256:512]) for j in range(2)], ["wkv_sb", "ckvn"], [pk])
            C.cp("act", Vt[:, 4 * tc + ti, :, 0:64], pj[:, 0:256].rearrange("p (h d) -> p h d", h=4), [pk], ["Vt_%d" % tc])

    def head(tc, hh):
        Qh = Qhs[tc % 2]
        qk = "Qh%d" % (tc % 2)
        po, pok = C.bank("po", PO)
        nkt = 4 * tc + 4

        def score(j):
            pairs = [(Kh[:, hh, j * 128:(j + 1) * 128], Qh[:, hh, :])]
            rd = [qk, "Kh_%d" % (j // 4), "Kh_pad", "Qh_pad"]
            if j >= 4 * tc:
                m = j - 4 * tc
                pairs.append((C.ident[:, :], dmask[:, m * 512:(m + 1) * 512]))
                rd += ["ident", "dmask_sb"]
            return pairs, rd

        def vfn(j):
            return Vt[:, j, hh, :], ["Vt_%d" % (j // 4), "Vt_pad"]
        attn_loop(C, list(range(nkt)), score, SC_MLA, vfn, po, pok, SB_, pts, "pt", after_first=flushA, hooks={2: flushB})

        def finA():
            C.cp("dve", rsrow[64:65, :], po[64:65, :], [pok], ["rsrow"])
            pj, pk = C.bank("pj", PJ)
            C.mmlist([(pj[:, r:r + 1], rsrow[64:65, r * 128:(r + 1) * 128], C.onesf[64:65, 0:1], True, True) for r in range(4)],
                     ["rsrow", "onesf"], [pk])
            C.ts("dve", rec4[:, :], pj[:, 0:4], 1e-30, ALU.add, [pk], ["rec4"])
            C.recip(rec4[:, :], rec4[:, :], ["rec4"], ["rec4"])
            C.cp("dve", wb[:, :, :], rec4[:, 0:4].unsqueeze(2).to_broadcast([128, 4, 64]), ["rec4"], ["wb"])

        def finB():
            pj2, pk2 = C.bank("pj", PJ)
            C.mmlist([(pj2[0:64, r * 128:(r + 1) * 128], wb[:, r, :], C.identf[:, :], True, True) for r in range(4)],
                     ["wb", "identf"], [pk2])
            C.cp("act", bcs[:, :], pj2[0:64, :], [pk2], ["bcs"])
            oi = C.rot("ot", 2)
            C.tt("dve", ots[oi][:, :], po[0:64, :], bcs[:, :], ALU.mult, [pok, "bcs"], ["ot%d" % oi])
            osink(hh, tc, ots[oi], "ot%d" % oi)
        pendA.append(finA)
        pendB.append(finB)

    st = stage_T(0)
    stage_P1(0, *st)
    stage_P2(0)
    for tc in range(8):
        head(tc, 0)
        if tc < 7:
            st = stage_T(tc + 1)
        head(tc, 1)
        if tc < 7:
            stage_P1(tc + 1, *st)
        head(tc, 2)
        if tc < 7:
            stage_P2(tc + 1)
        head(tc, 3)
    flush()


def phase_nsa(C, name, L, G, hsrc, hkey, osink, pre=None):
    P = C.P
    pre = pre or {}
    C.begin_phase(name, keep_top=bool(pre))
    NW = 780
    C.setup_norm()
    gA = C.load_const("gA_sb", L["gAn"], [128, 8], F32)
    wA = pre["wA_sb"] if "wA_sb" in pre else C.load_w("wA_sb", L["wAn"], 8, NW)
    w1k = pre["w1k_sb"] if "w1k_sb" in pre else C.load_w("w1k_sb", L["w1k"], 16, 128)
    w1v = pre["w1v_sb"] if "w1v_sb" in pre else C.load_w("w1v_sb", L["w1v"], 16, 128)
    w2k = C.load_w("w2k_sb", L["w2k"], 1, 64)
    w2v = C.load_w("w2v_sb", L["w2v"], 1, 64)
    posk = C.load_w("posk_sb", L["posk"], 1, 16)
    posv = C.load_w("posv_sb", L["posv"], 1, 16)
    maskc = pre["maskc_sb"] if "maskc_sb" in pre else C.load_const("maskc_sb", G["maskc"], [128, 2 * S], BF16)
    eall = C.sb("eall_sb", [128, S], BF16)
    C.memset("pool", eall[64:128, :], 0.0, ["eall_sb"])
    P.dma("pool", eall[0:64, :], G["eall"], writes=["eall_sb"])
    selbias = pre["selbias_sb"] if "selbias_sb" in pre else C.load_const("selbias_sb", G["selbias"], [128, 2048], BF16)
    ovl = C.load_const("ovl_sb", G["ovl"], [128, 130], BF16)
    selg = C.load_const("selg_sb", G["selg"], [12, 768], F32)
    dm4 = C.load_const("dm4_sb", G["dm4"], [128, 512], BF16)
    wm4 = C.load_const("wm4_sb", G["wm4"], [128, 512], BF16)

    Qa = C.sb("Qa", [128, 32, 512], BF16)
    Kw = C.sb("Kw", [128, S], BF16)
    Ks = C.sb("Ks", [128, S], BF16)
    Kc = C.sb("Kc", [128, 256], BF16)
    C.memset("pool", Qa[64:128, :, :], 0.0, ["Qa_aug"])
    C.memset("pool", Kw[64:128, :], 0.0, ["Kw_aug"])
    C.memset("pool", Ks[64:128, :], 0.0, ["Ks_aug"])
    C.memset("pool", Kc[64:128, :], 0.0, ["Kc_aug"])
    P.dma("pool", Qa[64:68, :, :], G["qaug"].rearrange("p (a b) -> p a b", a=32), writes=["Qa_aug"])
    P.dma("pool", Kw[64:68, :], G["kaug"], writes=["Kw_aug"])
    P.dma("pool", Ks[64:68, :], G["kaug"], writes=["Ks_aug"])
    P.dma("pool", Kc[64:68, :], G["kaugc"], writes=["Kc_aug"])
    kc2 = C.sb("kc2", [128, S + 32], BF16)
    vc2 = C.sb("vc2", [128, S + 32], BF16)
    C.memset("pool", kc2[:, S:S + 32], 0.0, ["kc2_tail"])
    C.memset("pool", vc2[:, S:S + 32], 0.0, ["vc2_tail"])
    Vs = C.sb("Vs", [128, 32, 128], BF16)
    Vw = C.sb("Vw", [128, 32, 128], BF16)
    Vc = C.sb("Vc", [128, 2, 128], BF16)
    for (vt_, keys_) in ((Vs, ["Vs_%d" % c for c in range(8)]), (Vw, ["Vw_%d" % c for c in range(8)]), (Vc, ["Vc"])):
        C.memset("pool", vt_[:, :, 65:128], 0.0, keys_)
        C.memset("pool", vt_[:, :, 64:65], 1.0, keys_)
    Gtm = C.sb("Gtm", [128, 32, 12], F32)
    nTs = [C.sb("nT%d" % i, [128, 8, 512], BF16) for i in range(2)]

    PJ = [0, 1]
    SB_ = [2, 3]
    POC, POS, POW = 4, 5, 6

    for tc in range(8):
        t0 = tc * 512
        nb_ = C.rot("nT", 2)
        nT = nTs[nb_]
        nkeys = []
        for ti in range(4):
            hb = C.rot("ht", 2)
            P.dma("sp", C.ht[hb][:], hsrc(4 * tc + ti), reads=[hkey(4 * tc + ti)], writes=["ht%d" % hb])
            nk = "nT%d_%d" % (nb_, ti)
            C.norm_T(C.ht[hb][:], "ht%d" % hb, nT, nk, ti * 128, gA, "gA_sb")
            nkeys.append(nk)

        def proj(c0, m, rows=128):
            pj, pk = C.bank("pj", PJ)
            C.mmg(pj[0:rows, :], [(wA[:, k, c0:c0 + m], nT[:, k, :]) for k in range(8)], ["wA_sb"] + nkeys, [pk])
            return pj, pk
        for r in range(4):
            pj, pk = proj(r * 64, 64, 64)
            C.cp("act", Qa[0:64, 4 * tc:4 * tc + 4, r * 128:(r + 1) * 128],
                 pj[0:64, :].rearrange("p (a b) -> p a b", a=4), [pk], ["Qa_%d" % tc])
        for (c0, dst, dk) in ((256, kc2, "kc2"), (384, vc2, "vc2")):
            pj, pk = proj(c0, 128)
            C.cp("act", dst[0:64, t0:t0 + 512], pj[0:64, :], [pk], [dk])
            if tc == 0:
                C.cp("dve", dst[64:128, 0:511], pj[64:128, 1:512], [pk], [dk])
            else:
                C.cp("dve", dst[64:128, t0 - 1:t0 + 511], pj[64:128, :], [pk], [dk])
        pj, pk = proj(512, 64, 64)
        C.cp("act", Ks[0:64, t0:t0 + 512], pj[0:64, :], [pk], ["Ks_%d" % tc])
        pj, pk = proj(576, 64, 64)
        C.cp("act", Kw[0:64, t0:t0 + 512], pj[0:64, :], [pk], ["Kw_%d" % tc])
        for ti in range(4):
            pj, pk = C.bank("pj", PJ)
            C.mmg(pj[:, 0:128], [(nT[:, k, ti * 128:(ti + 1) * 128], wA[:, k, 640:768]) for k in range(8)],
                  ["wA_sb"] + nkeys, [pk])
            pg, pgk = C.bank("pj", PJ)
            C.mmg(pg[:, 0:12], [(nT[:, k, ti * 128:(ti + 1) * 128], wA[:, k, 768:780]) for k in range(8)],
                  ["wA_sb"] + nkeys, [pgk])
            C.cp("dve", Gtm[:, 4 * tc + ti, :], pg[:, 0:12], [pgk], ["Gtm_%d" % tc])
            C.cp("act", Vs[:, 4 * tc + ti, 0:64], pj[:, 0:64], [pk], ["Vs_%d" % tc])
            C.cp("dve", Vw[:, 4 * tc + ti, 0:64], pj[:, 64:128], [pk], ["Vw_%d" % tc])

    C.act(Gtm[:, :, :], Gtm[:, :, :], AF.Sigmoid, ["Gtm_%d" % c for c in range(8)], ["Gtm_%d" % c for c in range(8)])
    xs = C.sb("xs", [128, 256], F32)
    x2 = C.sb("x2", [128, 256], F32)
    hid = C.sb("hid", [128, 256], BF16)
    cbias = C.sb("cbias", [128, 1], F32)
    for (src, skey, w1, w1key, pos, poskey, isk) in ((kc2, "kc2", w1k, "w1k_sb", posk, "posk_sb", True),
                                                     (vc2, "vc2", w1v, "w1v_sb", posv, "posv_sb", False)):
        pj, pk = C.bank("pj", PJ)
        C.mmg(pj[:, 0:1], [(w1[:, j, :], pos[:, 0, j:j + 1]) for j in range(16)], [w1key, poskey], [pk])
        C.cp("act", cbias[:], pj[:, 0:1], [pk], ["cbias"])
        pj, pk = C.bank("pj", PJ)
        C.mmg(pj[:, 0:255], [(w1[:, j, :], src[:, 2 * j:2 * j + 16 * 255:16]) for j in range(16)],
              [w1key, skey, skey + "_tail"], [pk])
        C.memset("dve", xs[:, 255:256], 0.0, ["xs"])
        C.act(xs[:, 0:255], pj[:, 0:255], AF.Identity, [pk, "cbias"], ["xs"], bias=cbias[:])
        C.tt("dve", x2[:], xs[:], xs[:], ALU.mult, ["xs"], ["x2"])
        C.ts("dve", x2[:], x2[:], 0.044715, ALU.mult, ["x2"], ["x2"], s2=1.0, op1=ALU.add)
        C.tt("dve", x2[:], x2[:], xs[:], ALU.mult, ["x2", "xs"], ["x2"])
        C.act(x2[:], x2[:], AF.Sigmoid, ["x2"], ["x2"], scale=1.5957691216057308)
        C.tt("dve", hid[:], xs[:], x2[:], ALU.mult, ["x2", "xs"], ["hid"])
        if isk:
            pj, pk = C.bank("pj", PJ)
            C.mm(pj[0:64, 0:256], w2k[:, 0, :], hid[:], True, True, ["w2k_sb", "hid"], [pk])
            C.cp("act", Kc[0:64, :], pj[0:64, 0:256], [pk], ["Kc"])
        else:
            for nt in range(2):
                pj, pk = C.bank("pj", PJ)
                C.mm(pj[:, 0:64], hid[:, nt * 128:(nt + 1) * 128], w2v[:, 0, :], True, True, ["w2v_sb", "hid"], [pk])
                C.cp("act", Vc[:, nt, 0:64], pj[:, 0:64], [pk], ["Vc"])

    ptc = [[C.sb("ptc%d_%d" % (i, j), [128, 512], BF16) for j in range(2)] for i in range(2)]
    pts = [C.sb("pt%d" % i, [128, 512], BF16) for i in range(4)]
    imp = C.sb("imp", [128, 64], F32)
    wk = C.sb("wk", [128, 64], F32)
    m8 = C.sb("m8", [128, 16], F32)
    rci = C.sb("rci", [128, 4], F32)
    selb = C.sb("selb", [128, 64], BF16)
    selT4s = [C.sb("selT4_%d" % i, [128, 512], BF16) for i in range(2)]
    for i in range(2):
        C.memset("pool", selT4s[i][64:128, :], 0.0, ["selT4_%d" % i])
    rsrows = [C.sb("rsrow%d" % i, [65, 512], F32) for i in range(3)]
    bcss = [C.sb("bcs%d" % i, [64, 512], F32) for i in range(3)]
    tmpos = [C.sb("tmpo%d" % i, [64, 512], F32) for i in range(3)]
    rec4s = [C.sb("rec4_%d" % i, [128, 4], F32) for i in range(3)]
    wbs = [C.sb("wb%d" % i, [128, 4, 64], F32) for i in range(3)]
    oacc = [C.sb("oacc%d" % i, [64, 512], F32) for i in range(2)]
    oaccb = [C.sb("oaccb%d" % i, [64, 512], BF16) for i in range(2)]
    PJ = [0, 7]
    SB_ = [1, 2, 3]
    pending = []

    def flush():
        tl = list(pending)
        del pending[:]
        while any(tl):
            for t in tl:
                if t:
                    t.pop(0)()

    def finalize_steps(po, pok, br, qb, first, rec_src=None):
        ai = qb % 2
        acc, ak = oacc[ai], "oacc%d" % ai
        rsrow, bcs, tmpo, rec4, wb = rsrows[br], bcss[br], tmpos[br], rec4s[br], wbs[br]
        rk, bk, tk, r4k, wk_ = "rsrow%d" % br, "bcs%d" % br, "tmpo%d" % br, "rec4_%d" % br, "wb%d" % br

        def s1():
            if rec_src is None:
                C.cp("dve", rsrow[64:65, :], po[64:65, :], [pok], [rk])
                pj, pk = C.bank("pj", PJ)
                C.mmlist([(pj[:, r:r + 1], rsrow[64:65, r * 128:(r + 1) * 128], C.onesf[64:65, 0:1], True, True) for r in range(4)],
                         [rk, "onesf"], [pk])
                C.ts("dve", rec4[:, :], pj[:, 0:4], 1e-30, ALU.add, [pk], [r4k])

        def s2():
            if rec_src is None:
                C.recip(rec4[:, :], rec4[:, :], [r4k], [r4k])
                recap, reckey = rec4, r4k
            else:
                recap, reckey = rec_src
            C.tt("dve", wb[:, :, :], recap[:, 0:4].unsqueeze(2).to_broadcast([128, 4, 64]),
                 Gtm[:, qb, br:12:3].unsqueeze(2).to_broadcast([128, 4, 64]), ALU.mult, [reckey, "Gtm_%d" % (qb // 4)], [wk_])

        def s3():
            pj2, pk2 = C.bank("pj", PJ)
            C.mmlist([(pj2[0:64, r * 128:(r + 1) * 128], wb[:, r, :], C.identf[:, :], True, True) for r in range(4)],
                     [wk_, "identf"], [pk2])
            C.cp("act", bcs[:, :], pj2[0:64, :], [pk2], [bk])

        def s4():
            if first:
                C.tt("dve", acc[:, :], po[0:64, :], bcs[:, :], ALU.mult, [pok, bk], [ak])
            else:
                C.tt("dve", tmpo[:, :], po[0:64, :], bcs[:, :], ALU.mult, [pok, bk], [tk])
                C.tt("dve", acc[:, :], acc[:, :], tmpo[:, :], ALU.add, [tk, ak], [ak])
        return [s1, s2, s3, s4]

    def qinfo(qb):
        return Qa[:, qb, :], ["Qa_%d" % (qb // 4), "Qa_aug"]

    def cmp_stage(qb):
        q_rhs, qkeys = qinfo(qb)
        pc = ptc[qb % 2]
        selT4, stk = selT4s[qb % 2], "selT4_%d" % (qb % 2)
        ntn = 1 if qb < 16 else 2
        po = C.banks[POC]
        for nt in range(ntn):
            sbk, sk = C.bank("s", SB_)
            items = [(sbk[:, :], Kc[:, nt * 128:(nt + 1) * 128], q_rhs, True, False)]
            for r in range(4):
                items.append((sbk[:, r * 128:(r + 1) * 128], C.ident[:, :],
                              maskc[:, nt * S + qb * 128:nt * S + (qb + 1) * 128], False, r == 3))
            C.mmlist(items, qkeys + ["Kc", "Kc_aug", "ident", "maskc_sb"], [sk])
            C.act(pc[nt][:], sbk[:, :], AF.Exp, [sk], ["ptc%d_%d" % (qb % 2, nt)], scale=SC_NSA)
        for nt in range(ntn):
            C.mm(po[:, :], Vc[:, nt, :], pc[nt][:], nt == 0, nt == ntn - 1,
                 ["Vc", "ptc%d_%d" % (qb % 2, nt)], ["bank%d" % POC])
        pj, pk = C.bank("pj", PJ)
        items = []
        for r in range(4):
            for nt in range(ntn):
                items.append((pj[:, r * 65:(r + 1) * 65], pc[nt][:, r * 128:(r + 1) * 128], ovl[:, nt * 65:(nt + 1) * 65],
                              nt == 0, nt == ntn - 1))
        C.mmlist(items, ["ovl_sb"] + ["ptc%d_%d" % (qb % 2, nt) for nt in range(ntn)], [pk])
        for r in range(4):
            C.ts("dve", rci[:, r:r + 1], pj[:, r * 65 + 64:r * 65 + 65], 1e-30, ALU.add, [pk], ["rci"])
        C.recip(rci[:, 0:4], rci[:, 0:4], ["rci"], ["rci"])
        for r in range(4):
            prev = selbias[:, qb * 64:(qb + 1) * 64] if r == 0 else imp[:]
            C.stt("dve", imp[:], pj[:, r * 65:r * 65 + 64], rci[:, r:r + 1], prev, ALU.mult, ALU.add,
                  [pk, "rci", "imp", "selbias_sb"], ["imp"])
        P.op("dve", lambda e: e.max(out=m8[:, 0:8], in_=imp[:]), ["imp"], ["m8"])
        P.op("dve", lambda e: e.match_replace(out=wk[:], in_to_replace=m8[:, 0:8], in_values=imp[:], imm_value=-1e9),
             ["imp", "m8"], ["wk"])
        P.op("dve", lambda e: e.max(out=m8[:, 8:16], in_=wk[:]), ["wk"], ["m8"])
        C.ts("dve", wk[:], imp[:], m8[:, 15:16], ALU.is_ge, ["imp", "m8"], ["wk"])
        C.ts("dve", selb[:], wk[:], -NEG, ALU.mult, ["wk"], ["selb"], s2=NEG, op1=ALU.add)

        def s0(qb=qb, selT4=selT4, stk=stk):
            C.tr(C.pst[0:64, 0:128], selb[:, :], C.ident[:, :], ["selb", "ident"], ["pst"])
            for r in range(4):
                C.cp("act" if r % 2 == 0 else "dve", selT4[0:64, r * 128:(r + 1) * 128], C.pst[0:64, 0:128], ["pst"], [stk])
        C.cp("dve", rec4s[0][:, :], rci[:, 0:4], ["rci"], ["rec4_0"])
        pending.append([s0] + finalize_steps(po, "bank%d" % POC, 0, qb, True, rec_src=(rec4s[0], "rec4_0")))

    def sel_stage(qb):
        q_rhs, qkeys = qinfo(qb)
        selT4, stk = selT4s[qb % 2], "selT4_%d" % (qb % 2)
        po = C.banks[POS]

        def score(kt):
            pairs = [(Ks[:, kt * 128:(kt + 1) * 128], q_rhs), (eall[:, kt * 128:(kt + 1) * 128], selT4[:, :])]
            rd = qkeys + ["Ks_%d" % (kt // 4), "Ks_aug", "eall_sb", stk]
            if kt == qb:
                pairs.append((C.ident[:, :], dm4[:, :]))
                rd += ["ident", "dm4_sb"]
            return pairs, rd
        attn_loop(C, list(range(qb + 1)), score, SC_NSA, lambda kt: (Vs[:, kt, :], ["Vs_%d" % (kt // 4)]),
                  po, "bank%d" % POS, SB_, pts, "pt", after_first=None)

        def s5(qb=qb):
            ai = qb % 2
            C.cp("act", oaccb[ai][:, :], oacc[ai][:, :], ["oacc%d" % ai], ["oaccb%d" % ai])
            osink(qb, oaccb[ai], "oaccb%d" % ai)
        pending.append(finalize_steps(po, "bank%d" % POS, 1, qb, False) + [s5])

    def win_stage(qb):
        q_rhs, qkeys = qinfo(qb)
        po = C.banks[POW]
        k0 = max(0, qb - 4)

        def score(kt):
            pairs = [(Kw[:, kt * 128:(kt + 1) * 128], q_rhs)]
            rd = qkeys + ["Kw_%d" % (kt // 4), "Kw_aug"]
            if kt == qb:
                pairs.append((C.ident[:, :], dm4[:, :]))
                rd += ["ident", "dm4_sb"]
            if kt == qb - 4:
                pairs.append((C.ident[:, :], wm4[:, :]))
                rd += ["ident", "wm4_sb"]
            return pairs, rd
        attn_loop(C, list(range(k0, qb + 1)), score, SC_NSA, lambda kt: (Vw[:, kt, :], ["Vw_%d" % (kt // 4)]),
                  po, "bank%d" % POW, SB_, pts, "pt", after_first=flush)

        pending.append(finalize_steps(po, "bank%d" % POW, 2, qb, False))

    cmp_stage(0)
    for qb in range(32):
        win_stage(qb)
        if qb + 1 < 32:
            cmp_stage(qb + 1)
        sel_stage(qb)
    flush()


def phase_ffn(C, name, L, G, final, hown, hownkey, hhalo, hhalokey, oall, osink):
    P = C.P
    C.begin_phase(name)
    C.setup_norm()
    fl = C.flags
    g2 = C.load_const("g2_sb", L["g2"], [128, 8], F32)
    cw = C.load_const("cw_sb", L["cw"], [128, 176], F32)
    if final:
        gF = C.sb("gF_sb", [128, DM], F32)
        P.dma("pool", gF[:], G["gF"][0:1, :].partition_broadcast(128), writes=["gF_sb"])
    wup = C.sb("wup_sb", [128, 8, 5632], BF16)
    wup_src = L["wup"].rearrange("(k p) c -> p k c", p=128)
    for i0 in range(0, 22, 4):
        n4 = min(4, 22 - i0)
        for base in (i0, 22 + i0):
            P.dma("pool", wup[:, :, base * 128:(base + n4) * 128], wup_src[:, :, base * 128:(base + n4) * 128],
                  writes=["wup_%d" % fc for fc in range(base, base + n4)])
    wdn = C.sb("wdn_sb", [128, 22, DM], BF16)

    def load_wdn():
        for k in range(22):
            P.dma("pool", wdn[:, k, :], L["wdn"][k * 128:(k + 1) * 128, :], writes=["wdn_sb"])
    wo_d = L["wo"]

    NC_ = 256
    aT = [C.sb("aT%d" % i, [128, NC_], BF16) for i in range(4)]
    hm = C.sb("hm", [128, 2, DM], F32)
    n2T = C.sb("n2T", [128, 8, NC_], BF16)
    oTb = C.sb("oTb", [128, 8, NC_], BF16)
    oa = [C.sb("oa%d" % i, [128, NC_], BF16) for i in range(2)]
    ob = [C.sb("ob%d" % i, [128, NC_], BF16) for i in range(2)]
    wob = [C.sb("wob%d" % i, [128, DM], BF16) for i in range(4)]
    ubuf = [C.sb("ubuf%d" % i, [128, NC_ + 2], F32) for i in range(3)]
    tb = [C.sb("tb%d" % i, [128, NC_], F32) for i in range(4)]
    sg = C.sb("sg", [128, NC_], F32)
    carry = C.sb("carry", [128, 44, 2], F32)
    res = C.sb("res", [128, DM], F32)
    ss2 = C.sb("ss2", [128, 1], F32)

    ACC = [0, 1, 2, 3]
    UP = [4, 5, 6]

    def chunk(ci, halo):
        nt_ = 1 if halo else 2
        ncol = nt_ * 128
        c0 = 1920 if halo else ci * 256
        for k in range(8):
            b = C.rot("oa", 2)
            if halo:
                P.dma("sp", oa[b][:, 0:ncol], oall[0][k * 128:(k + 1) * 128, c0:c0 + ncol], reads=["oall0"], writes=["oa%d" % b])
                C.ts("dve", oTb[:, k, 0:ncol], oa[b][:, 0:ncol], fl[:, 1:2], ALU.mult, ["oa%d" % b, "flags"], ["oTb"])
            else:
                P.dma("sp", oa[b][:, 0:ncol], oall[0][k * 128:(k + 1) * 128, c0:c0 + ncol], reads=["oall0"], writes=["oa%d" % b])
                P.dma("sp", ob[b][:, 0:ncol], oall[1][k * 128:(k + 1) * 128, c0:c0 + ncol], reads=["oall1"], writes=["ob%d" % b])
                C.ts("dve", oa[b][:, 0:ncol], oa[b][:, 0:ncol], fl[:, 0:1], ALU.mult, ["oa%d" % b, "flags"], ["oa%d" % b])
                C.stt("dve", oTb[:, k, 0:ncol], ob[b][:, 0:ncol], fl[:, 1:2], oa[b][:, 0:ncol], ALU.mult, ALU.add,
                      ["oa%d" % b, "ob%d" % b, "flags"], ["oTb"])
        for k in range(8):
            wb = C.rot("wob", 4)
            P.dma("pool", wob[wb][:, :], wo_d[k * 128:(k + 1) * 128, :], writes=["wob%d" % wb])
            items = []
            for ti in range(nt_):
                for hf in range(2):
                    items.append((C.banks[ACC[ti * 2 + hf]][:, :], oTb[:, k, ti * 128:(ti + 1) * 128],
                                  wob[wb][:, hf * 512:(hf + 1) * 512], k == 0, k == 7))
            C.mmlist(items, ["oTb", "wob%d" % wb], ["bank%d" % ACC[i] for i in range(nt_ * 2)])
        if ci == 0 and not halo:
            load_wdn()
        for ti in range(nt_):
            hb = C.rot("ht", 2)
            if halo:
                P.dma("sp", C.ht[hb][:], hhalo, reads=[hhalokey], writes=["ht%d" % hb])
                C.ts("dve", C.ht[hb][:], C.ht[hb][:], fl[:, 1:2], ALU.mult, ["ht%d" % hb, "flags"], ["ht%d" % hb])
            else:
                P.dma("sp", C.ht[hb][:], hown(2 * ci + ti), reads=[hownkey(2 * ci + ti)], writes=["ht%d" % hb])
            for hf in range(2):
                C.tt("dve", hm[:, ti, hf * 512:(hf + 1) * 512], C.banks[ACC[ti * 2 + hf]][:, :],
                     C.ht[hb][:, hf * 512:(hf + 1) * 512], ALU.add, ["bank%d" % ACC[ti * 2 + hf], "ht%d" % hb], ["hm%d" % ti])
            C.norm_T(hm[:, ti, :], "hm%d" % ti, n2T, "n2T_%d" % ti, ti * 128, g2, "g2_sb")
        nkeys = ["n2T_%d" % ti for ti in range(nt_)]
        dq = []

        def down(i, ai):
            items = []
            for ti in range(nt_):
                for hf in range(2):
                    items.append((C.banks[ACC[ti * 2 + hf]][:, :], aT[ai][:, ti * 128:(ti + 1) * 128],
                                  wdn[:, i, hf * 512:(hf + 1) * 512], i == 0, i == 21))
            C.mmlist(items, ["aT%d" % ai, "wdn_sb"], ["bank%d" % ACC[q] for q in range(nt_ * 2)])
        for i in range(22):
            tfin = []
            for part in range(2):
                fc = i + 22 * part
                up, upk = C.bank("up", UP)
                C.mmg(up[:, 0:ncol], [(wup[:, k, fc * 128:(fc + 1) * 128], n2T[:, k, 0:ncol]) for k in range(8)],
                      ["wup_%d" % fc] + nkeys, [upk])
                ub = C.rot("ubuf", 3)
                u = ubuf[ub]
                uk = "ubuf%d" % ub
                C.cp("pool", u[:, 0:2], carry[:, fc, :], ["carry%d" % fc], [uk])
                C.cp("act", u[:, 2:2 + ncol], up[:, 0:ncol], [upk], [uk])
                if not halo:
                    ta = C.rot("tb", 4)
                    C.act(tb[ta][:, 0:ncol], up[:, 0:ncol], AF.Identity, [upk, "cw_sb"], ["tb%d" % ta],
                          bias=cw[:, fc * 4 + 3:fc * 4 + 4], scale=cw[:, fc * 4 + 2:fc * 4 + 3])
                    C.stt("dve", tb[ta][:, 0:ncol], u[:, 1:1 + ncol], cw[:, fc * 4 + 1:fc * 4 + 2], tb[ta][:, 0:ncol],
                          ALU.mult, ALU.add, [uk, "cw_sb", "tb%d" % ta], ["tb%d" % ta])
                    C.stt("dve", tb[ta][:, 0:ncol], u[:, 0:ncol], cw[:, fc * 4:fc * 4 + 1], tb[ta][:, 0:ncol],
                          ALU.mult, ALU.add, [uk, "cw_sb", "tb%d" % ta], ["tb%d" % ta])
                    tfin.append(ta)
                C.cp("pool", carry[:, fc, :], u[:, ncol:ncol + 2], [uk], ["carry%d" % fc])
            if not halo:
                C.act(sg[:, 0:ncol], tb[tfin[0]][:, 0:ncol], AF.Silu, ["tb%d" % tfin[0]], ["sg"])
                ai = C.rot("aT", 4)
                C.tt("dve", aT[ai][:, 0:ncol], sg[:, 0:ncol], tb[tfin[1]][:, 0:ncol], ALU.mult,
                     ["sg", "tb%d" % tfin[1]], ["aT%d" % ai])
                dq.append((i, ai))
                if len(dq) > 2:
                    down(*dq.pop(0))
        while dq:
            down(*dq.pop(0))
        if halo:
            return
        for ti in range(nt_):
            for hf in range(2):
                bk = ACC[ti * 2 + hf]
                C.tt("dve", res[:, hf * 512:(hf + 1) * 512], C.banks[bk][:, :], hm[:, ti, hf * 512:(hf + 1) * 512],
                     ALU.add, ["bank%d" % bk, "hm%d" % ti], ["res"])
            if final:
                C.memset("dve", ss2[:], 0.0, ["ss2"])
                C.act(C.junk[:], res[:], AF.Square, ["res", "ss2"], ["junk", "ss2"], accum=ss2[:])
                C.act(ss2[:], ss2[:], AF.Sqrt, ["ss2", "epsn"], ["ss2"], bias=C.epsn[:], scale=1.0 / DM)
                C.recip(ss2[:], ss2[:], ["ss2"], ["ss2"])
                C.stt("dve", res[:], res[:], ss2[:, 0:1], gF[:], ALU.mult, ALU.mult, ["res", "ss2", "gF_sb"], ["res"])
            osink(2 * ci + ti, res, "res")

    C.memset("pool", carry[:], 0.0, ["carry%d" % fc for fc in range(44)])
    chunk(0, True)
    for c in range(8):
        chunk(c, False)


LAYER_IN = [("wAm", [DM, 832]), ("gAm", [128, 8]), ("wq", [384, 768]), ("gq", [128, 3]), ("wkv", [256, 512]),
            ("gkv", [128, 2]), ("wAn", [DM, 780]), ("gAn", [128, 8]), ("w1k", [2048, 128]), ("w1v", [2048, 128]),
            ("w2k", [128, 64]), ("w2v", [128, 64]), ("posk", [128, 16]), ("posv", [128, 16]),
            ("wo", [DM, DM]), ("wup", [DM, 5632]), ("wdn", [2816, DM]), ("g2", [128, 8]), ("cw", [128, 176])]
GLOB_IN = [("ropeC", [32, S], F32), ("ropeS", [32, S], F32), ("dmask", [128, 2048], BF16),
           ("maskc", [128, 2 * S], BF16), ("eall", [64, S], BF16), ("selbias", [128, 2048], BF16),
           ("ovl", [128, 130], BF16), ("selg", [12, 768], F32), ("dm4", [128, 512], BF16), ("wm4", [128, 512], BF16),
           ("qaug", [4, 32 * 512], BF16), ("kaug", [4, S], BF16), ("kaugc", [4, 256], BF16), ("gF", [1, DM], F32)]
GROUPS = [[0, 1], [2, 3], [4, 5], [6, 7]]


def build_fused(nlayers=2):
    nc = bass.Bass("TRN2", target_bir_lowering=False)
    C = Ctx(nc)
    P = C.P
    x_d = C.dram("x", [S, DM], F32)
    xown_d = C.dram("xown", [2048, DM], F32)
    xhalo_d = C.dram("xhalo", [128, DM], F32)
    flags_d = C.dram("flags", [128, 2], F32)
    G = {n: C.dram(n, sh, dt) for (n, sh, dt) in GLOB_IN}
    Ls = [{n: C.dram("%s_%d" % (n, l), sh, F32) for (n, sh) in LAYER_IN} for l in range(nlayers)]
    out_d = C.dram("hout", [2048, DM], F32, out=True)
    P.dma("pool", C.flags[:], flags_d, writes=["flags"])

    omy = [[nc.dram_tensor("omy_%d_%d" % (l, c), [512, 2048], BF16) for c in range(2)] for l in range(nlayers)]
    oall = [[nc.dram_tensor("oall_%d_%d" % (l, c), [1024, 2048], BF16) for c in range(2)] for l in range(nlayers)]
    hmy = [nc.dram_tensor("hmy_%d" % j, [512, DM], F32) for j in range(4)]
    hall = [nc.dram_tensor("hall_%d" % j, [1024, DM], F32) for j in range(4)]

    for l in range(nlayers):
        if l == 0:
            hsrc = lambda g: x_d[g * 128:(g + 1) * 128, :]
            hkey = lambda g: "x"
        else:
            def hsrc(g):
                r, w = g // 16, g % 16
                return hall[w // 4][r * 512 + (w % 4) * 128:r * 512 + (w % 4) * 128 + 128, :]
            hkey = lambda g: "hall%d" % ((g % 16) // 4)

        def osink_mla(hh, tc, ot, otkey, l=l):
            c = tc // 4
            col = (tc % 4) * 512
            P.dma("sp", omy[l][c][hh * 64:(hh + 1) * 64, col:col + 512], ot[:, :], reads=[otkey], writes=["omy%d" % c])

        def osink_nsa(qb, ot, otkey, l=l):
            c = qb // 16
            col = (qb % 16) * 128
            P.dma("sp", omy[l][c][256:512, :].rearrange("(r d) t -> d r t", d=64)[:, :, col:col + 128],
                  ot[:, :].rearrange("p (r t) -> p r t", r=4), reads=[otkey], writes=["omy%d" % c])
            if qb % 16 == 15:
                P.cc("AllGather", GROUPS, omy[l][c].ap().opt(), oall[l][c].ap().opt(), reads=["omy%d" % c], writes=["oall%d" % c])

        pre = {}

        def prefetch_nsa(l=l, pre=pre):
            def ld(key, name, dram, nk, ncols):
                t = C.sb_top("np%d_%s" % (l, name), [128, nk, ncols], BF16)
                for k in range(nk):
                    P.dma("pool", t[:, k, :], dram[k * 128:(k + 1) * 128, :], writes=[key])
                pre[key] = t
            ld("wA_sb", "wA", Ls[l]["wAn"], 8, 780)
            ld("w1k_sb", "w1k", Ls[l]["w1k"], 16, 128)
            ld("w1v_sb", "w1v", Ls[l]["w1v"], 16, 128)
            t = C.sb_top("np%d_maskc" % l, [128, 2 * S], BF16)
            P.dma("pool", t[:], G["maskc"], writes=["maskc_sb"])
            pre["maskc_sb"] = t
            t = C.sb_top("np%d_selbias" % l, [128, 2048], BF16)
            P.dma("pool", t[:], G["selbias"], writes=["selbias_sb"])
            pre["selbias_sb"] = t
        phase_mla(C, "m%d_" % l, Ls[l], G, hsrc, hkey, osink_mla, after_weights=prefetch_nsa)
        phase_nsa(C, "n%d_" % l, Ls[l], G, hsrc, hkey, osink_nsa, pre=pre)
        final = (l == nlayers - 1)
        if l == 0:
            hown = lambda t: xown_d[t * 128:(t + 1) * 128, :]
            hownkey = lambda t: "xown"
            hhalo, hhalokey = xhalo_d, "xhalo"
        else:
            hown = lambda t: hmy[t // 4][(t % 4) * 128:(t % 4) * 128 + 128, :]
            hownkey = lambda t: "hmy%d" % (t // 4)
            hhalo, hhalokey = hall[3][384:512, :], "hall3"
        if final:
            def osink_ffn(t, res, rkey):
                P.dma("sp", out_d[t * 128:(t + 1) * 128, :], res[:], reads=[rkey])
        else:
            def osink_ffn(t, res, rkey):
                P.dma("sp", hmy[t // 4][(t % 4) * 128:(t % 4) * 128 + 128, :], res[:], reads=[rkey], writes=["hmy%d" % (t // 4)])
                if t % 4 == 3:
                    j = t // 4
                    P.cc("AllGather", GROUPS, hmy[j].ap().opt(), hall[j].ap().opt(), reads=["hmy%d" % j], writes=["hall%d" % j])
        phase_ffn(C, "f%d_" % l, Ls[l], G, final, hown, hownkey, hhalo, hhalokey, oall[l], osink_ffn)
    P.emit()
    return nc


def _pk(g, nk):
    return np.ascontiguousarray(np.asarray(g, np.float32).reshape(nk, 128).T)


def _consts():
    c = {}
    p = np.arange(128)[:, None]
    i512 = np.arange(512)[None, :]
    dm = np.zeros((128, 4, 512), np.float32)
    for m in range(4):
        dm[:, m, :] = np.where(128 * m + p <= i512, 0.0, NEG)
    c["dmask"] = dm.reshape(128, 2048).astype(NPBF)
    i128 = np.arange(128)[None, :]
    c["dm4"] = np.tile(np.where(p <= i128, 0.0, NEG), (1, 4)).astype(NPBF)
    c["wm4"] = np.tile(np.where(i128 < p, 0.0, NEG), (1, 4)).astype(NPBF)
    n = np.arange(256)[:, None]
    t = np.arange(S)[None, :]
    mc = np.where((t >= 16 * n + 31) & (n <= 254), 0.0, NEG).astype(np.float32)
    c["maskc"] = np.ascontiguousarray(mc.reshape(2, 128, S).transpose(1, 0, 2).reshape(128, 2 * S)).astype(NPBF)
    j = np.arange(64)[:, None]
    c["eall"] = (np.arange(S)[None, :] // 64 == j).astype(np.float32).astype(NPBF)
    tt_ = np.arange(S)
    cur = (tt_ // 64)[:, None]
    jj = np.arange(64)[None, :]
    sbias = np.zeros((S, 64), np.float32)
    sbias[np.broadcast_to(jj > cur, (S, 64))] = -1e4
    sbias[np.broadcast_to((jj == 0) | (jj == cur) | (jj == cur - 1), (S, 64))] = 1e4
    c["selbias"] = np.ascontiguousarray(sbias.reshape(32, 128, 64).transpose(1, 0, 2).reshape(128, 2048)).astype(NPBF)
    cs = (np.arange(256) * 16)[:, None]
    ss_ = (np.arange(64) * 64)[None, :]
    ov = ((cs < ss_ + 64) & (cs + 32 > ss_)).astype(np.float32)
    ov[255] = 0.0
    ov1 = np.concatenate([ov, np.ones((256, 1), np.float32)], 1)
    c["ovl"] = np.ascontiguousarray(ov1.reshape(2, 128, 65).transpose(1, 0, 2).reshape(128, 130)).astype(NPBF)
    sg = np.zeros((12, 12, 64), np.float32)
    for g in range(12):
        sg[g, g, :] = 1.0
    c["selg"] = sg.reshape(12, 768)
    k = np.arange(S)
    c["kaug"] = np.stack([np.ones(S), np.ones(S), k // 64, k % 64]).astype(np.float32).astype(NPBF)
    e = np.arange(256) * 16 + 31
    c["kaugc"] = np.stack([np.ones(256), np.ones(256), e // 64, e % 64]).astype(np.float32).astype(NPBF)
    inv = 1.0 / (10000.0 ** (np.arange(0, 32, 2, dtype=np.float32) / 32))
    ang = np.arange(S, dtype=np.float32)[:, None] * inv[None, :]
    cos, sin = np.cos(ang).T.astype(np.float32), np.sin(ang).T.astype(np.float32)
    c["ropeC"] = np.ascontiguousarray(np.concatenate([cos, cos], 0))
    c["ropeS"] = np.ascontiguousarray(np.concatenate([-sin, sin], 0))
    return c


def _qaug(group):
    slopes = np.exp2(-8.0 * np.arange(1, 9, dtype=np.float32) / 8)
    t = np.arange(S).reshape(32, 1, 128)
    out = np.zeros((4, 32, 4, 128), np.float32)
    for r in range(4):
        a = slopes[group * 4 + r] / SC_NSA
        out[0, :, r, :] = (-a * 64 * (t // 64))[:, 0, :]
        out[1, :, r, :] = (-a * (t % 64))[:, 0, :]
        out[2, :, r, :] = a * 64
        out[3, :, r, :] = a
    return out.reshape(4, 32 * 512).astype(NPBF)


_PROG = {}


def _prog(nlayers=2):
    if nlayers not in _PROG:
        _PROG[nlayers] = build_fused(nlayers)
    return _PROG[nlayers]


def _layer_maps(l, I, c):
    m = {}
    w_in, w_uq, w_ukv = I["w_in"][l], I["w_uq"][l], I["w_ukv"][l]
    sw = list(range(656, 672)) + list(range(640, 656))
    colsA = list(range(0, 640)) + list(range(0, 64)) + list(range(640, 672)) + list(range(0, 64)) + sw
    m["wAm"] = np.ascontiguousarray(w_in[:, colsA])
    m["gAm"] = _pk(I["attn_norm"][l], 8)
    qc, kc, vc = [], [], []
    for hh in range(4 * c, 4 * c + 4):
        base = 96 * hh
        nope = list(range(base, base + 64))
        rope = list(range(base + 64, base + 96))
        qc += nope + rope + nope + rope[16:] + rope[:16]
        kc += list(range(128 * hh, 128 * hh + 64))
        vc += list(range(128 * hh + 64, 128 * hh + 128))
    m["wq"] = np.ascontiguousarray(w_uq[:, qc])
    m["gq"] = _pk(I["q_norm"][l], 3)
    m["wkv"] = np.ascontiguousarray(w_ukv[:, kc + vc])
    m["gkv"] = _pk(I["kv_norm"][l], 2)
    g = c
    q0 = 672 + 256 * g
    o = 1184
    rng = lambda a: list(range(a, a + 64))
    kcc, vcc, ksc = rng(o + 64 * g), rng(o + 128 + 64 * g), rng(o + 256 + 64 * g)
    vsc, kwc, vwc = rng(o + 384 + 64 * g), rng(o + 512 + 64 * g), rng(o + 640 + 64 * g)
    gtc = list(range(1952 + 12 * g, 1952 + 12 * g + 12))
    cols = list(range(q0, q0 + 256)) + kcc + kcc + vcc + vcc + ksc + kwc + vsc + vwc + gtc
    m["wAn"] = np.ascontiguousarray(w_in[:, cols])
    m["gAn"] = m["gAm"]
    posT = lambda pz: np.ascontiguousarray(np.asarray(pz, np.float32).reshape(16, 128).T)
    m["w1k"] = np.ascontiguousarray(I["cmp_k_w1"][l].reshape(2048, 128))
    m["w1v"] = np.ascontiguousarray(I["cmp_v_w1"][l].reshape(2048, 128))
    m["w2k"] = np.ascontiguousarray(I["cmp_k_w2"][l])
    m["w2v"] = np.ascontiguousarray(I["cmp_v_w2"][l])
    m["posk"] = posT(I["cmp_pos_k"][l])
    m["posv"] = posT(I["cmp_pos_v"][l])
    perm = list(range(0, 256)) + list(range(512, 768)) + list(range(256, 512)) + list(range(768, 1024))
    m["wo"] = np.ascontiguousarray(I["w_o"][l][perm, :])
    m["wup"] = np.ascontiguousarray(I["w_up"][l])
    m["wdn"] = np.ascontiguousarray(I["w_down"][l])
    m["g2"] = _pk(I["ffn_norm"][l], 8)
    cwv = np.stack([I["conv_w"][l][0], I["conv_w"][l][1], I["conv_w"][l][2], I["conv_b"][l]], -1)
    m["cw"] = np.ascontiguousarray(cwv.reshape(44, 128, 4).transpose(1, 0, 2).reshape(128, 176)).astype(np.float32)
    return m


def make_maps(I, nlayers=2):
    cst = _consts()
    lm = [[_layer_maps(l, I, c) for c in range(2)] for l in range(nlayers)]
    maps = []
    for b in range(4):
        for c in range(2):
            m = {"x": np.ascontiguousarray(I["x"][b]),
                 "xown": np.ascontiguousarray(I["x"][b][2048 * c:2048 * c + 2048]),
                 "xhalo": np.ascontiguousarray(I["x"][b][1920:2048]),
                 "flags": np.ascontiguousarray(np.tile(np.array([[1.0 - c, float(c)]], np.float32), (128, 1)))}
            for (n, sh, dt) in GLOB_IN:
                if n == "qaug":
                    m[n] = _qaug(c)
                elif n == "gF":
                    m[n] = np.ascontiguousarray(np.asarray(I["final_norm"], np.float32).reshape(1, DM))
                else:
                    m[n] = cst[n]
            for l in range(nlayers):
                for k, v in lm[l][c].items():
                    m["%s_%d" % (k, l)] = v
            maps.append(m)
    return maps


def kernel(**inputs):
    I = {k: np.asarray(v, dtype=np.float32) for k, v in inputs.items()}
    res = run_bass_kernel_spmd(_prog(2), make_maps(I, 2), core_ids=list(range(8))).results
    out = np.empty((4, S, DM), np.float32)
    for b in range(4):
        for c in range(2):
            out[b, 2048 * c:2048 * c + 2048] = np.asarray(res[2 * b + c]["hout"])
    return out
```

```python
import numpy as np
import ml_dtypes
import concourse.bass as bass
import concourse.mybir as mybir
from concourse.bass_utils import run_bass_kernel_spmd

F32 = mybir.dt.float32
BF16 = mybir.dt.bfloat16
ALU = mybir.AluOpType
AF = mybir.ActivationFunctionType
AX = mybir.AxisListType
NPBF = ml_dtypes.bfloat16

ENGS = ["pe", "act", "dve", "pool", "sp"]
DMA_POOL = 12
S = 4096
DM = 1024
NEG = -30000.0
SC_MLA = 96 ** -0.5
SC_NSA = 0.125
EPS = 1e-6


class Prog:
    def __init__(self, nc):
        self.nc = nc
        self.ops = {e: [] for e in ENGS}
        self.lastw = {}
        self.readers = {}
        self.dma_n = {e: 0 for e in ENGS + ["cc"]}
        self.dma_sem_cnt = {}
        self.last_c = {}
        self.last_d = {}

    def sb(self, name, shape, dt):
        return self.nc.alloc_sbuf_tensor(name, list(shape), dt)

    def ps(self, name, shape, dt=F32):
        return self.nc.alloc_psum_tensor(name, list(shape), dt)

    def _add(self, eng, fn, reads, writes, dma, cc=False):
        op = dict(eng=eng, fn=fn, deps=[], dma=dma, marked=False, inc=(1 if cc else 16))
        deps = []
        for k in reads:
            w = self.lastw.get(k)
            if w is not None:
                deps.append(w)
        for k in writes:
            w = self.lastw.get(k)
            if w is not None:
                deps.append(w)
            deps.extend(self.readers.get(k, ()))
        seen = set()
        for d in deps:
            if id(d) in seen or d is op:
                continue
            seen.add(id(d))
            if (not d["dma"]) and d["eng"] == eng and eng in ("pe", "sp"):
                continue
            op["deps"].append(d)
            d["marked"] = True
        if dma:
            qn = "cc" if cc else eng
            q = self.dma_n[qn]
            self.dma_n[qn] += 1
            semkey = (qn, q % (4 if cc else DMA_POOL))
            m = self.dma_sem_cnt.get(semkey, 0) + 1
            self.dma_sem_cnt[semkey] = m
            op["dsem"] = semkey
            op["dval"] = op["inc"] * m
            op["marked"] = True
            self.last_d[semkey] = op
        else:
            self.last_c[eng] = op
        for k in reads:
            self.readers.setdefault(k, []).append(op)
        for k in writes:
            self.lastw[k] = op
            self.readers[k] = []
        self.ops[eng].append(op)
        return op

    def op(self, eng, fn, reads=(), writes=()):
        return self._add(eng, fn, list(reads), list(writes), False)

    def dma(self, eng, out, in_, reads=(), writes=()):
        return self._add(eng, lambda e: e.dma_start(out=out, in_=in_), list(reads), list(writes), True)

    def cc(self, kind, groups, in_ap, out_ap, reads=(), writes=()):
        return self._add("pool", lambda e: e.collective_compute(kind, ALU.bypass, replica_groups=groups,
                                                                ins=[in_ap], outs=[out_ap]),
                         list(reads), list(writes), True, cc=True)

    def barrier(self):
        deps = list(self.last_c.values()) + list(self.last_d.values())
        for d in deps:
            d["marked"] = True
        for e in ENGS:
            self.ops[e].append(dict(eng=e, fn=None, deps=list(deps), dma=False, marked=False, inc=0))
        self.lastw = {}
        self.readers = {}

    def emit(self):
        nc = self.nc
        csem = {e: nc.alloc_semaphore("c_" + e) for e in ENGS}
        dsem = {}
        for (qn, i) in self.dma_sem_cnt:
            dsem[(qn, i)] = nc.alloc_semaphore("d_%s_%d" % (qn, i))
        for e in ENGS:
            c = 0
            for o in self.ops[e]:
                if o["dma"] or o["fn"] is None:
                    continue
                if o["marked"]:
                    c += 1
                    o["cval"] = c
        all_dma = [o for e in ENGS for o in self.ops[e] if o["dma"]]

        def run(e, eng):
            seen = {}

            def wait(sem_key, sem, val):
                if seen.get(sem_key, 0) >= val:
                    return
                seen[sem_key] = val
                eng.wait_ge(sem, val)

            for o in self.ops[e]:
                for d in o["deps"]:
                    if d["dma"]:
                        wait(d["dsem"], dsem[d["dsem"]], d["dval"])
                    else:
                        wait(("c", d["eng"]), csem[d["eng"]], d["cval"])
                if o["fn"] is None:
                    continue
                if o["dma"]:
                    if o["dval"] > o["inc"]:
                        wait(o["dsem"], dsem[o["dsem"]], o["dval"] - o["inc"])
                    o["fn"](eng).then_inc(dsem[o["dsem"]], o["inc"])
                else:
                    ins = o["fn"](eng)
                    if o["marked"]:
                        ins.then_inc(csem[e], 1)
            if e == "sp":
                last = {}
                for o in all_dma:
                    last[o["dsem"]] = max(last.get(o["dsem"], 0), o["dval"])
                for k, v in last.items():
                    eng.wait_ge(dsem[k], v)

        with nc.Block() as block:
            @block.tensor
            def _(eng):
                run("pe", eng)

            @block.scalar
            def _(eng):
                run("act", eng)

            @block.vector
            def _(eng):
                run("dve", eng)

            @block.gpsimd
            def _(eng):
                run("pool", eng)

            @block.sync
            def _(eng):
                run("sp", eng)


def _nbytes(shape, dt):
    n = 1
    for d in shape[1:]:
        n *= d
    return n * (4 if dt == F32 else 2)


class Ctx:
    def __init__(self, nc):
        self.nc = nc
        self.P = P = Prog(nc)
        self.rots = {}
        self.pname = "g_"
        self.ident = nc.alloc_sbuf_tensor("ident", [128, 128], BF16)
        self.identf = nc.alloc_sbuf_tensor("identf", [128, 128], F32)
        self.onesf = nc.alloc_sbuf_tensor("onesf", [128, 128], F32)
        self.epsn = nc.alloc_sbuf_tensor("epsn", [128, 1], F32)
        self.flags = nc.alloc_sbuf_tensor("flags_sb", [128, 2], F32)
        identf, ident, onesf, epsn = self.identf, self.ident, self.onesf, self.epsn
        P.op("pool", lambda e: e.memset(identf[:], 0.0), writes=["identf"])
        P.op("pool", lambda e: e.affine_select(out=identf[:], in_=identf[:], pattern=[[-1, 128]],
                                                compare_op=ALU.not_equal, fill=1.0, base=0, channel_multiplier=1),
             reads=["identf"], writes=["identf"])
        P.op("dve", lambda e: e.tensor_copy(out=ident[:], in_=identf[:]), reads=["identf"], writes=["ident"])
        P.op("dve", lambda e: e.memset(onesf[:], 1.0), writes=["onesf"])
        P.op("dve", lambda e: e.memset(epsn[:], EPS), writes=["epsn"])
        self.pst = P.ps("pst", [128, 1024], BF16)
        self.banks = [P.ps("bank%d" % i, [128, 512], F32) for i in range(7)]
        self.banks.append(self.pst.bitcast(F32))
        self.base = ((int(nc.sbuf_base) + 63) // 64) * 64
        self.top = int(nc.sbuf_top)
        self.off = self.base
        self.top_off = self.top

    def begin_phase(self, name, keep_top=False):
        self.P.barrier()
        self.pname = name
        self.off = self.base
        if not keep_top:
            self.top_off = self.top

    def sb_top(self, name, shape, dt):
        nb = ((_nbytes(shape, dt) + 31) // 32) * 32
        self.top_off -= nb
        assert self.top_off >= self.off, ("SBUF overflow (top)", name)
        return self.nc.alloc_sbuf_tensor_at(name, list(shape), dt, offset=self.top_off)

    def sb(self, name, shape, dt):
        nb = ((_nbytes(shape, dt) + 31) // 32) * 32
        assert self.off + nb <= self.top_off, ("SBUF overflow", self.pname, name, self.off + nb - self.top_off)
        t = self.nc.alloc_sbuf_tensor_at(self.pname + name, list(shape), dt, offset=self.off)
        self.off += nb
        return t

    def rot(self, name, n):
        i = self.rots.get(name, 0) % n
        self.rots[name] = (i + 1) % n
        return i

    def dram(self, name, shape, dt, out=False):
        return self.nc.dram_tensor(name, list(shape), dt, kind="ExternalOutput" if out else "ExternalInput").ap()

    def mm(self, out, lhsT, rhs, start, stop, reads, writes):
        return self.P.op("pe", lambda e: e.matmul(out, lhsT=lhsT, rhs=rhs, start=start, stop=stop), reads, writes)

    def mmg(self, out, pairs, reads, writes, start=True, stop=True):
        n = len(pairs)

        def fn(e):
            ins = None
            for i, (l, r) in enumerate(pairs):
                ins = e.matmul(out, lhsT=l, rhs=r, start=(start and i == 0), stop=(stop and i == n - 1))
            return ins
        return self.P.op("pe", fn, reads, writes)

    def mmlist(self, items, reads, writes):
        def fn(e):
            ins = None
            for (o, l, r, st, sp) in items:
                ins = e.matmul(o, lhsT=l, rhs=r, start=st, stop=sp)
            return ins
        return self.P.op("pe", fn, reads, writes)

    def tr(self, out, in_, ident, reads, writes):
        return self.P.op("pe", lambda e: e.transpose(out, in_, ident), reads, writes)

    def act(self, out, in_, func, reads, writes, bias=None, scale=1.0, accum=None):
        kw = {}
        if bias is not None:
            kw["bias"] = bias
        if accum is not None:
            kw["accum_out"] = accum
        return self.P.op("act", lambda e: e.activation(out=out, in_=in_, func=func, scale=scale, **kw), reads, writes)

    def tt(self, eng, out, in0, in1, op, reads, writes):
        return self.P.op(eng, lambda e: e.tensor_tensor(out=out, in0=in0, in1=in1, op=op), reads, writes)

    def ts(self, eng, out, in0, s1, op0, reads, writes, s2=None, op1=None):
        if op1 is None:
            return self.P.op(eng, lambda e: e.tensor_scalar(out=out, in0=in0, scalar1=s1, scalar2=None, op0=op0), reads, writes)
        return self.P.op(eng, lambda e: e.tensor_scalar(out=out, in0=in0, scalar1=s1, scalar2=s2, op0=op0, op1=op1), reads, writes)

    def stt(self, eng, out, in0, scalar, in1, op0, op1, reads, writes):
        return self.P.op(eng, lambda e: e.scalar_tensor_tensor(out=out, in0=in0, scalar=scalar, in1=in1, op0=op0, op1=op1), reads, writes)

    def cp(self, eng, out, in_, reads, writes):
        if eng == "act":
            return self.P.op("act", lambda e: e.copy(out=out, in_=in_), reads, writes)
        return self.P.op(eng, lambda e: e.tensor_copy(out=out, in_=in_), reads, writes)

    def recip(self, out, in_, reads, writes):
        return self.P.op("dve", lambda e: e.reciprocal(out=out, in_=in_), reads, writes)

    def memset(self, eng, ap, val, writes):
        return self.P.op(eng, lambda e: e.memset(ap, val), [], writes)

    def bank(self, grp, idxs):
        i = idxs[self.rot(grp, len(idxs))]
        return self.banks[i], ("pst" if i == 7 else "bank%d" % i)

    def load_w(self, name, w_dram, nk, ncols):
        wsb = self.sb(name, [128, nk, ncols], BF16)
        for k in range(nk):
            self.P.dma("pool", wsb[:, k, :], w_dram[k * 128:(k + 1) * 128, :], writes=[name])
        return wsb

    def load_const(self, name, dram_ap, shape, dt, eng="pool"):
        t = self.sb(name, shape, dt)
        self.P.dma(eng, t[:], dram_ap, writes=[name])
        return t

    def setup_norm(self):
        self.ht = [self.sb("ht%d" % i, [128, DM], F32) for i in range(2)]
        self.junk = self.sb("junk", [128, DM], BF16)
        self.nb = self.sb("nb", [128, DM], BF16)
        self.ss = [self.sb("ss%d" % i, [128, 1], F32) for i in range(2)]

    def norm_T(self, src, srckey, nT, nkey, col0, g_sb, gkey):
        b = self.rot("ss", 2)
        ss = self.ss[b]
        sk = "ss%d" % b
        self.memset("dve", ss[:], 0.0, [sk])
        self.act(self.junk[:], src, AF.Square, [srckey, sk], ["junk", sk], accum=ss[:])
        self.act(ss[:], ss[:], AF.Sqrt, [sk, "epsn"], [sk], bias=self.epsn[:], scale=1.0 / DM)
        self.recip(ss[:], ss[:], [sk], [sk])
        self.ts("dve", self.nb[:], src, ss[:, 0:1], ALU.mult, [srckey, sk], ["nb"])
        pst = self.pst
        nb, ident = self.nb, self.ident

        def fn(e):
            ins = None
            for k in range(8):
                ins = e.transpose(pst[:, k * 128:(k + 1) * 128], nb[:, k * 128:(k + 1) * 128], ident[:])
            return ins
        self.P.op("pe", fn, ["nb", "ident"], ["pst"])
        self.tt("dve", nT[:, :, col0:col0 + 128], pst[:, :].rearrange("p (k t) -> p k t", k=8),
                g_sb[:, :].unsqueeze(2).to_broadcast([128, 8, 128]), ALU.mult, ["pst", gkey], [nkey])


def attn_loop(C, tiles, score_fn, scale, v_fn, po, pok, sbanks, pts, ptname, after_first=None, depth=2, hooks=None):
    n = len(tiles)
    issued = []

    def issue(i):
        sbk, sk = C.bank("s", sbanks)
        pairs, rd = score_fn(tiles[i])
        C.mmg(sbk[:, :], pairs, rd, [sk])
        issued.append((sbk, sk))
    for i in range(min(depth, n)):
        issue(i)
    if after_first is not None:
        after_first()
    for i in range(n):
        sbk, sk = issued[i]
        pi = C.rot(ptname, len(pts))
        C.act(pts[pi][:], sbk[:, :], AF.Exp, [sk], ["%s%d" % (ptname, pi)], scale=scale)
        if i + depth < n:
            issue(i + depth)
        lhsT, rd = v_fn(tiles[i])
        C.mm(po[:, :], lhsT, pts[pi][:], i == 0, i == n - 1, rd + ["%s%d" % (ptname, pi)], [pok])
        if hooks and i in hooks:
            hooks[i]()


def phase_mla(C, name, L, G, hsrc, hkey, osink, after_weights=None):
    P = C.P
    C.begin_phase(name)
    C.setup_norm()
    gA = C.load_const("gA_sb", L["gAm"], [128, 8], F32)
    gq = C.load_const("gq_sb", L["gq"], [128, 3], F32)
    gkv = C.load_const("gkv_sb", L["gkv"], [128, 2], F32)
    dmask = C.load_const("dmask_sb", G["dmask"], [128, 2048], BF16)
    wA = C.load_w("wA_sb", L["wAm"], 8, 832)
    wq = C.load_w("wq_sb", L["wq"], 3, 768)
    wkv = C.load_w("wkv_sb", L["wkv"], 2, 512)
    if after_weights is not None:
        after_weights()
    ropeC_d, ropeS_d = G["ropeC"], G["ropeS"]

    Kh = C.sb("Kh", [128, 4, S], BF16)
    C.memset("pool", Kh[96:128, :, :], 0.0, ["Kh_pad"])
    Vt = C.sb("Vt", [128, 32, 4, 128], BF16)
    C.memset("pool", Vt[:, :, :, 64:65], 1.0, ["Vt_%d" % c for c in range(8)])
    C.memset("pool", Vt[:, :, :, 65:128], 0.0, ["Vt_pad"])
    nTs = [C.sb("nT%d" % i, [128, 8, 512], BF16) for i in range(2)]
    Qhs = [C.sb("Qh%d" % i, [128, 4, 512], BF16) for i in range(2)]
    for i in range(2):
        C.memset("pool", Qhs[i][96:128, :, :], 0.0, ["Qh_pad"])
    zf = C.sb("zf", [128, 3, 512], F32)
    sq = C.sb("sq", [128, 3, 512], F32)
    rr = C.sb("rr", [128, 512], F32)
    cqn = C.sb("cqn", [128, 3, 512], BF16)
    ckvn = C.sb("ckvn", [128, 2, 512], BF16)
    Ct = C.sb("Ct", [96, 512], F32)
    St = C.sb("St", [96, 512], F32)
    t1 = C.sb("t1", [96, 512], F32)
    t2 = C.sb("t2", [96, 512], F32)
    pts = [C.sb("pt%d" % i, [128, 512], BF16) for i in range(4)]
    rsrow = C.sb("rsrow", [65, 512], F32)
    rec4 = C.sb("rec4", [128, 4], F32)
    wb = C.sb("wb", [128, 4, 64], F32)
    bcs = C.sb("bcs", [64, 512], F32)
    ots = [C.sb("ot%d" % i, [64, 512], BF16) for i in range(2)]

    PJ = [0, 1]
    SB_ = [2, 3, 4]
    PO = [5, 6]
    pendA = []
    pendB = []

    def flushA():
        while pendA:
            pendA.pop(0)()

    def flushB():
        flushA()
        while pendB:
            pendB.pop(0)()

    def flush():
        flushB()

    def latent(c0, nm, dim, dst, dkey, nT, nkeys, gl, glkey):
        for m in range(nm):
            pj, pk = C.bank("pj", PJ)
            C.mmg(pj[:, :], [(wA[:, k, c0 + m * 128:c0 + (m + 1) * 128], nT[:, k, :]) for k in range(8)],
                  ["wA_sb"] + nkeys, [pk])
            C.act(zf[:, m, :], pj[:, :], AF.Copy, [pk], ["zf%d" % m])
            C.act(sq[:, m, :], pj[:, :], AF.Square, [pk], ["sq%d" % m])
        pj, pk = C.bank("pj", PJ)
        C.mmg(pj[:, :], [(C.onesf[:, :], sq[:, m, :]) for m in range(nm)], ["onesf"] + ["sq%d" % m for m in range(nm)], [pk])
        C.act(rr[:], pj[:, :], AF.Sqrt, [pk, "epsn"], ["rr"], bias=C.epsn[:], scale=1.0 / dim)
        C.recip(rr[:], rr[:], ["rr"], ["rr"])
        for m in range(nm):
            C.stt("dve", dst[:, m, :], zf[:, m, :], gl[:, m:m + 1], rr[:], ALU.mult, ALU.mult, ["zf%d" % m, "rr", glkey], [dkey])

    def stage_T(tc):
        nT = nTs[tc % 2]
        nkeys = []
        for ti in range(4):
            hb = C.rot("ht", 2)
            P.dma("sp", C.ht[hb][:], hsrc(4 * tc + ti), reads=[hkey(4 * tc + ti)], writes=["ht%d" % hb])
            nk = "nT%d_%d" % (tc % 2, ti)
            C.norm_T(C.ht[hb][:], "ht%d" % hb, nT, nk, ti * 128, gA, "gA_sb")
            nkeys.append(nk)
        return nT, nkeys

    def stage_P1(tc, nT, nkeys):
        t0 = tc * 512
        latent(0, 3, 384.0, cqn, "cqn", nT, nkeys, gq, "gq_sb")
        latent(384, 2, 256.0, ckvn, "ckvn", nT, nkeys, gkv, "gkv_sb")
        P.dma("sp", Ct[64:96, :], ropeC_d[:, t0:t0 + 512], writes=["Ct"])
        P.dma("sp", St[64:96, :], ropeS_d[:, t0:t0 + 512], writes=["St"])
        pA, pAk = C.bank("pj", PJ)
        C.mmg(pA[0:96, :], [(wA[:, k, 640:736], nT[:, k, :]) for k in range(8)], ["wA_sb"] + nkeys, [pAk])
        C.tt("dve", t1[64:96, :], pA[64:96, :], Ct[64:96, :], ALU.mult, [pAk, "Ct"], ["t1"])
        pB, pBk = C.bank("pj", PJ)
        C.mmg(pB[0:96, :], [(wA[:, k, 736:832], nT[:, k, :]) for k in range(8)], ["wA_sb"] + nkeys, [pBk])
        C.tt("dve", t2[64:96, :], pB[64:96, :], St[64:96, :], ALU.mult, [pBk, "St"], ["t2"])
        for hh in range(4):
            C.tt("pool", Kh[64:96, hh, t0:t0 + 512], t1[64:96, :], t2[64:96, :], ALU.add, ["t1", "t2"], ["Kh_%d" % tc])

    def stage_P2(tc):
        t0 = tc * 512
        Qh = Qhs[tc % 2]
        qk = "Qh%d" % (tc % 2)
        for hh in range(4):
            pA, pAk = C.bank("pj", PJ)
            C.mmg(pA[0:96, :], [(wq[:, m, hh * 192:hh * 192 + 96], cqn[:, m, :]) for m in range(3)], ["wq_sb", "cqn"], [pAk])
            C.cp("act", Qh[0:64, hh, :], pA[0:64, :], [pAk], [qk])
            C.tt("dve", t1[64:96, :], pA[64:96, :], Ct[64:96, :], ALU.mult, [pAk, "Ct"], ["t1"])
            pB, pBk = C.bank("pj", PJ)
            C.mmg(pB[0:96, :], [(wq[:, m, hh * 192 + 96:hh * 192 + 192], cqn[:, m, :]) for m in range(3)], ["wq_sb", "cqn"], [pBk])
            C.tt("dve", t2[64:96, :], pB[64:96, :], St[64:96, :], ALU.mult, [pBk, "St"], ["t2"])
            C.tt("pool", Qh[64:96, hh, :], t1[64:96, :], t2[64:96, :], ALU.add, ["t1", "t2"], [qk])
        for hh in range(4):
            pj, pk = C.bank("pj", PJ)
            C.mmg(pj[0:64, :], [(wkv[:, j, hh * 64:(hh + 1) * 64], ckvn[:, j, :]) for j in range(2)], ["wkv_sb", "ckvn"], [pk])
            C.cp("act", Kh[0:64, hh, t0:t0 + 512], pj[0:64, :], [pk], ["Kh_%d" % tc])
        for ti in range(4):
            pj, pk = C.bank("pj", PJ)
            C.mmg(pj[:, 0:256], [(ckvn[:, j, ti * 128:(ti + 1) * 128], wkv[:, j, 256:512]) for j in range(2)], ["wkv_sb", "ckvn"], [pk])
            C.cp("act", Vt[:, 4 * tc + ti, :, 0:64], pj[:, 0:256].rearrange("p (h d) -> p h d", h=4), [pk], ["Vt_%d" % tc])

    def head(tc, hh):
        Qh = Qhs[tc % 2]
        qk = "Qh%d" % (tc % 2)
        po, pok = C.bank("po", PO)
        nkt = 4 * tc + 4

        def score(j):
            pairs = [(Kh[:, hh, j * 128:(j + 1) * 128], Qh[:, hh, :])]
            rd = [qk, "Kh_%d" % (j // 4), "Kh_pad", "Qh_pad"]
            if j >= 4 * tc:
                m = j - 4 * tc
                pairs.append((C.ident[:, :], dmask[:, m * 512:(m + 1) * 512]))
                rd += ["ident", "dmask_sb"]
            return pairs, rd

        def vfn(j):
            return Vt[:, j, hh, :], ["Vt_%d" % (j // 4), "Vt_pad"]
        attn_loop(C, list(range(nkt)), score, SC_MLA, vfn, po, pok, SB_, pts, "pt", after_first=flushA, hooks={2: flushB})

        def finA():
            C.cp("dve", rsrow[64:65, :], po[64:65, :], [pok], ["rsrow"])
            pj, pk = C.bank("pj", PJ)
            C.mmlist([(pj[:, r:r + 1], rsrow[64:65, r * 128:(r + 1) * 128], C.onesf[64:65, 0:1], True, True) for r in range(4)],
                     ["rsrow", "onesf"], [pk])
            C.ts("dve", rec4[:, :], pj[:, 0:4], 1e-30, ALU.add, [pk], ["rec4"])
            C.recip(rec4[:, :], rec4[:, :], ["rec4"], ["rec4"])
            C.cp("dve", wb[:, :, :], rec4[:, 0:4].unsqueeze(2).to_broadcast([128, 4, 64]), ["rec4"], ["wb"])

        def finB():
            pj2, pk2 = C.bank("pj", PJ)
            C.mmlist([(pj2[0:64, r * 128:(r + 1) * 128], wb[:, r, :], C.identf[:, :], True, True) for r in range(4)],
                     ["wb", "identf"], [pk2])
            C.cp("act", bcs[:, :], pj2[0:64, :], [pk2], ["bcs"])
            oi = C.rot("ot", 2)
            C.tt("dve", ots[oi][:, :], po[0:64, :], bcs[:, :], ALU.mult, [pok, "bcs"], ["ot%d" % oi])
            osink(hh, tc, ots[oi], "ot%d" % oi)
        pendA.append(finA)
        pendB.append(finB)

    st = stage_T(0)
    stage_P1(0, *st)
    stage_P2(0)
    for tc in range(8):
        head(tc, 0)
        if tc < 7:
            st = stage_T(tc + 1)
        head(tc, 1)
        if tc < 7:
            stage_P1(tc + 1, *st)
        head(tc, 2)
        if tc < 7:
            stage_P2(tc + 1)
        head(tc, 3)
    flush()


def phase_nsa(C, name, L, G, hsrc, hkey, osink, pre=None):
    P = C.P
    pre = pre or {}
    C.begin_phase(name, keep_top=bool(pre))
    NW = 780
    C.setup_norm()
    gA = C.load_const("gA_sb", L["gAn"], [128, 8], F32)
    wA = pre["wA_sb"] if "wA_sb" in pre else C.load_w("wA_sb", L["wAn"], 8, NW)
    w1k = pre["w1k_sb"] if "w1k_sb" in pre else C.load_w("w1k_sb", L["w1k"], 16, 128)
    w1v = pre["w1v_sb"] if "w1v_sb" in pre else C.load_w("w1v_sb", L["w1v"], 16, 128)
    w2k = C.load_w("w2k_sb", L["w2k"], 1, 64)
    w2v = C.load_w("w2v_sb", L["w2v"], 1, 64)
    posk = C.load_w("posk_sb", L["posk"], 1, 16)
    posv = C.load_w("posv_sb", L["posv"], 1, 16)
    maskc = pre["maskc_sb"] if "maskc_sb" in pre else C.load_const("maskc_sb", G["maskc"], [128, 2 * S], BF16)
    eall = C.sb("eall_sb", [128, S], BF16)
    C.memset("pool", eall[64:128, :], 0.0, ["eall_sb"])
    P.dma("pool", eall[0:64, :], G["eall"], writes=["eall_sb"])
    selbias = pre["selbias_sb"] if "selbias_sb" in pre else C.load_const("selbias_sb", G["selbias"], [128, 2048], BF16)
    ovl = C.load_const("ovl_sb", G["ovl"], [128, 130], BF16)
    selg = C.load_const("selg_sb", G["selg"], [12, 768], F32)
    dm4 = C.load_const("dm4_sb", G["dm4"], [128, 512], BF16)
    wm4 = C.load_const("wm4_sb", G["wm4"], [128, 512], BF16)

    Qa = C.sb("Qa", [128, 32, 512], BF16)
    Kw = C.sb("Kw", [128, S], BF16)
    Ks = C.sb("Ks", [128, S], BF16)
    Kc = C.sb("Kc", [128, 256], BF16)
    C.memset("pool", Qa[64:128, :, :], 0.0, ["Qa_aug"])
    C.memset("pool", Kw[64:128, :], 0.0, ["Kw_aug"])
    C.memset("pool", Ks[64:128, :], 0.0, ["Ks_aug"])
    C.memset("pool", Kc[64:128, :], 0.0, ["Kc_aug"])
    P.dma("pool", Qa[64:68, :, :], G["qaug"].rearrange("p (a b) -> p a b", a=32), writes=["Qa_aug"])
    P.dma("pool", Kw[64:68, :], G["kaug"], writes=["Kw_aug"])
    P.dma("pool", Ks[64:68, :], G["kaug"], writes=["Ks_aug"])
    P.dma("pool", Kc[64:68, :], G["kaugc"], writes=["Kc_aug"])
    kc2 = C.sb("kc2", [128, S + 32], BF16)
    vc2 = C.sb("vc2", [128, S + 32], BF16)
    C.memset("pool", kc2[:, S:S + 32], 0.0, ["kc2_tail"])
    C.memset("pool", vc2[:, S:S + 32], 0.0, ["vc2_tail"])
    Vs = C.sb("Vs", [128, 32, 128], BF16)
    Vw = C.sb("Vw", [128, 32, 128], BF16)
    Vc = C.sb("Vc", [128, 2, 128], BF16)
    for (vt_, keys_) in ((Vs, ["Vs_%d" % c for c in range(8)]), (Vw, ["Vw_%d" % c for c in range(8)]), (Vc, ["Vc"])):
        C.memset("pool", vt_[:, :, 65:128], 0.0, keys_)
        C.memset("pool", vt_[:, :, 64:65], 1.0, keys_)
    Gtm = C.sb("Gtm", [128, 32, 12], F32)
    nTs = [C.sb("nT%d" % i, [128, 8, 512], BF16) for i in range(2)]

    PJ = [0, 1]
    SB_ = [2, 3]
    POC, POS, POW = 4, 5, 6

    for tc in range(8):
        t0 = tc * 512
        nb_ = C.rot("nT", 2)
        nT = nTs[nb_]
        nkeys = []
        for ti in range(4):
            hb = C.rot("ht", 2)
            P.dma("sp", C.ht[hb][:], hsrc(4 * tc + ti), reads=[hkey(4 * tc + ti)], writes=["ht%d" % hb])
            nk = "nT%d_%d" % (nb_, ti)
            C.norm_T(C.ht[hb][:], "ht%d" % hb, nT, nk, ti * 128, gA, "gA_sb")
            nkeys.append(nk)

        def proj(c0, m, rows=128):
            pj, pk = C.bank("pj", PJ)
            C.mmg(pj[0:rows, :], [(wA[:, k, c0:c0 + m], nT[:, k, :]) for k in range(8)], ["wA_sb"] + nkeys, [pk])
            return pj, pk
        for r in range(4):
            pj, pk = proj(r * 64, 64, 64)
            C.cp("act", Qa[0:64, 4 * tc:4 * tc + 4, r * 128:(r + 1) * 128],
                 pj[0:64, :].rearrange("p (a b) -> p a b", a=4), [pk], ["Qa_%d" % tc])
        for (c0, dst, dk) in ((256, kc2, "kc2"), (384, vc2, "vc2")):
            pj, pk = proj(c0, 128)
            C.cp("act", dst[0:64, t0:t0 + 512], pj[0:64, :], [pk], [dk])
            if tc == 0:
                C.cp("dve", dst[64:128, 0:511], pj[64:128, 1:512], [pk], [dk])
            else:
                C.cp("dve", dst[64:128, t0 - 1:t0 + 511], pj[64:128, :], [pk], [dk])
        pj, pk = proj(512, 64, 64)
        C.cp("act", Ks[0:64, t0:t0 + 512], pj[0:64, :], [pk], ["Ks_%d" % tc])
        pj, pk = proj(576, 64, 64)
        C.cp("act", Kw[0:64, t0:t0 + 512], pj[0:64, :], [pk], ["Kw_%d" % tc])
        for ti in range(4):
            pj, pk = C.bank("pj", PJ)
            C.mmg(pj[:, 0:128], [(nT[:, k, ti * 128:(ti + 1) * 128], wA[:, k, 640:768]) for k in range(8)],
                  ["wA_sb"] + nkeys, [pk])
            pg, pgk = C.bank("pj", PJ)
            C.mmg(pg[:, 0:12], [(nT[:, k, ti * 128:(ti + 1) * 128], wA[:, k, 768:780]) for k in range(8)],
                  ["wA_sb"] + nkeys, [pgk])
            C.cp("dve", Gtm[:, 4 * tc + ti, :], pg[:, 0:12], [pgk], ["Gtm_%d" % tc])
            C.cp("act", Vs[:, 4 * tc + ti, 0:64], pj[:, 0:64], [pk], ["Vs_%d" % tc])
            C.cp("dve", Vw[:, 4 * tc + ti, 0:64], pj[:, 64:128], [pk], ["Vw_%d" % tc])

    C.act(Gtm[:, :, :], Gtm[:, :, :], AF.Sigmoid, ["Gtm_%d" % c for c in range(8)], ["Gtm_%d" % c for c in range(8)])
    xs = C.sb("xs", [128, 256], F32)
    x2 = C.sb("x2", [128, 256], F32)
    hid = C.sb("hid", [128, 256], BF16)
    cbias = C.sb("cbias", [128, 1], F32)
    for (src, skey, w1, w1key, pos, poskey, isk) in ((kc2, "kc2", w1k, "w1k_sb", posk, "posk_sb", True),
                                                     (vc2, "vc2", w1v, "w1v_sb", posv, "posv_sb", False)):
        pj, pk = C.bank("pj", PJ)
        C.mmg(pj[:, 0:1], [(w1[:, j, :], pos[:, 0, j:j + 1]) for j in range(16)], [w1key, poskey], [pk])
        C.cp("act", cbias[:], pj[:, 0:1], [pk], ["cbias"])
        pj, pk = C.bank("pj", PJ)
        C.mmg(pj[:, 0:255], [(w1[:, j, :], src[:, 2 * j:2 * j + 16 * 255:16]) for j in range(16)],
              [w1key, skey, skey + "_tail"], [pk])
        C.memset("dve", xs[:, 255:256], 0.0, ["xs"])
        C.act(xs[:, 0:255], pj[:, 0:255], AF.Identity, [pk, "cbias"], ["xs"], bias=cbias[:])
        C.tt("dve", x2[:], xs[:], xs[:], ALU.mult, ["xs"], ["x2"])
        C.ts("dve", x2[:], x2[:], 0.044715, ALU.mult, ["x2"], ["x2"], s2=1.0, op1=ALU.add)
        C.tt("dve", x2[:], x2[:], xs[:], ALU.mult, ["x2", "xs"], ["x2"])
        C.act(x2[:], x2[:], AF.Sigmoid, ["x2"], ["x2"], scale=1.5957691216057308)
        C.tt("dve", hid[:], xs[:], x2[:], ALU.mult, ["x2", "xs"], ["hid"])
        if isk:
            pj, pk = C.bank("pj", PJ)
            C.mm(pj[0:64, 0:256], w2k[:, 0, :], hid[:], True, True, ["w2k_sb", "hid"], [pk])
            C.cp("act", Kc[0:64, :], pj[0:64, 0:256], [pk], ["Kc"])
        else:
            for nt in range(2):
                pj, pk = C.bank("pj", PJ)
                C.mm(pj[:, 0:64], hid[:, nt * 128:(nt + 1) * 128], w2v[:, 0, :], True, True, ["w2v_sb", "hid"], [pk])
                C.cp("act", Vc[:, nt, 0:64], pj[:, 0:64], [pk], ["Vc"])

    ptc = [[C.sb("ptc%d_%d" % (i, j), [128, 512], BF16) for j in range(2)] for i in range(2)]
    pts = [C.sb("pt%d" % i, [128, 512], BF16) for i in range(4)]
    imp = C.sb("imp", [128, 64], F32)
    wk = C.sb("wk", [128, 64], F32)
    m8 = C.sb("m8", [128, 16], F32)
    rci = C.sb("rci", [128, 4], F32)
    selb = C.sb("selb", [128, 64], BF16)
    selT4s = [C.sb("selT4_%d" % i, [128, 512], BF16) for i in range(2)]
    for i in range(2):
        C.memset("pool", selT4s[i][64:128, :], 0.0, ["selT4_%d" % i])
    rsrows = [C.sb("rsrow%d" % i, [65, 512], F32) for i in range(3)]
    bcss = [C.sb("bcs%d" % i, [64, 512], F32) for i in range(3)]
    tmpos = [C.sb("tmpo%d" % i, [64, 512], F32) for i in range(3)]
    rec4s = [C.sb("rec4_%d" % i, [128, 4], F32) for i in range(3)]
    wbs = [C.sb("wb%d" % i, [128, 4, 64], F32) for i in range(3)]
    oacc = [C.sb("oacc%d" % i, [64, 512], F32) for i in range(2)]
    oaccb = [C.sb("oaccb%d" % i, [64, 512], BF16) for i in range(2)]
    PJ = [0, 7]
    SB_ = [1, 2, 3]
    pending = []

    def flush():
        tl = list(pending)
        del pending[:]
        while any(tl):
            for t in tl:
                if t:
                    t.pop(0)()

    def finalize_steps(po, pok, br, qb, first, rec_src=None):
        ai = qb % 2
        acc, ak = oacc[ai], "oacc%d" % ai
        rsrow, bcs, tmpo, rec4, wb = rsrows[br], bcss[br], tmpos[br], rec4s[br], wbs[br]
        rk, bk, tk, r4k, wk_ = "rsrow%d" % br, "bcs%d" % br, "tmpo%d" % br, "rec4_%d" % br, "wb%d" % br

        def s1():
            if rec_src is None:
                C.cp("dve", rsrow[64:65, :], po[64:65, :], [pok], [rk])
                pj, pk = C.bank("pj", PJ)
                C.mmlist([(pj[:, r:r + 1], rsrow[64:65, r * 128:(r + 1) * 128], C.onesf[64:65, 0:1], True, True) for r in range(4)],
                         [rk, "onesf"], [pk])
                C.ts("dve", rec4[:, :], pj[:, 0:4], 1e-30, ALU.add, [pk], [r4k])

        def s2():
            if rec_src is None:
                C.recip(rec4[:, :], rec4[:, :], [r4k], [r4k])
                recap, reckey = rec4, r4k
            else:
                recap, reckey = rec_src
            C.tt("dve", wb[:, :, :], recap[:, 0:4].unsqueeze(2).to_broadcast([128, 4, 64]),
                 Gtm[:, qb, br:12:3].unsqueeze(2).to_broadcast([128, 4, 64]), ALU.mult, [reckey, "Gtm_%d" % (qb // 4)], [wk_])

        def s3():
            pj2, pk2 = C.bank("pj", PJ)
            C.mmlist([(pj2[0:64, r * 128:(r + 1) * 128], wb[:, r, :], C.identf[:, :], True, True) for r in range(4)],
                     [wk_, "identf"], [pk2])
            C.cp("act", bcs[:, :], pj2[0:64, :], [pk2], [bk])

        def s4():
            if first:
                C.tt("dve", acc[:, :], po[0:64, :], bcs[:, :], ALU.mult, [pok, bk], [ak])
            else:
                C.tt("dve", tmpo[:, :], po[0:64, :], bcs[:, :], ALU.mult, [pok, bk], [tk])
                C.tt("dve", acc[:, :], acc[:, :], tmpo[:, :], ALU.add, [tk, ak], [ak])
        return [s1, s2, s3, s4]

    def qinfo(qb):
        return Qa[:, qb, :], ["Qa_%d" % (qb // 4), "Qa_aug"]

    def cmp_stage(qb):
        q_rhs, qkeys = qinfo(qb)
        pc = ptc[qb % 2]
        selT4, stk = selT4s[qb % 2], "selT4_%d" % (qb % 2)
        ntn = 1 if qb < 16 else 2
        po = C.banks[POC]
        for nt in range(ntn):
            sbk, sk = C.bank("s", SB_)
            items = [(sbk[:, :], Kc[:, nt * 128:(nt + 1) * 128], q_rhs, True, False)]
            for r in range(4):
                items.append((sbk[:, r * 128:(r + 1) * 128], C.ident[:, :],
                              maskc[:, nt * S + qb * 128:nt * S + (qb + 1) * 128], False, r == 3))
            C.mmlist(items, qkeys + ["Kc", "Kc_aug", "ident", "maskc_sb"], [sk])
            C.act(pc[nt][:], sbk[:, :], AF.Exp, [sk], ["ptc%d_%d" % (qb % 2, nt)], scale=SC_NSA)
        for nt in range(ntn):
            C.mm(po[:, :], Vc[:, nt, :], pc[nt][:], nt == 0, nt == ntn - 1,
                 ["Vc", "ptc%d_%d" % (qb % 2, nt)], ["bank%d" % POC])
        pj, pk = C.bank("pj", PJ)
        items = []
        for r in range(4):
            for nt in range(ntn):
                items.append((pj[:, r * 65:(r + 1) * 65], pc[nt][:, r * 128:(r + 1) * 128], ovl[:, nt * 65:(nt + 1) * 65],
                              nt == 0, nt == ntn - 1))
        C.mmlist(items, ["ovl_sb"] + ["ptc%d_%d" % (qb % 2, nt) for nt in range(ntn)], [pk])
        for r in range(4):
            C.ts("dve", rci[:, r:r + 1], pj[:, r * 65 + 64:r * 65 + 65], 1e-30, ALU.add, [pk], ["rci"])
        C.recip(rci[:, 0:4], rci[:, 0:4], ["rci"], ["rci"])
        for r in range(4):
            prev = selbias[:, qb * 64:(qb + 1) * 64] if r == 0 else imp[:]
            C.stt("dve", imp[:], pj[:, r * 65:r * 65 + 64], rci[:, r:r + 1], prev, ALU.mult, ALU.add,
                  [pk, "rci", "imp", "selbias_sb"], ["imp"])
        P.op("dve", lambda e: e.max(out=m8[:, 0:8], in_=imp[:]), ["imp"], ["m8"])
        P.op("dve", lambda e: e.match_replace(out=wk[:], in_to_replace=m8[:, 0:8], in_values=imp[:], imm_value=-1e9),
             ["imp", "m8"], ["wk"])
        P.op("dve", lambda e: e.max(out=m8[:, 8:16], in_=wk[:]), ["wk"], ["m8"])
        C.ts("dve", wk[:], imp[:], m8[:, 15:16], ALU.is_ge, ["imp", "m8"], ["wk"])
        C.ts("dve", selb[:], wk[:], -NEG, ALU.mult, ["wk"], ["selb"], s2=NEG, op1=ALU.add)

        def s0(qb=qb, selT4=selT4, stk=stk):
            C.tr(C.pst[0:64, 0:128], selb[:, :], C.ident[:, :], ["selb", "ident"], ["pst"])
            for r in range(4):
                C.cp("act" if r % 2 == 0 else "dve", selT4[0:64, r * 128:(r + 1) * 128], C.pst[0:64, 0:128], ["pst"], [stk])
        C.cp("dve", rec4s[0][:, :], rci[:, 0:4], ["rci"], ["rec4_0"])
        pending.append([s0] + finalize_steps(po, "bank%d" % POC, 0, qb, True, rec_src=(rec4s[0], "rec4_0")))

    def sel_stage(qb):
        q_rhs, qkeys = qinfo(qb)
        selT4, stk = selT4s[qb % 2], "selT4_%d" % (qb % 2)
        po = C.banks[POS]

        def score(kt):
            pairs = [(Ks[:, kt * 128:(kt + 1) * 128], q_rhs)]
            rd = qkeys + ["Ks_%d" % (kt // 4), "Ks_aug"]
            if kt == qb:
                pairs.append((C.ident[:, :], dm4[:, :]))
                rd += ["ident", "dm4_sb"]
            else:
                pairs.append((eall[:, kt * 128:(kt + 1) * 128], selT4[:, :]))
                rd += ["eall_sb", stk]
            return pairs, rd
        attn_loop(C, list(range(qb + 1)), score, SC_NSA, lambda kt: (Vs[:, kt, :], ["Vs_%d" % (kt // 4)]),
                  po, "bank%d" % POS, SB_, pts, "pt", after_first=None)

        def s5(qb=qb):
            ai = qb % 2
            C.cp("act", oaccb[ai][:, :], oacc[ai][:, :], ["oacc%d" % ai], ["oaccb%d" % ai])
            osink(qb, oaccb[ai], "oaccb%d" % ai)
        pending.append(finalize_steps(po, "bank%d" % POS, 1, qb, False) + [s5])

    def win_stage(qb):
        q_rhs, qkeys = qinfo(qb)
        po = C.banks[POW]
        k0 = max(0, qb - 4)

        def score(kt):
            pairs = [(Kw[:, kt * 128:(kt + 1) * 128], q_rhs)]
            rd = qkeys + ["Kw_%d" % (kt // 4), "Kw_aug"]
            if kt == qb:
                pairs.append((C.ident[:, :], dm4[:, :]))
                rd += ["ident", "dm4_sb"]
            if kt == qb - 4:
                pairs.append((C.ident[:, :], wm4[:, :]))
                rd += ["ident", "wm4_sb"]
            return pairs, rd
        attn_loop(C, list(range(k0, qb + 1)), score, SC_NSA, lambda kt: (Vw[:, kt, :], ["Vw_%d" % (kt // 4)]),
                  po, "bank%d" % POW, SB_, pts, "pt", after_first=flush)

        pending.append(finalize_steps(po, "bank%d" % POW, 2, qb, False))

    cmp_stage(0)
    for qb in range(32):
        win_stage(qb)
        if qb + 1 < 32:
            cmp_stage(qb + 1)
        sel_stage(qb)
    flush()


def phase_ffn(C, name, L, G, final, hown, hownkey, hhalo, hhalokey, oall, osink):
    P = C.P
    C.begin_phase(name)
    C.setup_norm()
    fl = C.flags
    g2 = C.load_const("g2_sb", L["g2"], [128, 8], F32)
    cw = C.load_const("cw_sb", L["cw"], [128, 176], F32)
    if final:
        gF = C.sb("gF_sb", [128, DM], F32)
        P.dma("pool", gF[:], G["gF"][0:1, :].partition_broadcast(128), writes=["gF_sb"])
    wup = C.sb("wup_sb", [128, 8, 5632], BF16)
    wup_src = L["wup"].rearrange("(k p) c -> p k c", p=128)
    for i0 in range(0, 22, 4):
        n4 = min(4, 22 - i0)
        for base in (i0, 22 + i0):
            P.dma("pool", wup[:, :, base * 128:(base + n4) * 128], wup_src[:, :, base * 128:(base + n4) * 128],
                  writes=["wup_%d" % fc for fc in range(base, base + n4)])
    wdn = C.sb("wdn_sb", [128, 22, DM], BF16)

    def load_wdn():
        for k in range(22):
            P.dma("pool", wdn[:, k, :], L["wdn"][k * 128:(k + 1) * 128, :], writes=["wdn_sb"])
    wo_d = L["wo"]

    NC_ = 256
    aT = [C.sb("aT%d" % i, [128, NC_], BF16) for i in range(4)]
    hm = C.sb("hm", [128, 2, DM], F32)
    n2T = C.sb("n2T", [128, 8, NC_], BF16)
    oTb = C.sb("oTb", [128, 8, NC_], BF16)
    oa = [C.sb("oa%d" % i, [128, NC_], BF16) for i in range(2)]
    ob = [C.sb("ob%d" % i, [128, NC_], BF16) for i in range(2)]
    wob = [C.sb("wob%d" % i, [128, DM], BF16) for i in range(4)]
    ubuf = [C.sb("ubuf%d" % i, [128, NC_ + 2], F32) for i in range(3)]
    tb = [C.sb("tb%d" % i, [128, NC_], F32) for i in range(4)]
    sg = C.sb("sg", [128, NC_], F32)
    carry = C.sb("carry", [128, 44, 2], F32)
    res = C.sb("res", [128, DM], F32)
    ss2 = C.sb("ss2", [128, 1], F32)

    ACC = [0, 1, 2, 3]
    UP = [4, 5, 6]

    def chunk(ci, halo):
        nt_ = 1 if halo else 2
        ncol = nt_ * 128
        c0 = 1920 if halo else ci * 256
        for k in range(8):
            b = C.rot("oa", 2)
            if halo:
                P.dma("sp", oa[b][:, 0:ncol], oall[0][k * 128:(k + 1) * 128, c0:c0 + ncol], reads=["oall0"], writes=["oa%d" % b])
                C.ts("dve", oTb[:, k, 0:ncol], oa[b][:, 0:ncol], fl[:, 1:2], ALU.mult, ["oa%d" % b, "flags"], ["oTb"])
            else:
                P.dma("sp", oa[b][:, 0:ncol], oall[0][k * 128:(k + 1) * 128, c0:c0 + ncol], reads=["oall0"], writes=["oa%d" % b])
                P.dma("sp", ob[b][:, 0:ncol], oall[1][k * 128:(k + 1) * 128, c0:c0 + ncol], reads=["oall1"], writes=["ob%d" % b])
                C.ts("dve", oa[b][:, 0:ncol], oa[b][:, 0:ncol], fl[:, 0:1], ALU.mult, ["oa%d" % b, "flags"], ["oa%d" % b])
                C.stt("dve", oTb[:, k, 0:ncol], ob[b][:, 0:ncol], fl[:, 1:2], oa[b][:, 0:ncol], ALU.mult, ALU.add,
                      ["oa%d" % b, "ob%d" % b, "flags"], ["oTb"])
        for k in range(8):
            wb = C.rot("wob", 4)
            P.dma("pool", wob[wb][:, :], wo_d[k * 128:(k + 1) * 128, :], writes=["wob%d" % wb])
            items = []
            for ti in range(nt_):
                for hf in range(2):
                    items.append((C.banks[ACC[ti * 2 + hf]][:, :], oTb[:, k, ti * 128:(ti + 1) * 128],
                                  wob[wb][:, hf * 512:(hf + 1) * 512], k == 0, k == 7))
            C.mmlist(items, ["oTb", "wob%d" % wb], ["bank%d" % ACC[i] for i in range(nt_ * 2)])
        if ci == 0 and not halo:
            load_wdn()
        for ti in range(nt_):
            hb = C.rot("ht", 2)
            if halo:
                P.dma("sp", C.ht[hb][:], hhalo, reads=[hhalokey], writes=["ht%d" % hb])
                C.ts("dve", C.ht[hb][:], C.ht[hb][:], fl[:, 1:2], ALU.mult, ["ht%d" % hb, "flags"], ["ht%d" % hb])
            else:
                P.dma("sp", C.ht[hb][:], hown(2 * ci + ti), reads=[hownkey(2 * ci + ti)], writes=["ht%d" % hb])
            for hf in range(2):
                C.tt("dve", hm[:, ti, hf * 512:(hf + 1) * 512], C.banks[ACC[ti * 2 + hf]][:, :],
                     C.ht[hb][:, hf * 512:(hf + 1) * 512], ALU.add, ["bank%d" % ACC[ti * 2 + hf], "ht%d" % hb], ["hm%d" % ti])
            C.norm_T(hm[:, ti, :], "hm%d" % ti, n2T, "n2T_%d" % ti, ti * 128, g2, "g2_sb")
        nkeys = ["n2T_%d" % ti for ti in range(nt_)]
        dq = []

        def down(i, ai):
            items = []
            for ti in range(nt_):
                for hf in range(2):
                    items.append((C.banks[ACC[ti * 2 + hf]][:, :], aT[ai][:, ti * 128:(ti + 1) * 128],
                                  wdn[:, i, hf * 512:(hf + 1) * 512], i == 0, i == 21))
            C.mmlist(items, ["aT%d" % ai, "wdn_sb"], ["bank%d" % ACC[q] for q in range(nt_ * 2)])
        for i in range(22):
            tfin = []
            for part in range(2):
                fc = i + 22 * part
                up, upk = C.bank("up", UP)
                if halo:
                    C.mmg(up[:, 0:2], [(wup[:, k, fc * 128:(fc + 1) * 128], n2T[:, k, ncol - 2:ncol]) for k in range(8)],
                          ["wup_%d" % fc] + nkeys, [upk])
                    C.cp("act", carry[:, fc, :], up[:, 0:2], [upk], ["carry%d" % fc])
                    continue
                C.mmg(up[:, 0:ncol], [(wup[:, k, fc * 128:(fc + 1) * 128], n2T[:, k, 0:ncol]) for k in range(8)],
                      ["wup_%d" % fc] + nkeys, [upk])
                ub = C.rot("ubuf", 3)
                u = ubuf[ub]
                uk = "ubuf%d" % ub
                C.cp("pool", u[:, 0:2], carry[:, fc, :], ["carry%d" % fc], [uk])
                C.cp("act", u[:, 2:2 + ncol], up[:, 0:ncol], [upk], [uk])
                if not halo:
                    ta = C.rot("tb", 4)
                    C.act(tb[ta][:, 0:ncol], up[:, 0:ncol], AF.Identity, [upk, "cw_sb"], ["tb%d" % ta],
                          bias=cw[:, fc * 4 + 3:fc * 4 + 4], scale=cw[:, fc * 4 + 2:fc * 4 + 3])
                    C.stt("dve", tb[ta][:, 0:ncol], u[:, 1:1 + ncol], cw[:, fc * 4 + 1:fc * 4 + 2], tb[ta][:, 0:ncol],
                          ALU.mult, ALU.add, [uk, "cw_sb", "tb%d" % ta], ["tb%d" % ta])
                    C.stt("dve", tb[ta][:, 0:ncol], u[:, 0:ncol], cw[:, fc * 4:fc * 4 + 1], tb[ta][:, 0:ncol],
                          ALU.mult, ALU.add, [uk, "cw_sb", "tb%d" % ta], ["tb%d" % ta])
                    tfin.append(ta)
                C.cp("pool", carry[:, fc, :], u[:, ncol:ncol + 2], [uk], ["carry%d" % fc])
            if not halo:
                C.act(sg[:, 0:ncol], tb[tfin[0]][:, 0:ncol], AF.Silu, ["tb%d" % tfin[0]], ["sg"])
                ai = C.rot("aT", 4)
                C.tt("dve", aT[ai][:, 0:ncol], sg[:, 0:ncol], tb[tfin[1]][:, 0:ncol], ALU.mult,
                     ["sg", "tb%d" % tfin[1]], ["aT%d" % ai])
                dq.append((i, ai))
                if len(dq) > 2:
                    down(*dq.pop(0))
        while dq:
            down(*dq.pop(0))
        if halo:
            return
        for ti in range(nt_):
            for hf in range(2):
                bk = ACC[ti * 2 + hf]
                C.tt("dve", res[:, hf * 512:(hf + 1) * 512], C.banks[bk][:, :], hm[:, ti, hf * 512:(hf + 1) * 512],
                     ALU.add, ["bank%d" % bk, "hm%d" % ti], ["res"])
            if final:
                C.memset("dve", ss2[:], 0.0, ["ss2"])
                C.act(C.junk[:], res[:], AF.Square, ["res", "ss2"], ["junk", "ss2"], accum=ss2[:])
                C.act(ss2[:], ss2[:], AF.Sqrt, ["ss2", "epsn"], ["ss2"], bias=C.epsn[:], scale=1.0 / DM)
                C.recip(ss2[:], ss2[:], ["ss2"], ["ss2"])
                C.stt("dve", res[:], res[:], ss2[:, 0:1], gF[:], ALU.mult, ALU.mult, ["res", "ss2", "gF_sb"], ["res"])
            osink(2 * ci + ti, res, "res")

    C.memset("pool", carry[:], 0.0, ["carry%d" % fc for fc in range(44)])
    chunk(0, True)
    for c in range(8):
        chunk(c, False)


LAYER_IN = [("wAm", [DM, 832]), ("gAm", [128, 8]), ("wq", [384, 768]), ("gq", [128, 3]), ("wkv", [256, 512]),
            ("gkv", [128, 2]), ("wAn", [DM, 780]), ("gAn", [128, 8]), ("w1k", [2048, 128]), ("w1v", [2048, 128]),
            ("w2k", [128, 64]), ("w2v", [128, 64]), ("posk", [128, 16]), ("posv", [128, 16]),
            ("wo", [DM, DM]), ("wup", [DM, 5632]), ("wdn", [2816, DM]), ("g2", [128, 8]), ("cw", [128, 176])]
GLOB_IN = [("ropeC", [32, S], F32), ("ropeS", [32, S], F32), ("dmask", [128, 2048], BF16),
           ("maskc", [128, 2 * S], BF16), ("eall", [64, S], BF16), ("selbias", [128, 2048], BF16),
           ("ovl", [128, 130], BF16), ("selg", [12, 768], F32), ("dm4", [128, 512], BF16), ("wm4", [128, 512], BF16),
           ("qaug", [4, 32 * 512], BF16), ("kaug", [4, S], BF16), ("kaugc", [4, 256], BF16), ("gF", [1, DM], F32)]
GROUPS = [[0, 1], [2, 3], [4, 5], [6, 7]]


def build_fused(nlayers=2):
    nc = bass.Bass("TRN2", target_bir_lowering=False)
    C = Ctx(nc)
    P = C.P
    x_d = C.dram("x", [S, DM], F32)
    xown_d = C.dram("xown", [2048, DM], F32)
    xhalo_d = C.dram("xhalo", [128, DM], F32)
    flags_d = C.dram("flags", [128, 2], F32)
    G = {n: C.dram(n, sh, dt) for (n, sh, dt) in GLOB_IN}
    Ls = [{n: C.dram("%s_%d" % (n, l), sh, F32) for (n, sh) in LAYER_IN} for l in range(nlayers)]
    out_d = C.dram("hout", [2048, DM], F32, out=True)
    P.dma("pool", C.flags[:], flags_d, writes=["flags"])

    omy = [[nc.dram_tensor("omy_%d_%d" % (l, c), [512, 2048], BF16) for c in range(2)] for l in range(nlayers)]
    oall = [[nc.dram_tensor("oall_%d_%d" % (l, c), [1024, 2048], BF16) for c in range(2)] for l in range(nlayers)]
    hmy = [nc.dram_tensor("hmy_%d" % j, [512, DM], F32) for j in range(4)]
    hall = [nc.dram_tensor("hall_%d" % j, [1024, DM], F32) for j in range(4)]

    for l in range(nlayers):
        if l == 0:
            hsrc = lambda g: x_d[g * 128:(g + 1) * 128, :]
            hkey = lambda g: "x"
        else:
            def hsrc(g):
                r, w = g // 16, g % 16
                return hall[w // 4][r * 512 + (w % 4) * 128:r * 512 + (w % 4) * 128 + 128, :]
            hkey = lambda g: "hall%d" % ((g % 16) // 4)

        def osink_mla(hh, tc, ot, otkey, l=l):
            c = tc // 4
            col = (tc % 4) * 512
            P.dma("sp", omy[l][c][hh * 64:(hh + 1) * 64, col:col + 512], ot[:, :], reads=[otkey], writes=["omy%d" % c])

        def osink_nsa(qb, ot, otkey, l=l):
            c = qb // 16
            col = (qb % 16) * 128
            P.dma("sp", omy[l][c][256:512, :].rearrange("(r d) t -> d r t", d=64)[:, :, col:col + 128],
                  ot[:, :].rearrange("p (r t) -> p r t", r=4), reads=[otkey], writes=["omy%d" % c])
            if qb % 16 == 15:
                P.cc("AllGather", GROUPS, omy[l][c].ap().opt(), oall[l][c].ap().opt(), reads=["omy%d" % c], writes=["oall%d" % c])

        pre = {}

        def prefetch_nsa(l=l, pre=pre):
            def ld(key, name, dram, nk, ncols):
                t = C.sb_top("np%d_%s" % (l, name), [128, nk, ncols], BF16)
                for k in range(nk):
                    P.dma("pool", t[:, k, :], dram[k * 128:(k + 1) * 128, :], writes=[key])
                pre[key] = t
            ld("wA_sb", "wA", Ls[l]["wAn"], 8, 780)
            ld("w1k_sb", "w1k", Ls[l]["w1k"], 16, 128)
            ld("w1v_sb", "w1v", Ls[l]["w1v"], 16, 128)
            t = C.sb_top("np%d_maskc" % l, [128, 2 * S], BF16)
            P.dma("pool", t[:], G["maskc"], writes=["maskc_sb"])
            pre["maskc_sb"] = t
            t = C.sb_top("np%d_selbias" % l, [128, 2048], BF16)
            P.dma("pool", t[:], G["selbias"], writes=["selbias_sb"])
            pre["selbias_sb"] = t
        phase_mla(C, "m%d_" % l, Ls[l], G, hsrc, hkey, osink_mla, after_weights=prefetch_nsa)
        phase_nsa(C, "n%d_" % l, Ls[l], G, hsrc, hkey, osink_nsa, pre=pre)
        final = (l == nlayers - 1)
        if l == 0:
            hown = lambda t: xown_d[t * 128:(t + 1) * 128, :]
            hownkey = lambda t: "xown"
            hhalo, hhalokey = xhalo_d, "xhalo"
        else:
            hown = lambda t: hmy[t // 4][(t % 4) * 128:(t % 4) * 128 + 128, :]
            hownkey = lambda t: "hmy%d" % (t // 4)
            hhalo, hhalokey = hall[3][384:512, :], "hall3"
        if final:
            def osink_ffn(t, res, rkey):
                P.dma("sp", out_d[t * 128:(t + 1) * 128, :], res[:], reads=[rkey])
        else:
            def osink_ffn(t, res, rkey):
                P.dma("sp", hmy[t // 4][(t % 4) * 128:(t % 4) * 128 + 128, :], res[:], reads=[rkey], writes=["hmy%d" % (t // 4)])
                if t % 4 == 3:
                    j = t // 4
                    P.cc("AllGather", GROUPS, hmy[j].ap().opt(), hall[j].ap().opt(), reads=["hmy%d" % j], writes=["hall%d" % j])
        phase_ffn(C, "f%d_" % l, Ls[l], G, final, hown, hownkey, hhalo, hhalokey, oall[l], osink_ffn)
    P.emit()
    return nc


def _pk(g, nk):
    return np.ascontiguousarray(np.asarray(g, np.float32).reshape(nk, 128).T)


def _consts():
    c = {}
    p = np.arange(128)[:, None]
    i512 = np.arange(512)[None, :]
    dm = np.zeros((128, 4, 512), np.float32)
    for m in range(4):
        dm[:, m, :] = np.where(128 * m + p <= i512, 0.0, NEG)
    c["dmask"] = dm.reshape(128, 2048).astype(NPBF)
    i128 = np.arange(128)[None, :]
    c["dm4"] = np.tile(np.where(p <= i128, 0.0, NEG), (1, 4)).astype(NPBF)
    c["wm4"] = np.tile(np.where(i128 < p, 0.0, NEG), (1, 4)).astype(NPBF)
    n = np.arange(256)[:, None]
    t = np.arange(S)[None, :]
    mc = np.where((t >= 16 * n + 31) & (n <= 254), 0.0, NEG).astype(np.float32)
    c["maskc"] = np.ascontiguousarray(mc.reshape(2, 128, S).transpose(1, 0, 2).reshape(128, 2 * S)).astype(NPBF)
    j = np.arange(64)[:, None]
    c["eall"] = (np.arange(S)[None, :] // 64 == j).astype(np.float32).astype(NPBF)
    tt_ = np.arange(S)
    cur = (tt_ // 64)[:, None]
    jj = np.arange(64)[None, :]
    sbias = np.zeros((S, 64), np.float32)
    sbias[np.broadcast_to(jj > cur, (S, 64))] = -1e4
    sbias[np.broadcast_to((jj == 0) | (jj == cur) | (jj == cur - 1), (S, 64))] = 1e4
    c["selbias"] = np.ascontiguousarray(sbias.reshape(32, 128, 64).transpose(1, 0, 2).reshape(128, 2048)).astype(NPBF)
    cs = (np.arange(256) * 16)[:, None]
    ss_ = (np.arange(64) * 64)[None, :]
    ov = ((cs < ss_ + 64) & (cs + 32 > ss_)).astype(np.float32)
    ov[255] = 0.0
    ov1 = np.concatenate([ov, np.ones((256, 1), np.float32)], 1)
    c["ovl"] = np.ascontiguousarray(ov1.reshape(2, 128, 65).transpose(1, 0, 2).reshape(128, 130)).astype(NPBF)
    sg = np.zeros((12, 12, 64), np.float32)
    for g in range(12):
        sg[g, g, :] = 1.0
    c["selg"] = sg.reshape(12, 768)
    k = np.arange(S)
    c["kaug"] = np.stack([np.ones(S), np.ones(S), k // 64, k % 64]).astype(np.float32).astype(NPBF)
    e = np.arange(256) * 16 + 31
    c["kaugc"] = np.stack([np.ones(256), np.ones(256), e // 64, e % 64]).astype(np.float32).astype(NPBF)
    inv = 1.0 / (10000.0 ** (np.arange(0, 32, 2, dtype=np.float32) / 32))
    ang = np.arange(S, dtype=np.float32)[:, None] * inv[None, :]
    cos, sin = np.cos(ang).T.astype(np.float32), np.sin(ang).T.astype(np.float32)
    c["ropeC"] = np.ascontiguousarray(np.concatenate([cos, cos], 0))
    c["ropeS"] = np.ascontiguousarray(np.concatenate([-sin, sin], 0))
    return c


def _qaug(group):
    slopes = np.exp2(-8.0 * np.arange(1, 9, dtype=np.float32) / 8)
    t = np.arange(S).reshape(32, 1, 128)
    out = np.zeros((4, 32, 4, 128), np.float32)
    for r in range(4):
        a = slopes[group * 4 + r] / SC_NSA
        out[0, :, r, :] = (-a * 64 * (t // 64))[:, 0, :]
        out[1, :, r, :] = (-a * (t % 64))[:, 0, :]
        out[2, :, r, :] = a * 64
        out[3, :, r, :] = a
    return out.reshape(4, 32 * 512).astype(NPBF)


_PROG = {}


def _prog(nlayers=2):
    if nlayers not in _PROG:
        _PROG[nlayers] = build_fused(nlayers)
    return _PROG[nlayers]


def _layer_maps(l, I, c):
    m = {}
    w_in, w_uq, w_ukv = I["w_in"][l], I["w_uq"][l], I["w_ukv"][l]
    sw = list(range(656, 672)) + list(range(640, 656))
    colsA = list(range(0, 640)) + list(range(0, 64)) + list(range(640, 672)) + list(range(0, 64)) + sw
    m["wAm"] = np.ascontiguousarray(w_in[:, colsA])
    m["gAm"] = _pk(I["attn_norm"][l], 8)
    qc, kc, vc = [], [], []
    for hh in range(4 * c, 4 * c + 4):
        base = 96 * hh
        nope = list(range(base, base + 64))
        rope = list(range(base + 64, base + 96))
        qc += nope + rope + nope + rope[16:] + rope[:16]
        kc += list(range(128 * hh, 128 * hh + 64))
        vc += list(range(128 * hh + 64, 128 * hh + 128))
    m["wq"] = np.ascontiguousarray(w_uq[:, qc])
    m["gq"] = _pk(I["q_norm"][l], 3)
    m["wkv"] = np.ascontiguousarray(w_ukv[:, kc + vc])
    m["gkv"] = _pk(I["kv_norm"][l], 2)
    g = c
    q0 = 672 + 256 * g
    o = 1184
    rng = lambda a: list(range(a, a + 64))
    kcc, vcc, ksc = rng(o + 64 * g), rng(o + 128 + 64 * g), rng(o + 256 + 64 * g)
    vsc, kwc, vwc = rng(o + 384 + 64 * g), rng(o + 512 + 64 * g), rng(o + 640 + 64 * g)
    gtc = list(range(1952 + 12 * g, 1952 + 12 * g + 12))
    cols = list(range(q0, q0 + 256)) + kcc + kcc + vcc + vcc + ksc + kwc + vsc + vwc + gtc
    m["wAn"] = np.ascontiguousarray(w_in[:, cols])
    m["gAn"] = m["gAm"]
    posT = lambda pz: np.ascontiguousarray(np.asarray(pz, np.float32).reshape(16, 128).T)
    m["w1k"] = np.ascontiguousarray(I["cmp_k_w1"][l].reshape(2048, 128))
    m["w1v"] = np.ascontiguousarray(I["cmp_v_w1"][l].reshape(2048, 128))
    m["w2k"] = np.ascontiguousarray(I["cmp_k_w2"][l])
    m["w2v"] = np.ascontiguousarray(I["cmp_v_w2"][l])
    m["posk"] = posT(I["cmp_pos_k"][l])
    m["posv"] = posT(I["cmp_pos_v"][l])
    perm = list(range(0, 256)) + list(range(512, 768)) + list(range(256, 512)) + list(range(768, 1024))
    m["wo"] = np.ascontiguousarray(I["w_o"][l][perm, :])
    m["wup"] = np.ascontiguousarray(I["w_up"][l])
    m["wdn"] = np.ascontiguousarray(I["w_down"][l])
    m["g2"] = _pk(I["ffn_norm"][l], 8)
    cwv = np.stack([I["conv_w"][l][0], I["conv_w"][l][1], I["conv_w"][l][2], I["conv_b"][l]], -1)
    m["cw"] = np.ascontiguousarray(cwv.reshape(44, 128, 4).transpose(1, 0, 2).reshape(128, 176)).astype(np.float32)
    return m


def make_maps(I, nlayers=2):
    cst = _consts()
    lm = [[_layer_maps(l, I, c) for c in range(2)] for l in range(nlayers)]
    maps = []
    for b in range(4):
        for c in range(2):
            m = {"x": np.ascontiguousarray(I["x"][b]),
                 "xown": np.ascontiguousarray(I["x"][b][2048 * c:2048 * c + 2048]),
                 "xhalo": np.ascontiguousarray(I["x"][b][1920:2048]),
                 "flags": np.ascontiguousarray(np.tile(np.array([[1.0 - c, float(c)]], np.float32), (128, 1)))}
            for (n, sh, dt) in GLOB_IN:
                if n == "qaug":
                    m[n] = _qaug(c)
                elif n == "gF":
                    m[n] = np.ascontiguousarray(np.asarray(I["final_norm"], np.float32).reshape(1, DM))
                else:
                    m[n] = cst[n]
            for l in range(nlayers):
                for k, v in lm[l][c].items():
                    m["%s_%d" % (k, l)] = v
            maps.append(m)
    return maps


def kernel(**inputs):
    I = {k: np.asarray(v, dtype=np.float32) for k, v in inputs.items()}
    res = run_bass_kernel_spmd(_prog(2), make_maps(I, 2), core_ids=list(range(8))).results
    out = np.empty((4, S, DM), np.float32)
    for b in range(4):
        for c in range(2):
            out[b, 2048 * c:2048 * c + 2048] = np.asarray(res[2 * b + c]["hout"])
    return out
```

```python
import numpy as np
import ml_dtypes
import concourse.bass as bass
import concourse.mybir as mybir
from concourse.bass_utils import run_bass_kernel_spmd

F32 = mybir.dt.float32
BF16 = mybir.dt.bfloat16
ALU = mybir.AluOpType
AF = mybir.ActivationFunctionType
AX = mybir.AxisListType
NPBF = ml_dtypes.bfloat16

ENGS = ["pe", "act", "dve", "pool", "sp"]
DMA_POOL = 12
S = 4096
DM = 1024
NEG = -30000.0
SC_MLA = 96 ** -0.5
SC_NSA = 0.125
EPS = 1e-6


class Prog:
    def __init__(self, nc):
        self.nc = nc
        self.ops = {e: [] for e in ENGS}
        self.lastw = {}
        self.readers = {}
        self.dma_n = {e: 0 for e in ENGS + ["cc"]}
        self.dma_sem_cnt = {}
        self.last_c = {}
        self.last_d = {}

    def sb(self, name, shape, dt):
        return self.nc.alloc_sbuf_tensor(name, list(shape), dt)

    def ps(self, name, shape, dt=F32):
        return self.nc.alloc_psum_tensor(name, list(shape), dt)

    def _add(self, eng, fn, reads, writes, dma, cc=False):
        op = dict(eng=eng, fn=fn, deps=[], dma=dma, marked=False, inc=(1 if cc else 16))
        deps = []
        for k in reads:
            w = self.lastw.get(k)
            if w is not None:
                deps.append(w)
        for k in writes:
            w = self.lastw.get(k)
            if w is not None:
                deps.append(w)
            deps.extend(self.readers.get(k, ()))
        seen = set()
        for d in deps:
            if id(d) in seen or d is op:
                continue
            seen.add(id(d))
            if (not d["dma"]) and d["eng"] == eng and eng in ("pe", "sp"):
                continue
            op["deps"].append(d)
            d["marked"] = True
        if dma:
            qn = "cc" if cc else eng
            q = self.dma_n[qn]
            self.dma_n[qn] += 1
            semkey = (qn, q % (4 if cc else DMA_POOL))
            m = self.dma_sem_cnt.get(semkey, 0) + 1
            self.dma_sem_cnt[semkey] = m
            op["dsem"] = semkey
            op["dval"] = op["inc"] * m
            op["marked"] = True
            self.last_d[semkey] = op
        else:
            self.last_c[eng] = op
        for k in reads:
            self.readers.setdefault(k, []).append(op)
        for k in writes:
            self.lastw[k] = op
            self.readers[k] = []
        self.ops[eng].append(op)
        return op

    def op(self, eng, fn, reads=(), writes=()):
        return self._add(eng, fn, list(reads), list(writes), False)

    def dma(self, eng, out, in_, reads=(), writes=()):
        return self._add(eng, lambda e: e.dma_start(out=out, in_=in_), list(reads), list(writes), True)

    def cc(self, kind, groups, in_ap, out_ap, reads=(), writes=()):
        return self._add("pool", lambda e: e.collective_compute(kind, ALU.bypass, replica_groups=groups,
                                                                ins=[in_ap], outs=[out_ap]),
                         list(reads), list(writes), True, cc=True)

    def barrier(self):
        deps = list(self.last_c.values()) + list(self.last_d.values())
        for d in deps:
            d["marked"] = True
        for e in ENGS:
            self.ops[e].append(dict(eng=e, fn=None, deps=list(deps), dma=False, marked=False, inc=0))
        self.lastw = {}
        self.readers = {}

    def emit(self):
        nc = self.nc
        csem = {e: nc.alloc_semaphore("c_" + e) for e in ENGS}
        dsem = {}
        for (qn, i) in self.dma_sem_cnt:
            dsem[(qn, i)] = nc.alloc_semaphore("d_%s_%d" % (qn, i))
        for e in ENGS:
            c = 0
            for o in self.ops[e]:
                if o["dma"] or o["fn"] is None:
                    continue
                if o["marked"]:
                    c += 1
                    o["cval"] = c
        all_dma = [o for e in ENGS for o in self.ops[e] if o["dma"]]

        def run(e, eng):
            seen = {}

            def wait(sem_key, sem, val):
                if seen.get(sem_key, 0) >= val:
                    return
                seen[sem_key] = val
                eng.wait_ge(sem, val)

            for o in self.ops[e]:
                for d in o["deps"]:
                    if d["dma"]:
                        wait(d["dsem"], dsem[d["dsem"]], d["dval"])
                    else:
                        wait(("c", d["eng"]), csem[d["eng"]], d["cval"])
                if o["fn"] is None:
                    continue
                if o["dma"]:
                    if o["dval"] > o["inc"]:
                        wait(o["dsem"], dsem[o["dsem"]], o["dval"] - o["inc"])
                    o["fn"](eng).then_inc(dsem[o["dsem"]], o["inc"])
                else:
                    ins = o["fn"](eng)
                    if o["marked"]:
                        ins.then_inc(csem[e], 1)
            if e == "sp":
                last = {}
                for o in all_dma:
                    last[o["dsem"]] = max(last.get(o["dsem"], 0), o["dval"])
                for k, v in last.items():
                    eng.wait_ge(dsem[k], v)

        with nc.Block() as block:
            @block.tensor
            def _(eng):
                run("pe", eng)

            @block.scalar
            def _(eng):
                run("act", eng)

            @block.vector
            def _(eng):
                run("dve", eng)

            @block.gpsimd
            def _(eng):
                run("pool", eng)

            @block.sync
            def _(eng):
                run("sp", eng)


def _nbytes(shape, dt):
    n = 1
    for d in shape[1:]:
        n *= d
    return n * (4 if dt == F32 else 2)


class Ctx:
    def __init__(self, nc):
        self.nc = nc
        self.P = P = Prog(nc)
        self.rots = {}
        self.pname = "g_"
        self.ident = nc.alloc_sbuf_tensor("ident", [128, 128], BF16)
        self.identf = nc.alloc_sbuf_tensor("identf", [128, 128], F32)
        self.onesf = nc.alloc_sbuf_tensor("onesf", [128, 128], F32)
        self.epsn = nc.alloc_sbuf_tensor("epsn", [128, 1], F32)
        self.flags = nc.alloc_sbuf_tensor("flags_sb", [128, 2], F32)
        identf, ident, onesf, epsn = self.identf, self.ident, self.onesf, self.epsn
        P.op("pool", lambda e: e.memset(identf[:], 0.0), writes=["identf"])
        P.op("pool", lambda e: e.affine_select(out=identf[:], in_=identf[:], pattern=[[-1, 128]],
                                                compare_op=ALU.not_equal, fill=1.0, base=0, channel_multiplier=1),
             reads=["identf"], writes=["identf"])
        P.op("dve", lambda e: e.tensor_copy(out=ident[:], in_=identf[:]), reads=["identf"], writes=["ident"])
        P.op("dve", lambda e: e.memset(onesf[:], 1.0), writes=["onesf"])
        P.op("dve", lambda e: e.memset(epsn[:], EPS), writes=["epsn"])
        self.pst = P.ps("pst", [128, 1024], BF16)
        self.banks = [P.ps("bank%d" % i, [128, 512], F32) for i in range(7)]
        self.banks.append(self.pst.bitcast(F32))
        self.base = ((int(nc.sbuf_base) + 63) // 64) * 64
        self.top = int(nc.sbuf_top)
        self.off = self.base
        self.top_off = self.top

    def begin_phase(self, name, keep_top=False):
        self.P.barrier()
        self.pname = name
        self.off = self.base
        if not keep_top:
            self.top_off = self.top

    def sb_top(self, name, shape, dt):
        nb = ((_nbytes(shape, dt) + 31) // 32) * 32
        self.top_off -= nb
        assert self.top_off >= self.off, ("SBUF overflow (top)", name)
        return self.nc.alloc_sbuf_tensor_at(name, list(shape), dt, offset=self.top_off)

    def sb(self, name, shape, dt):
        nb = ((_nbytes(shape, dt) + 31) // 32) * 32
        assert self.off + nb <= self.top_off, ("SBUF overflow", self.pname, name, self.off + nb - self.top_off)
        t = self.nc.alloc_sbuf_tensor_at(self.pname + name, list(shape), dt, offset=self.off)
        self.off += nb
        return t

    def rot(self, name, n):
        i = self.rots.get(name, 0) % n
        self.rots[name] = (i + 1) % n
        return i

    def dram(self, name, shape, dt, out=False):
        return self.nc.dram_tensor(name, list(shape), dt, kind="ExternalOutput" if out else "ExternalInput").ap()

    def mm(self, out, lhsT, rhs, start, stop, reads, writes):
        return self.P.op("pe", lambda e: e.matmul(out, lhsT=lhsT, rhs=rhs, start=start, stop=stop), reads, writes)

    def mmg(self, out, pairs, reads, writes, start=True, stop=True):
        n = len(pairs)

        def fn(e):
            ins = None
            for i, (l, r) in enumerate(pairs):
                ins = e.matmul(out, lhsT=l, rhs=r, start=(start and i == 0), stop=(stop and i == n - 1))
            return ins
        return self.P.op("pe", fn, reads, writes)

    def mmlist(self, items, reads, writes):
        def fn(e):
            ins = None
            for (o, l, r, st, sp) in items:
                ins = e.matmul(o, lhsT=l, rhs=r, start=st, stop=sp)
            return ins
        return self.P.op("pe", fn, reads, writes)

    def tr(self, out, in_, ident, reads, writes):
        return self.P.op("pe", lambda e: e.transpose(out, in_, ident), reads, writes)

    def act(self, out, in_, func, reads, writes, bias=None, scale=1.0, accum=None):
        kw = {}
        if bias is not None:
            kw["bias"] = bias
        if accum is not None:
            kw["accum_out"] = accum
        return self.P.op("act", lambda e: e.activation(out=out, in_=in_, func=func, scale=scale, **kw), reads, writes)

    def tt(self, eng, out, in0, in1, op, reads, writes):
        return self.P.op(eng, lambda e: e.tensor_tensor(out=out, in0=in0, in1=in1, op=op), reads, writes)

    def ts(self, eng, out, in0, s1, op0, reads, writes, s2=None, op1=None):
        if op1 is None:
            return self.P.op(eng, lambda e: e.tensor_scalar(out=out, in0=in0, scalar1=s1, scalar2=None, op0=op0), reads, writes)
        return self.P.op(eng, lambda e: e.tensor_scalar(out=out, in0=in0, scalar1=s1, scalar2=s2, op0=op0, op1=op1), reads, writes)

    def stt(self, eng, out, in0, scalar, in1, op0, op1, reads, writes):
        return self.P.op(eng, lambda e: e.scalar_tensor_tensor(out=out, in0=in0, scalar=scalar, in1=in1, op0=op0, op1=op1), reads, writes)

    def cp(self, eng, out, in_, reads, writes):
        if eng == "act":
            return self.P.op("act", lambda e: e.copy(out=out, in_=in_), reads, writes)
        return self.P.op(eng, lambda e: e.tensor_copy(out=out, in_=in_), reads, writes)

    def recip(self, out, in_, reads, writes):
        return self.P.op("dve", lambda e: e.reciprocal(out=out, in_=in_), reads, writes)

    def memset(self, eng, ap, val, writes):
        return self.P.op(eng, lambda e: e.memset(ap, val), [], writes)

    def bank(self, grp, idxs):
        i = idxs[self.rot(grp, len(idxs))]
        return self.banks[i], ("pst" if i == 7 else "bank%d" % i)

    def load_w(self, name, w_dram, nk, ncols):
        wsb = self.sb(name, [128, nk, ncols], BF16)
        for k in range(nk):
            self.P.dma("pool", wsb[:, k, :], w_dram[k * 128:(k + 1) * 128, :], writes=[name])
        return wsb

    def load_const(self, name, dram_ap, shape, dt, eng="pool"):
        t = self.sb(name, shape, dt)
        self.P.dma(eng, t[:], dram_ap, writes=[name])
        return t

    def setup_norm(self):
        self.ht = [self.sb("ht%d" % i, [128, DM], F32) for i in range(2)]
        self.junk = self.sb("junk", [128, DM], BF16)
        self.nb = self.sb("nb", [128, DM], BF16)
        self.ss = [self.sb("ss%d" % i, [128, 1], F32) for i in range(2)]

    def norm_T(self, src, srckey, nT, nkey, col0, g_sb, gkey):
        b = self.rot("ss", 2)
        ss = self.ss[b]
        sk = "ss%d" % b
        self.memset("dve", ss[:], 0.0, [sk])
        self.act(self.junk[:], src, AF.Square, [srckey, sk], ["junk", sk], accum=ss[:])
        self.act(ss[:], ss[:], AF.Sqrt, [sk, "epsn"], [sk], bias=self.epsn[:], scale=1.0 / DM)
        self.recip(ss[:], ss[:], [sk], [sk])
        self.ts("dve", self.nb[:], src, ss[:, 0:1], ALU.mult, [srckey, sk], ["nb"])
        pst = self.pst
        nb, ident = self.nb, self.ident

        def fn(e):
            ins = None
            for k in range(8):
                ins = e.transpose(pst[:, k * 128:(k + 1) * 128], nb[:, k * 128:(k + 1) * 128], ident[:])
            return ins
        self.P.op("pe", fn, ["nb", "ident"], ["pst"])
        self.tt("dve", nT[:, :, col0:col0 + 128], pst[:, :].rearrange("p (k t) -> p k t", k=8),
                g_sb[:, :].unsqueeze(2).to_broadcast([128, 8, 128]), ALU.mult, ["pst", gkey], [nkey])


def attn_loop(C, tiles, score_fn, scale, v_fn, po, pok, sbanks, pts, ptname, after_first=None, depth=2, hooks=None, col0=None):
    n = len(tiles)
    issued = []

    def issue(i):
        sbk, sk = C.bank("s", sbanks)
        pairs, rd = score_fn(tiles[i])
        c0 = col0(tiles[i]) if col0 is not None else 0
        C.mmg(sbk[:, c0:], pairs, rd, [sk])
        issued.append((sbk, sk, c0))
    for i in range(min(depth, n)):
        issue(i)
    if after_first is not None:
        after_first()
    for i in range(n):
        sbk, sk, c0 = issued[i]
        pi = C.rot(ptname, len(pts))
        C.act(pts[pi][:, c0:], sbk[:, c0:], AF.Exp, [sk], ["%s%d" % (ptname, pi)], scale=scale)
        if i + depth < n:
            issue(i + depth)
        lhsT, rd = v_fn(tiles[i])
        C.mm(po[:, c0:], lhsT, pts[pi][:, c0:], i == 0, i == n - 1, rd + ["%s%d" % (ptname, pi)], [pok])
        if hooks and i in hooks:
            hooks[i]()


def phase_mla(C, name, L, G, hsrc, hkey, osink, after_weights=None):
    P = C.P
    C.begin_phase(name)
    C.setup_norm()
    gA = C.load_const("gA_sb", L["gAm"], [128, 8], F32)
    gq = C.load_const("gq_sb", L["gq"], [128, 3], F32)
    gkv = C.load_const("gkv_sb", L["gkv"], [128, 2], F32)
    dmask = C.load_const("dmask_sb", G["dmask"], [128, 2048], BF16)
    wA = C.load_w("wA_sb", L["wAm"], 8, 832)
    wq = C.load_w("wq_sb", L["wq"], 3, 768)
    wkv = C.load_w("wkv_sb", L["wkv"], 2, 512)
    if after_weights is not None:
        after_weights()
    ropeC_d, ropeS_d = G["ropeC"], G["ropeS"]

    Kh = C.sb("Kh", [128, 4, S], BF16)
    C.memset("pool", Kh[96:128, :, :], 0.0, ["Kh_pad"])
    Vt = C.sb("Vt", [128, 32, 4, 128], BF16)
    C.memset("pool", Vt[:, :, :, 64:65], 1.0, ["Vt_%d" % c for c in range(8)])
    C.memset("pool", Vt[:, :, :, 65:128], 0.0, ["Vt_pad"])
    nTs = [C.sb("nT%d" % i, [128, 8, 512], BF16) for i in range(2)]
    Qhs = [C.sb("Qh%d" % i, [128, 4, 512], BF16) for i in range(2)]
    for i in range(2):
        C.memset("pool", Qhs[i][96:128, :, :], 0.0, ["Qh_pad"])
    zf = C.sb("zf", [128, 3, 512], F32)
    sq = C.sb("sq", [128, 3, 512], F32)
    rr = C.sb("rr", [128, 512], F32)
    cqn = C.sb("cqn", [128, 3, 512], BF16)
    ckvn = C.sb("ckvn", [128, 2, 512], BF16)
    Ct = C.sb("Ct", [96, 512], F32)
    St = C.sb("St", [96, 512], F32)
    t1 = C.sb("t1", [96, 512], F32)
    t2 = C.sb("t2", [96, 512], F32)
    pts = [C.sb("pt%d" % i, [128, 512], BF16) for i in range(4)]
    rsrow = C.sb("rsrow", [65, 512], F32)
    rec4 = C.sb("rec4", [128, 4], F32)
    wb = C.sb("wb", [128, 4, 64], F32)
    bcs = C.sb("bcs", [64, 512], F32)
    ots = [C.sb("ot%d" % i, [64, 512], BF16) for i in range(2)]

    PJ = [0, 1]
    SB_ = [2, 3, 4]
    PO = [5, 6]
    pendA = []
    pendB = []

    def flushA():
        while pendA:
            pendA.pop(0)()

    def flushB():
        flushA()
        while pendB:
            pendB.pop(0)()

    def flush():
        flushB()

    def latent(c0, nm, dim, dst, dkey, nT, nkeys, gl, glkey):
        for m in range(nm):
            pj, pk = C.bank("pj", PJ)
            C.mmg(pj[:, :], [(wA[:, k, c0 + m * 128:c0 + (m + 1) * 128], nT[:, k, :]) for k in range(8)],
                  ["wA_sb"] + nkeys, [pk])
            C.act(zf[:, m, :], pj[:, :], AF.Copy, [pk], ["zf%d" % m])
            C.act(sq[:, m, :], pj[:, :], AF.Square, [pk], ["sq%d" % m])
        pj, pk = C.bank("pj", PJ)
        C.mmg(pj[:, :], [(C.onesf[:, :], sq[:, m, :]) for m in range(nm)], ["onesf"] + ["sq%d" % m for m in range(nm)], [pk])
        C.act(rr[:], pj[:, :], AF.Sqrt, [pk, "epsn"], ["rr"], bias=C.epsn[:], scale=1.0 / dim)
        C.recip(rr[:], rr[:], ["rr"], ["rr"])
        for m in range(nm):
            C.stt("dve", dst[:, m, :], zf[:, m, :], gl[:, m:m + 1], rr[:], ALU.mult, ALU.mult, ["zf%d" % m, "rr", glkey], [dkey])

    def stage_T(tc):
        nT = nTs[tc % 2]
        nkeys = []
        for ti in range(4):
            hb = C.rot("ht", 2)
            P.dma("sp", C.ht[hb][:], hsrc(4 * tc + ti), reads=[hkey(4 * tc + ti)], writes=["ht%d" % hb])
            nk = "nT%d_%d" % (tc % 2, ti)
            C.norm_T(C.ht[hb][:], "ht%d" % hb, nT, nk, ti * 128, gA, "gA_sb")
            nkeys.append(nk)
        return nT, nkeys

    def stage_P1(tc, nT, nkeys):
        t0 = tc * 512
        latent(0, 3, 384.0, cqn, "cqn", nT, nkeys, gq, "gq_sb")
        latent(384, 2, 256.0, ckvn, "ckvn", nT, nkeys, gkv, "gkv_sb")
        P.dma("sp", Ct[64:96, :], ropeC_d[:, t0:t0 + 512], writes=["Ct"])
        P.dma("sp", St[64:96, :], ropeS_d[:, t0:t0 + 512], writes=["St"])
        pA, pAk = C.bank("pj", PJ)
        C.mmg(pA[0:96, :], [(wA[:, k, 640:736], nT[:, k, :]) for k in range(8)], ["wA_sb"] + nkeys, [pAk])
        C.tt("dve", t1[64:96, :], pA[64:96, :], Ct[64:96, :], ALU.mult, [pAk, "Ct"], ["t1"])
        pB, pBk = C.bank("pj", PJ)
        C.mmg(pB[0:96, :], [(wA[:, k, 736:832], nT[:, k, :]) for k in range(8)], ["wA_sb"] + nkeys, [pBk])
        C.tt("dve", t2[64:96, :], pB[64:96, :], St[64:96, :], ALU.mult, [pBk, "St"], ["t2"])
        for hh in range(4):
            C.tt("pool", Kh[64:96, hh, t0:t0 + 512], t1[64:96, :], t2[64:96, :], ALU.add, ["t1", "t2"], ["Kh_%d" % tc])

    def stage_P2(tc):
        t0 = tc * 512
        Qh = Qhs[tc % 2]
        qk = "Qh%d" % (tc % 2)
        for hh in range(4):
            pA, pAk = C.bank("pj", PJ)
            C.mmg(pA[0:96, :], [(wq[:, m, hh * 192:hh * 192 + 96], cqn[:, m, :]) for m in range(3)], ["wq_sb", "cqn"], [pAk])
            C.cp("act", Qh[0:64, hh, :], pA[0:64, :], [pAk], [qk])
            C.tt("dve", t1[64:96, :], pA[64:96, :], Ct[64:96, :], ALU.mult, [pAk, "Ct"], ["t1"])
            pB, pBk = C.bank("pj", PJ)
            C.mmg(pB[0:96, :], [(wq[:, m, hh * 192 + 96:hh * 192 + 192], cqn[:, m, :]) for m in range(3)], ["wq_sb", "cqn"], [pBk])
            C.tt("dve", t2[64:96, :], pB[64:96, :], St[64:96, :], ALU.mult, [pBk, "St"], ["t2"])
            C.tt("pool", Qh[64:96, hh, :], t1[64:96, :], t2[64:96, :], ALU.add, ["t1", "t2"], [qk])
        for hh in range(4):
            pj, pk = C.bank("pj", PJ)
            C.mmg(pj[0:64, :], [(wkv[:, j, hh * 64:(hh + 1) * 64], ckvn[:, j, :]) for j in range(2)], ["wkv_sb", "ckvn"], [pk])
            C.cp("act", Kh[0:64, hh, t0:t0 + 512], pj[0:64, :], [pk], ["Kh_%d" % tc])
        for ti in range(4):
            pj, pk = C.bank("pj", PJ)
            C.mmg(pj[:, 0:256], [(ckvn[:, j, ti * 128:(ti + 1) * 128], wkv[:, j, 256:512]) for j in range(2)], ["wkv_sb", "ckvn"], [pk])
            C.cp("act", Vt[:, 4 * tc + ti, :, 0:64], pj[:, 0:256].rearrange("p (h d) -> p h d", h=4), [pk], ["Vt_%d" % tc])

    def head(tc, hh):
        Qh = Qhs[tc % 2]
        qk = "Qh%d" % (tc % 2)
        po, pok = C.bank("po", PO)
        nkt = 4 * tc + 4

        def c0f(j):
            return 128 * (j - 4 * tc) if j >= 4 * tc else 0

        def score(j):
            c0 = c0f(j)
            pairs = [(Kh[:, hh, j * 128:(j + 1) * 128], Qh[:, hh, c0:])]
            rd = [qk, "Kh_%d" % (j // 4), "Kh_pad", "Qh_pad"]
            if j >= 4 * tc:
                m = j - 4 * tc
                pairs.append((C.ident[:, :], dmask[:, m * 512 + c0:(m + 1) * 512]))
                rd += ["ident", "dmask_sb"]
            return pairs, rd

        def vfn(j):
            return Vt[:, j, hh, :], ["Vt_%d" % (j // 4), "Vt_pad"]
        attn_loop(C, list(range(nkt)), score, SC_MLA, vfn, po, pok, SB_, pts, "pt", after_first=flushA, hooks={2: flushB}, col0=c0f)

        def finA():
            C.cp("dve", rsrow[64:65, :], po[64:65, :], [pok], ["rsrow"])
            pj, pk = C.bank("pj", PJ)
            C.mmlist([(pj[:, r:r + 1], rsrow[64:65, r * 128:(r + 1) * 128], C.onesf[64:65, 0:1], True, True) for r in range(4)],
                     ["rsrow", "onesf"], [pk])
            C.ts("dve", rec4[:, :], pj[:, 0:4], 1e-30, ALU.add, [pk], ["rec4"])
            C.recip(rec4[:, :], rec4[:, :], ["rec4"], ["rec4"])
            C.cp("dve", wb[:, :, :], rec4[:, 0:4].unsqueeze(2).to_broadcast([128, 4, 64]), ["rec4"], ["wb"])

        def finB():
            pj2, pk2 = C.bank("pj", PJ)
            C.mmlist([(pj2[0:64, r * 128:(r + 1) * 128], wb[:, r, :], C.identf[:, :], True, True) for r in range(4)],
                     ["wb", "identf"], [pk2])
            C.cp("act", bcs[:, :], pj2[0:64, :], [pk2], ["bcs"])
            oi = C.rot("ot", 2)
            C.tt("dve", ots[oi][:, :], po[0:64, :], bcs[:, :], ALU.mult, [pok, "bcs"], ["ot%d" % oi])
            osink(hh, tc, ots[oi], "ot%d" % oi)
        pendA.append(finA)
        pendB.append(finB)

    st = stage_T(0)
    stage_P1(0, *st)
    stage_P2(0)
    for tc in range(8):
        head(tc, 0)
        if tc < 7:
            st = stage_T(tc + 1)
        head(tc, 1)
        if tc < 7:
            stage_P1(tc + 1, *st)
        head(tc, 2)
        if tc < 7:
            stage_P2(tc + 1)
        head(tc, 3)
    flush()


def phase_nsa(C, name, L, G, hsrc, hkey, osink, pre=None):
    P = C.P
    pre = pre or {}
    C.begin_phase(name, keep_top=bool(pre))
    NW = 780
    C.setup_norm()
    gA = C.load_const("gA_sb", L["gAn"], [128, 8], F32)
    wA = pre["wA_sb"] if "wA_sb" in pre else C.load_w("wA_sb", L["wAn"], 8, NW)
    w1k = pre["w1k_sb"] if "w1k_sb" in pre else C.load_w("w1k_sb", L["w1k"], 16, 128)
    w1v = pre["w1v_sb"] if "w1v_sb" in pre else C.load_w("w1v_sb", L["w1v"], 16, 128)
    w2k = C.load_w("w2k_sb", L["w2k"], 1, 64)
    w2v = C.load_w("w2v_sb", L["w2v"], 1, 64)
    posk = C.load_w("posk_sb", L["posk"], 1, 16)
    posv = C.load_w("posv_sb", L["posv"], 1, 16)
    maskc = pre["maskc_sb"] if "maskc_sb" in pre else C.load_const("maskc_sb", G["maskc"], [128, 2 * S], BF16)
    eall = C.sb("eall_sb", [128, S], BF16)
    C.memset("pool", eall[64:128, :], 0.0, ["eall_sb"])
    P.dma("pool", eall[0:64, :], G["eall"], writes=["eall_sb"])
    selbias = pre["selbias_sb"] if "selbias_sb" in pre else C.load_const("selbias_sb", G["selbias"], [128, 2048], BF16)
    ovl = C.load_const("ovl_sb", G["ovl"], [128, 130], BF16)
    selg = C.load_const("selg_sb", G["selg"], [12, 768], F32)
    dm4 = C.load_const("dm4_sb", G["dm4"], [128, 512], BF16)
    wm4 = C.load_const("wm4_sb", G["wm4"], [128, 512], BF16)

    Qa = C.sb("Qa", [128, 32, 512], BF16)
    Kw = C.sb("Kw", [128, S], BF16)
    Ks = C.sb("Ks", [128, S], BF16)
    Kc = C.sb("Kc", [128, 256], BF16)
    C.memset("pool", Qa[64:128, :, :], 0.0, ["Qa_aug"])
    C.memset("pool", Kw[64:128, :], 0.0, ["Kw_aug"])
    C.memset("pool", Ks[64:128, :], 0.0, ["Ks_aug"])
    C.memset("pool", Kc[64:128, :], 0.0, ["Kc_aug"])
    P.dma("pool", Qa[64:68, :, :], G["qaug"].rearrange("p (a b) -> p a b", a=32), writes=["Qa_aug"])
    P.dma("pool", Kw[64:68, :], G["kaug"], writes=["Kw_aug"])
    P.dma("pool", Ks[64:68, :], G["kaug"], writes=["Ks_aug"])
    P.dma("pool", Kc[64:68, :], G["kaugc"], writes=["Kc_aug"])
    kc2 = C.sb("kc2", [128, S + 32], BF16)
    vc2 = C.sb("vc2", [128, S + 32], BF16)
    C.memset("pool", kc2[:, S:S + 32], 0.0, ["kc2_tail"])
    C.memset("pool", vc2[:, S:S + 32], 0.0, ["vc2_tail"])
    Vs = C.sb("Vs", [128, 32, 128], BF16)
    Vw = C.sb("Vw", [128, 32, 128], BF16)
    Vc = C.sb("Vc", [128, 2, 128], BF16)
    for (vt_, keys_) in ((Vs, ["Vs_%d" % c for c in range(8)]), (Vw, ["Vw_%d" % c for c in range(8)]), (Vc, ["Vc"])):
        C.memset("pool", vt_[:, :, 65:128], 0.0, keys_)
        C.memset("pool", vt_[:, :, 64:65], 1.0, keys_)
    Gtm = C.sb("Gtm", [128, 32, 12], F32)
    nTs = [C.sb("nT%d" % i, [128, 8, 512], BF16) for i in range(2)]

    PJ = [0, 1]
    SB_ = [2, 3]
    POC, POS, POW = 4, 5, 6

    for tc in range(8):
        t0 = tc * 512
        nb_ = C.rot("nT", 2)
        nT = nTs[nb_]
        nkeys = []
        for ti in range(4):
            hb = C.rot("ht", 2)
            P.dma("sp", C.ht[hb][:], hsrc(4 * tc + ti), reads=[hkey(4 * tc + ti)], writes=["ht%d" % hb])
            nk = "nT%d_%d" % (nb_, ti)
            C.norm_T(C.ht[hb][:], "ht%d" % hb, nT, nk, ti * 128, gA, "gA_sb")
            nkeys.append(nk)

        def proj(c0, m, rows=128):
            pj, pk = C.bank("pj", PJ)
            C.mmg(pj[0:rows, :], [(wA[:, k, c0:c0 + m], nT[:, k, :]) for k in range(8)], ["wA_sb"] + nkeys, [pk])
            return pj, pk
        for r in range(4):
            pj, pk = proj(r * 64, 64, 64)
            C.cp("act", Qa[0:64, 4 * tc:4 * tc + 4, r * 128:(r + 1) * 128],
                 pj[0:64, :].rearrange("p (a b) -> p a b", a=4), [pk], ["Qa_%d" % tc])
        for (c0, dst, dk) in ((256, kc2, "kc2"), (384, vc2, "vc2")):
            pj, pk = proj(c0, 128)
            C.cp("act", dst[0:64, t0:t0 + 512], pj[0:64, :], [pk], [dk])
            if tc == 0:
                C.cp("dve", dst[64:128, 0:511], pj[64:128, 1:512], [pk], [dk])
            else:
                C.cp("dve", dst[64:128, t0 - 1:t0 + 511], pj[64:128, :], [pk], [dk])
        pj, pk = proj(512, 64, 64)
        C.cp("act", Ks[0:64, t0:t0 + 512], pj[0:64, :], [pk], ["Ks_%d" % tc])
        pj, pk = proj(576, 64, 64)
        C.cp("act", Kw[0:64, t0:t0 + 512], pj[0:64, :], [pk], ["Kw_%d" % tc])
        for ti in range(4):
            pj, pk = C.bank("pj", PJ)
            C.mmg(pj[:, 0:128], [(nT[:, k, ti * 128:(ti + 1) * 128], wA[:, k, 640:768]) for k in range(8)],
                  ["wA_sb"] + nkeys, [pk])
            pg, pgk = C.bank("pj", PJ)
            C.mmg(pg[:, 0:12], [(nT[:, k, ti * 128:(ti + 1) * 128], wA[:, k, 768:780]) for k in range(8)],
                  ["wA_sb"] + nkeys, [pgk])
            C.cp("dve", Gtm[:, 4 * tc + ti, :], pg[:, 0:12], [pgk], ["Gtm_%d" % tc])
            C.cp("act", Vs[:, 4 * tc + ti, 0:64], pj[:, 0:64], [pk], ["Vs_%d" % tc])
            C.cp("dve", Vw[:, 4 * tc + ti, 0:64], pj[:, 64:128], [pk], ["Vw_%d" % tc])

    C.act(Gtm[:, :, :], Gtm[:, :, :], AF.Sigmoid, ["Gtm_%d" % c for c in range(8)], ["Gtm_%d" % c for c in range(8)])
    xs = C.sb("xs", [128, 256], F32)
    x2 = C.sb("x2", [128, 256], F32)
    hid = C.sb("hid", [128, 256], BF16)
    cbias = C.sb("cbias", [128, 1], F32)
    for (src, skey, w1, w1key, pos, poskey, isk) in ((kc2, "kc2", w1k, "w1k_sb", posk, "posk_sb", True),
                                                     (vc2, "vc2", w1v, "w1v_sb", posv, "posv_sb", False)):
        pj, pk = C.bank("pj", PJ)
        C.mmg(pj[:, 0:1], [(w1[:, j, :], pos[:, 0, j:j + 1]) for j in range(16)], [w1key, poskey], [pk])
        C.cp("act", cbias[:], pj[:, 0:1], [pk], ["cbias"])
        pj, pk = C.bank("pj", PJ)
        C.mmg(pj[:, 0:255], [(w1[:, j, :], src[:, 2 * j:2 * j + 16 * 255:16]) for j in range(16)],
              [w1key, skey, skey + "_tail"], [pk])
        C.memset("dve", xs[:, 255:256], 0.0, ["xs"])
        C.act(xs[:, 0:255], pj[:, 0:255], AF.Identity, [pk, "cbias"], ["xs"], bias=cbias[:])
        C.tt("dve", x2[:], xs[:], xs[:], ALU.mult, ["xs"], ["x2"])
        C.ts("dve", x2[:], x2[:], 0.044715, ALU.mult, ["x2"], ["x2"], s2=1.0, op1=ALU.add)
        C.tt("dve", x2[:], x2[:], xs[:], ALU.mult, ["x2", "xs"], ["x2"])
        C.act(x2[:], x2[:], AF.Sigmoid, ["x2"], ["x2"], scale=1.5957691216057308)
        C.tt("dve", hid[:], xs[:], x2[:], ALU.mult, ["x2", "xs"], ["hid"])
        if isk:
            pj, pk = C.bank("pj", PJ)
            C.mm(pj[0:64, 0:256], w2k[:, 0, :], hid[:], True, True, ["w2k_sb", "hid"], [pk])
            C.cp("act", Kc[0:64, :], pj[0:64, 0:256], [pk], ["Kc"])
        else:
            for nt in range(2):
                pj, pk = C.bank("pj", PJ)
                C.mm(pj[:, 0:64], hid[:, nt * 128:(nt + 1) * 128], w2v[:, 0, :], True, True, ["w2v_sb", "hid"], [pk])
                C.cp("act", Vc[:, nt, 0:64], pj[:, 0:64], [pk], ["Vc"])

    ptc = [[C.sb("ptc%d_%d" % (i, j), [128, 512], BF16) for j in range(2)] for i in range(2)]
    pts = [C.sb("pt%d" % i, [128, 512], BF16) for i in range(4)]
    imp = C.sb("imp", [128, 64], F32)
    wk = C.sb("wk", [128, 64], F32)
    m8 = C.sb("m8", [128, 16], F32)
    rci = C.sb("rci", [128, 4], F32)
    selb = C.sb("selb", [128, 64], BF16)
    selT4s = [C.sb("selT4_%d" % i, [128, 512], BF16) for i in range(2)]
    for i in range(2):
        C.memset("pool", selT4s[i][64:128, :], 0.0, ["selT4_%d" % i])
    rsrows = [C.sb("rsrow%d" % i, [65, 512], F32) for i in range(3)]
    bcss = [C.sb("bcs%d" % i, [64, 512], F32) for i in range(3)]
    tmpos = [C.sb("tmpo%d" % i, [64, 512], F32) for i in range(3)]
    rec4s = [C.sb("rec4_%d" % i, [128, 4], F32) for i in range(3)]
    wbs = [C.sb("wb%d" % i, [128, 4, 64], F32) for i in range(3)]
    oacc = [C.sb("oacc%d" % i, [64, 512], F32) for i in range(2)]
    oaccb = [C.sb("oaccb%d" % i, [64, 512], BF16) for i in range(2)]
    PJ = [0, 7]
    SB_ = [1, 2, 3]
    pending = []

    def flush():
        tl = list(pending)
        del pending[:]
        while any(tl):
            for t in tl:
                if t:
                    t.pop(0)()

    def finalize_steps(po, pok, br, qb, first, rec_src=None):
        ai = qb % 2
        acc, ak = oacc[ai], "oacc%d" % ai
        rsrow, bcs, tmpo, rec4, wb = rsrows[br], bcss[br], tmpos[br], rec4s[br], wbs[br]
        rk, bk, tk, r4k, wk_ = "rsrow%d" % br, "bcs%d" % br, "tmpo%d" % br, "rec4_%d" % br, "wb%d" % br

        def s1():
            if rec_src is None:
                C.cp("dve", rsrow[64:65, :], po[64:65, :], [pok], [rk])
                pj, pk = C.bank("pj", PJ)
                C.mmlist([(pj[:, r:r + 1], rsrow[64:65, r * 128:(r + 1) * 128], C.onesf[64:65, 0:1], True, True) for r in range(4)],
                         [rk, "onesf"], [pk])
                C.ts("dve", rec4[:, :], pj[:, 0:4], 1e-30, ALU.add, [pk], [r4k])

        def s2():
            if rec_src is None:
                C.recip(rec4[:, :], rec4[:, :], [r4k], [r4k])
                recap, reckey = rec4, r4k
            else:
                recap, reckey = rec_src
            C.tt("dve", wb[:, :, :], recap[:, 0:4].unsqueeze(2).to_broadcast([128, 4, 64]),
                 Gtm[:, qb, br:12:3].unsqueeze(2).to_broadcast([128, 4, 64]), ALU.mult, [reckey, "Gtm_%d" % (qb // 4)], [wk_])

        def s3():
            pj2, pk2 = C.bank("pj", PJ)
            C.mmlist([(pj2[0:64, r * 128:(r + 1) * 128], wb[:, r, :], C.identf[:, :], True, True) for r in range(4)],
                     [wk_, "identf"], [pk2])
            C.cp("act", bcs[:, :], pj2[0:64, :], [pk2], [bk])

        def s4():
            if first:
                C.tt("dve", acc[:, :], po[0:64, :], bcs[:, :], ALU.mult, [pok, bk], [ak])
            else:
                C.tt("dve", tmpo[:, :], po[0:64, :], bcs[:, :], ALU.mult, [pok, bk], [tk])
                C.tt("dve", acc[:, :], acc[:, :], tmpo[:, :], ALU.add, [tk, ak], [ak])
        return [s1, s2, s3, s4]

    def qinfo(qb):
        return Qa[:, qb, :], ["Qa_%d" % (qb // 4), "Qa_aug"]

    def cmp_stage(qb):
        q_rhs, qkeys = qinfo(qb)
        pc = ptc[qb % 2]
        selT4, stk = selT4s[qb % 2], "selT4_%d" % (qb % 2)
        ntn = 1 if qb < 16 else 2
        po = C.banks[POC]
        for nt in range(ntn):
            sbk, sk = C.bank("s", SB_)
            items = [(sbk[:, :], Kc[:, nt * 128:(nt + 1) * 128], q_rhs, True, False)]
            for r in range(4):
                items.append((sbk[:, r * 128:(r + 1) * 128], C.ident[:, :],
                              maskc[:, nt * S + qb * 128:nt * S + (qb + 1) * 128], False, r == 3))
            C.mmlist(items, qkeys + ["Kc", "Kc_aug", "ident", "maskc_sb"], [sk])
            C.act(pc[nt][:], sbk[:, :], AF.Exp, [sk], ["ptc%d_%d" % (qb % 2, nt)], scale=SC_NSA)
        for nt in range(ntn):
            C.mm(po[:, :], Vc[:, nt, :], pc[nt][:], nt == 0, nt == ntn - 1,
                 ["Vc", "ptc%d_%d" % (qb % 2, nt)], ["bank%d" % POC])
        pj, pk = C.bank("pj", PJ)
        items = []
        for r in range(4):
            for nt in range(ntn):
                items.append((pj[:, r * 65:(r + 1) * 65], pc[nt][:, r * 128:(r + 1) * 128], ovl[:, nt * 65:(nt + 1) * 65],
                              nt == 0, nt == ntn - 1))
        C.mmlist(items, ["ovl_sb"] + ["ptc%d_%d" % (qb % 2, nt) for nt in range(ntn)], [pk])
        for r in range(4):
            C.ts("dve", rci[:, r:r + 1], pj[:, r * 65 + 64:r * 65 + 65], 1e-30, ALU.add, [pk], ["rci"])
        C.recip(rci[:, 0:4], rci[:, 0:4], ["rci"], ["rci"])
        for r in range(4):
            prev = selbias[:, qb * 64:(qb + 1) * 64] if r == 0 else imp[:]
            C.stt("dve", imp[:], pj[:, r * 65:r * 65 + 64], rci[:, r:r + 1], prev, ALU.mult, ALU.add,
                  [pk, "rci", "imp", "selbias_sb"], ["imp"])
        P.op("dve", lambda e: e.max(out=m8[:, 0:8], in_=imp[:]), ["imp"], ["m8"])
        P.op("dve", lambda e: e.match_replace(out=wk[:], in_to_replace=m8[:, 0:8], in_values=imp[:], imm_value=-1e9),
             ["imp", "m8"], ["wk"])
        P.op("dve", lambda e: e.max(out=m8[:, 8:16], in_=wk[:]), ["wk"], ["m8"])
        C.ts("dve", wk[:], imp[:], m8[:, 15:16], ALU.is_ge, ["imp", "m8"], ["wk"])
        C.ts("dve", selb[:], wk[:], -NEG, ALU.mult, ["wk"], ["selb"], s2=NEG, op1=ALU.add)

        def s0(qb=qb, selT4=selT4, stk=stk):
            C.tr(C.pst[0:64, 0:128], selb[:, :], C.ident[:, :], ["selb", "ident"], ["pst"])
            for r in range(4):
                C.cp("act" if r % 2 == 0 else "dve", selT4[0:64, r * 128:(r + 1) * 128], C.pst[0:64, 0:128], ["pst"], [stk])
        C.cp("dve", rec4s[0][:, :], rci[:, 0:4], ["rci"], ["rec4_0"])
        pending.append([s0] + finalize_steps(po, "bank%d" % POC, 0, qb, True, rec_src=(rec4s[0], "rec4_0")))

    def sel_stage(qb):
        q_rhs, qkeys = qinfo(qb)
        selT4, stk = selT4s[qb % 2], "selT4_%d" % (qb % 2)
        po = C.banks[POS]

        def score(kt):
            pairs = [(Ks[:, kt * 128:(kt + 1) * 128], q_rhs)]
            rd = qkeys + ["Ks_%d" % (kt // 4), "Ks_aug"]
            if kt == qb:
                pairs.append((C.ident[:, :], dm4[:, :]))
                rd += ["ident", "dm4_sb"]
            else:
                pairs.append((eall[:, kt * 128:(kt + 1) * 128], selT4[:, :]))
                rd += ["eall_sb", stk]
            return pairs, rd
        attn_loop(C, list(range(qb + 1)), score, SC_NSA, lambda kt: (Vs[:, kt, :], ["Vs_%d" % (kt // 4)]),
                  po, "bank%d" % POS, SB_, pts, "pt", after_first=None)

        def s5(qb=qb):
            ai = qb % 2
            C.cp("act", oaccb[ai][:, :], oacc[ai][:, :], ["oacc%d" % ai], ["oaccb%d" % ai])
            osink(qb, oaccb[ai], "oaccb%d" % ai)
        pending.append(finalize_steps(po, "bank%d" % POS, 1, qb, False) + [s5])

    def win_stage(qb):
        q_rhs, qkeys = qinfo(qb)
        po = C.banks[POW]
        k0 = max(0, qb - 4)

        def score(kt):
            pairs = [(Kw[:, kt * 128:(kt + 1) * 128], q_rhs)]
            rd = qkeys + ["Kw_%d" % (kt // 4), "Kw_aug"]
            if kt == qb:
                pairs.append((C.ident[:, :], dm4[:, :]))
                rd += ["ident", "dm4_sb"]
            if kt == qb - 4:
                pairs.append((C.ident[:, :], wm4[:, :]))
                rd += ["ident", "wm4_sb"]
            return pairs, rd
        attn_loop(C, list(range(k0, qb + 1)), score, SC_NSA, lambda kt: (Vw[:, kt, :], ["Vw_%d" % (kt // 4)]),
                  po, "bank%d" % POW, SB_, pts, "pt", after_first=flush)

        pending.append(finalize_steps(po, "bank%d" % POW, 2, qb, False))

    cmp_stage(0)
    for qb in range(32):
        win_stage(qb)
        if qb + 1 < 32:
            cmp_stage(qb + 1)
        sel_stage(qb)
    flush()


def phase_ffn(C, name, L, G, final, hown, hownkey, hhalo, hhalokey, oall, osink):
    P = C.P
    C.begin_phase(name)
    C.setup_norm()
    fl = C.flags
    g2 = C.load_const("g2_sb", L["g2"], [128, 8], F32)
    cw = C.load_const("cw_sb", L["cw"], [128, 176], F32)
    if final:
        gF = C.sb("gF_sb", [128, DM], F32)
        P.dma("pool", gF[:], G["gF"][0:1, :].partition_broadcast(128), writes=["gF_sb"])
    wup = C.sb("wup_sb", [128, 8, 5632], BF16)
    wup_src = L["wup"].rearrange("(k p) c -> p k c", p=128)
    for i0 in range(0, 22, 4):
        n4 = min(4, 22 - i0)
        for base in (i0, 22 + i0):
            P.dma("pool", wup[:, :, base * 128:(base + n4) * 128], wup_src[:, :, base * 128:(base + n4) * 128],
                  writes=["wup_%d" % fc for fc in range(base, base + n4)])
    wdn = C.sb("wdn_sb", [128, 22, DM], BF16)

    def load_wdn():
        for k in range(22):
            P.dma("pool", wdn[:, k, :], L["wdn"][k * 128:(k + 1) * 128, :], writes=["wdn_sb"])
    wo_d = L["wo"]

    NC_ = 256
    aT = [C.sb("aT%d" % i, [128, NC_], BF16) for i in range(4)]
    hm = C.sb("hm", [128, 2, DM], F32)
    n2T = C.sb("n2T", [128, 8, NC_], BF16)
    oTb = C.sb("oTb", [128, 8, NC_], BF16)
    oa = [C.sb("oa%d" % i, [128, NC_], BF16) for i in range(2)]
    ob = [C.sb("ob%d" % i, [128, NC_], BF16) for i in range(2)]
    wob = [C.sb("wob%d" % i, [128, DM], BF16) for i in range(4)]
    ubuf = [C.sb("ubuf%d" % i, [128, NC_ + 2], F32) for i in range(3)]
    tb = [C.sb("tb%d" % i, [128, NC_], F32) for i in range(4)]
    sg = C.sb("sg", [128, NC_], F32)
    carry = C.sb("carry", [128, 44, 2], F32)
    res = C.sb("res", [128, DM], F32)
    ss2 = C.sb("ss2", [128, 1], F32)

    ACC = [0, 1, 2, 3]
    UP = [4, 5, 6]

    def chunk(ci, halo):
        nt_ = 1 if halo else 2
        ncol = nt_ * 128
        c0 = 1920 if halo else ci * 256
        for k in range(8):
            b = C.rot("oa", 2)
            if halo:
                P.dma("sp", oa[b][:, 0:ncol], oall[0][k * 128:(k + 1) * 128, c0:c0 + ncol], reads=["oall0"], writes=["oa%d" % b])
                C.ts("dve", oTb[:, k, 0:ncol], oa[b][:, 0:ncol], fl[:, 1:2], ALU.mult, ["oa%d" % b, "flags"], ["oTb"])
            else:
                P.dma("sp", oa[b][:, 0:ncol], oall[0][k * 128:(k + 1) * 128, c0:c0 + ncol], reads=["oall0"], writes=["oa%d" % b])
                P.dma("sp", ob[b][:, 0:ncol], oall[1][k * 128:(k + 1) * 128, c0:c0 + ncol], reads=["oall1"], writes=["ob%d" % b])
                C.ts("dve", oa[b][:, 0:ncol], oa[b][:, 0:ncol], fl[:, 0:1], ALU.mult, ["oa%d" % b, "flags"], ["oa%d" % b])
                C.stt("dve", oTb[:, k, 0:ncol], ob[b][:, 0:ncol], fl[:, 1:2], oa[b][:, 0:ncol], ALU.mult, ALU.add,
                      ["oa%d" % b, "ob%d" % b, "flags"], ["oTb"])
        for k in range(8):
            wb = C.rot("wob", 4)
            P.dma("pool", wob[wb][:, :], wo_d[k * 128:(k + 1) * 128, :], writes=["wob%d" % wb])
            items = []
            for ti in range(nt_):
                for hf in range(2):
                    items.append((C.banks[ACC[ti * 2 + hf]][:, :], oTb[:, k, ti * 128:(ti + 1) * 128],
                                  wob[wb][:, hf * 512:(hf + 1) * 512], k == 0, k == 7))
            C.mmlist(items, ["oTb", "wob%d" % wb], ["bank%d" % ACC[i] for i in range(nt_ * 2)])
        if ci == 0 and not halo:
            load_wdn()
        for ti in range(nt_):
            hb = C.rot("ht", 2)
            if halo:
                P.dma("sp", C.ht[hb][:], hhalo, reads=[hhalokey], writes=["ht%d" % hb])
                C.ts("dve", C.ht[hb][:], C.ht[hb][:], fl[:, 1:2], ALU.mult, ["ht%d" % hb, "flags"], ["ht%d" % hb])
            else:
                P.dma("sp", C.ht[hb][:], hown(2 * ci + ti), reads=[hownkey(2 * ci + ti)], writes=["ht%d" % hb])
            for hf in range(2):
                C.tt("dve", hm[:, ti, hf * 512:(hf + 1) * 512], C.banks[ACC[ti * 2 + hf]][:, :],
                     C.ht[hb][:, hf * 512:(hf + 1) * 512], ALU.add, ["bank%d" % ACC[ti * 2 + hf], "ht%d" % hb], ["hm%d" % ti])
            C.norm_T(hm[:, ti, :], "hm%d" % ti, n2T, "n2T_%d" % ti, ti * 128, g2, "g2_sb")
        nkeys = ["n2T_%d" % ti for ti in range(nt_)]
        dq = []

        def down(i, ai):
            items = []
            for ti in range(nt_):
                for hf in range(2):
                    items.append((C.banks[ACC[ti * 2 + hf]][:, :], aT[ai][:, ti * 128:(ti + 1) * 128],
                                  wdn[:, i, hf * 512:(hf + 1) * 512], i == 0, i == 21))
            C.mmlist(items, ["aT%d" % ai, "wdn_sb"], ["bank%d" % ACC[q] for q in range(nt_ * 2)])
        for i in range(22):
            tfin = []
            for part in range(2):
                fc = i + 22 * part
                up, upk = C.bank("up", UP)
                if halo:
                    C.mmg(up[:, 0:2], [(wup[:, k, fc * 128:(fc + 1) * 128], n2T[:, k, ncol - 2:ncol]) for k in range(8)],
                          ["wup_%d" % fc] + nkeys, [upk])
                    C.cp("act", carry[:, fc, :], up[:, 0:2], [upk], ["carry%d" % fc])
                    continue
                C.mmg(up[:, 0:ncol], [(wup[:, k, fc * 128:(fc + 1) * 128], n2T[:, k, 0:ncol]) for k in range(8)],
                      ["wup_%d" % fc] + nkeys, [upk])
                ub = C.rot("ubuf", 3)
                u = ubuf[ub]
                uk = "ubuf%d" % ub
                C.cp("pool", u[:, 0:2], carry[:, fc, :], ["carry%d" % fc], [uk])
                C.cp("act", u[:, 2:2 + ncol], up[:, 0:ncol], [upk], [uk])
                if not halo:
                    ta = C.rot("tb", 4)
                    C.act(tb[ta][:, 0:ncol], up[:, 0:ncol], AF.Identity, [upk, "cw_sb"], ["tb%d" % ta],
                          bias=cw[:, fc * 4 + 3:fc * 4 + 4], scale=cw[:, fc * 4 + 2:fc * 4 + 3])
                    C.stt("dve", tb[ta][:, 0:ncol], u[:, 1:1 + ncol], cw[:, fc * 4 + 1:fc * 4 + 2], tb[ta][:, 0:ncol],
                          ALU.mult, ALU.add, [uk, "cw_sb", "tb%d" % ta], ["tb%d" % ta])
                    C.stt("dve", tb[ta][:, 0:ncol], u[:, 0:ncol], cw[:, fc * 4:fc * 4 + 1], tb[ta][:, 0:ncol],
                          ALU.mult, ALU.add, [uk, "cw_sb", "tb%d" % ta], ["tb%d" % ta])
                    tfin.append(ta)
                C.cp("pool", carry[:, fc, :], u[:, ncol:ncol + 2], [uk], ["carry%d" % fc])
            if not halo:
                C.act(sg[:, 0:ncol], tb[tfin[0]][:, 0:ncol], AF.Silu, ["tb%d" % tfin[0]], ["sg"])
                ai = C.rot("aT", 4)
                C.tt("dve", aT[ai][:, 0:ncol], sg[:, 0:ncol], tb[tfin[1]][:, 0:ncol], ALU.mult,
                     ["sg", "tb%d" % tfin[1]], ["aT%d" % ai])
                dq.append((i, ai))
                if len(dq) > 2:
                    down(*dq.pop(0))
        while dq:
            down(*dq.pop(0))
        if halo:
            return
        for ti in range(nt_):
            for hf in range(2):
                bk = ACC[ti * 2 + hf]
                C.tt("dve", res[:, hf * 512:(hf + 1) * 512], C.banks[bk][:, :], hm[:, ti, hf * 512:(hf + 1) * 512],
                     ALU.add, ["bank%d" % bk, "hm%d" % ti], ["res"])
            if final:
                C.memset("dve", ss2[:], 0.0, ["ss2"])
                C.act(C.junk[:], res[:], AF.Square, ["res", "ss2"], ["junk", "ss2"], accum=ss2[:])
                C.act(ss2[:], ss2[:], AF.Sqrt, ["ss2", "epsn"], ["ss2"], bias=C.epsn[:], scale=1.0 / DM)
                C.recip(ss2[:], ss2[:], ["ss2"], ["ss2"])
                C.stt("dve", res[:], res[:], ss2[:, 0:1], gF[:], ALU.mult, ALU.mult, ["res", "ss2", "gF_sb"], ["res"])
            osink(2 * ci + ti, res, "res")

    C.memset("pool", carry[:], 0.0, ["carry%d" % fc for fc in range(44)])
    chunk(0, True)
    for c in range(8):
        chunk(c, False)


LAYER_IN = [("wAm", [DM, 832]), ("gAm", [128, 8]), ("wq", [384, 768]), ("gq", [128, 3]), ("wkv", [256, 512]),
            ("gkv", [128, 2]), ("wAn", [DM, 780]), ("gAn", [128, 8]), ("w1k", [2048, 128]), ("w1v", [2048, 128]),
            ("w2k", [128, 64]), ("w2v", [128, 64]), ("posk", [128, 16]), ("posv", [128, 16]),
            ("wo", [DM, DM]), ("wup", [DM, 5632]), ("wdn", [2816, DM]), ("g2", [128, 8]), ("cw", [128, 176])]
GLOB_IN = [("ropeC", [32, S], F32), ("ropeS", [32, S], F32), ("dmask", [128, 2048], BF16),
           ("maskc", [128, 2 * S], BF16), ("eall", [64, S], BF16), ("selbias", [128, 2048], BF16),
           ("ovl", [128, 130], BF16), ("selg", [12, 768], F32), ("dm4", [128, 512], BF16), ("wm4", [128, 512], BF16),
           ("qaug", [4, 32 * 512], BF16), ("kaug", [4, S], BF16), ("kaugc", [4, 256], BF16), ("gF", [1, DM], F32)]
GROUPS = [[0, 1], [2, 3], [4, 5], [6, 7]]


def build_fused(nlayers=2):
    nc = bass.Bass("TRN2", target_bir_lowering=False)
    C = Ctx(nc)
    P = C.P
    x_d = C.dram("x", [S, DM], F32)
    xown_d = C.dram("xown", [2048, DM], F32)
    xhalo_d = C.dram("xhalo", [128, DM], F32)
    flags_d = C.dram("flags", [128, 2], F32)
    G = {n: C.dram(n, sh, dt) for (n, sh, dt) in GLOB_IN}
    Ls = [{n: C.dram("%s_%d" % (n, l), sh, F32) for (n, sh) in LAYER_IN} for l in range(nlayers)]
    out_d = C.dram("hout", [2048, DM], F32, out=True)
    P.dma("pool", C.flags[:], flags_d, writes=["flags"])

    omy = [[nc.dram_tensor("omy_%d_%d" % (l, c), [512, 2048], BF16) for c in range(2)] for l in range(nlayers)]
    oall = [[nc.dram_tensor("oall_%d_%d" % (l, c), [1024, 2048], BF16) for c in range(2)] for l in range(nlayers)]
    hmy = [nc.dram_tensor("hmy_%d" % j, [512, DM], F32) for j in range(4)]
    hall = [nc.dram_tensor("hall_%d" % j, [1024, DM], F32) for j in range(4)]

    for l in range(nlayers):
        if l == 0:
            hsrc = lambda g: x_d[g * 128:(g + 1) * 128, :]
            hkey = lambda g: "x"
        else:
            def hsrc(g):
                r, w = g // 16, g % 16
                return hall[w // 4][r * 512 + (w % 4) * 128:r * 512 + (w % 4) * 128 + 128, :]
            hkey = lambda g: "hall%d" % ((g % 16) // 4)

        def osink_mla(hh, tc, ot, otkey, l=l):
            c = tc // 4
            col = (tc % 4) * 512
            P.dma("sp", omy[l][c][hh * 64:(hh + 1) * 64, col:col + 512], ot[:, :], reads=[otkey], writes=["omy%d" % c])

        def osink_nsa(qb, ot, otkey, l=l):
            c = qb // 16
            col = (qb % 16) * 128
            P.dma("sp", omy[l][c][256:512, :].rearrange("(r d) t -> d r t", d=64)[:, :, col:col + 128],
                  ot[:, :].rearrange("p (r t) -> p r t", r=4), reads=[otkey], writes=["omy%d" % c])
            if qb % 16 == 15:
                P.cc("AllGather", GROUPS, omy[l][c].ap().opt(), oall[l][c].ap().opt(), reads=["omy%d" % c], writes=["oall%d" % c])

        pre = {}

        def prefetch_nsa(l=l, pre=pre):
            def ld(key, name, dram, nk, ncols):
                t = C.sb_top("np%d_%s" % (l, name), [128, nk, ncols], BF16)
                for k in range(nk):
                    P.dma("pool", t[:, k, :], dram[k * 128:(k + 1) * 128, :], writes=[key])
                pre[key] = t
            ld("wA_sb", "wA", Ls[l]["wAn"], 8, 780)
            ld("w1k_sb", "w1k", Ls[l]["w1k"], 16, 128)
            ld("w1v_sb", "w1v", Ls[l]["w1v"], 16, 128)
            t = C.sb_top("np%d_maskc" % l, [128, 2 * S], BF16)
            P.dma("pool", t[:], G["maskc"], writes=["maskc_sb"])
            pre["maskc_sb"] = t
            t = C.sb_top("np%d_selbias" % l, [128, 2048], BF16)
            P.dma("pool", t[:], G["selbias"], writes=["selbias_sb"])
            pre["selbias_sb"] = t
        phase_mla(C, "m%d_" % l, Ls[l], G, hsrc, hkey, osink_mla, after_weights=prefetch_nsa)
        phase_nsa(C, "n%d_" % l, Ls[l], G, hsrc, hkey, osink_nsa, pre=pre)
        final = (l == nlayers - 1)
        if l == 0:
            hown = lambda t: xown_d[t * 128:(t + 1) * 128, :]
            hownkey = lambda t: "xown"
            hhalo, hhalokey = xhalo_d, "xhalo"
        else:
            hown = lambda t: hmy[t // 4][(t % 4) * 128:(t % 4) * 128 + 128, :]
            hownkey = lambda t: "hmy%d" % (t // 4)
            hhalo, hhalokey = hall[3][384:512, :], "hall3"
        if final:
            def osink_ffn(t, res, rkey):
                P.dma("sp", out_d[t * 128:(t + 1) * 128, :], res[:], reads=[rkey])
        else:
            def osink_ffn(t, res, rkey):
                P.dma("sp", hmy[t // 4][(t % 4) * 128:(t % 4) * 128 + 128, :], res[:], reads=[rkey], writes=["hmy%d" % (t // 4)])
                if t % 4 == 3:
                    j = t // 4
                    P.cc("AllGather", GROUPS, hmy[j].ap().opt(), hall[j].ap().opt(), reads=["hmy%d" % j], writes=["hall%d" % j])
        phase_ffn(C, "f%d_" % l, Ls[l], G, final, hown, hownkey, hhalo, hhalokey, oall[l], osink_ffn)
    P.emit()
    return nc


def _pk(g, nk):
    return np.ascontiguousarray(np.asarray(g, np.float32).reshape(nk, 128).T)


def _consts():
    c = {}
    p = np.arange(128)[:, None]
    i512 = np.arange(512)[None, :]
    dm = np.zeros((128, 4, 512), np.float32)
    for m in range(4):
        dm[:, m, :] = np.where(128 * m + p <= i512, 0.0, NEG)
    c["dmask"] = dm.reshape(128, 2048).astype(NPBF)
    i128 = np.arange(128)[None, :]
    c["dm4"] = np.tile(np.where(p <= i128, 0.0, NEG), (1, 4)).astype(NPBF)
    c["wm4"] = np.tile(np.where(i128 < p, 0.0, NEG), (1, 4)).astype(NPBF)
    n = np.arange(256)[:, None]
    t = np.arange(S)[None, :]
    mc = np.where((t >= 16 * n + 31) & (n <= 254), 0.0, NEG).astype(np.float32)
    c["maskc"] = np.ascontiguousarray(mc.reshape(2, 128, S).transpose(1, 0, 2).reshape(128, 2 * S)).astype(NPBF)
    j = np.arange(64)[:, None]
    c["eall"] = (np.arange(S)[None, :] // 64 == j).astype(np.float32).astype(NPBF)
    tt_ = np.arange(S)
    cur = (tt_ // 64)[:, None]
    jj = np.arange(64)[None, :]
    sbias = np.zeros((S, 64), np.float32)
    sbias[np.broadcast_to(jj > cur, (S, 64))] = -1e4
    sbias[np.broadcast_to((jj == 0) | (jj == cur) | (jj == cur - 1), (S, 64))] = 1e4
    c["selbias"] = np.ascontiguousarray(sbias.reshape(32, 128, 64).transpose(1, 0, 2).reshape(128, 2048)).astype(NPBF)
    cs = (np.arange(256) * 16)[:, None]
    ss_ = (np.arange(64) * 64)[None, :]
    ov = ((cs < ss_ + 64) & (cs + 32 > ss_)).astype(np.float32)
    ov[255] = 0.0
    ov1 = np.concatenate([ov, np.ones((256, 1), np.float32)], 1)
    c["ovl"] = np.ascontiguousarray(ov1.reshape(2, 128, 65).transpose(1, 0, 2).reshape(128, 130)).astype(NPBF)
    sg = np.zeros((12, 12, 64), np.float32)
    for g in range(12):
        sg[g, g, :] = 1.0
    c["selg"] = sg.reshape(12, 768)
    k = np.arange(S)
    c["kaug"] = np.stack([np.ones(S), np.ones(S), k // 64, k % 64]).astype(np.float32).astype(NPBF)
    e = np.arange(256) * 16 + 31
    c["kaugc"] = np.stack([np.ones(256), np.ones(256), e // 64, e % 64]).astype(np.float32).astype(NPBF)
    inv = 1.0 / (10000.0 ** (np.arange(0, 32, 2, dtype=np.float32) / 32))
    ang = np.arange(S, dtype=np.float32)[:, None] * inv[None, :]
    cos, sin = np.cos(ang).T.astype(np.float32), np.sin(ang).T.astype(np.float32)
    c["ropeC"] = np.ascontiguousarray(np.concatenate([cos, cos], 0))
    c["ropeS"] = np.ascontiguousarray(np.concatenate([-sin, sin], 0))
    return c


def _qaug(group):
    slopes = np.exp2(-8.0 * np.arange(1, 9, dtype=np.float32) / 8)
    t = np.arange(S).reshape(32, 1, 128)
    out = np.zeros((4, 32, 4, 128), np.float32)
    for r in range(4):
        a = slopes[group * 4 + r] / SC_NSA
        out[0, :, r, :] = (-a * 64 * (t // 64))[:, 0, :]
        out[1, :, r, :] = (-a * (t % 64))[:, 0, :]
        out[2, :, r, :] = a * 64
        out[3, :, r, :] = a
    return out.reshape(4, 32 * 512).astype(NPBF)


_PROG = {}


def _prog(nlayers=2):
    if nlayers not in _PROG:
        _PROG[nlayers] = build_fused(nlayers)
    return _PROG[nlayers]


def _layer_maps(l, I, c):
    m = {}
    w_in, w_uq, w_ukv = I["w_in"][l], I["w_uq"][l], I["w_ukv"][l]
    sw = list(range(656, 672)) + list(range(640, 656))
    colsA = list(range(0, 640)) + list(range(0, 64)) + list(range(640, 672)) + list(range(0, 64)) + sw
    m["wAm"] = np.ascontiguousarray(w_in[:, colsA])
    m["gAm"] = _pk(I["attn_norm"][l], 8)
    qc, kc, vc = [], [], []
    for hh in range(4 * c, 4 * c + 4):
        base = 96 * hh
        nope = list(range(base, base + 64))
        rope = list(range(base + 64, base + 96))
        qc += nope + rope + nope + rope[16:] + rope[:16]
        kc += list(range(128 * hh, 128 * hh + 64))
        vc += list(range(128 * hh + 64, 128 * hh + 128))
    m["wq"] = np.ascontiguousarray(w_uq[:, qc])
    m["gq"] = _pk(I["q_norm"][l], 3)
    m["wkv"] = np.ascontiguousarray(w_ukv[:, kc + vc])
    m["gkv"] = _pk(I["kv_norm"][l], 2)
    g = c
    q0 = 672 + 256 * g
    o = 1184
    rng = lambda a: list(range(a, a + 64))
    kcc, vcc, ksc = rng(o + 64 * g), rng(o + 128 + 64 * g), rng(o + 256 + 64 * g)
    vsc, kwc, vwc = rng(o + 384 + 64 * g), rng(o + 512 + 64 * g), rng(o + 640 + 64 * g)
    gtc = list(range(1952 + 12 * g, 1952 + 12 * g + 12))
    cols = list(range(q0, q0 + 256)) + kcc + kcc + vcc + vcc + ksc + kwc + vsc + vwc + gtc
    m["wAn"] = np.ascontiguousarray(w_in[:, cols])
    m["gAn"] = m["gAm"]
    posT = lambda pz: np.ascontiguousarray(np.asarray(pz, np.float32).reshape(16, 128).T)
    m["w1k"] = np.ascontiguousarray(I["cmp_k_w1"][l].reshape(2048, 128))
    m["w1v"] = np.ascontiguousarray(I["cmp_v_w1"][l].reshape(2048, 128))
    m["w2k"] = np.ascontiguousarray(I["cmp_k_w2"][l])
    m["w2v"] = np.ascontiguousarray(I["cmp_v_w2"][l])
    m["posk"] = posT(I["cmp_pos_k"][l])
    m["posv"] = posT(I["cmp_pos_v"][l])
    perm = list(range(0, 256)) + list(range(512, 768)) + list(range(256, 512)) + list(range(768, 1024))
    m["wo"] = np.ascontiguousarray(I["w_o"][l][perm, :])
    m["wup"] = np.ascontiguousarray(I["w_up"][l])
    m["wdn"] = np.ascontiguousarray(I["w_down"][l])
    m["g2"] = _pk(I["ffn_norm"][l], 8)
    cwv = np.stack([I["conv_w"][l][0], I["conv_w"][l][1], I["conv_w"][l][2], I["conv_b"][l]], -1)
    m["cw"] = np.ascontiguousarray(cwv.reshape(44, 128, 4).transpose(1, 0, 2).reshape(128, 176)).astype(np.float32)
    return m


def make_maps(I, nlayers=2):
    cst = _consts()
    lm = [[_layer_maps(l, I, c) for c in range(2)] for l in range(nlayers)]
    maps = []
    for b in range(4):
        for c in range(2):
            m = {"x": np.ascontiguousarray(I["x"][b]),
                 "xown": np.ascontiguousarray(I["x"][b][2048 * c:2048 * c + 2048]),
                 "xhalo": np.ascontiguousarray(I["x"][b][1920:2048]),
                 "flags": np.ascontiguousarray(np.tile(np.array([[1.0 - c, float(c)]], np.float32), (128, 1)))}
            for (n, sh, dt) in GLOB_IN:
                if n == "qaug":
                    m[n] = _qaug(c)
                elif n == "gF":
                    m[n] = np.ascontiguousarray(np.asarray(I["final_norm"], np.float32).reshape(1, DM))
                else:
                    m[n] = cst[n]
            for l in range(nlayers):
                for k, v in lm[l][c].items():
                    m["%s_%d" % (k, l)] = v
            maps.append(m)
    return maps


def kernel(**inputs):
    I = {k: np.asarray(v, dtype=np.float32) for k, v in inputs.items()}
    res = run_bass_kernel_spmd(_prog(2), make_maps(I, 2), core_ids=list(range(8))).results
    out = np.empty((4, S, DM), np.float32)
    for b in range(4):
        for c in range(2):
            out[b, 2048 * c:2048 * c + 2048] = np.asarray(res[2 * b + c]["hout"])
    return out
```

```python
import numpy as np
import ml_dtypes
import concourse.bass as bass
import concourse.mybir as mybir
from concourse.bass_utils import run_bass_kernel_spmd

F32 = mybir.dt.float32
BF16 = mybir.dt.bfloat16
ALU = mybir.AluOpType
AF = mybir.ActivationFunctionType
AX = mybir.AxisListType
NPBF = ml_dtypes.bfloat16

ENGS = ["pe", "act", "dve", "pool", "sp"]
DMA_POOL = 12
S = 4096
DM = 1024
NEG = -30000.0
SC_MLA = 96 ** -0.5
SC_NSA = 0.125
EPS = 1e-6


class Prog:
    def __init__(self, nc):
        self.nc = nc
        self.ops = {e: [] for e in ENGS}
        self.lastw = {}
        self.readers = {}
        self.dma_n = {e: 0 for e in ENGS + ["cc"]}
        self.dma_sem_cnt = {}
        self.last_c = {}
        self.last_d = {}

    def sb(self, name, shape, dt):
        return self.nc.alloc_sbuf_tensor(name, list(shape), dt)

    def ps(self, name, shape, dt=F32):
        return self.nc.alloc_psum_tensor(name, list(shape), dt)

    def _add(self, eng, fn, reads, writes, dma, cc=False):
        op = dict(eng=eng, fn=fn, deps=[], dma=dma, marked=False, inc=(1 if cc else 16))
        deps = []
        for k in reads:
            w = self.lastw.get(k)
            if w is not None:
                deps.append(w)
        for k in writes:
            w = self.lastw.get(k)
            if w is not None:
                deps.append(w)
            deps.extend(self.readers.get(k, ()))
        seen = set()
        for d in deps:
            if id(d) in seen or d is op:
                continue
            seen.add(id(d))
            if (not d["dma"]) and d["eng"] == eng and eng in ("pe", "sp"):
                continue
            op["deps"].append(d)
            d["marked"] = True
        if dma:
            qn = "cc" if cc else eng
            q = self.dma_n[qn]
            self.dma_n[qn] += 1
            semkey = (qn, q % (4 if cc else DMA_POOL))
            m = self.dma_sem_cnt.get(semkey, 0) + 1
            self.dma_sem_cnt[semkey] = m
            op["dsem"] = semkey
            op["dval"] = op["inc"] * m
            op["marked"] = True
            self.last_d[semkey] = op
        else:
            self.last_c[eng] = op
        for k in reads:
            self.readers.setdefault(k, []).append(op)
        for k in writes:
            self.lastw[k] = op
            self.readers[k] = []
        self.ops[eng].append(op)
        return op

    def op(self, eng, fn, reads=(), writes=()):
        return self._add(eng, fn, list(reads), list(writes), False)

    def dma(self, eng, out, in_, reads=(), writes=()):
        return self._add(eng, lambda e: e.dma_start(out=out, in_=in_), list(reads), list(writes), True)

    def cc(self, kind, groups, in_ap, out_ap, reads=(), writes=()):
        return self._add("pool", lambda e: e.collective_compute(kind, ALU.bypass, replica_groups=groups,
                                                                ins=[in_ap], outs=[out_ap]),
                         list(reads), list(writes), True, cc=True)

    def barrier(self):
        deps = list(self.last_c.values()) + list(self.last_d.values())
        for d in deps:
            d["marked"] = True
        for e in ENGS:
            self.ops[e].append(dict(eng=e, fn=None, deps=list(deps), dma=False, marked=False, inc=0))
        self.lastw = {}
        self.readers = {}

    def emit(self):
        nc = self.nc
        csem = {e: nc.alloc_semaphore("c_" + e) for e in ENGS}
        dsem = {}
        for (qn, i) in self.dma_sem_cnt:
            dsem[(qn, i)] = nc.alloc_semaphore("d_%s_%d" % (qn, i))
        for e in ENGS:
            c = 0
            for o in self.ops[e]:
                if o["dma"] or o["fn"] is None:
                    continue
                if o["marked"]:
                    c += 1
                    o["cval"] = c
        all_dma = [o for e in ENGS for o in self.ops[e] if o["dma"]]

        def run(e, eng):
            seen = {}

            def wait(sem_key, sem, val):
                if seen.get(sem_key, 0) >= val:
                    return
                seen[sem_key] = val
                eng.wait_ge(sem, val)

            for o in self.ops[e]:
                for d in o["deps"]:
                    if d["dma"]:
                        wait(d["dsem"], dsem[d["dsem"]], d["dval"])
                    else:
                        wait(("c", d["eng"]), csem[d["eng"]], d["cval"])
                if o["fn"] is None:
                    continue
                if o["dma"]:
                    if o["dval"] > o["inc"]:
                        wait(o["dsem"], dsem[o["dsem"]], o["dval"] - o["inc"])
                    o["fn"](eng).then_inc(dsem[o["dsem"]], o["inc"])
                else:
                    ins = o["fn"](eng)
                    if o["marked"]:
                        ins.then_inc(csem[e], 1)
            if e == "sp":
                last = {}
                for o in all_dma:
                    last[o["dsem"]] = max(last.get(o["dsem"], 0), o["dval"])
                for k, v in last.items():
                    eng.wait_ge(dsem[k], v)

        with nc.Block() as block:
            @block.tensor
            def _(eng):
                run("pe", eng)

            @block.scalar
            def _(eng):
                run("act", eng)

            @block.vector
            def _(eng):
                run("dve", eng)

            @block.gpsimd
            def _(eng):
                run("pool", eng)

            @block.sync
            def _(eng):
                run("sp", eng)


def _nbytes(shape, dt):
    n = 1
    for d in shape[1:]:
        n *= d
    return n * (4 if dt == F32 else 2)


class Ctx:
    def __init__(self, nc):
        self.nc = nc
        self.P = P = Prog(nc)
        self.rots = {}
        self.pname = "g_"
        self.ident = nc.alloc_sbuf_tensor("ident", [128, 128], BF16)
        self.identf = nc.alloc_sbuf_tensor("identf", [128, 128], F32)
        self.onesf = nc.alloc_sbuf_tensor("onesf", [128, 128], F32)
        self.epsn = nc.alloc_sbuf_tensor("epsn", [128, 1], F32)
        self.flags = nc.alloc_sbuf_tensor("flags_sb", [128, 2], F32)
        identf, ident, onesf, epsn = self.identf, self.ident, self.onesf, self.epsn
        P.op("pool", lambda e: e.memset(identf[:], 0.0), writes=["identf"])
        P.op("pool", lambda e: e.affine_select(out=identf[:], in_=identf[:], pattern=[[-1, 128]],
                                                compare_op=ALU.not_equal, fill=1.0, base=0, channel_multiplier=1),
             reads=["identf"], writes=["identf"])
        P.op("dve", lambda e: e.tensor_copy(out=ident[:], in_=identf[:]), reads=["identf"], writes=["ident"])
        P.op("dve", lambda e: e.memset(onesf[:], 1.0), writes=["onesf"])
        P.op("dve", lambda e: e.memset(epsn[:], EPS), writes=["epsn"])
        self.pst = P.ps("pst", [128, 1024], BF16)
        self.banks = [P.ps("bank%d" % i, [128, 512], F32) for i in range(7)]
        self.banks.append(self.pst.bitcast(F32))
        self.base = ((int(nc.sbuf_base) + 63) // 64) * 64
        self.top = int(nc.sbuf_top)
        self.off = self.base
        self.top_off = self.top

    def begin_phase(self, name, keep_top=False):
        self.P.barrier()
        self.pname = name
        self.off = self.base
        if not keep_top:
            self.top_off = self.top

    def sb_top(self, name, shape, dt):
        nb = ((_nbytes(shape, dt) + 31) // 32) * 32
        self.top_off -= nb
        assert self.top_off >= self.off, ("SBUF overflow (top)", name)
        return self.nc.alloc_sbuf_tensor_at(name, list(shape), dt, offset=self.top_off)

    def sb(self, name, shape, dt):
        nb = ((_nbytes(shape, dt) + 31) // 32) * 32
        assert self.off + nb <= self.top_off, ("SBUF overflow", self.pname, name, self.off + nb - self.top_off)
        t = self.nc.alloc_sbuf_tensor_at(self.pname + name, list(shape), dt, offset=self.off)
        self.off += nb
        return t

    def rot(self, name, n):
        i = self.rots.get(name, 0) % n
        self.rots[name] = (i + 1) % n
        return i

    def dram(self, name, shape, dt, out=False):
        return self.nc.dram_tensor(name, list(shape), dt, kind="ExternalOutput" if out else "ExternalInput").ap()

    def mm(self, out, lhsT, rhs, start, stop, reads, writes):
        return self.P.op("pe", lambda e: e.matmul(out, lhsT=lhsT, rhs=rhs, start=start, stop=stop), reads, writes)

    def mmg(self, out, pairs, reads, writes, start=True, stop=True):
        n = len(pairs)

        def fn(e):
            ins = None
            for i, (l, r) in enumerate(pairs):
                ins = e.matmul(out, lhsT=l, rhs=r, start=(start and i == 0), stop=(stop and i == n - 1))
            return ins
        return self.P.op("pe", fn, reads, writes)

    def mmlist(self, items, reads, writes):
        def fn(e):
            ins = None
            for (o, l, r, st, sp) in items:
                ins = e.matmul(o, lhsT=l, rhs=r, start=st, stop=sp)
            return ins
        return self.P.op("pe", fn, reads, writes)

    def tr(self, out, in_, ident, reads, writes):
        return self.P.op("pe", lambda e: e.transpose(out, in_, ident), reads, writes)

    def act(self, out, in_, func, reads, writes, bias=None, scale=1.0, accum=None):
        kw = {}
        if bias is not None:
            kw["bias"] = bias
        if accum is not None:
            kw["accum_out"] = accum
        return self.P.op("act", lambda e: e.activation(out=out, in_=in_, func=func, scale=scale, **kw), reads, writes)

    def tt(self, eng, out, in0, in1, op, reads, writes):
        return self.P.op(eng, lambda e: e.tensor_tensor(out=out, in0=in0, in1=in1, op=op), reads, writes)

    def ts(self, eng, out, in0, s1, op0, reads, writes, s2=None, op1=None):
        if op1 is None:
            return self.P.op(eng, lambda e: e.tensor_scalar(out=out, in0=in0, scalar1=s1, scalar2=None, op0=op0), reads, writes)
        return self.P.op(eng, lambda e: e.tensor_scalar(out=out, in0=in0, scalar1=s1, scalar2=s2, op0=op0, op1=op1), reads, writes)

    def stt(self, eng, out, in0, scalar, in1, op0, op1, reads, writes):
        return self.P.op(eng, lambda e: e.scalar_tensor_tensor(out=out, in0=in0, scalar=scalar, in1=in1, op0=op0, op1=op1), reads, writes)

    def cp(self, eng, out, in_, reads, writes):
        if eng == "act":
            return self.P.op("act", lambda e: e.copy(out=out, in_=in_), reads, writes)
        return self.P.op(eng, lambda e: e.tensor_copy(out=out, in_=in_), reads, writes)

    def recip(self, out, in_, reads, writes):
        return self.P.op("dve", lambda e: e.reciprocal(out=out, in_=in_), reads, writes)

    def memset(self, eng, ap, val, writes):
        return self.P.op(eng, lambda e: e.memset(ap, val), [], writes)

    def bank(self, grp, idxs):
        i = idxs[self.rot(grp, len(idxs))]
        return self.banks[i], ("pst" if i == 7 else "bank%d" % i)

    def load_w(self, name, w_dram, nk, ncols):
        wsb = self.sb(name, [128, nk, ncols], BF16)
        for k in range(nk):
            self.P.dma("pool", wsb[:, k, :], w_dram[k * 128:(k + 1) * 128, :], writes=[name])
        return wsb

    def load_const(self, name, dram_ap, shape, dt, eng="pool"):
        t = self.sb(name, shape, dt)
        self.P.dma(eng, t[:], dram_ap, writes=[name])
        return t

    def setup_norm(self):
        self.ht = [self.sb("ht%d" % i, [128, DM], F32) for i in range(2)]
        self.junk = self.sb("junk", [128, DM], BF16)
        self.nb = self.sb("nb", [128, DM], BF16)
        self.ss = [self.sb("ss%d" % i, [128, 1], F32) for i in range(2)]

    def norm_T(self, src, srckey, nT, nkey, col0, g_sb, gkey):
        b = self.rot("ss", 2)
        ss = self.ss[b]
        sk = "ss%d" % b
        self.memset("dve", ss[:], 0.0, [sk])
        self.act(self.junk[:], src, AF.Square, [srckey, sk], ["junk", sk], accum=ss[:])
        self.act(ss[:], ss[:], AF.Sqrt, [sk, "epsn"], [sk], bias=self.epsn[:], scale=1.0 / DM)
        self.recip(ss[:], ss[:], [sk], [sk])
        self.ts("dve", self.nb[:], src, ss[:, 0:1], ALU.mult, [srckey, sk], ["nb"])
        pst = self.pst
        nb, ident = self.nb, self.ident

        def fn(e):
            ins = None
            for k in range(8):
                ins = e.transpose(pst[:, k * 128:(k + 1) * 128], nb[:, k * 128:(k + 1) * 128], ident[:])
            return ins
        self.P.op("pe", fn, ["nb", "ident"], ["pst"])
        self.tt("dve", nT[:, :, col0:col0 + 128], pst[:, :].rearrange("p (k t) -> p k t", k=8),
                g_sb[:, :].unsqueeze(2).to_broadcast([128, 8, 128]), ALU.mult, ["pst", gkey], [nkey])


def attn_loop(C, tiles, score_fn, scale, v_fn, po, pok, sbanks, pts, ptname, after_first=None, depth=2, hooks=None, col0=None):
    n = len(tiles)
    issued = []

    def issue(i):
        sbk, sk = C.bank("s", sbanks)
        pairs, rd = score_fn(tiles[i])
        c0 = col0(tiles[i]) if col0 is not None else 0
        C.mmg(sbk[:, c0:], pairs, rd, [sk])
        issued.append((sbk, sk, c0))
    for i in range(min(depth, n)):
        issue(i)
    if after_first is not None:
        after_first()
    for i in range(n):
        sbk, sk, c0 = issued[i]
        pi = C.rot(ptname, len(pts))
        C.act(pts[pi][:, c0:], sbk[:, c0:], AF.Exp, [sk], ["%s%d" % (ptname, pi)], scale=scale)
        if i + depth < n:
            issue(i + depth)
        lhsT, rd = v_fn(tiles[i])
        C.mm(po[:, c0:], lhsT, pts[pi][:, c0:], i == 0, i == n - 1, rd + ["%s%d" % (ptname, pi)], [pok])
        if hooks and i in hooks:
            hooks[i]()


def phase_mla(C, name, L, G, hsrc, hkey, osink, after_weights=None):
    P = C.P
    C.begin_phase(name)
    C.setup_norm()
    gA = C.load_const("gA_sb", L["gAm"], [128, 8], F32)
    gq = C.load_const("gq_sb", L["gq"], [128, 3], F32)
    gkv = C.load_const("gkv_sb", L["gkv"], [128, 2], F32)
    dmask = C.load_const("dmask_sb", G["dmask"], [128, 2048], BF16)
    wA = C.load_w("wA_sb", L["wAm"], 8, 832)
    wq = C.load_w("wq_sb", L["wq"], 3, 768)
    wkv = C.load_w("wkv_sb", L["wkv"], 2, 512)
    if after_weights is not None:
        after_weights()
    ropeC_d, ropeS_d = G["ropeC"], G["ropeS"]

    Kh = C.sb("Kh", [128, 4, S], BF16)
    C.memset("pool", Kh[96:128, :, :], 0.0, ["Kh_pad"])
    Vt = C.sb("Vt", [128, 32, 4, 128], BF16)
    C.memset("pool", Vt[:, :, :, 64:65], 1.0, ["Vt_%d" % c for c in range(8)])
    C.memset("pool", Vt[:, :, :, 65:128], 0.0, ["Vt_pad"])
    nTs = [C.sb("nT%d" % i, [128, 8, 512], BF16) for i in range(2)]
    Qhs = [C.sb("Qh%d" % i, [128, 4, 512], BF16) for i in range(2)]
    for i in range(2):
        C.memset("pool", Qhs[i][96:128, :, :], 0.0, ["Qh_pad"])
    zf = C.sb("zf", [128, 3, 512], F32)
    sq = C.sb("sq", [128, 3, 512], F32)
    rr = C.sb("rr", [128, 512], F32)
    cqn = C.sb("cqn", [128, 3, 512], BF16)
    ckvn = C.sb("ckvn", [128, 2, 512], BF16)
    Ct = C.sb("Ct", [96, 512], F32)
    St = C.sb("St", [96, 512], F32)
    t1 = C.sb("t1", [96, 512], F32)
    t2 = C.sb("t2", [96, 512], F32)
    pts = [C.sb("pt%d" % i, [128, 512], BF16) for i in range(4)]
    rsrow = C.sb("rsrow", [65, 512], F32)
    rec4 = C.sb("rec4", [128, 4], F32)
    wb = C.sb("wb", [128, 4, 64], F32)
    bcs = C.sb("bcs", [64, 512], F32)
    ots = [C.sb("ot%d" % i, [64, 512], BF16) for i in range(2)]

    PJ = [0, 1]
    SB_ = [2, 3, 4]
    PO = [5, 6]
    pendA = []
    pendB = []

    def flushA():
        while pendA:
            pendA.pop(0)()

    def flushB():
        flushA()
        while pendB:
            pendB.pop(0)()

    def flush():
        flushB()

    def latent(c0, nm, dim, dst, dkey, nT, nkeys, gl, glkey):
        for m in range(nm):
            pj, pk = C.bank("pj", PJ)
            C.mmg(pj[:, :], [(wA[:, k, c0 + m * 128:c0 + (m + 1) * 128], nT[:, k, :]) for k in range(8)],
                  ["wA_sb"] + nkeys, [pk])
            C.act(zf[:, m, :], pj[:, :], AF.Copy, [pk], ["zf%d" % m])
            C.act(sq[:, m, :], pj[:, :], AF.Square, [pk], ["sq%d" % m])
        pj, pk = C.bank("pj", PJ)
        C.mmg(pj[:, :], [(C.onesf[:, :], sq[:, m, :]) for m in range(nm)], ["onesf"] + ["sq%d" % m for m in range(nm)], [pk])
        C.act(rr[:], pj[:, :], AF.Sqrt, [pk, "epsn"], ["rr"], bias=C.epsn[:], scale=1.0 / dim)
        C.recip(rr[:], rr[:], ["rr"], ["rr"])
        for m in range(nm):
            C.stt("dve", dst[:, m, :], zf[:, m, :], gl[:, m:m + 1], rr[:], ALU.mult, ALU.mult, ["zf%d" % m, "rr", glkey], [dkey])

    def stage_T(tc):
        nT = nTs[tc % 2]
        nkeys = []
        for ti in range(4):
            hb = C.rot("ht", 2)
            P.dma("sp", C.ht[hb][:], hsrc(4 * tc + ti), reads=[hkey(4 * tc + ti)], writes=["ht%d" % hb])
            nk = "nT%d_%d" % (tc % 2, ti)
            C.norm_T(C.ht[hb][:], "ht%d" % hb, nT, nk, ti * 128, gA, "gA_sb")
            nkeys.append(nk)
        return nT, nkeys

    def stage_P1(tc, nT, nkeys):
        t0 = tc * 512
        latent(0, 3, 384.0, cqn, "cqn", nT, nkeys, gq, "gq_sb")
        latent(384, 2, 256.0, ckvn, "ckvn", nT, nkeys, gkv, "gkv_sb")
        P.dma("sp", Ct[64:96, :], ropeC_d[:, t0:t0 + 512], writes=["Ct"])
        P.dma("sp", St[64:96, :], ropeS_d[:, t0:t0 + 512], writes=["St"])
        pA, pAk = C.bank("pj", PJ)
        C.mmg(pA[0:96, :], [(wA[:, k, 640:736], nT[:, k, :]) for k in range(8)], ["wA_sb"] + nkeys, [pAk])
        C.tt("dve", t1[64:96, :], pA[64:96, :], Ct[64:96, :], ALU.mult, [pAk, "Ct"], ["t1"])
        pB, pBk = C.bank("pj", PJ)
        C.mmg(pB[0:96, :], [(wA[:, k, 736:832], nT[:, k, :]) for k in range(8)], ["wA_sb"] + nkeys, [pBk])
        C.tt("dve", t2[64:96, :], pB[64:96, :], St[64:96, :], ALU.mult, [pBk, "St"], ["t2"])
        for hh in range(4):
            C.tt("pool", Kh[64:96, hh, t0:t0 + 512], t1[64:96, :], t2[64:96, :], ALU.add, ["t1", "t2"], ["Kh_%d" % tc])

    def stage_P2(tc):
        t0 = tc * 512
        Qh = Qhs[tc % 2]
        qk = "Qh%d" % (tc % 2)
        for hh in range(4):
            pA, pAk = C.bank("pj", PJ)
            C.mmg(pA[0:96, :], [(wq[:, m, hh * 192:hh * 192 + 96], cqn[:, m, :]) for m in range(3)], ["wq_sb", "cqn"], [pAk])
            C.cp("act", Qh[0:64, hh, :], pA[0:64, :], [pAk], [qk])
            C.tt("dve", t1[64:96, :], pA[64:96, :], Ct[64:96, :], ALU.mult, [pAk, "Ct"], ["t1"])
            pB, pBk = C.bank("pj", PJ)
            C.mmg(pB[0:96, :], [(wq[:, m, hh * 192 + 96:hh * 192 + 192], cqn[:, m, :]) for m in range(3)], ["wq_sb", "cqn"], [pBk])
            C.tt("dve", t2[64:96, :], pB[64:96, :], St[64:96, :], ALU.mult, [pBk, "St"], ["t2"])
            C.tt("pool", Qh[64:96, hh, :], t1[64:96, :], t2[64:96, :], ALU.add, ["t1", "t2"], [qk])
        for hh in range(4):
            pj, pk = C.bank("pj", PJ)
            C.mmg(pj[0:64, :], [(wkv[:, j, hh * 64:(hh + 1) * 64], ckvn[:, j, :]) for j in range(2)], ["wkv_sb", "ckvn"], [pk])
            C.cp("act", Kh[0:64, hh, t0:t0 + 512], pj[0:64, :], [pk], ["Kh_%d" % tc])
        for ti in range(4):
            pj, pk = C.bank("pj", PJ)
            C.mmg(pj[:, 0:256], [(ckvn[:, j, ti * 128:(ti + 1) * 128], wkv[:, j, 256:512]) for j in range(2)], ["wkv_sb", "ckvn"], [pk])
            C.cp("act", Vt[:, 4 * tc + ti, :, 0:64], pj[:, 0:256].rearrange("p (h d) -> p h d", h=4), [pk], ["Vt_%d" % tc])

    def head(tc, hh):
        Qh = Qhs[tc % 2]
        qk = "Qh%d" % (tc % 2)
        po, pok = C.bank("po", PO)
        nkt = 4 * tc + 4

        def c0f(j):
            return 128 * (j - 4 * tc) if j >= 4 * tc else 0

        def score(j):
            c0 = c0f(j)
            pairs = [(Kh[:, hh, j * 128:(j + 1) * 128], Qh[:, hh, c0:])]
            rd = [qk, "Kh_%d" % (j // 4), "Kh_pad", "Qh_pad"]
            if j >= 4 * tc:
                m = j - 4 * tc
                pairs.append((C.ident[:, :], dmask[:, m * 512 + c0:(m + 1) * 512]))
                rd += ["ident", "dmask_sb"]
            return pairs, rd

        def vfn(j):
            return Vt[:, j, hh, :], ["Vt_%d" % (j // 4), "Vt_pad"]
        attn_loop(C, list(range(nkt)), score, SC_MLA, vfn, po, pok, SB_, pts, "pt", after_first=flushA, hooks={2: flushB}, col0=c0f)

        def finA():
            C.cp("dve", rsrow[64:65, :], po[64:65, :], [pok], ["rsrow"])
            pj, pk = C.bank("pj", PJ)
            C.mmlist([(pj[:, r:r + 1], rsrow[64:65, r * 128:(r + 1) * 128], C.onesf[64:65, 0:1], True, True) for r in range(4)],
                     ["rsrow", "onesf"], [pk])
            C.ts("dve", rec4[:, :], pj[:, 0:4], 1e-30, ALU.add, [pk], ["rec4"])
            C.recip(rec4[:, :], rec4[:, :], ["rec4"], ["rec4"])
            C.cp("dve", wb[:, :, :], rec4[:, 0:4].unsqueeze(2).to_broadcast([128, 4, 64]), ["rec4"], ["wb"])

        def finB():
            pj2, pk2 = C.bank("pj", PJ)
            C.mmlist([(pj2[0:64, r * 128:(r + 1) * 128], wb[:, r, :], C.identf[:, :], True, True) for r in range(4)],
                     ["wb", "identf"], [pk2])
            C.cp("act", bcs[:, :], pj2[0:64, :], [pk2], ["bcs"])
            oi = C.rot("ot", 2)
            C.tt("dve", ots[oi][:, :], po[0:64, :], bcs[:, :], ALU.mult, [pok, "bcs"], ["ot%d" % oi])
            osink(hh, tc, ots[oi], "ot%d" % oi)
        pendA.append(finA)
        pendB.append(finB)

    st = stage_T(0)
    stage_P1(0, *st)
    stage_P2(0)
    for tc in range(8):
        head(tc, 0)
        if tc < 7:
            st = stage_T(tc + 1)
        head(tc, 1)
        if tc < 7:
            stage_P1(tc + 1, *st)
        head(tc, 2)
        if tc < 7:
            stage_P2(tc + 1)
        head(tc, 3)
    flush()


def phase_nsa(C, name, L, G, hsrc, hkey, osink, pre=None):
    P = C.P
    pre = pre or {}
    C.begin_phase(name, keep_top=bool(pre))
    NW = 780
    C.setup_norm()
    gA = C.load_const("gA_sb", L["gAn"], [128, 8], F32)
    wA = pre["wA_sb"] if "wA_sb" in pre else C.load_w("wA_sb", L["wAn"], 8, NW)
    w1k = pre["w1k_sb"] if "w1k_sb" in pre else C.load_w("w1k_sb", L["w1k"], 16, 128)
    w1v = pre["w1v_sb"] if "w1v_sb" in pre else C.load_w("w1v_sb", L["w1v"], 16, 128)
    w2k = C.load_w("w2k_sb", L["w2k"], 1, 64)
    w2v = C.load_w("w2v_sb", L["w2v"], 1, 64)
    posk = C.load_w("posk_sb", L["posk"], 1, 16)
    posv = C.load_w("posv_sb", L["posv"], 1, 16)
    maskc = pre["maskc_sb"] if "maskc_sb" in pre else C.load_const("maskc_sb", G["maskc"], [128, 2 * S], BF16)
    eall = C.sb("eall_sb", [128, S], BF16)
    C.memset("pool", eall[64:128, :], 0.0, ["eall_sb"])
    P.dma("pool", eall[0:64, :], G["eall"], writes=["eall_sb"])
    selbias = pre["selbias_sb"] if "selbias_sb" in pre else C.load_const("selbias_sb", G["selbias"], [128, 2048], BF16)
    ovl = C.load_const("ovl_sb", G["ovl"], [128, 130], BF16)
    selg = C.load_const("selg_sb", G["selg"], [12, 768], F32)
    dm4 = C.load_const("dm4_sb", G["dm4"], [128, 512], BF16)
    wm4 = C.load_const("wm4_sb", G["wm4"], [128, 512], BF16)

    Qa = C.sb("Qa", [128, 32, 512], BF16)
    Kw = C.sb("Kw", [128, S], BF16)
    Ks = C.sb("Ks", [128, S], BF16)
    Kc = C.sb("Kc", [128, 256], BF16)
    C.memset("pool", Qa[64:128, :, :], 0.0, ["Qa_aug"])
    C.memset("pool", Kw[64:128, :], 0.0, ["Kw_aug"])
    C.memset("pool", Ks[64:128, :], 0.0, ["Ks_aug"])
    C.memset("pool", Kc[64:128, :], 0.0, ["Kc_aug"])
    P.dma("pool", Qa[64:68, :, :], G["qaug"].rearrange("p (a b) -> p a b", a=32), writes=["Qa_aug"])
    P.dma("pool", Kw[64:68, :], G["kaug"], writes=["Kw_aug"])
    P.dma("pool", Ks[64:68, :], G["kaug"], writes=["Ks_aug"])
    P.dma("pool", Kc[64:68, :], G["kaugc"], writes=["Kc_aug"])
    kc2 = C.sb("kc2", [128, S + 32], BF16)
    vc2 = C.sb("vc2", [128, S + 32], BF16)
    C.memset("pool", kc2[:, S:S + 32], 0.0, ["kc2_tail"])
    C.memset("pool", vc2[:, S:S + 32], 0.0, ["vc2_tail"])
    Vs = C.sb("Vs", [128, 32, 128], BF16)
    Vw = C.sb("Vw", [128, 32, 128], BF16)
    Vc = C.sb("Vc", [128, 2, 128], BF16)
    for (vt_, keys_) in ((Vs, ["Vs_%d" % c for c in range(8)]), (Vw, ["Vw_%d" % c for c in range(8)]), (Vc, ["Vc"])):
        C.memset("pool", vt_[:, :, 65:128], 0.0, keys_)
        C.memset("pool", vt_[:, :, 64:65], 1.0, keys_)
    Gtm = C.sb("Gtm", [128, 32, 12], F32)
    nTs = [C.sb("nT%d" % i, [128, 8, 512], BF16) for i in range(2)]

    PJ = [0, 1]
    SB_ = [2, 3]
    POC, POS, POW = 4, 5, 6

    for tc in range(8):
        t0 = tc * 512
        nb_ = C.rot("nT", 2)
        nT = nTs[nb_]
        nkeys = []
        for ti in range(4):
            hb = C.rot("ht", 2)
            P.dma("sp", C.ht[hb][:], hsrc(4 * tc + ti), reads=[hkey(4 * tc + ti)], writes=["ht%d" % hb])
            nk = "nT%d_%d" % (nb_, ti)
            C.norm_T(C.ht[hb][:], "ht%d" % hb, nT, nk, ti * 128, gA, "gA_sb")
            nkeys.append(nk)

        def proj(c0, m, rows=128):
            pj, pk = C.bank("pj", PJ)
            C.mmg(pj[0:rows, :], [(wA[:, k, c0:c0 + m], nT[:, k, :]) for k in range(8)], ["wA_sb"] + nkeys, [pk])
            return pj, pk
        for r in range(4):
            pj, pk = proj(r * 64, 64, 64)
            C.cp("act", Qa[0:64, 4 * tc:4 * tc + 4, r * 128:(r + 1) * 128],
                 pj[0:64, :].rearrange("p (a b) -> p a b", a=4), [pk], ["Qa_%d" % tc])
        for (c0, dst, dk) in ((256, kc2, "kc2"), (384, vc2, "vc2")):
            pj, pk = proj(c0, 128)
            C.cp("act", dst[0:64, t0:t0 + 512], pj[0:64, :], [pk], [dk])
            if tc == 0:
                C.cp("dve", dst[64:128, 0:511], pj[64:128, 1:512], [pk], [dk])
            else:
                C.cp("dve", dst[64:128, t0 - 1:t0 + 511], pj[64:128, :], [pk], [dk])
        pj, pk = proj(512, 64, 64)
        C.cp("act", Ks[0:64, t0:t0 + 512], pj[0:64, :], [pk], ["Ks_%d" % tc])
        pj, pk = proj(576, 64, 64)
        C.cp("act", Kw[0:64, t0:t0 + 512], pj[0:64, :], [pk], ["Kw_%d" % tc])
        for ti in range(4):
            pj, pk = C.bank("pj", PJ)
            C.mmg(pj[:, 0:128], [(nT[:, k, ti * 128:(ti + 1) * 128], wA[:, k, 640:768]) for k in range(8)],
                  ["wA_sb"] + nkeys, [pk])
            pg, pgk = C.bank("pj", PJ)
            C.mmg(pg[:, 0:12], [(nT[:, k, ti * 128:(ti + 1) * 128], wA[:, k, 768:780]) for k in range(8)],
                  ["wA_sb"] + nkeys, [pgk])
            C.cp("dve", Gtm[:, 4 * tc + ti, :], pg[:, 0:12], [pgk], ["Gtm_%d" % tc])
            C.cp("act", Vs[:, 4 * tc + ti, 0:64], pj[:, 0:64], [pk], ["Vs_%d" % tc])
            C.cp("dve", Vw[:, 4 * tc + ti, 0:64], pj[:, 64:128], [pk], ["Vw_%d" % tc])

    C.act(Gtm[:, :, :], Gtm[:, :, :], AF.Sigmoid, ["Gtm_%d" % c for c in range(8)], ["Gtm_%d" % c for c in range(8)])
    xs = C.sb("xs", [128, 256], F32)
    x2 = C.sb("x2", [128, 256], F32)
    hid = C.sb("hid", [128, 256], BF16)
    cbias = C.sb("cbias", [128, 1], F32)
    for (src, skey, w1, w1key, pos, poskey, isk) in ((kc2, "kc2", w1k, "w1k_sb", posk, "posk_sb", True),
                                                     (vc2, "vc2", w1v, "w1v_sb", posv, "posv_sb", False)):
        pj, pk = C.bank("pj", PJ)
        C.mmg(pj[:, 0:1], [(w1[:, j, :], pos[:, 0, j:j + 1]) for j in range(16)], [w1key, poskey], [pk])
        C.cp("act", cbias[:], pj[:, 0:1], [pk], ["cbias"])
        pj, pk = C.bank("pj", PJ)
        C.mmg(pj[:, 0:255], [(w1[:, j, :], src[:, 2 * j:2 * j + 16 * 255:16]) for j in range(16)],
              [w1key, skey, skey + "_tail"], [pk])
        C.memset("dve", xs[:, 255:256], 0.0, ["xs"])
        C.act(xs[:, 0:255], pj[:, 0:255], AF.Identity, [pk, "cbias"], ["xs"], bias=cbias[:])
        C.tt("dve", x2[:], xs[:], xs[:], ALU.mult, ["xs"], ["x2"])
        C.ts("dve", x2[:], x2[:], 0.044715, ALU.mult, ["x2"], ["x2"], s2=1.0, op1=ALU.add)
        C.tt("dve", x2[:], x2[:], xs[:], ALU.mult, ["x2", "xs"], ["x2"])
        C.act(x2[:], x2[:], AF.Sigmoid, ["x2"], ["x2"], scale=1.5957691216057308)
        C.tt("dve", hid[:], xs[:], x2[:], ALU.mult, ["x2", "xs"], ["hid"])
        if isk:
            pj, pk = C.bank("pj", PJ)
            C.mm(pj[0:64, 0:256], w2k[:, 0, :], hid[:], True, True, ["w2k_sb", "hid"], [pk])
            C.cp("act", Kc[0:64, :], pj[0:64, 0:256], [pk], ["Kc"])
        else:
            for nt in range(2):
                pj, pk = C.bank("pj", PJ)
                C.mm(pj[:, 0:64], hid[:, nt * 128:(nt + 1) * 128], w2v[:, 0, :], True, True, ["w2v_sb", "hid"], [pk])
                C.cp("act", Vc[:, nt, 0:64], pj[:, 0:64], [pk], ["Vc"])

    ptc = [[C.sb("ptc%d_%d" % (i, j), [128, 512], BF16) for j in range(2)] for i in range(2)]
    pts = [C.sb("pt%d" % i, [128, 512], BF16) for i in range(4)]
    imp = C.sb("imp", [128, 64], F32)
    wk = C.sb("wk", [128, 64], F32)
    m8 = C.sb("m8", [128, 16], F32)
    rci = C.sb("rci", [128, 4], F32)
    selb = C.sb("selb", [128, 64], BF16)
    selT4s = [C.sb("selT4_%d" % i, [128, 512], BF16) for i in range(2)]
    for i in range(2):
        C.memset("pool", selT4s[i][64:128, :], 0.0, ["selT4_%d" % i])
    rsrows = [C.sb("rsrow%d" % i, [65, 512], F32) for i in range(3)]
    bcss = [C.sb("bcs%d" % i, [64, 512], F32) for i in range(3)]
    tmpos = [C.sb("tmpo%d" % i, [64, 512], F32) for i in range(3)]
    rec4s = [C.sb("rec4_%d" % i, [128, 4], F32) for i in range(3)]
    wbs = [C.sb("wb%d" % i, [128, 4, 64], F32) for i in range(3)]
    oacc = [C.sb("oacc%d" % i, [64, 512], F32) for i in range(2)]
    oaccb = [C.sb("oaccb%d" % i, [64, 512], BF16) for i in range(2)]
    PJ = [0, 7]
    SB_ = [1, 2, 3]
    pending = []

    def flush():
        tl = list(pending)
        del pending[:]
        while any(tl):
            for t in tl:
                if t:
                    t.pop(0)()

    def finalize_steps(po, pok, br, qb, first, rec_src=None):
        ai = qb % 2
        acc, ak = oacc[ai], "oacc%d" % ai
        rsrow, bcs, tmpo, rec4, wb = rsrows[br], bcss[br], tmpos[br], rec4s[br], wbs[br]
        rk, bk, tk, r4k, wk_ = "rsrow%d" % br, "bcs%d" % br, "tmpo%d" % br, "rec4_%d" % br, "wb%d" % br

        def s1():
            if rec_src is None:
                C.cp("dve", rsrow[64:65, :], po[64:65, :], [pok], [rk])
                pj, pk = C.bank("pj", PJ)
                C.mmlist([(pj[:, r:r + 1], rsrow[64:65, r * 128:(r + 1) * 128], C.onesf[64:65, 0:1], True, True) for r in range(4)],
                         [rk, "onesf"], [pk])
                C.ts("dve", rec4[:, :], pj[:, 0:4], 1e-30, ALU.add, [pk], [r4k])

        def s2():
            if rec_src is None:
                C.recip(rec4[:, :], rec4[:, :], [r4k], [r4k])
                recap, reckey = rec4, r4k
            else:
                recap, reckey = rec_src
            C.tt("dve", wb[:, :, :], recap[:, 0:4].unsqueeze(2).to_broadcast([128, 4, 64]),
                 Gtm[:, qb, br:12:3].unsqueeze(2).to_broadcast([128, 4, 64]), ALU.mult, [reckey, "Gtm_%d" % (qb // 4)], [wk_])

        def s3():
            pj2, pk2 = C.bank("pj", PJ)
            C.mmlist([(pj2[0:64, r * 128:(r + 1) * 128], wb[:, r, :], C.identf[:, :], True, True) for r in range(4)],
                     [wk_, "identf"], [pk2])
            C.cp("act", bcs[:, :], pj2[0:64, :], [pk2], [bk])

        def s4():
            if first:
                C.tt("dve", acc[:, :], po[0:64, :], bcs[:, :], ALU.mult, [pok, bk], [ak])
            else:
                C.tt("dve", tmpo[:, :], po[0:64, :], bcs[:, :], ALU.mult, [pok, bk], [tk])
                C.tt("dve", acc[:, :], acc[:, :], tmpo[:, :], ALU.add, [tk, ak], [ak])
        return [s1, s2, s3, s4]

    def qinfo(qb):
        return Qa[:, qb, :], ["Qa_%d" % (qb // 4), "Qa_aug"]

    def cmp_stage(qb):
        q_rhs, qkeys = qinfo(qb)
        pc = ptc[qb % 2]
        selT4, stk = selT4s[qb % 2], "selT4_%d" % (qb % 2)
        ntn = 1 if qb < 16 else 2
        po = C.banks[POC]
        for nt in range(ntn):
            sbk, sk = C.bank("s", SB_)
            full_vis = (nt == 0 and qb >= 17)
            items = [(sbk[:, :], Kc[:, nt * 128:(nt + 1) * 128], q_rhs, True, full_vis)]
            for r in range(0 if full_vis else 4):
                items.append((sbk[:, r * 128:(r + 1) * 128], C.ident[:, :],
                              maskc[:, nt * S + qb * 128:nt * S + (qb + 1) * 128], False, r == 3))
            C.mmlist(items, qkeys + ["Kc", "Kc_aug", "ident", "maskc_sb"], [sk])
            C.act(pc[nt][:], sbk[:, :], AF.Exp, [sk], ["ptc%d_%d" % (qb % 2, nt)], scale=SC_NSA)
        for nt in range(ntn):
            C.mm(po[:, :], Vc[:, nt, :], pc[nt][:], nt == 0, nt == ntn - 1,
                 ["Vc", "ptc%d_%d" % (qb % 2, nt)], ["bank%d" % POC])
        pj, pk = C.bank("pj", PJ)
        items = []
        for r in range(4):
            for nt in range(ntn):
                items.append((pj[:, r * 65:(r + 1) * 65], pc[nt][:, r * 128:(r + 1) * 128], ovl[:, nt * 65:(nt + 1) * 65],
                              nt == 0, nt == ntn - 1))
        C.mmlist(items, ["ovl_sb"] + ["ptc%d_%d" % (qb % 2, nt) for nt in range(ntn)], [pk])
        for r in range(4):
            C.ts("dve", rci[:, r:r + 1], pj[:, r * 65 + 64:r * 65 + 65], 1e-30, ALU.add, [pk], ["rci"])
        C.recip(rci[:, 0:4], rci[:, 0:4], ["rci"], ["rci"])
        for r in range(4):
            prev = selbias[:, qb * 64:(qb + 1) * 64] if r == 0 else imp[:]
            C.stt("dve", imp[:], pj[:, r * 65:r * 65 + 64], rci[:, r:r + 1], prev, ALU.mult, ALU.add,
                  [pk, "rci", "imp", "selbias_sb"], ["imp"])
        P.op("dve", lambda e: e.max(out=m8[:, 0:8], in_=imp[:]), ["imp"], ["m8"])
        P.op("dve", lambda e: e.match_replace(out=wk[:], in_to_replace=m8[:, 0:8], in_values=imp[:], imm_value=-1e9),
             ["imp", "m8"], ["wk"])
        P.op("dve", lambda e: e.max(out=m8[:, 8:16], in_=wk[:]), ["wk"], ["m8"])
        C.ts("dve", wk[:], imp[:], m8[:, 15:16], ALU.is_ge, ["imp", "m8"], ["wk"])
        C.ts("dve", selb[:], wk[:], -NEG, ALU.mult, ["wk"], ["selb"], s2=NEG, op1=ALU.add)

        def s0(qb=qb, selT4=selT4, stk=stk):
            C.tr(C.pst[0:64, 0:128], selb[:, :], C.ident[:, :], ["selb", "ident"], ["pst"])
            for r in range(4):
                C.cp("act" if r % 2 == 0 else "dve", selT4[0:64, r * 128:(r + 1) * 128], C.pst[0:64, 0:128], ["pst"], [stk])
        C.cp("dve", rec4s[0][:, :], rci[:, 0:4], ["rci"], ["rec4_0"])
        pending.append([s0] + finalize_steps(po, "bank%d" % POC, 0, qb, True, rec_src=(rec4s[0], "rec4_0")))

    def sel_stage(qb):
        q_rhs, qkeys = qinfo(qb)
        selT4, stk = selT4s[qb % 2], "selT4_%d" % (qb % 2)
        po = C.banks[POS]

        def score(kt):
            pairs = [(Ks[:, kt * 128:(kt + 1) * 128], q_rhs)]
            rd = qkeys + ["Ks_%d" % (kt // 4), "Ks_aug"]
            if kt == qb:
                pairs.append((C.ident[:, :], dm4[:, :]))
                rd += ["ident", "dm4_sb"]
            else:
                pairs.append((eall[:, kt * 128:(kt + 1) * 128], selT4[:, :]))
                rd += ["eall_sb", stk]
            return pairs, rd
        attn_loop(C, list(range(qb + 1)), score, SC_NSA, lambda kt: (Vs[:, kt, :], ["Vs_%d" % (kt // 4)]),
                  po, "bank%d" % POS, SB_, pts, "pt", after_first=None)

        def s5(qb=qb):
            ai = qb % 2
            C.cp("act", oaccb[ai][:, :], oacc[ai][:, :], ["oacc%d" % ai], ["oaccb%d" % ai])
            osink(qb, oaccb[ai], "oaccb%d" % ai)
        pending.append(finalize_steps(po, "bank%d" % POS, 1, qb, False) + [s5])

    def win_stage(qb):
        q_rhs, qkeys = qinfo(qb)
        po = C.banks[POW]
        k0 = max(0, qb - 4)

        def score(kt):
            pairs = [(Kw[:, kt * 128:(kt + 1) * 128], q_rhs)]
            rd = qkeys + ["Kw_%d" % (kt // 4), "Kw_aug"]
            if kt == qb:
                pairs.append((C.ident[:, :], dm4[:, :]))
                rd += ["ident", "dm4_sb"]
            if kt == qb - 4:
                pairs.append((C.ident[:, :], wm4[:, :]))
                rd += ["ident", "wm4_sb"]
            return pairs, rd
        attn_loop(C, list(range(k0, qb + 1)), score, SC_NSA, lambda kt: (Vw[:, kt, :], ["Vw_%d" % (kt // 4)]),
                  po, "bank%d" % POW, SB_, pts, "pt", after_first=flush)

        pending.append(finalize_steps(po, "bank%d" % POW, 2, qb, False))

    cmp_stage(0)
    for qb in range(32):
        win_stage(qb)
        if qb + 1 < 32:
            cmp_stage(qb + 1)
        sel_stage(qb)
    flush()


def phase_ffn(C, name, L, G, final, hown, hownkey, hhalo, hhalokey, oall, osink):
    P = C.P
    C.begin_phase(name)
    C.setup_norm()
    fl = C.flags
    g2 = C.load_const("g2_sb", L["g2"], [128, 8], F32)
    cw = C.load_const("cw_sb", L["cw"], [128, 176], F32)
    if final:
        gF = C.sb("gF_sb", [128, DM], F32)
        P.dma("pool", gF[:], G["gF"][0:1, :].partition_broadcast(128), writes=["gF_sb"])
    wup = C.sb("wup_sb", [128, 8, 5632], BF16)
    wup_src = L["wup"].rearrange("(k p) c -> p k c", p=128)
    for i0 in range(0, 22, 4):
        n4 = min(4, 22 - i0)
        for base in (i0, 22 + i0):
            P.dma("pool", wup[:, :, base * 128:(base + n4) * 128], wup_src[:, :, base * 128:(base + n4) * 128],
                  writes=["wup_%d" % fc for fc in range(base, base + n4)])
    wdn = C.sb("wdn_sb", [128, 22, DM], BF16)

    def load_wdn():
        for k in range(22):
            P.dma("pool", wdn[:, k, :], L["wdn"][k * 128:(k + 1) * 128, :], writes=["wdn_sb"])
    wo_d = L["wo"]

    NC_ = 256
    aT = [C.sb("aT%d" % i, [128, NC_], BF16) for i in range(4)]
    hm = C.sb("hm", [128, 2, DM], F32)
    n2T = C.sb("n2T", [128, 8, NC_], BF16)
    oTb = C.sb("oTb", [128, 8, NC_], BF16)
    oa = [C.sb("oa%d" % i, [128, NC_], BF16) for i in range(2)]
    ob = [C.sb("ob%d" % i, [128, NC_], BF16) for i in range(2)]
    wob = [C.sb("wob%d" % i, [128, DM], BF16) for i in range(4)]
    ubuf = [C.sb("ubuf%d" % i, [128, NC_ + 2], F32) for i in range(3)]
    tb = [C.sb("tb%d" % i, [128, NC_], F32) for i in range(4)]
    sg = C.sb("sg", [128, NC_], F32)
    carry = C.sb("carry", [128, 44, 2], F32)
    res = C.sb("res", [128, DM], F32)
    ss2 = C.sb("ss2", [128, 1], F32)

    ACC = [0, 1, 2, 3]
    UP = [4, 5, 6]

    def chunk(ci, halo):
        nt_ = 1 if halo else 2
        ncol = nt_ * 128
        c0 = 1920 if halo else ci * 256
        for k in range(8):
            b = C.rot("oa", 2)
            if halo:
                P.dma("sp", oa[b][:, 0:ncol], oall[0][k * 128:(k + 1) * 128, c0:c0 + ncol], reads=["oall0"], writes=["oa%d" % b])
                C.ts("dve", oTb[:, k, 0:ncol], oa[b][:, 0:ncol], fl[:, 1:2], ALU.mult, ["oa%d" % b, "flags"], ["oTb"])
            else:
                P.dma("sp", oa[b][:, 0:ncol], oall[0][k * 128:(k + 1) * 128, c0:c0 + ncol], reads=["oall0"], writes=["oa%d" % b])
                P.dma("sp", ob[b][:, 0:ncol], oall[1][k * 128:(k + 1) * 128, c0:c0 + ncol], reads=["oall1"], writes=["ob%d" % b])
                C.ts("dve", oa[b][:, 0:ncol], oa[b][:, 0:ncol], fl[:, 0:1], ALU.mult, ["oa%d" % b, "flags"], ["oa%d" % b])
                C.stt("dve", oTb[:, k, 0:ncol], ob[b][:, 0:ncol], fl[:, 1:2], oa[b][:, 0:ncol], ALU.mult, ALU.add,
                      ["oa%d" % b, "ob%d" % b, "flags"], ["oTb"])
        for k in range(8):
            wb = C.rot("wob", 4)
            P.dma("pool", wob[wb][:, :], wo_d[k * 128:(k + 1) * 128, :], writes=["wob%d" % wb])
            items = []
            for ti in range(nt_):
                for hf in range(2):
                    items.append((C.banks[ACC[ti * 2 + hf]][:, :], oTb[:, k, ti * 128:(ti + 1) * 128],
                                  wob[wb][:, hf * 512:(hf + 1) * 512], k == 0, k == 7))
            C.mmlist(items, ["oTb", "wob%d" % wb], ["bank%d" % ACC[i] for i in range(nt_ * 2)])
        if ci == 0 and not halo:
            load_wdn()
        for ti in range(nt_):
            hb = C.rot("ht", 2)
            if halo:
                P.dma("sp", C.ht[hb][:], hhalo, reads=[hhalokey], writes=["ht%d" % hb])
                C.ts("dve", C.ht[hb][:], C.ht[hb][:], fl[:, 1:2], ALU.mult, ["ht%d" % hb, "flags"], ["ht%d" % hb])
            else:
                P.dma("sp", C.ht[hb][:], hown(2 * ci + ti), reads=[hownkey(2 * ci + ti)], writes=["ht%d" % hb])
            for hf in range(2):
                C.tt("dve", hm[:, ti, hf * 512:(hf + 1) * 512], C.banks[ACC[ti * 2 + hf]][:, :],
                     C.ht[hb][:, hf * 512:(hf + 1) * 512], ALU.add, ["bank%d" % ACC[ti * 2 + hf], "ht%d" % hb], ["hm%d" % ti])
            C.norm_T(hm[:, ti, :], "hm%d" % ti, n2T, "n2T_%d" % ti, ti * 128, g2, "g2_sb")
        nkeys = ["n2T_%d" % ti for ti in range(nt_)]
        dq = []

        def down(i, ai):
            items = []
            for ti in range(nt_):
                for hf in range(2):
                    items.append((C.banks[ACC[ti * 2 + hf]][:, :], aT[ai][:, ti * 128:(ti + 1) * 128],
                                  wdn[:, i, hf * 512:(hf + 1) * 512], i == 0, i == 21))
            C.mmlist(items, ["aT%d" % ai, "wdn_sb"], ["bank%d" % ACC[q] for q in range(nt_ * 2)])
        for i in range(22):
            tfin = []
            for part in range(2):
                fc = i + 22 * part
                up, upk = C.bank("up", UP)
                if halo:
                    C.mmg(up[:, 0:2], [(wup[:, k, fc * 128:(fc + 1) * 128], n2T[:, k, ncol - 2:ncol]) for k in range(8)],
                          ["wup_%d" % fc] + nkeys, [upk])
                    C.cp("act", carry[:, fc, :], up[:, 0:2], [upk], ["carry%d" % fc])
                    continue
                C.mmg(up[:, 0:ncol], [(wup[:, k, fc * 128:(fc + 1) * 128], n2T[:, k, 0:ncol]) for k in range(8)],
                      ["wup_%d" % fc] + nkeys, [upk])
                ub = C.rot("ubuf", 3)
                u = ubuf[ub]
                uk = "ubuf%d" % ub
                C.cp("pool", u[:, 0:2], carry[:, fc, :], ["carry%d" % fc], [uk])
                C.cp("act", u[:, 2:2 + ncol], up[:, 0:ncol], [upk], [uk])
                if not halo:
                    ta = C.rot("tb", 4)
                    C.act(tb[ta][:, 0:ncol], up[:, 0:ncol], AF.Identity, [upk, "cw_sb"], ["tb%d" % ta],
                          bias=cw[:, fc * 4 + 3:fc * 4 + 4], scale=cw[:, fc * 4 + 2:fc * 4 + 3])
                    C.stt("dve", tb[ta][:, 0:ncol], u[:, 1:1 + ncol], cw[:, fc * 4 + 1:fc * 4 + 2], tb[ta][:, 0:ncol],
                          ALU.mult, ALU.add, [uk, "cw_sb", "tb%d" % ta], ["tb%d" % ta])
                    C.stt("dve", tb[ta][:, 0:ncol], u[:, 0:ncol], cw[:, fc * 4:fc * 4 + 1], tb[ta][:, 0:ncol],
                          ALU.mult, ALU.add, [uk, "cw_sb", "tb%d" % ta], ["tb%d" % ta])
                    tfin.append(ta)
                C.cp("pool", carry[:, fc, :], u[:, ncol:ncol + 2], [uk], ["carry%d" % fc])
            if not halo:
                C.act(sg[:, 0:ncol], tb[tfin[0]][:, 0:ncol], AF.Silu, ["tb%d" % tfin[0]], ["sg"])
                ai = C.rot("aT", 4)
                C.tt("dve", aT[ai][:, 0:ncol], sg[:, 0:ncol], tb[tfin[1]][:, 0:ncol], ALU.mult,
                     ["sg", "tb%d" % tfin[1]], ["aT%d" % ai])
                dq.append((i, ai))
                if len(dq) > 2:
                    down(*dq.pop(0))
        while dq:
            down(*dq.pop(0))
        if halo:
            return
        for ti in range(nt_):
            for hf in range(2):
                bk = ACC[ti * 2 + hf]
                C.tt("dve", res[:, hf * 512:(hf + 1) * 512], C.banks[bk][:, :], hm[:, ti, hf * 512:(hf + 1) * 512],
                     ALU.add, ["bank%d" % bk, "hm%d" % ti], ["res"])
            if final:
                C.memset("dve", ss2[:], 0.0, ["ss2"])
                C.act(C.junk[:], res[:], AF.Square, ["res", "ss2"], ["junk", "ss2"], accum=ss2[:])
                C.act(ss2[:], ss2[:], AF.Sqrt, ["ss2", "epsn"], ["ss2"], bias=C.epsn[:], scale=1.0 / DM)
                C.recip(ss2[:], ss2[:], ["ss2"], ["ss2"])
                C.stt("dve", res[:], res[:], ss2[:, 0:1], gF[:], ALU.mult, ALU.mult, ["res", "ss2", "gF_sb"], ["res"])
            osink(2 * ci + ti, res, "res")

    C.memset("pool", carry[:], 0.0, ["carry%d" % fc for fc in range(44)])
    chunk(0, True)
    for c in range(8):
        chunk(c, False)


LAYER_IN = [("wAm", [DM, 832]), ("gAm", [128, 8]), ("wq", [384, 768]), ("gq", [128, 3]), ("wkv", [256, 512]),
            ("gkv", [128, 2]), ("wAn", [DM, 780]), ("gAn", [128, 8]), ("w1k", [2048, 128]), ("w1v", [2048, 128]),
            ("w2k", [128, 64]), ("w2v", [128, 64]), ("posk", [128, 16]), ("posv", [128, 16]),
            ("wo", [DM, DM]), ("wup", [DM, 5632]), ("wdn", [2816, DM]), ("g2", [128, 8]), ("cw", [128, 176])]
GLOB_IN = [("ropeC", [32, S], F32), ("ropeS", [32, S], F32), ("dmask", [128, 2048], BF16),
           ("maskc", [128, 2 * S], BF16), ("eall", [64, S], BF16), ("selbias", [128, 2048], BF16),
           ("ovl", [128, 130], BF16), ("selg", [12, 768], F32), ("dm4", [128, 512], BF16), ("wm4", [128, 512], BF16),
           ("qaug", [4, 32 * 512], BF16), ("kaug", [4, S], BF16), ("kaugc", [4, 256], BF16), ("gF", [1, DM], F32)]
GROUPS = [[0, 1], [2, 3], [4, 5], [6, 7]]


def build_fused(nlayers=2):
    nc = bass.Bass("TRN2", target_bir_lowering=False)
    C = Ctx(nc)
    P = C.P
    x_d = C.dram("x", [S, DM], F32)
    xown_d = C.dram("xown", [2048, DM], F32)
    xhalo_d = C.dram("xhalo", [128, DM], F32)
    flags_d = C.dram("flags", [128, 2], F32)
    G = {n: C.dram(n, sh, dt) for (n, sh, dt) in GLOB_IN}
    Ls = [{n: C.dram("%s_%d" % (n, l), sh, F32) for (n, sh) in LAYER_IN} for l in range(nlayers)]
    out_d = C.dram("hout", [2048, DM], F32, out=True)
    P.dma("pool", C.flags[:], flags_d, writes=["flags"])

    omy = [[nc.dram_tensor("omy_%d_%d" % (l, c), [512, 2048], BF16) for c in range(2)] for l in range(nlayers)]
    oall = [[nc.dram_tensor("oall_%d_%d" % (l, c), [1024, 2048], BF16) for c in range(2)] for l in range(nlayers)]
    hmy = [nc.dram_tensor("hmy_%d" % j, [512, DM], F32) for j in range(4)]
    hall = [nc.dram_tensor("hall_%d" % j, [1024, DM], F32) for j in range(4)]

    for l in range(nlayers):
        if l == 0:
            hsrc = lambda g: x_d[g * 128:(g + 1) * 128, :]
            hkey = lambda g: "x"
        else:
            def hsrc(g):
                r, w = g // 16, g % 16
                return hall[w // 4][r * 512 + (w % 4) * 128:r * 512 + (w % 4) * 128 + 128, :]
            hkey = lambda g: "hall%d" % ((g % 16) // 4)

        def osink_mla(hh, tc, ot, otkey, l=l):
            c = tc // 4
            col = (tc % 4) * 512
            P.dma("sp", omy[l][c][hh * 64:(hh + 1) * 64, col:col + 512], ot[:, :], reads=[otkey], writes=["omy%d" % c])

        def osink_nsa(qb, ot, otkey, l=l):
            c = qb // 16
            col = (qb % 16) * 128
            P.dma("sp", omy[l][c][256:512, :].rearrange("(r d) t -> d r t", d=64)[:, :, col:col + 128],
                  ot[:, :].rearrange("p (r t) -> p r t", r=4), reads=[otkey], writes=["omy%d" % c])
            if qb % 16 == 15:
                P.cc("AllGather", GROUPS, omy[l][c].ap().opt(), oall[l][c].ap().opt(), reads=["omy%d" % c], writes=["oall%d" % c])

        pre = {}

        def prefetch_nsa(l=l, pre=pre):
            def ld(key, name, dram, nk, ncols):
                t = C.sb_top("np%d_%s" % (l, name), [128, nk, ncols], BF16)
                for k in range(nk):
                    P.dma("pool", t[:, k, :], dram[k * 128:(k + 1) * 128, :], writes=[key])
                pre[key] = t
            ld("wA_sb", "wA", Ls[l]["wAn"], 8, 780)
            ld("w1k_sb", "w1k", Ls[l]["w1k"], 16, 128)
            ld("w1v_sb", "w1v", Ls[l]["w1v"], 16, 128)
            t = C.sb_top("np%d_maskc" % l, [128, 2 * S], BF16)
            P.dma("pool", t[:], G["maskc"], writes=["maskc_sb"])
            pre["maskc_sb"] = t
            t = C.sb_top("np%d_selbias" % l, [128, 2048], BF16)
            P.dma("pool", t[:], G["selbias"], writes=["selbias_sb"])
            pre["selbias_sb"] = t
        phase_mla(C, "m%d_" % l, Ls[l], G, hsrc, hkey, osink_mla, after_weights=prefetch_nsa)
        phase_nsa(C, "n%d_" % l, Ls[l], G, hsrc, hkey, osink_nsa, pre=pre)
        final = (l == nlayers - 1)
        if l == 0:
            hown = lambda t: xown_d[t * 128:(t + 1) * 128, :]
            hownkey = lambda t: "xown"
            hhalo, hhalokey = xhalo_d, "xhalo"
        else:
            hown = lambda t: hmy[t // 4][(t % 4) * 128:(t % 4) * 128 + 128, :]
            hownkey = lambda t: "hmy%d" % (t // 4)
            hhalo, hhalokey = hall[3][384:512, :], "hall3"
        if final:
            def osink_ffn(t, res, rkey):
                P.dma("sp", out_d[t * 128:(t + 1) * 128, :], res[:], reads=[rkey])
        else:
            def osink_ffn(t, res, rkey):
                P.dma("sp", hmy[t // 4][(t % 4) * 128:(t % 4) * 128 + 128, :], res[:], reads=[rkey], writes=["hmy%d" % (t // 4)])
                if t % 4 == 3:
                    j = t // 4
                    P.cc("AllGather", GROUPS, hmy[j].ap().opt(), hall[j].ap().opt(), reads=["hmy%d" % j], writes=["hall%d" % j])
        phase_ffn(C, "f%d_" % l, Ls[l], G, final, hown, hownkey, hhalo, hhalokey, oall[l], osink_ffn)
    P.emit()
    return nc


def _pk(g, nk):
    return np.ascontiguousarray(np.asarray(g, np.float32).reshape(nk, 128).T)


def _consts():
    c = {}
    p = np.arange(128)[:, None]
    i512 = np.arange(512)[None, :]
    dm = np.zeros((128, 4, 512), np.float32)
    for m in range(4):
        dm[:, m, :] = np.where(128 * m + p <= i512, 0.0, NEG)
    c["dmask"] = dm.reshape(128, 2048).astype(NPBF)
    i128 = np.arange(128)[None, :]
    c["dm4"] = np.tile(np.where(p <= i128, 0.0, NEG), (1, 4)).astype(NPBF)
    c["wm4"] = np.tile(np.where(i128 < p, 0.0, NEG), (1, 4)).astype(NPBF)
    n = np.arange(256)[:, None]
    t = np.arange(S)[None, :]
    mc = np.where((t >= 16 * n + 31) & (n <= 254), 0.0, NEG).astype(np.float32)
    c["maskc"] = np.ascontiguousarray(mc.reshape(2, 128, S).transpose(1, 0, 2).reshape(128, 2 * S)).astype(NPBF)
    j = np.arange(64)[:, None]
    c["eall"] = (np.arange(S)[None, :] // 64 == j).astype(np.float32).astype(NPBF)
    tt_ = np.arange(S)
    cur = (tt_ // 64)[:, None]
    jj = np.arange(64)[None, :]
    sbias = np.zeros((S, 64), np.float32)
    sbias[np.broadcast_to(jj > cur, (S, 64))] = -1e4
    sbias[np.broadcast_to((jj == 0) | (jj == cur) | (jj == cur - 1), (S, 64))] = 1e4
    c["selbias"] = np.ascontiguousarray(sbias.reshape(32, 128, 64).transpose(1, 0, 2).reshape(128, 2048)).astype(NPBF)
    cs = (np.arange(256) * 16)[:, None]
    ss_ = (np.arange(64) * 64)[None, :]
    ov = ((cs < ss_ + 64) & (cs + 32 > ss_)).astype(np.float32)
    ov[255] = 0.0
    ov1 = np.concatenate([ov, np.ones((256, 1), np.float32)], 1)
    c["ovl"] = np.ascontiguousarray(ov1.reshape(2, 128, 65).transpose(1, 0, 2).reshape(128, 130)).astype(NPBF)
    sg = np.zeros((12, 12, 64), np.float32)
    for g in range(12):
        sg[g, g, :] = 1.0
    c["selg"] = sg.reshape(12, 768)
    k = np.arange(S)
    c["kaug"] = np.stack([np.ones(S), np.ones(S), k // 64, k % 64]).astype(np.float32).astype(NPBF)
    e = np.arange(256) * 16 + 31
    c["kaugc"] = np.stack([np.ones(256), np.ones(256), e // 64, e % 64]).astype(np.float32).astype(NPBF)
    inv = 1.0 / (10000.0 ** (np.arange(0, 32, 2, dtype=np.float32) / 32))
    ang = np.arange(S, dtype=np.float32)[:, None] * inv[None, :]
    cos, sin = np.cos(ang).T.astype(np.float32), np.sin(ang).T.astype(np.float32)
    c["ropeC"] = np.ascontiguousarray(np.concatenate([cos, cos], 0))
    c["ropeS"] = np.ascontiguousarray(np.concatenate([-sin, sin], 0))
    return c


def _qaug(group):
    slopes = np.exp2(-8.0 * np.arange(1, 9, dtype=np.float32) / 8)
    t = np.arange(S).reshape(32, 1, 128)
    out = np.zeros((4, 32, 4, 128), np.float32)
    for r in range(4):
        a = slopes[group * 4 + r] / SC_NSA
        out[0, :, r, :] = (-a * 64 * (t // 64))[:, 0, :]
        out[1, :, r, :] = (-a * (t % 64))[:, 0, :]
        out[2, :, r, :] = a * 64
        out[3, :, r, :] = a
    return out.reshape(4, 32 * 512).astype(NPBF)


_PROG = {}


def _prog(nlayers=2):
    if nlayers not in _PROG:
        _PROG[nlayers] = build_fused(nlayers)
    return _PROG[nlayers]


def _layer_maps(l, I, c):
    m = {}
    w_in, w_uq, w_ukv = I["w_in"][l], I["w_uq"][l], I["w_ukv"][l]
    sw = list(range(656, 672)) + list(range(640, 656))
    colsA = list(range(0, 640)) + list(range(0, 64)) + list(range(640, 672)) + list(range(0, 64)) + sw
    m["wAm"] = np.ascontiguousarray(w_in[:, colsA])
    m["gAm"] = _pk(I["attn_norm"][l], 8)
    qc, kc, vc = [], [], []
    for hh in range(4 * c, 4 * c + 4):
        base = 96 * hh
        nope = list(range(base, base + 64))
        rope = list(range(base + 64, base + 96))
        qc += nope + rope + nope + rope[16:] + rope[:16]
        kc += list(range(128 * hh, 128 * hh + 64))
        vc += list(range(128 * hh + 64, 128 * hh + 128))
    m["wq"] = np.ascontiguousarray(w_uq[:, qc])
    m["gq"] = _pk(I["q_norm"][l], 3)
    m["wkv"] = np.ascontiguousarray(w_ukv[:, kc + vc])
    m["gkv"] = _pk(I["kv_norm"][l], 2)
    g = c
    q0 = 672 + 256 * g
    o = 1184
    rng = lambda a: list(range(a, a + 64))
    kcc, vcc, ksc = rng(o + 64 * g), rng(o + 128 + 64 * g), rng(o + 256 + 64 * g)
    vsc, kwc, vwc = rng(o + 384 + 64 * g), rng(o + 512 + 64 * g), rng(o + 640 + 64 * g)
    gtc = list(range(1952 + 12 * g, 1952 + 12 * g + 12))
    cols = list(range(q0, q0 + 256)) + kcc + kcc + vcc + vcc + ksc + kwc + vsc + vwc + gtc
    m["wAn"] = np.ascontiguousarray(w_in[:, cols])
    m["gAn"] = m["gAm"]
    posT = lambda pz: np.ascontiguousarray(np.asarray(pz, np.float32).reshape(16, 128).T)
    m["w1k"] = np.ascontiguousarray(I["cmp_k_w1"][l].reshape(2048, 128))
    m["w1v"] = np.ascontiguousarray(I["cmp_v_w1"][l].reshape(2048, 128))
    m["w2k"] = np.ascontiguousarray(I["cmp_k_w2"][l])
    m["w2v"] = np.ascontiguousarray(I["cmp_v_w2"][l])
    m["posk"] = posT(I["cmp_pos_k"][l])
    m["posv"] = posT(I["cmp_pos_v"][l])
    perm = list(range(0, 256)) + list(range(512, 768)) + list(range(256, 512)) + list(range(768, 1024))
    m["wo"] = np.ascontiguousarray(I["w_o"][l][perm, :])
    m["wup"] = np.ascontiguousarray(I["w_up"][l])
    m["wdn"] = np.ascontiguousarray(I["w_down"][l])
    m["g2"] = _pk(I["ffn_norm"][l], 8)
    cwv = np.stack([I["conv_w"][l][0], I["conv_w"][l][1], I["conv_w"][l][2], I["conv_b"][l]], -1)
    m["cw"] = np.ascontiguousarray(cwv.reshape(44, 128, 4).transpose(1, 0, 2).reshape(128, 176)).astype(np.float32)
    return m


def make_maps(I, nlayers=2):
    cst = _consts()
    lm = [[_layer_maps(l, I, c) for c in range(2)] for l in range(nlayers)]
    maps = []
    for b in range(4):
        for c in range(2):
            m = {"x": np.ascontiguousarray(I["x"][b]),
                 "xown": np.ascontiguousarray(I["x"][b][2048 * c:2048 * c + 2048]),
                 "xhalo": np.ascontiguousarray(I["x"][b][1920:2048]),
                 "flags": np.ascontiguousarray(np.tile(np.array([[1.0 - c, float(c)]], np.float32), (128, 1)))}
            for (n, sh, dt) in GLOB_IN:
                if n == "qaug":
                    m[n] = _qaug(c)
                elif n == "gF":
                    m[n] = np.ascontiguousarray(np.asarray(I["final_norm"], np.float32).reshape(1, DM))
                else:
                    m[n] = cst[n]
            for l in range(nlayers):
                for k, v in lm[l][c].items():
                    m["%s_%d" % (k, l)] = v
            maps.append(m)
    return maps


def kernel(**inputs):
    I = {k: np.asarray(v, dtype=np.float32) for k, v in inputs.items()}
    res = run_bass_kernel_spmd(_prog(2), make_maps(I, 2), core_ids=list(range(8))).results
    out = np.empty((4, S, DM), np.float32)
    for b in range(4):
        for c in range(2):
            out[b, 2048 * c:2048 * c + 2048] = np.asarray(res[2 * b + c]["hout"])
    return out
```
